# Optimizing a Trainium2 kernel written in Bass

```python
import jax, jax.numpy as jnp
from jax import lax
import numpy as np

D_MODEL = 1024
BATCH = 32
SEQ = 2048
DEPTH = 2

N_MIXERS = 2
N_A_LAYERS = (DEPTH + 1) // 2
N_B_LAYERS = DEPTH // 2
RMS_EPS = 1e-6
LRU_WIDTH = D_MODEL
LRU_HEADS = 4
LRU_BLOCK = LRU_WIDTH // LRU_HEADS
CONV_WIDTH = 4
LRU_C = 8.0
RWKV_HEAD = 64
RWKV_HEADS = D_MODEL // RWKV_HEAD
DECAY_LORA = 64
AAA_LORA = 64
GATE_LORA = 128
RWKV_GN_EPS = 64e-5
MEM_LEN = 256
MEM_HEADS = 4
MEM_HEAD_DIM = D_MODEL // MEM_HEADS
D_FF = 4 * D_MODEL

kernel_name = "hybrid_rglru_rwkv7_memxattn"


def rms_norm(x, g):
    xf = x.astype(jnp.float32)
    y = xf * lax.rsqrt(jnp.mean(xf * xf, axis=-1, keepdims=True) + RMS_EPS)
    return (y * g.astype(jnp.float32)).astype(x.dtype)


def _lru_combine(c1, c2):
    a1, b1 = c1
    a2, b2 = c2
    return a1 * a2, a2 * b1 + b2


def rglru_mixer(x, conv_w, conv_b, w_in, b_in, gate_w, gate_b, lam, w_out, b_out):
    B, S, _ = x.shape
    proj = x @ w_in + b_in
    y_branch, u = jnp.split(proj, 2, axis=-1)
    y_branch = jax.nn.gelu(y_branch, approximate=True)
    u_pad = jnp.pad(u, ((0, 0), (CONV_WIDTH - 1, 0), (0, 0)))
    conv = conv_b + u_pad[:, 0:S] * conv_w[0]
    for tap in range(1, CONV_WIDTH):
        conv = conv + u_pad[:, tap:tap + S] * conv_w[tap]
    ub = conv.reshape(B, S, LRU_HEADS, LRU_BLOCK)
    gates = jax.nn.sigmoid(jnp.einsum('bshi,ghij->gbshj', ub, gate_w) + gate_b[:, None, None])
    r_gate = gates[0].reshape(B, S, LRU_WIDTH).astype(jnp.float32)
    i_gate = gates[1].reshape(B, S, LRU_WIDTH).astype(jnp.float32)
    log_a = -LRU_C * r_gate * jax.nn.softplus(-lam.astype(jnp.float32))
    a = jnp.exp(log_a)
    mult = jnp.sqrt(-jnp.expm1(2.0 * log_a))
    b = mult * i_gate * conv.astype(jnp.float32)
    _, h = lax.associative_scan(_lru_combine, (a, b), axis=1)
    return (h.astype(x.dtype) * y_branch) @ w_out + b_out


def rwkv7_mixer(x, mu, w_rkv, w0, w1, w2, a0, a1, a2, g1, g2, k_k, k_a, r_k, gn_g, gn_b, w_o):
    B, S, D = x.shape
    H, N = RWKV_HEADS, RWKV_HEAD
    x_prev = jnp.pad(x, ((0, 0), (1, 0), (0, 0)))[:, :S]
    xx = x_prev - x
    r = (x + xx * mu[0]) @ w_rkv[0]
    xw = x + xx * mu[1]
    k = (x + xx * mu[2]) @ w_rkv[1]
    v = (x + xx * mu[3]) @ w_rkv[2]
    xa = x + xx * mu[4]
    xg = x + xx * mu[5]
    w_log = -jax.nn.softplus(-(w0 + jnp.tanh(xw @ w1) @ w2).astype(jnp.float32)) - 0.5
    decay = jnp.exp(-jnp.exp(w_log))
    a = jax.nn.sigmoid(a0 + (xa @ a1) @ a2)
    g = jax.nn.sigmoid(xg @ g1) @ g2
    kk = (k * k_k).reshape(B, S, H, N).astype(jnp.float32)
    kk = kk / jnp.maximum(jnp.linalg.norm(kk, axis=-1, keepdims=True), 1e-12)
    k = k * (1.0 + (a - 1.0) * k_a)

    rh = r.reshape(B, S, H, N).astype(jnp.float32)
    kh = k.reshape(B, S, H, N).astype(jnp.float32)
    vh = v.reshape(B, S, H, N).astype(jnp.float32)
    wh = decay.reshape(B, S, H, N)
    ah = a.reshape(B, S, H, N).astype(jnp.float32)
    rem_a = -kk
    rem_b = kk * ah

    def step(state, inp):
        r_t, w_t, k_t, v_t, a_t, b_t = inp
        sa = jnp.einsum('bhij,bhj->bhi', state, a_t)
        state = (state * w_t[:, :, None, :] + sa[..., None] * b_t[:, :, None, :]
                 + v_t[..., None] * k_t[:, :, None, :])
        y_t = jnp.einsum('bhij,bhj->bhi', state, r_t)
        return state, y_t

    seq_inputs = tuple(jnp.moveaxis(t, 1, 0) for t in (rh, wh, kh, vh, rem_a, rem_b))
    state0 = jnp.zeros((B, H, N, N), jnp.float32)
    _, ys = lax.scan(step, state0, seq_inputs)
    y = jnp.moveaxis(ys, 0, 1)
    mean = jnp.mean(y, axis=-1, keepdims=True)
    var = jnp.mean(jnp.square(y - mean), axis=-1, keepdims=True)
    yn = ((y - mean) * lax.rsqrt(var + RWKV_GN_EPS)).reshape(B, S, D)
    yn = yn * gn_g.astype(jnp.float32) + gn_b.astype(jnp.float32)
    bonus = jnp.sum(rh * kh * r_k.astype(jnp.float32), axis=-1, keepdims=True) * vh
    out = (yn + bonus.reshape(B, S, D)).astype(x.dtype)
    return (out * g) @ w_o


def mem_cross_attention(h, mem_n, w_q, w_kv, w_o):
    B, S, D = h.shape
    M = mem_n.shape[1]
    q = (h @ w_q).reshape(B, S, MEM_HEADS, MEM_HEAD_DIM)
    kv = (mem_n @ w_kv).reshape(B, M, 2, MEM_HEADS, MEM_HEAD_DIM)
    k, v = kv[:, :, 0], kv[:, :, 1]
    s = jnp.einsum('bqhd,bkhd->bhqk', q, k).astype(jnp.float32) * (MEM_HEAD_DIM ** -0.5)
    p = jax.nn.softmax(s, axis=-1).astype(h.dtype)
    o = jnp.einsum('bhqk,bkhd->bqhd', p, v).reshape(B, S, D)
    return o @ w_o


def sqrelu_mlp(h, w_up, w_down):
    return jnp.square(jax.nn.relu(h @ w_up)) @ w_down


def setup_inputs(seed: int = 0) -> dict:
    key = jax.random.key(seed)
    ks = iter(jax.random.split(key, 48))
    f32 = jnp.float32

    def nrm(shape, scale):
        return jax.random.normal(next(ks), shape, f32) * scale

    def unif(shape, lo, hi):
        return jax.random.uniform(next(ks), shape, f32, lo, hi)

    D, NA, NB = D_MODEL, N_A_LAYERS, N_B_LAYERS
    x = nrm((BATCH, SEQ, D), 1.0)
    mem = nrm((BATCH, MEM_LEN, D), 1.0)
    ln_gains = 1.0 + nrm((DEPTH, 6, D), 0.05)
    mem_norm = 1.0 + nrm((D,), 0.05)
    a_conv_w = nrm((NA, CONV_WIDTH, LRU_WIDTH), CONV_WIDTH ** -0.5)
    a_conv_b = nrm((NA, LRU_WIDTH), 0.01)
    a_w_in = nrm((NA, D, 2 * LRU_WIDTH), D ** -0.5)
    a_b_in = nrm((NA, 2 * LRU_WIDTH), 0.01)
    a_gate_w = nrm((NA, 2, LRU_HEADS, LRU_BLOCK, LRU_BLOCK), LRU_BLOCK ** -0.5)
    a_gate_b = nrm((NA, 2, LRU_HEADS, LRU_BLOCK), 0.01)
    u = unif((NA, LRU_WIDTH), 0.81, 0.998)
    sp = -0.5 * jnp.log(u)
    a_lambda = -jnp.log(jnp.expm1(sp))
    a_w_out = nrm((NA, LRU_WIDTH, D), LRU_WIDTH ** -0.5)
    a_b_out = nrm((NA, D), 0.01)
    b_mu = unif((NB, 6, D), 0.0, 1.0)
    b_w_rkv = nrm((NB, 3, D, D), D ** -0.5)
    b_w0 = unif((NB, D), -6.0, -1.0)
    b_w1 = nrm((NB, D, DECAY_LORA), D ** -0.5)
    b_w2 = nrm((NB, DECAY_LORA, D), 0.1 * DECAY_LORA ** -0.5)
    b_a0 = nrm((NB, D), 0.1)
    b_a1 = nrm((NB, D, AAA_LORA), D ** -0.5)
    b_a2 = nrm((NB, AAA_LORA, D), 0.1 * AAA_LORA ** -0.5)
    b_g1 = nrm((NB, D, GATE_LORA), D ** -0.5)
    b_g2 = nrm((NB, GATE_LORA, D), GATE_LORA ** -0.5)
    b_k_k = 0.85 + nrm((NB, D), 0.05)
    b_k_a = 1.0 + nrm((NB, D), 0.05)
    b_r_k = nrm((NB, RWKV_HEADS, RWKV_HEAD), 0.1)
    b_gn_g = 1.0 + nrm((NB, D), 0.05)
    b_gn_b = nrm((NB, D), 0.01)
    b_w_o = nrm((NB, D, D), D ** -0.5)
    c_w_q = nrm((DEPTH, D, D), D ** -0.5)
    c_w_kv = nrm((DEPTH, D, 2 * D), D ** -0.5)
    c_w_o = nrm((DEPTH, D, D), D ** -0.5)
    m_w_up = nrm((DEPTH, D, D_FF), D ** -0.5)
    m_w_down = nrm((DEPTH, D_FF, D), D_FF ** -0.5)
    return {"x": x, "mem": mem, "ln_gains": ln_gains, "mem_norm": mem_norm,
            "a_conv_w": a_conv_w, "a_conv_b": a_conv_b, "a_w_in": a_w_in, "a_b_in": a_b_in,
            "a_gate_w": a_gate_w, "a_gate_b": a_gate_b, "a_lambda": a_lambda,
            "a_w_out": a_w_out, "a_b_out": a_b_out,
            "b_mu": b_mu, "b_w_rkv": b_w_rkv, "b_w0": b_w0, "b_w1": b_w1, "b_w2": b_w2,
            "b_a0": b_a0, "b_a1": b_a1, "b_a2": b_a2, "b_g1": b_g1, "b_g2": b_g2,
            "b_k_k": b_k_k, "b_k_a": b_k_a, "b_r_k": b_r_k, "b_gn_g": b_gn_g, "b_gn_b": b_gn_b,
            "b_w_o": b_w_o,
            "c_w_q": c_w_q, "c_w_kv": c_w_kv, "c_w_o": c_w_o,
            "m_w_up": m_w_up, "m_w_down": m_w_down}


def reference(x, mem, ln_gains, mem_norm,
              a_conv_w, a_conv_b, a_w_in, a_b_in, a_gate_w, a_gate_b, a_lambda, a_w_out, a_b_out,
              b_mu, b_w_rkv, b_w0, b_w1, b_w2, b_a0, b_a1, b_a2, b_g1, b_g2,
              b_k_k, b_k_a, b_r_k, b_gn_g, b_gn_b, b_w_o,
              c_w_q, c_w_kv, c_w_o, m_w_up, m_w_down):
    mem_n = rms_norm(mem, mem_norm)
    for i in range(DEPTH):
        g = ln_gains[i]
        j = i // N_MIXERS
        hn = rms_norm(x, g[0])
        if i % N_MIXERS == 0:
            t = rglru_mixer(hn, a_conv_w[j], a_conv_b[j], a_w_in[j], a_b_in[j], a_gate_w[j],
                            a_gate_b[j], a_lambda[j], a_w_out[j], a_b_out[j])
        else:
            t = rwkv7_mixer(hn, b_mu[j], b_w_rkv[j], b_w0[j], b_w1[j], b_w2[j], b_a0[j],
                            b_a1[j], b_a2[j], b_g1[j], b_g2[j], b_k_k[j], b_k_a[j], b_r_k[j],
                            b_gn_g[j], b_gn_b[j], b_w_o[j])
        x = x + rms_norm(t, g[1])
        c = mem_cross_attention(rms_norm(x, g[2]), mem_n, c_w_q[i], c_w_kv[i], c_w_o[i])
        x = x + rms_norm(c, g[3])
        m = sqrelu_mlp(rms_norm(x, g[4]), m_w_up[i], m_w_down[i])
        x = x + rms_norm(m, g[5])
    return x
```

```python
import numpy as np
from contextlib import ExitStack
import concourse.bass as bass
import concourse.mybir as mybir
from concourse.bass_utils import run_bass_kernel_spmd

F32 = mybir.dt.float32
BF16 = mybir.dt.bfloat16
FP16 = mybir.dt.float16
AF = mybir.ActivationFunctionType
ALU = mybir.AluOpType
AX = mybir.AxisListType
import os as _os0
TDT = mybir.dt.float32r if _os0.environ.get('TDT', 'r') == 'r' else mybir.dt.float32

D = 1024
T = 512
MEM = 256
PW = 4096
NRING = 3

VEC_ORDER = [('ln_gains', 12), ('mem_norm', 1), ('a_conv_w', 4), ('a_conv_b', 1), ('a_b_in', 2), ('a_gate_b', 2),
             ('a_lambda', 1), ('a_b_out', 1), ('b_mu', 6), ('b_w0', 1), ('b_a0', 1), ('b_k_k', 1), ('b_k_a', 1),
             ('b_r_k', 1), ('b_gn_g', 1), ('b_gn_b', 1)]
VOFF = {}
_o = 0
for _n, _c in VEC_ORDER:
    VOFF[_n] = _o
    _o += _c
NVEC = _o


def pack_vecs(inp):
    rows = [np.asarray(inp[n], np.float32).reshape(-1) for n, _ in VEC_ORDER]
    v = np.concatenate(rows)
    assert v.size == NVEC * D
    return np.ascontiguousarray(v.reshape(NVEC * 8, 128))


def _mat_pieces(W, MW):
    K, N = W.shape
    KC = K // 128
    NPc = N // MW
    a = W.reshape(KC, 128, NPc, MW).transpose(2, 1, 0, 3).reshape(NPc, 128, KC * MW)
    if KC * MW < PW:
        a = np.concatenate([a, np.zeros((NPc, 128, PW - KC * MW), np.float32)], axis=2)
    return a


PIECES = {}
PGAIN = []


def _layout():
    PIECES.clear()
    PGAIN.clear()

    def add(name, cnt, gain, KC, MW):
        PIECES[name] = (len(PGAIN), cnt)
        for _ in range(cnt):
            PGAIN.append((None if gain is None else [(0, MW, ('VT', gain))], KC, MW))

    g = VOFF['ln_gains']
    add('w_in', 4, g + 0, 8, 512)
    add('gates', 1, None, 16, 256)
    add('a_w_out', 2, None, 8, 512)
    for l in range(2):
        add(f'wq{l}', 2, g + 6 * l + 2, 8, 512)
        add(f'wkv{l}', 4, VOFF['mem_norm'], 8, 512)
        add(f'wo{l}', 2, None, 8, 512)
        add(f'up{l}', 8, g + 6 * l + 4, 8, 512)
        add(f'down{l}', 8, None, 32, 128)
    for var in range(2):
        PIECES['rkv' + 'AB'[var]] = (len(PGAIN), 6)
        for mix in (0, 0, 2, 2, 3, 3):
            PGAIN.append(([(0, 512, ('DV', 2 * mix + var))], 8, 512))
    for var in range(2):
        PIECES['loraA' + 'ab'[var]] = (len(PGAIN), 1)
        PGAIN.append(([(0, 64, ('DV', 2 * 1 + var)), (64, 128, ('DV', 2 * 4 + var)), (128, 256, ('DV', 2 * 5 + var))], 8, 256))
    add('loraB', 1, None, 3, 1024)
    add('b_w_o', 2, None, 8, 512)


_layout()
NPIECE = len(PGAIN)


def pack_weights(inp):
    f = lambda k: np.asarray(inp[k], np.float32)
    out = np.zeros((NPIECE, 128, PW), np.float32)

    def put(name, arr):
        i0, cnt = PIECES[name]
        assert arr.shape[0] == cnt, (name, arr.shape)
        out[i0:i0 + cnt] = arr

    put('w_in', _mat_pieces(f('a_w_in')[0], 512))
    gw = f('a_gate_w')[0].reshape(8, 2, 128, 256)
    put('gates', gw.transpose(2, 0, 1, 3).reshape(1, 128, 8 * 2 * 256))
    put('a_w_out', _mat_pieces(f('a_w_out')[0], 512))
    for l in range(2):
        put(f'wq{l}', _mat_pieces(f('c_w_q')[l], 512))
        put(f'wkv{l}', _mat_pieces(f('c_w_kv')[l], 512))
        put(f'wo{l}', _mat_pieces(f('c_w_o')[l], 512))
        put(f'up{l}', _mat_pieces(f('m_w_up')[l], 512))
        put(f'down{l}', _mat_pieces(f('m_w_down')[l], 128))
    rkv = f('b_w_rkv')[0]
    rk6 = np.concatenate([_mat_pieces(rkv[i], 512) for i in range(3)], axis=0)
    put('rkvA', rk6)
    put('rkvB', rk6)
    la = np.concatenate([f('b_w1')[0], f('b_a1')[0], f('b_g1')[0]], axis=1)
    put('loraAa', _mat_pieces(la, 256))
    put('loraAb', _mat_pieces(la, 256))
    lb = np.zeros((128, 3, 1024), np.float32)
    lb[:64, 0] = f('b_w2')[0]
    lb[64:, 1] = f('b_a2')[0]
    lb[:, 2] = f('b_g2')[0]
    put('loraB', np.concatenate([lb.reshape(1, 128, 3072), np.zeros((1, 128, PW - 3072), np.float32)], axis=2))
    put('b_w_o', _mat_pieces(f('b_w_o')[0], 512))
    return out


class _Rec:
    def __init__(self):
        self.name = None

    def __getattr__(self, name):
        def f(*args, **kwargs):
            self.name, self.args, self.kwargs = name, args, kwargs
            return self
        return f


class Sched:
    ENG = ('pe', 'act', 'dve', 'pool', 'sp')

    def __init__(self, nc, es):
        self.nc = nc
        self.es = es
        self.ops = {e: [] for e in self.ENG}
        self.sem = {e: es.enter_context(nc.semaphore('s_' + e)) for e in self.ENG}
        self.cnt = {e: 0 for e in self.ENG}
        self.known = {e: {} for e in self.ENG}
        self.last_w = {}
        self.reads = {}
        self.dsem = {}
        self.dcnt = {}
        self.nops = 0

    def _deps(self, eng, reads, writes):
        acc = {}

        def need(dep):
            s, v = dep
            if acc.get(s, 0) < v:
                acc[s] = v
        for b in reads:
            w = self.last_w.get(b)
            if w:
                need(w)
        for b in writes:
            w = self.last_w.get(b)
            if w:
                need(w)
            for r in self.reads.get(b, {}).items():
                need(r)
        for s, v in acc.items():
            if self.known[eng].get(s, 0) >= v:
                continue
            if s == eng and eng in ('pe', 'sp'):
                continue
            if s in self.cnt:
                assert v <= self.cnt[s], f"wait on unsignaled {s} {v} > {self.cnt[s]}"
                semh = self.sem[s]
            else:
                semh = self.dsem[s]
            self.known[eng][s] = v
            self.ops[eng].append(lambda e, semh=semh, v=v: e.wait_ge(semh, v))

    def _record(self, reads, writes, tag):
        for b in reads:
            d = self.reads.setdefault(b, {})
            if d.get(tag[0], 0) < tag[1]:
                d[tag[0]] = tag[1]
        for b in writes:
            self.last_w[b] = tag
            self.reads[b] = {}

    def op(self, eng, fn, reads=(), writes=(), signal=True):
        self.nops += 1
        self._deps(eng, reads, writes)
        val = self.cnt[eng] + 1
        rec = _Rec()
        fn(rec)
        assert rec.name is not None
        if signal:
            self.cnt[eng] += 1
            semh = self.sem[eng]
            self.ops[eng].append(lambda e, r=rec, semh=semh: getattr(e, r.name)(*r.args, **r.kwargs).then_inc(semh, 1))
        else:
            self.ops[eng].append(lambda e, r=rec: getattr(e, r.name)(*r.args, **r.kwargs))
        self._record(reads, writes, (eng, val))

    def dma(self, eng, out, in_, reads=(), writes=(), key=None):
        self.nops += 1
        self._deps(eng, reads, writes)
        if key not in self.dsem:
            self.dsem[key] = self.es.enter_context(self.nc.semaphore('d_' + key))
            self.dcnt[key] = 0
        self.dcnt[key] += 16
        semh = self.dsem[key]
        self.ops[eng].append(lambda e, out=out, in_=in_, semh=semh: e.dma_start(out=out, in_=in_).then_inc(semh, 16))
        self._record(reads, writes, (key, self.dcnt[key]))

    def barrier(self):
        snap = dict(self.cnt)
        dsnap = dict(self.dcnt)
        for eng in self.ENG:
            for s, v in snap.items():
                if s == eng or v == 0 or self.known[eng].get(s, 0) >= v:
                    continue
                self.known[eng][s] = v
                self.ops[eng].append(lambda e, semh=self.sem[s], v=v: e.wait_ge(semh, v))
            for s, v in dsnap.items():
                if self.known[eng].get(s, 0) >= v:
                    continue
                self.known[eng][s] = v
                self.ops[eng].append(lambda e, semh=self.dsem[s], v=v: e.wait_ge(semh, v))

    def finish(self, eng, keys):
        acc = {}
        for b in keys:
            w = self.last_w.get(b)
            if w and acc.get(w[0], 0) < w[1]:
                acc[w[0]] = w[1]
        for s, v in acc.items():
            semh = self.sem[s] if s in self.cnt else self.dsem[s]
            self.ops[eng].append(lambda e, semh=semh, v=v: e.wait_ge(semh, v))

    def emit(self):
        with self.nc.Block() as block:
            @block.tensor
            def _(e):
                for f in self.ops['pe']:
                    f(e)

            @block.scalar
            def _(e):
                for f in self.ops['act']:
                    f(e)

            @block.vector
            def _(e):
                for f in self.ops['dve']:
                    f(e)

            @block.gpsimd
            def _(e):
                for f in self.ops['pool']:
                    f(e)

            @block.sync
            def _(e):
                for f in self.ops['sp']:
                    f(e)


STAGES = ['load', 'A', 'C0', 'M0', 'B', 'C1', 'M1']
BSTOP = [9]


def build(NB, S, stop='M1', use_gelu=True):
    NT = S // T
    nstage = STAGES.index(stop)
    nc = bass.Bass("TRN2", target_bir_lowering=False)
    x_d = nc.dram_tensor("x", [NB * S, D], F32, kind="ExternalInput").ap()
    mem_d = nc.dram_tensor("mem", [NB * MEM, D], F32, kind="ExternalInput").ap()
    vec_d = nc.dram_tensor("vecs", [NVEC * 8, 128], F32, kind="ExternalInput").ap()
    wts_d = nc.dram_tensor("wts", [NPIECE, 128, PW], F32, kind="ExternalInput").ap()
    msk_d = nc.dram_tensor("masks", [128, 8 * 128], F32, kind="ExternalInput").ap()
    y_d = nc.dram_tensor("y", [NB * S, D], F32, kind="ExternalOutput").ap()
    wsc = nc.dram_tensor("wsc", [NPIECE, 128, PW], BF16, kind="Internal").ap()

    with ExitStack() as es:
        S_ = Sched(nc, es)
        op = S_.op

        def sb(name, shape, dt):
            return es.enter_context(nc.sbuf_tensor(name, shape, dt))

        VT = sb("VT", [128, NVEC * 8], F32)
        ident = sb("ident", [128, 128], F32)
        identb = sb("identb", [128, 128], BF16)
        onesb = sb("onesb", [128, 128], BF16)
        CV2 = sb("CV2", [128, 16], F32)
        DV = sb("DV", [128, 13 * 8], F32)
        ps = [es.enter_context(nc.psum_tensor(f"ps{i}", [128, 512], F32)) for i in range(8)]
        bank_ctr = [0]

        reserved = set()

        def nbank():
            for _ in range(17):
                b = bank_ctr[0] % 8
                bank_ctr[0] += 1
                if b not in reserved:
                    return b
            raise RuntimeError('no free PSUM bank')

        def vcol(name, idx=0, c=0):
            j = (VOFF[name] + idx) * 8 + c
            return VT[:, j:j + 1]

        op('pool', lambda e: e.memset(ident[:], 0.0), writes=['ident'])
        op('pool', lambda e: e.affine_select(out=ident[:], in_=ident[:], pattern=[[-1, 128]], base=0,
                                             channel_multiplier=1, compare_op=ALU.not_equal, fill=1.0),
           reads=['ident'], writes=['ident'])
        op('pool', lambda e: e.tensor_copy(out=identb[:], in_=ident[:]), reads=['ident'], writes=['identb'])
        op('pool', lambda e: e.memset(onesb[:], 1.0), writes=['onesb'])

        with ExitStack() as es0:
            def sb0(name, shape, dt):
                return es0.enter_context(nc.sbuf_tensor(name, shape, dt))
            vst = [sb0(f"vst{i}", [128, 128], F32) for i in range(3)]
            nrows = NVEC * 8
            for i in range(3):
                r0 = i * 128
                r1 = min(nrows, r0 + 128)
                n = r1 - r0
                S_.dma('sp', vst[i][0:n, :], vec_d[r0:r1, :], writes=[f'vst{i}'], key=f'vst{i}')
                b = nbank()
                op('pe', lambda e, i=i, n=n, b=b: e.transpose(out=ps[b][:, 0:n], in_=vst[i][0:n, :], identity=ident[0:n, 0:n]),
                   reads=[f'vst{i}', 'ident'], writes=[f'ps{b}'])
                op('act', lambda e, r0=r0, n=n, b=b: e.activation(out=VT[:, r0:r0 + n], in_=ps[b][:, 0:n], func=AF.Copy),
                   reads=[f'ps{b}'], writes=['VT'])
            lam = VT[:, VOFF['a_lambda'] * 8:VOFF['a_lambda'] * 8 + 8]
            op('act', lambda e: e.activation(out=CV2[:, 0:8], in_=lam, func=AF.Exp, scale=-1.0), reads=['VT'], writes=['CV2'])
            op('act', lambda e: e.activation(out=CV2[:, 0:8], in_=CV2[:, 0:8], func=AF.Ln, bias=1.0), reads=['CV2'], writes=['CV2'])
            op('act', lambda e: e.activation(out=CV2[:, 0:8], in_=CV2[:, 0:8], func=AF.Copy, scale=-8.0), reads=['CV2'], writes=['CV2'])

            g6 = VT[:, (VOFF['ln_gains'] + 6) * 8:(VOFF['ln_gains'] + 6) * 8 + 8]
            for mi in range(6):
                mu_i = VT[:, (VOFF['b_mu'] + mi) * 8:(VOFF['b_mu'] + mi) * 8 + 8]
                op('dve', lambda e, mi=mi, mu_i=mu_i: e.tensor_tensor(out=DV[:, (2 * mi + 1) * 8:(2 * mi + 2) * 8], in0=mu_i, in1=g6, op=ALU.mult),
                   reads=['VT'], writes=['DV'])
                op('dve', lambda e, mi=mi: e.tensor_tensor(out=DV[:, (2 * mi) * 8:(2 * mi + 1) * 8], in0=g6, in1=DV[:, (2 * mi + 1) * 8:(2 * mi + 2) * 8], op=ALU.subtract),
                   reads=['VT', 'DV'], writes=['DV'])
            ka = VT[:, VOFF['b_k_a'] * 8:VOFF['b_k_a'] * 8 + 8]
            op('dve', lambda e: e.tensor_scalar(out=DV[:, 96:104], in0=ka, scalar1=-1.0, scalar2=1.0, op0=ALU.mult, op1=ALU.add),
               reads=['VT'], writes=['DV'])

            NST = 3
            stf = [sb0(f"stf{i}", [128, PW], F32) for i in range(NST)]
            stb = [sb0(f"stb{i}", [128, PW], BF16) for i in range(NST)]
            for pi in range(NPIECE):
                k = pi % NST
                gain, KC, MW = PGAIN[pi]
                S_.dma('sp', stf[k][:], wts_d[pi], writes=[f'stf{k}'], key=f'stf{k}')
                eng = ('dve', 'pool')[pi % 2] if gain is not None else ('act', 'dve', 'pool')[pi % 3]
                if gain is None:
                    if eng == 'act':
                        op('act', lambda e, k=k: e.activation(out=stb[k][:], in_=stf[k][:], func=AF.Copy),
                           reads=[f'stf{k}'], writes=[f'stb{k}'])
                    else:
                        op(eng, lambda e, k=k: e.tensor_copy(out=stb[k][:], in_=stf[k][:]),
                           reads=[f'stf{k}'], writes=[f'stb{k}'])
                else:
                    for kc in range(KC):
                        for (c0, c1, (tab, gi)) in gain:
                            gc = (VT if tab == 'VT' else DV)[:, gi * 8 + kc:gi * 8 + kc + 1]
                            op(eng, lambda e, k=k, kc=kc, MW=MW, gc=gc, c0=c0, c1=c1: e.tensor_scalar(
                                out=stb[k][:, kc * MW + c0:kc * MW + c1], in0=stf[k][:, kc * MW + c0:kc * MW + c1],
                                scalar1=gc, scalar2=1.0, op0=ALU.mult, op1=ALU.mult),
                               reads=[f'stf{k}', 'VT', 'DV'], writes=[f'stb{k}'])
                S_.dma('act', wsc[pi], stb[k][:], reads=[f'stb{k}'], writes=['wsc'], key=f'wsc{k}')
        S_.barrier()

        import os as _os2
        _ex = int(_os2.environ.get('EXTRA_SBUF', '0'))
        if _ex:
            DUMMY = sb('DUMMY', [128, _ex * 256], F32)
            op('pool', lambda e: e.memset(DUMMY[:, _ex * 256 - 512:], 1.0), writes=['DUMMY'])
        X = sb("X", [128, 8, T], F32)
        XN = sb("XN", [128, 8, T], BF16)
        SQ = sb("SQ", [128, 8, T], BF16)
        RS = sb("RS", [128, T], F32)
        F1 = sb("F1", [128, 8, T], F32)
        F2 = sb("F2", [128, 8, T + 4], F32)
        F3 = sb("F3", [128, 8, T], F32)
        BH = sb("BH", [128, 32, T], BF16)
        PT = [sb(f"PT{i}", [128, T], F32) for i in range(4)]
        TB = [sb(f"TB{i}", [128, T], F32) for i in range(5)]
        ring = [sb(f"ring{i}", [128, PW], BF16) for i in range(NRING)]
        xin = [sb("xin0", [128, D], F32)] * 2
        KT = [sb(f"KT{l}", [128, 8, MEM], BF16) for l in range(2)]
        VV = [sb(f"VV{l}", [128, 2, D], BF16) for l in range(2)]
        HST = sb("HST", [128, 8], F32)
        SMX = sb("SMX", [128, 48], F32)


        XNP = sb("XNP", [128, 8, T + 8], BF16)
        RW = sb("RW", [128, 6400], BF16)
        STt = sb("STt", [128, 8, 64], BF16)
        GC = sb("GC", [128, 32], F32)
        XL = sb("XL", [128, 8], BF16)
        SCR = sb("SCR", [128, 8], F32)
        IDH = sb("IDH", [128, 128], FP16)
        MSK = sb("MSK", [128, 8, 128], BF16)
        MK2 = sb("MK2", [128, 512], BF16)
        ID2 = sb("ID2", [128, 256], BF16)
        BOb = sb("BOb", [128, 128], BF16)
        BO64 = sb("BO64", [128, 128], BF16)
        S_.dma('sp', xin[0][:], msk_d, writes=['xin0'], key='xin0')
        op('dve', lambda e: e.tensor_copy(out=MSK[:].rearrange("p a t -> p (a t)"), in_=xin[0][:]), reads=['xin0'], writes=['MSK'])
        op('dve', lambda e: e.tensor_copy(out=IDH[:], in_=ident[:]), reads=['ident'], writes=['IDH'])
        op('pool', lambda e: e.memset(MK2[:], 1.0), writes=['MK2'])
        for kind in range(4):
            op('pool', lambda e, kind=kind: e.affine_select(out=MK2[:, kind * 128:(kind + 1) * 128], in_=MK2[:, kind * 128:(kind + 1) * 128],
                                                            pattern=[[1, 128]], base=0, channel_multiplier=-1,
                                                            compare_op=(ALU.is_gt if kind % 2 == 0 else ALU.is_ge), fill=0.0),
               reads=['MK2'], writes=['MK2'])
        op('pool', lambda e: e.memset(ID2[:], 0.0), writes=['ID2'])
        for hs in range(2):
            op('pool', lambda e, hs=hs: e.affine_select(out=ID2[64 * hs:64 * hs + 64, :], in_=ID2[64 * hs:64 * hs + 64, :],
                                                        pattern=[[0, 4], [-1, 64]], base=0, channel_multiplier=1,
                                                        compare_op=ALU.not_equal, fill=1.0), reads=['ID2'], writes=['ID2'])
        op('pool', lambda e: e.memset(BOb[:], 0.0), writes=['BOb'])
        op('pool', lambda e: e.memset(BO64[:], 0.0), writes=['BO64'])
        for hs in range(2):
            op('pool', lambda e, hs=hs: e.memset(BOb[64 * hs:64 * hs + 64, 64 * hs:64 * hs + 64], 1.0), reads=['BOb'], writes=['BOb'])
            op('pool', lambda e, hs=hs: e.memset(BO64[64 * hs:64 * hs + 64, 64 * hs:64 * hs + 64], 1.0 / 64.0), reads=['BO64'], writes=['BO64'])

        def bhb(i):
            return BH[:, 8 * i:8 * (i + 1), :]

        BHF = BH[:].rearrange("p a t -> p (a t)").bitcast(F32)

        def bhf(i, c):
            o = (i * 8 + c) * T
            return BHF[:, o:o + T]

        def gk(i, c):
            k = i * 8 + c
            return [f'BH{2 * k}', f'BH{2 * k + 1}']

        seq = []
        for b in range(NB):
            if nstage >= 2:
                seq += [('wkv0', i) for i in range(4)]
            if nstage >= 5:
                seq += [('wkv1', i) for i in range(4)]
            for i in range(NT):
                if nstage >= 1:
                    seq += [('w_in', j) for j in range(4)] + [('gates', 0)] + [('a_w_out', j) for j in range(2)]
                if nstage >= 2:
                    seq += [('wq0', j) for j in range(2)] + [('wo0', j) for j in range(2)]
                if nstage >= 3:
                    seq += [('up0', j) for j in range(8)] + [('down0', j) for j in range(8)]
                if nstage >= 4:
                    seq += [('loraAa', 0), ('loraAb', 0), ('loraB', 0)]
                    for pj in (2, 3, 0, 1, 4, 5):
                        seq += [('rkvA', pj), ('rkvB', pj)]
                    seq += [('b_w_o', j) for j in range(2)]
                if nstage >= 5:
                    seq += [('wq1', j) for j in range(2)] + [('wo1', j) for j in range(2)]
                if nstage >= 6:
                    seq += [('up1', j) for j in range(8)] + [('down1', j) for j in range(8)]
        wstate = {'issued': 0, 'used': 0}

        def w_issue():
            k = wstate['issued']
            if k >= len(seq):
                return
            name, j = seq[k]
            pi = PIECES[name][0] + j
            slot = k % NRING
            S_.dma('sp', ring[slot][:], wsc[pi], writes=[f'ring{slot}'], key=f'ring{slot}')
            wstate['issued'] += 1

        def w_next(name, j):
            k = wstate['used']
            assert seq[k] == (name, j), (seq[k], name, j)
            prev_live = k >= 1 and seq[k - 1][0] in ('rkvA', 'loraAa') and seq[k][0] in ('rkvB', 'loraAb')
            retired = k - 2 if prev_live else k - 1
            while wstate['issued'] < min(len(seq), retired + NRING + 1):
                w_issue()
            wstate['used'] += 1
            slot = k % NRING
            return ring[slot], f'ring{slot}'

        def ones_norm(src_keys):
            b = nbank()
            for c in range(8):
                op('pe', lambda e, c=c, b=b: e.matmul(ps[b][:, :], lhsT=onesb[:], rhs=SQ[:, c, :], start=(c == 0), stop=(c == 7)),
                   reads=['onesb'] + [f'SQ{c}'], writes=[f'ps{b}'], signal=(c == 7))
            op('act', lambda e, b=b: e.activation(out=PT[3][:], in_=ps[b][:, :], func=AF.Ln, scale=1.0 / D, bias=1e-6),
               reads=[f'ps{b}'], writes=['PT3'])
            op('act', lambda e: e.activation(out=RS[:], in_=PT[3][:], func=AF.Exp, scale=-0.5), reads=['PT3'], writes=['RS'])

        def norm_in():
            op('act', lambda e: e.activation(out=SQ[:], in_=X[:], func=AF.Square),
               reads=[f'X{c}' for c in range(8)], writes=[f'SQ{c}' for c in range(8)])
            ones_norm(None)
            for c in range(8):
                eng = 'pool' if c % 3 == 2 else 'dve'
                op(eng, lambda e, c=c: e.tensor_tensor(out=XN[:, c, :], in0=X[:, c, :], in1=RS[:], op=ALU.mult),
                   reads=[f'X{c}', 'RS'], writes=[f'XN{c}'])

        def post_norm(gidx):
            ones_norm(None)
            for c in range(8):
                op('pool', lambda e, c=c: e.tensor_tensor(out=F1[:, c, :], in0=F1[:, c, :], in1=RS[:], op=ALU.mult),
                   reads=[f'F1_{c}', 'RS'], writes=[f'F1_{c}'])
                gc = vcol('ln_gains', gidx, c)
                op('dve', lambda e, c=c, gc=gc: e.scalar_tensor_tensor(out=X[:, c, :], in0=F1[:, c, :], scalar=gc, in1=X[:, c, :],
                                                                        op0=ALU.mult, op1=ALU.add),
                   reads=[f'F1_{c}', f'X{c}', 'VT'], writes=[f'X{c}'])

        def proj(wname, npieces, src, srckeys, KC, evac, mper=4, n=T):
            for pj in range(npieces):
                rg, rkey = w_next(wname, pj)
                MW = mper * 128
                for ml in range(mper):
                    m = pj * mper + ml
                    b = nbank()
                    for kc in range(KC):
                        op('pe', lambda e, rg=rg, kc=kc, ml=ml, b=b, MW=MW: e.matmul(
                            ps[b][:, 0:n], lhsT=rg[:, kc * MW + ml * 128:kc * MW + (ml + 1) * 128], rhs=src(kc),
                            start=(kc == 0), stop=(kc == KC - 1)),
                           reads=[rkey, srckeys(kc)], writes=[f'ps{b}'], signal=(kc == KC - 1))
                    evac(m, ps[b][:, 0:n], f'ps{b}')

        def evac_branch(bias_name):
            def ev(m, p, pk):
                if bias_name is None:
                    op('act', lambda e, m=m, p=p: e.activation(out=F1[:, m, :], in_=p, func=AF.Copy),
                       reads=[pk], writes=[f'F1_{m}'])
                    op('act', lambda e, m=m, p=p: e.activation(out=SQ[:, m, :], in_=p, func=AF.Square),
                       reads=[pk], writes=[f'SQ{m}'])
                else:
                    bc = vcol(bias_name, 0, m)
                    op('act', lambda e, m=m, p=p, bc=bc: e.activation(out=F1[:, m, :], in_=p, func=AF.Identity, bias=bc),
                       reads=[pk, 'VT'], writes=[f'F1_{m}'])
                    op('act', lambda e, m=m, p=p, bc=bc: e.activation(out=SQ[:, m, :], in_=p, func=AF.Square, bias=bc),
                       reads=[pk, 'VT'], writes=[f'SQ{m}'])
            return ev

        xkeys = [f'X{c}' for c in range(8)]

        def load_tile(b, i):
            for tb in range(4):
                r0 = b * S + i * T + tb * 128
                if tb % 2 == 0:
                    srcs = [xin[0][:, 0:512], xin[0][:, 512:1024]]
                    bkeys = ['xin0', 'xin0']
                    S_.dma('sp', xin[0][:], x_d[r0:r0 + 128, :], writes=['xin0'], key='xin0')
                else:
                    srcs = [PT[0][:], PT[1][:]]
                    bkeys = ['PT0', 'PT1']
                    for h_ in range(2):
                        S_.dma('sp', PT[h_][:], x_d[r0:r0 + 128, h_ * 512:(h_ + 1) * 512], writes=[bkeys[h_]], key=f'ptio{h_}')
                for half in range(2):
                    bk = nbank()
                    for cl in range(4):
                        c = half * 4 + cl
                        op('pe', lambda e: e.transpose(out=ps[bk][:, cl * 128:(cl + 1) * 128], in_=srcs[half][:, cl * 128:(cl + 1) * 128], identity=ident[:]),
                           reads=[bkeys[half], 'ident'], writes=[f'ps{bk}'], signal=(cl == 3))
                    op('act', lambda e: e.activation(
                        out=X[:, half * 4:half * 4 + 4, tb * 128:(tb + 1) * 128],
                        in_=ps[bk][:, :].rearrange("p (c t) -> p c t", c=4), func=AF.Copy),
                       reads=[f'ps{bk}'], writes=[f'X{c}' for c in range(half * 4, half * 4 + 4)])

        def store_tile(b, i):
            for tb in range(4):
                r0 = b * S + i * T + tb * 128
                if tb % 2 == 0:
                    dsts = [xin[0][:, 0:512], xin[0][:, 512:1024]]
                    bkeys = ['xin0', 'xin0']
                else:
                    dsts = [PT[0][:], PT[1][:]]
                    bkeys = ['PT0', 'PT1']
                for half in range(2):
                    bk = nbank()
                    for cl in range(4):
                        c = half * 4 + cl
                        op('pe', lambda e: e.transpose(out=ps[bk][:, cl * 128:(cl + 1) * 128], in_=X[:, c, tb * 128:(tb + 1) * 128], identity=ident[:]),
                           reads=[f'X{c}', 'ident'], writes=[f'ps{bk}'], signal=(cl == 3))
                    op('act', lambda e: e.activation(out=dsts[half], in_=ps[bk][:, :], func=AF.Copy),
                       reads=[f'ps{bk}'], writes=[bkeys[half]])
                if tb % 2 == 0:
                    S_.dma('sp', y_d[r0:r0 + 128, :], xin[0][:], reads=['xin0'], writes=['y'], key='xin0')
                else:
                    for h_ in range(2):
                        S_.dma('sp', y_d[r0:r0 + 128, h_ * 512:(h_ + 1) * 512], PT[h_][:], reads=[bkeys[h_]], writes=['y'], key=f'ptio{h_}')

        def stage_A(b, i):
            norm_in()
            vb = VOFF['a_b_in']

            def ev_in(m, p, pk):
                if m < 8:
                    bc = VT[:, vb * 8 + m:vb * 8 + m + 1]
                    if use_gelu:
                        op('act', lambda e, m=m, p=p, bc=bc: e.activation(out=F1[:, m, :], in_=p, func=AF.Gelu_apprx_tanh, bias=bc),
                           reads=[pk, 'VT'], writes=[f'F1_{m}'])
                    else:
                        op('act', lambda e, m=m, p=p, bc=bc: e.activation(out=F1[:, m, :], in_=p, func=AF.Identity, bias=bc),
                           reads=[pk, 'VT'], writes=[f'F1_{m}'])
                        op('pool', lambda e, m=m: e.tensor_tensor(out=PT[0][:], in0=F1[:, m, :], in1=F1[:, m, :], op=ALU.mult),
                           reads=[f'F1_{m}'], writes=['PT0'])
                        op('dve', lambda e: e.tensor_scalar(out=PT[0][:], in0=PT[0][:], scalar1=0.044715, scalar2=1.0, op0=ALU.mult, op1=ALU.add),
                           reads=['PT0'], writes=['PT0'])
                        op('pool', lambda e, m=m: e.tensor_tensor(out=PT[0][:], in0=PT[0][:], in1=F1[:, m, :], op=ALU.mult),
                           reads=[f'F1_{m}', 'PT0'], writes=['PT0'])
                        op('act', lambda e: e.activation(out=PT[0][:], in_=PT[0][:], func=AF.Sigmoid, scale=1.5957691216057308),
                           reads=['PT0'], writes=['PT0'])
                        op('dve', lambda e, m=m: e.tensor_tensor(out=F1[:, m, :], in0=F1[:, m, :], in1=PT[0][:], op=ALU.mult),
                           reads=[f'F1_{m}', 'PT0'], writes=[f'F1_{m}'])
                else:
                    c = m - 8
                    bc = VT[:, vb * 8 + m:vb * 8 + m + 1]
                    op('act', lambda e, c=c, p=p, bc=bc: e.activation(out=F2[:, c, 4:T + 4], in_=p, func=AF.Identity, bias=bc),
                       reads=[pk, 'VT'], writes=[f'F2_{c}'])
                    cw = [vcol('a_conv_w', k, c) for k in range(4)]
                    cb = vcol('a_conv_b', 0, c)
                    op('dve', lambda e, c=c, cw=cw, cb=cb: e.tensor_scalar(out=F3[:, c, :], in0=F2[:, c, 1:T + 1], scalar1=cw[0], scalar2=cb,
                                                                        op0=ALU.mult, op1=ALU.add),
                       reads=[f'F2_{c}', 'VT'], writes=[f'F3_{c}'])
                    for k in range(1, 4):
                        op('dve', lambda e, c=c, k=k, cw=cw: e.scalar_tensor_tensor(out=F3[:, c, :], in0=F2[:, c, 1 + k:T + 1 + k], scalar=cw[k],
                                                                                in1=F3[:, c, :], op0=ALU.mult, op1=ALU.add),
                           reads=[f'F2_{c}', f'F3_{c}', 'VT'], writes=[f'F3_{c}'])
                    op('pool', lambda e, c=c: e.tensor_copy(out=F2[:, c, 1:4], in_=F2[:, c, T + 1:T + 4]),
                       reads=[f'F2_{c}'], writes=[f'F2_{c}'])
                    op('pool', lambda e, c=c: e.tensor_copy(out=SQ[:, c, :], in_=F3[:, c, :]),
                       reads=[f'F3_{c}'], writes=[f'SQ{c}'])

            if i == 0:
                for c in range(8):
                    op('pool', lambda e, c=c: e.memset(F2[:, c, 0:4], 0.0), writes=[f'F2_{c}'])
                op('pool', lambda e: e.memset(HST[:], 0.0), writes=['HST'])
            proj('w_in', 4, lambda kc: XN[:, kc, :], lambda kc: f'XN{kc}', 8, ev_in)
            rg, rkey = w_next('gates', 0)
            gb = VOFF['a_gate_b']
            for gi in range(2):
                for c in range(8):
                    h, j = c // 2, c % 2
                    bk = nbank()
                    for kc in range(2):
                        o = ((gi * 4 + h) * 2 + kc) * 256 + j * 128
                        op('pe', lambda e, o=o, h=h, kc=kc, bk=bk: e.matmul(ps[bk][:, :], lhsT=rg[:, o:o + 128], rhs=SQ[:, 2 * h + kc, :],
                                                                          start=(kc == 0), stop=(kc == 1)),
                           reads=[rkey, f'SQ{2*h+kc}'], writes=[f'ps{bk}'], signal=(kc == 1))
                    bc = VT[:, (gb + gi) * 8 + c:(gb + gi) * 8 + c + 1]
                    op('act', lambda e, gi=gi, c=c, bk=bk, bc=bc: e.activation(out=bhf(gi, c), in_=ps[bk][:, :], func=AF.Sigmoid, bias=bc),
                       reads=[f'ps{bk}', 'VT'], writes=gk(gi, c))
            for c in range(8):
                cc = CV2[:, c:c + 1]
                op('act', lambda e, c=c, cc=cc: e.activation(out=bhf(0, c), in_=bhf(0, c), func=AF.Exp, scale=cc),
                   reads=gk(0, c) + ['CV2'], writes=gk(0, c))
                op('pool', lambda e, c=c: e.tensor_tensor(out=F2[:, c, 4:T + 4], in0=bhf(0, c), in1=bhf(0, c), op=ALU.mult),
                   reads=gk(0, c) + [f'F2_{c}'], writes=[f'F2_{c}'])
            for c in range(8):
                op('act', lambda e, c=c: e.activation(out=F2[:, c, 4:T + 4], in_=F2[:, c, 4:T + 4], func=AF.Sqrt, scale=-1.0, bias=1.0),
                   reads=[f'F2_{c}'], writes=[f'F2_{c}'])
            for c in range(8):
                op('dve', lambda e, c=c: e.tensor_tensor(out=bhf(1, c), in0=bhf(1, c), in1=F2[:, c, 4:T + 4], op=ALU.mult),
                   reads=gk(1, c) + [f'F2_{c}'], writes=gk(1, c))
                op('pool', lambda e, c=c: e.tensor_tensor(out=bhf(1, c), in0=bhf(1, c), in1=F3[:, c, :], op=ALU.mult),
                   reads=gk(1, c) + [f'F3_{c}'], writes=gk(1, c))
                op('dve', lambda e, c=c: e.tensor_tensor_scan(out=F3[:, c, :], data0=bhf(0, c), data1=bhf(1, c), initial=HST[:, c:c + 1],
                                                             op0=ALU.mult, op1=ALU.add),
                   reads=gk(0, c) + gk(1, c) + ['HST', f'F3_{c}'], writes=[f'F3_{c}'])
                op('pool', lambda e, c=c: e.tensor_copy(out=HST[:, c:c + 1], in_=F3[:, c, T - 1:T]),
                   reads=[f'F3_{c}'], writes=['HST'])
                op('pool', lambda e, c=c: e.tensor_tensor(out=XN[:, c, :], in0=F3[:, c, :], in1=F1[:, c, :], op=ALU.mult),
                   reads=[f'F3_{c}', f'F1_{c}'], writes=[f'XN{c}'])
            proj('a_w_out', 2, lambda kc: XN[:, kc, :], lambda kc: f'XN{kc}', 8, evac_branch('a_b_out'))
            post_norm(1)

        def mem_prep(b, layers):
            MT = F1[:].rearrange("p c t -> p (c t)")[:, 0:8 * MEM].rearrange("p (c t) -> p c t", c=8)
            MN = XN[:].rearrange("p c t -> p (c t)")[:, 0:8 * MEM].rearrange("p (c t) -> p c t", c=8)
            MSQ = SQ[:].rearrange("p c t -> p (c t)")[:, 0:8 * MEM].rearrange("p (c t) -> p c t", c=8)
            f1k = [f'F1_{c}' for c in range(8)]
            xnk = [f'XN{c}' for c in range(8)]
            sqk = [f'SQ{c}' for c in range(8)]
            for tb in range(2):
                r0 = b * MEM + tb * 128
                xb = xin[0]
                S_.dma('sp', xb[:], mem_d[r0:r0 + 128, :], writes=['xin0'], key='xin0')
                for half in range(2):
                    bk = nbank()
                    for cl in range(4):
                        c = half * 4 + cl
                        op('pe', lambda e, xb=xb, c=c, cl=cl, bk=bk: e.transpose(out=ps[bk][:, cl * 128:(cl + 1) * 128],
                                                                                in_=xb[:, c * 128:(c + 1) * 128], identity=ident[:]),
                           reads=['xin0', 'ident'], writes=[f'ps{bk}'], signal=(cl == 3))
                    op('act', lambda e, half=half, tb=tb, bk=bk: e.activation(
                        out=MT[:, half * 4:half * 4 + 4, tb * 128:(tb + 1) * 128],
                        in_=ps[bk][:, :].rearrange("p (c t) -> p c t", c=4), func=AF.Copy),
                       reads=[f'ps{bk}'], writes=f1k)
            op('act', lambda e: e.activation(out=MSQ, in_=MT, func=AF.Square), reads=f1k, writes=sqk)
            bk = nbank()
            for c in range(8):
                op('pe', lambda e, c=c, bk=bk: e.matmul(ps[bk][:, 0:MEM], lhsT=onesb[:], rhs=MSQ[:, c, :], start=(c == 0), stop=(c == 7)),
                   reads=['onesb'] + sqk, writes=[f'ps{bk}'], signal=(c == 7))
            op('act', lambda e, bk=bk: e.activation(out=PT[3][:, 0:MEM], in_=ps[bk][:, 0:MEM], func=AF.Ln, scale=1.0 / D, bias=1e-6),
               reads=[f'ps{bk}'], writes=['PT3'])
            op('act', lambda e: e.activation(out=RS[:, 0:MEM], in_=PT[3][:, 0:MEM], func=AF.Exp, scale=-0.5), reads=['PT3'], writes=['RS'])
            for c in range(8):
                op('dve', lambda e, c=c: e.tensor_tensor(out=MN[:, c, :], in0=MT[:, c, :], in1=RS[:, 0:MEM], op=ALU.mult),
                   reads=f1k + ['RS'], writes=xnk)
            for l in layers:
                for pj in range(2):
                    rg, rkey = w_next(f'wkv{l}', pj)
                    for ml in range(4):
                        m = pj * 4 + ml
                        bk = nbank()
                        for kc in range(8):
                            op('pe', lambda e, rg=rg, kc=kc, ml=ml, bk=bk: e.matmul(
                                ps[bk][:, 0:MEM], lhsT=rg[:, kc * 512 + ml * 128:kc * 512 + (ml + 1) * 128], rhs=MN[:, kc, :],
                                start=(kc == 0), stop=(kc == 7)),
                               reads=[rkey] + xnk, writes=[f'ps{bk}'], signal=(kc == 7))
                        op('act', lambda e, l=l, m=m, bk=bk: e.activation(out=KT[l][:, m, :], in_=ps[bk][:, 0:MEM], func=AF.Copy),
                           reads=[f'ps{bk}'], writes=[f'KT{l}'])
                for pj in range(2):
                    rg, rkey = w_next(f'wkv{l}', 2 + pj)
                    for mc in range(2):
                        bk = nbank()
                        for kc in range(8):
                            op('pe', lambda e, rg=rg, kc=kc, mc=mc, bk=bk: e.matmul(
                                ps[bk][:, :], lhsT=MN[:, kc, mc * 128:(mc + 1) * 128], rhs=rg[:, kc * 512:(kc + 1) * 512],
                                start=(kc == 0), stop=(kc == 7)),
                               reads=[rkey] + xnk, writes=[f'ps{bk}'], signal=(kc == 7))
                        op('act', lambda e, l=l, mc=mc, pj=pj, bk=bk: e.activation(out=VV[l][:, mc, pj * 512:(pj + 1) * 512], in_=ps[bk][:, :], func=AF.Copy),
                           reads=[f'ps{bk}'], writes=[f'VV{l}'])

        def stage_C(l):
            norm_in()
            QT = bhb(0)
            PN = bhb(1)
            PTt = bhb(2)
            OT = bhb(3)

            def ev_q(m, p, pk):
                op('act', lambda e, m=m, p=p: e.activation(out=QT[:, m, :], in_=p, func=AF.Copy, scale=1.0 / 16.0),
                   reads=[pk], writes=[f'BH{m}'])
            proj(f'wq{l}', 2, lambda kc: XN[:, kc, :], lambda kc: f'XN{kc}', 8, ev_q)
            for tb in range(4):
                pn = PN[:, 2 * tb:2 * tb + 2, :].rearrange("p a t -> p (a t)")
                pex = F3[:, 2 * tb:2 * tb + 2, :].rearrange("p a t -> p (a t)")
                banks = [nbank(), nbank()]
                for h in range(4):
                    bk = banks[h // 2]
                    for dc in range(2):
                        op('pe', lambda e, h=h, dc=dc, bk=bk, tb=tb: e.matmul(
                            ps[bk][:, (h % 2) * 256:(h % 2 + 1) * 256], lhsT=QT[:, 2 * h + dc, tb * 128:(tb + 1) * 128],
                            rhs=KT[l][:, 2 * h + dc, :], start=(dc == 0), stop=(dc == 1)),
                           reads=[f'BH{2*h+dc}', f'KT{l}'], writes=[f'ps{bk}'], signal=(dc == 1))
                for hb in range(2):
                    bk = banks[hb]
                    op('dve', lambda e, hb=hb, bk=bk, tb=tb: e.tensor_reduce(
                        out=SMX[:, tb * 4 + 2 * hb:tb * 4 + 2 * hb + 2], in_=ps[bk][:, :].rearrange("p (h k) -> p h k", h=2),
                        axis=AX.X, op=ALU.max, negate=True),
                       reads=[f'ps{bk}'], writes=[f'SMXm{tb}'])
                for h in range(4):
                    bk = banks[h // 2]
                    op('act', lambda e, h=h, bk=bk, tb=tb, pex=pex: e.activation(
                        out=pex[:, h * 256:(h + 1) * 256], in_=ps[bk][:, (h % 2) * 256:(h % 2 + 1) * 256], func=AF.Exp,
                        bias=SMX[:, tb * 4 + h:tb * 4 + h + 1], accum_out=SMX[:, 16 + tb * 4 + h:16 + tb * 4 + h + 1]),
                       reads=[f'ps{bk}', f'SMXm{tb}'], writes=[f'F3_{2*tb}', f'F3_{2*tb+1}', f'SMXs{tb}'])
                op('dve', lambda e, tb=tb: e.reciprocal(out=SMX[:, 32 + tb * 4:32 + tb * 4 + 4], in_=SMX[:, 16 + tb * 4:16 + tb * 4 + 4]),
                   reads=[f'SMXs{tb}'], writes=[f'SMXr{tb}'])
                for h in range(4):
                    op('dve', lambda e, h=h, tb=tb, pn=pn, pex=pex: e.tensor_scalar(
                        out=pn[:, h * 256:(h + 1) * 256], in0=pex[:, h * 256:(h + 1) * 256],
                        scalar1=SMX[:, 32 + tb * 4 + h:32 + tb * 4 + h + 1], scalar2=None, op0=ALU.mult),
                       reads=[f'F3_{2*tb}', f'F3_{2*tb+1}', f'SMXr{tb}'], writes=[f'BH{8+2*tb}', f'BH{9+2*tb}'])
                bk = nbank()
                psb = ps[bk][:, :].bitcast(BF16)
                for hm in range(8):
                    op('pe', lambda e, hm=hm, pn=pn, psb=psb: e.transpose(out=psb[:, hm * 128:(hm + 1) * 128], in_=pn[:, hm * 128:(hm + 1) * 128],
                                                                          identity=identb[:]),
                       reads=[f'BH{8+2*tb}', f'BH{9+2*tb}', 'identb'], writes=[f'ps{bk}'], signal=(hm == 7))
                op('act', lambda e, tb=tb, psb=psb: e.activation(out=PTt[:, :, tb * 128:(tb + 1) * 128],
                                                                 in_=psb.rearrange("p (a t) -> p a t", a=8), func=AF.Copy),
                   reads=[f'ps{bk}'], writes=[f'BH{16+a}' for a in range(8)])
            for m in range(8):
                h = m // 2
                bk = nbank()
                for mc in range(2):
                    op('pe', lambda e, m=m, h=h, mc=mc, bk=bk: e.matmul(ps[bk][:, :], lhsT=VV[l][:, mc, m * 128:(m + 1) * 128],
                                                                      rhs=PTt[:, 2 * h + mc, :], start=(mc == 0), stop=(mc == 1)),
                       reads=[f'VV{l}', f'BH{16+2*h+mc}'], writes=[f'ps{bk}'], signal=(mc == 1))
                op('act', lambda e, m=m, bk=bk: e.activation(out=OT[:, m, :], in_=ps[bk][:, :], func=AF.Copy),
                   reads=[f'ps{bk}'], writes=[f'BH{24+m}'])
            proj(f'wo{l}', 2, lambda kc: OT[:, kc, :], lambda kc: f'BH{24+kc}', 8, evac_branch(None))
            post_norm(6 * l + 3)

        def stage_M(l):
            norm_in()
            cnt = [0]

            def ev_up(m, p, pk):
                k = cnt[0] % 2
                cnt[0] += 1
                op('act', lambda e, p=p, k=k: e.activation(out=PT[k][:], in_=p, func=AF.Square), reads=[pk], writes=[f'PT{k}'])
                op('dve', lambda e, m=m, p=p, k=k: e.scalar_tensor_tensor(out=BH[:, m, :], in0=p, scalar=0.0, in1=PT[k][:],
                                                                          op0=ALU.is_gt, op1=ALU.mult),
                   reads=[pk, f'PT{k}'], writes=[f'BH{m}'])
            proj(f'up{l}', 8, lambda kc: XN[:, kc, :], lambda kc: f'XN{kc}', 8, ev_up)
            proj(f'down{l}', 8, lambda kc: BH[:, kc, :], lambda kc: f'BH{kc}', 32, evac_branch(None), mper=1)
            post_norm(6 * l + 5)

        _rw = [0]

        def rw_alloc(n):
            o = _rw[0]
            _rw[0] += n
            return RW[:, o:o + n]
        TM = rw_alloc(2048)
        U4 = rw_alloc(2048)
        Pb = rw_alloc(512)
        AU = rw_alloc(512)
        Wt = rw_alloc(256)
        LWA = rw_alloc(512)
        PTb = rw_alloc(512)
        LG = PTb
        F3B = F3[:].rearrange("p c t -> p (c t)").bitcast(BF16)
        RH = F3B[:, 0:4096]
        GT = F3B[:, 4096:6144]
        HH = F3B[:, 6144:8192]
        ARf = BH[:, 0:16, :].rearrange("p a t -> p (a t)")
        f3k = [f'F3_{c}' for c in range(8)]

        def ar_kind(fc, kind):
            return ARf[:, fc * 1024:(fc + 1) * 1024].rearrange("p (c k t) -> p c k t", c=4, k=2)[:, :, kind, :]

        def stage_B(b, i):
            for c in range(8):
                if i == 0:
                    op('pool', lambda e, c=c: e.memset(XNP[:, c, 0:8], 0.0), writes=[f'XNP{c}'])
                else:
                    op('pool', lambda e, c=c: e.tensor_copy(out=XNP[:, c, 7:8], in_=XL[:, c:c + 1]), reads=['XL', f'XNP{c}'], writes=[f'XNP{c}'])
            if i == 0:
                op('pool', lambda e: e.memset(STt[:], 0.0), writes=['STt'])
            op('act', lambda e: e.activation(out=SQ[:], in_=X[:], func=AF.Square), reads=xkeys, writes=[f'SQ{c}' for c in range(8)])
            ones_norm(None)
            for c in range(8):
                eng = 'pool' if c % 3 == 2 else 'dve'
                op(eng, lambda e, c=c: e.tensor_tensor(out=XNP[:, c, 8:T + 8], in0=X[:, c, :], in1=RS[:], op=ALU.mult),
                   reads=[f'X{c}', 'RS'], writes=[f'XNP{c}'])

            def proj2(nameA, nameB, pj, evac):
                rgA, kA = w_next(nameA, pj)
                rgB, kB = w_next(nameB, pj)
                return rgA, kA, rgB, kB

            def mm16(rgA, kA, rgB, kB, col0, ncol, bk, MW):
                for v_, (rg, rk, off) in enumerate(((rgA, kA, 8), (rgB, kB, 7))):
                    for kc in range(8):
                        op('pe', lambda e, rg=rg, kc=kc, off=off, v_=v_: e.matmul(
                            ps[bk][0:ncol, :], lhsT=rg[:, kc * MW + col0:kc * MW + col0 + ncol], rhs=XNP[:, kc, off:off + T],
                            start=(v_ == 0 and kc == 0), stop=(v_ == 1 and kc == 7)),
                           reads=[rk, f'XNP{kc}'], writes=[f'ps{bk}'], signal=(v_ == 1 and kc == 7))

            if BSTOP[0] <= 0.1:
                return
            rgA, kA = w_next('loraAa', 0)
            rgB, kB = w_next('loraAb', 0)
            bk = nbank()
            mm16(rgA, kA, rgB, kB, 0, 128, bk, 256)
            op('act', lambda e, bk=bk: e.activation(out=LWA[0:64, :], in_=ps[bk][0:64, :], func=AF.Tanh), reads=[f'ps{bk}'], writes=['LWA0'])
            op('act', lambda e, bk=bk: e.activation(out=LWA[64:128, :], in_=ps[bk][64:128, :], func=AF.Copy), reads=[f'ps{bk}'], writes=['LWA1'])
            bk = nbank()
            mm16(rgA, kA, rgB, kB, 128, 128, bk, 256)
            op('act', lambda e, bk=bk: e.activation(out=LG[:, :], in_=ps[bk][:, :], func=AF.Sigmoid), reads=[f'ps{bk}'], writes=['PTb'])
            if BSTOP[0] <= 0.3:
                return
            rgL, kL = w_next('loraB', 0)
            for m in range(8):
                bk = nbank()
                op('pe', lambda e, m=m, bk=bk: e.matmul(ps[bk][:, :], lhsT=rgL[0:64, m * 128:(m + 1) * 128], rhs=LWA[0:64, :], start=True, stop=True),
                   reads=[kL, 'LWA0'], writes=[f'ps{bk}'])
                op('act', lambda e, m=m, bk=bk: e.activation(out=PT[0][:], in_=ps[bk][:, :], func=AF.Sigmoid, bias=vcol('b_w0', 0, m)),
                   reads=[f'ps{bk}', 'VT'], writes=['PT0'])
                op('act', lambda e: e.activation(out=PT[0][:], in_=PT[0][:], func=AF.Copy, scale=-0.6065306597126334), reads=['PT0'], writes=['PT0'])
                for c in range(4):
                    op('dve', lambda e, m=m, c=c: e.tensor_tensor_scan(out=F1[:, m, c * 128:(c + 1) * 128], data0=onesb[:], data1=PT[0][:, c * 128:(c + 1) * 128],
                                                                       initial=0.0, op0=ALU.mult, op1=ALU.add),
                       reads=['PT0', 'onesb'], writes=[f'F1_{m}'])
                op('pool', lambda e, m=m: e.tensor_tensor(out=F3[:, m, :], in0=F1[:, m, :], in1=PT[0][:], op=ALU.subtract),
                   reads=[f'F1_{m}', 'PT0'], writes=[f'F3_{m}'])
                bk = nbank()
                op('pe', lambda e, m=m, bk=bk: e.matmul(ps[bk][:, :], lhsT=rgL[64:128, 1024 + m * 128:1024 + (m + 1) * 128], rhs=LWA[64:128, :], start=True, stop=True),
                   reads=[kL, 'LWA1'], writes=[f'ps{bk}'])
                op('act', lambda e, m=m, bk=bk: e.activation(out=F2[:, m, 4:T + 4], in_=ps[bk][:, :], func=AF.Sigmoid, bias=vcol('b_a0', 0, m)),
                   reads=[f'ps{bk}', 'VT'], writes=[f'F2_{m}'])
                bk = nbank()
                op('pe', lambda e, m=m, bk=bk: e.matmul(ps[bk][:, :], lhsT=rgL[:, 2048 + m * 128:2048 + (m + 1) * 128], rhs=LG[:, :], start=True, stop=True),
                   reads=[kL, 'PTb'], writes=[f'ps{bk}'])
                op('act', lambda e, m=m, bk=bk: e.activation(out=SQ[:, m, :], in_=ps[bk][:, :], func=AF.Copy), reads=[f'ps{bk}'], writes=[f'SQ{m}'])
            if BSTOP[0] <= 0.5:
                return
            op('act', lambda e: e.activation(out=GC[:].rearrange("p (a c) -> p a c", a=8),
                                             in_=F1[:].rearrange("p a (c t) -> p a c t", c=4)[:, :, :, 127], func=AF.Exp),
               reads=[f'F1_{c}' for c in range(8)], writes=['GC'])
            omk = lambda m: DV[:, 96 + m:97 + m]
            if BSTOP[0] <= 0.6:
                return
            TMf = RW[:, 0:2048].bitcast(F32)
            tsets = [(PT[0][:], PT[1][:], PT[2][:], PT[3][:], PTb, ['PT0'], ['PT1'], ['PT2'], ['PT3'], ['PTb']),
                     (xin[0][:, 0:512], xin[0][:, 512:1024], TMf[:, 0:512], TMf[:, 512:1024], LWA, ['xa'], ['xb'], ['tma'], ['tmb'], ['LWA0', 'LWA1'])]
            op('pool', lambda e: e.memset(SCR[:, 3:4], 0.0), reads=[], writes=['xin0', 'TM', 'xa', 'xb', 'tma', 'tmb'])
            for pj in (2, 3):
                rgA, kA = w_next('rkvA', pj)
                rgB, kB = w_next('rkvB', pj)
                for ml in range(4):
                    m = (pj - 2) * 4 + ml
                    bk = nbank()
                    mm16(rgA, kA, rgB, kB, ml * 128, 128, bk, 512)
                    pk = f'ps{bk}'
                    t0, t1, t2, t3, tq, k0, k1, k2, k3, kq = tsets[m % 2]
                    kkc = vcol('b_k_k', 0, m)
                    op('act', lambda e, bk=bk, kkc=kkc: e.activation(out=t0, in_=ps[bk][:, :], func=AF.Copy, scale=kkc), reads=[pk, 'VT'], writes=[*k0])
                    op('act', lambda e, bk=bk, kkc=kkc: e.activation(out=tq, in_=ps[bk][:, :], func=AF.Square, scale=kkc), reads=[pk, 'VT'], writes=[*kq])
                    b2 = nbank()
                    op('pe', lambda e, b2=b2: e.matmul(ps[b2][:, :], lhsT=BOb[:], rhs=tq, start=True, stop=True), reads=['BOb', *kq], writes=[f'ps{b2}'])
                    op('act', lambda e, b2=b2: e.activation(out=t1, in_=ps[b2][:, :], func=AF.Ln, bias=1e-24), reads=[f'ps{b2}'], writes=[*k1])
                    op('act', lambda e: e.activation(out=t1, in_=t1, func=AF.Exp, scale=-0.5), reads=[*k1], writes=[*k1])
                    op('dve', lambda e: e.tensor_tensor(out=t0, in0=t0, in1=t1, op=ALU.mult), reads=[*k0, *k1], writes=[*k0])
                    op('act', lambda e, m=m: e.activation(out=t2, in_=F3[:, m, :], func=AF.Exp), reads=[f'F3_{m}'], writes=[*k2])
                    op('dve', lambda e, m=m: e.scalar_tensor_tensor(out=ar_kind(m, 0), in0=t0.rearrange("p (c t) -> p c t", c=4), scalar=-1.0,
                                                                    in1=t2.rearrange("p (c t) -> p c t", c=4), op0=ALU.mult, op1=ALU.mult),
                       reads=[*k0, *k2], writes=[f'BH{2*m}', f'BH{2*m+1}'])
                    op('act', lambda e, m=m: e.activation(out=t3, in_=F1[:, m, :], func=AF.Exp, scale=-1.0), reads=[f'F1_{m}'], writes=[*k3])
                    op('pool', lambda e, m=m: e.tensor_tensor(out=t0, in0=t0, in1=F2[:, m, 4:T + 4], op=ALU.mult), reads=[*k0, f'F2_{m}'], writes=[*k0])
                    op('dve', lambda e, m=m: e.tensor_tensor(out=BH[:, 16 + m, :], in0=t0, in1=t3, op=ALU.mult), reads=[*k0, *k3], writes=[f'BH{16+m}'])
                    kac = vcol('b_k_a', 0, m)
                    op('dve', lambda e, m=m, kac=kac: e.tensor_scalar(out=t1, in0=F2[:, m, 4:T + 4], scalar1=kac, scalar2=omk(m), op0=ALU.mult, op1=ALU.add),
                       reads=[f'F2_{m}', 'VT', 'DV', *k1], writes=[*k1])
                    op('dve', lambda e, m=m, bk=bk: e.tensor_tensor(out=F2[:, m, 4:T + 4], in0=ps[bk][:, :], in1=t1, op=ALU.mult),
                       reads=[pk, *k1, f'F2_{m}'], writes=[f'F2_{m}'])
                    op('pool', lambda e, m=m: e.tensor_tensor(out=BH[:, 24 + m, :], in0=F2[:, m, 4:T + 4], in1=t3, op=ALU.mult),
                       reads=[f'F2_{m}', *k3], writes=[f'BH{24+m}'])
            if BSTOP[0] <= 0.7:
                return
            for pj in (0, 1):
                rgA, kA = w_next('rkvA', pj)
                rgB, kB = w_next('rkvB', pj)
                for ml in range(4):
                    m = pj * 4 + ml
                    bk = nbank()
                    mm16(rgA, kA, rgB, kB, ml * 128, 128, bk, 512)
                    pk = f'ps{bk}'
                    t0, t1, t2, t3, tq, k0, k1, k2, k3, kq = tsets[m % 2]
                    op('act', lambda e, m=m: e.activation(out=t2, in_=F1[:, m, :], func=AF.Exp), reads=[f'F1_{m}'], writes=[*k2])
                    op('dve', lambda e, m=m, bk=bk: e.tensor_tensor(out=ar_kind(m, 1), in0=ps[bk][:, :].rearrange("p (c t) -> p c t", c=4),
                                                                    in1=t2.rearrange("p (c t) -> p c t", c=4), op=ALU.mult),
                       reads=[pk, *k2], writes=[f'BH{2*m}', f'BH{2*m+1}'])
                    rkc = vcol('b_r_k', 0, m)
                    op('dve', lambda e, m=m, bk=bk, rkc=rkc: e.scalar_tensor_tensor(out=tq, in0=ps[bk][:, :], scalar=rkc, in1=F2[:, m, 4:T + 4],
                                                                                  op0=ALU.mult, op1=ALU.mult),
                       reads=[pk, 'VT', f'F2_{m}', *kq], writes=[*kq])
                    b2 = nbank()
                    op('pe', lambda e, b2=b2: e.matmul(ps[b2][:, :], lhsT=BOb[:], rhs=tq, start=True, stop=True), reads=['BOb', *kq], writes=[f'ps{b2}'])
                    op('act', lambda e, m=m, b2=b2: e.activation(out=F2[:, m, 4:T + 4], in_=ps[b2][:, :], func=AF.Copy), reads=[f'ps{b2}'], writes=[f'F2_{m}'])
            op('pool', lambda e: e.memset(SCR[:, 4:5], 0.0), reads=[], writes=['xin0', 'TM', 'xa', 'xb', 'tma', 'tmb'])
            if BSTOP[0] <= 0.8:
                return
            for pj in (4, 5):
                rgA, kA = w_next('rkvA', pj)
                rgB, kB = w_next('rkvB', pj)
                for ml in range(4):
                    m = (pj - 4) * 4 + ml
                    bk = nbank()
                    mm16(rgA, kA, rgB, kB, ml * 128, 128, bk, 512)
                    pk = f'ps{bk}'
                    import os as _os
                    _pm = int(_os.environ.get('P5MODE', '0'))
                    if _pm in (0, 1):
                        op('dve', lambda e, m=m, bk=bk: e.tensor_copy(out=XN[:, m, :], in_=ps[bk][:, :]), reads=[pk], writes=[f'XN{m}'])
                    if _pm in (0, 2):
                        op('dve', lambda e, m=m, bk=bk: e.tensor_tensor(out=F2[:, m, 4:T + 4], in0=ps[bk][:, :], in1=F2[:, m, 4:T + 4], op=ALU.mult),
                           reads=[pk, f'F2_{m}'], writes=[f'F2_{m}'])
            if BSTOP[0] <= 1:
                return
            TM4 = TM.rearrange("p (c k f) -> p c k f", c=4, k=4)
            op('pool', lambda e: e.tensor_copy(out=XL[:].rearrange("p (c o) -> p c o", o=1), in_=XNP[:, :, T + 7:T + 8]), reads=[f'XNP{c}' for c in range(8)], writes=['XL'])
            XNPf = XNP[:].rearrange("p c t -> p (c t)")
            sets = []
            for si in range(2):
                TBh = [TB[j][:].bitcast(FP16) for j in range(5)]
                if si == 0:
                    d = dict(Q=[TBh[0], TBh[1]], kQ=['TB0', 'TB1'], B5=TBh[4][:, 0:512], kB5='TB4s0',
                             U4=U4, Pb=Pb, AU=AU, Wt=Wt, kS=['U4', 'Pb', 'AU', 'Wt'])
                else:
                    d = dict(Q=[TBh[2], TBh[3]], kQ=['TB2', 'TB3'], B5=TBh[4][:, 512:1024], kB5='TB4s1',
                             U4=XNPf[:, 0:2048], Pb=XNPf[:, 2048:2560], AU=XNPf[:, 2560:3072], Wt=XNPf[:, 3072:3328], kS=['U4b', 'Pbb', 'AUb', 'Wtb'])
                sets.append(d)
            hk1 = [k + f'h{hf}' for k in sets[1]['kS'][1:] for hf in range(2)]
            op('pool', lambda e: e.memset(SCR[:, 0:1], 0.0), reads=[], writes=[f'XNP{c}' for c in range(8)] + sets[1]['kS'] + hk1)

            def head_prologue(fc, hs, st):
                hsl = slice(64 * hs, 64 * hs + 64)
                U4_ = st['U4']
                kU = [st['kS'][0]]
                At = lambda c: ARf[hsl, fc * 1024 + c * 256:fc * 1024 + c * 256 + 128]
                ARc = lambda c: ARf[hsl, fc * 1024 + c * 256:fc * 1024 + (c + 1) * 256]
                Bt = lambda c: BH[hsl, 16 + fc, c * 128:(c + 1) * 128]
                Kt = lambda c: BH[hsl, 24 + fc, c * 128:(c + 1) * 128]
                kAR = [f'BH{2*fc}', f'BH{2*fc+1}']
                bA = nbank(); reserved.add(bA)
                for c in range(4):
                    op('pe', lambda e, c=c: e.matmul(ps[bA][:, c * 128:(c + 1) * 128], lhsT=At(c), rhs=Bt(c), start=True, stop=True),
                       reads=kAR + [f'BH{16+fc}'], writes=[f'ps{bA}'], signal=(c == 3))
                bAT = nbank(); reserved.add(bAT)
                for c in range(4):
                    op('pe', lambda e, c=c: e.matmul(ps[bAT][:, c * 128:(c + 1) * 128], lhsT=Bt(c), rhs=At(c), start=True, stop=True),
                       reads=kAR + [f'BH{16+fc}'], writes=[f'ps{bAT}'], signal=(c == 3))
                for c in range(4):
                    bB = nbank()
                    op('pe', lambda e, c=c, bB=bB: e.matmul(ps[bB][:, 0:256], lhsT=Bt(c), rhs=ARc(c), start=True, stop=True),
                       reads=kAR + [f'BH{16+fc}'], writes=[f'ps{bB}'], signal=False)
                    op('pe', lambda e, c=c, bB=bB: e.matmul(ps[bB][:, 256:512], lhsT=Kt(c), rhs=ARc(c), start=True, stop=True),
                       reads=kAR + [f'BH{24+fc}'], writes=[f'ps{bB}'])
                    op('dve', lambda e, c=c, bB=bB: e.tensor_tensor(out=U4_[:, c * 512:(c + 1) * 512], in0=ps[bB][:, :], in1=MK2[:], op=ALU.mult),
                       reads=[f'ps{bB}', 'MK2'], writes=kU)
                return bA, bAT

            def half_steps(fc, hs, st, hf, bA, bAT, done):
                hsl = slice(64 * hs, 64 * hs + 64)
                fsl = slice(64 * hs, 64 * hs + 64)
                cs_ = slice(hf * 256, (hf + 1) * 256)
                QQ = st['Q'][hf]
                XX, TP = QQ[:, 0:512], QQ[:, 512:1024]
                B1, B2 = XX[:, 0:256], XX[:, 256:512]
                B3, B4 = TP[:, 0:256], TP[:, 256:512]
                B5 = st['B5'][:, cs_]
                kx = st['kQ'][hf]
                k1, k2, k3, k4, k5 = [kx + 'a'], [kx + 'b'], [kx + 'c'], [kx + 'd'], [st['kB5'] + f'h{hf}']
                dt_ = FP16
                idm = None
                U44 = st['U4'].rearrange("p (u k t) -> p u k t", u=4, k=4)
                Pb_ = st['Pb'][:, hf * 256:(hf + 1) * 256]
                AU_ = st['AU'][:, hf * 256:(hf + 1) * 256]
                Wt_ = st['Wt'][:, hf * 128:(hf + 1) * 128]
                kU = [st['kS'][0]]
                kP, kAU, kW = [[k + f'h{hf}'] for k in st['kS'][1:]]
                kAR = [f'BH{2*fc}', f'BH{2*fc+1}']
                v2 = lambda t: t.rearrange("p (u t) -> p u t", u=2)
                mskb = lambda j: MSK[:, j:j + 1, :].to_broadcast([128, 2, 128])
                idb = ident[:].rearrange("p (o t) -> p o t", o=1).to_broadcast([128, 2, 128])
                W_ = lambda t: t
                held = []

                def gbank():
                    bk_ = nbank()
                    reserved.add(bk_)
                    held.append(bk_)
                    return bk_

                def gfree(bk_):
                    reserved.discard(bk_)
                    held.remove(bk_)

                def mm2(bk_, col0, lhs, rhs, rk, acc=None, acck=None, last=True):
                    for u in range(2):
                        us = slice(u * 128, (u + 1) * 128)
                        os_ = slice(col0 + u * 128, col0 + (u + 1) * 128)
                        if acc is not None:
                            op('pe', lambda e: e.matmul(ps[bk_][:, os_], lhsT=W_(idm), rhs=W_(acc[:, us]), start=True, stop=False),
                               reads=acck + ['ident'], writes=[f'ps{bk_}'], signal=False)
                        op('pe', lambda e: e.matmul(ps[bk_][:, os_], lhsT=W_(lhs[:, us]), rhs=W_(rhs[:, us]), start=(acc is None), stop=True),
                           reads=rk, writes=[f'ps{bk_}'], signal=(last and u == 1))

                def cp(eng, dst, kd, bk_, col0, n):
                    if eng == 'act':
                        op('act', lambda e: e.activation(out=W_(dst), in_=ps[bk_][:, col0:col0 + n], func=AF.Copy), reads=[f'ps{bk_}'], writes=kd)
                    else:
                        op('dve', lambda e: e.tensor_copy(out=W_(dst), in_=ps[bk_][:, col0:col0 + n]), reads=[f'ps{bk_}'], writes=kd)

                op('dve', lambda e: e.tensor_tensor(out=v2(W_(B1)), in0=v2(ps[bA][:, cs_]), in1=mskb(0), op=ALU.mult), reads=[f'ps{bA}', 'MSK'], writes=k1)
                op('dve', lambda e: e.tensor_tensor(out=v2(W_(B2)), in0=v2(ps[bAT][:, cs_]), in1=mskb(1), op=ALU.mult), reads=[f'ps{bAT}', 'MSK'], writes=k2)
                e3 = 'pool'
                op(e3, lambda e: e.tensor_tensor(out=W_(TP[:, :]).rearrange("p (u t) -> p u t", u=4), in0=XX[:, :].rearrange("p (u t) -> p u t", u=4),
                                                 in1=ident[:].rearrange("p (o t) -> p o t", o=1).to_broadcast([128, 4, 128]), op=ALU.add),
                   reads=k1 + k2 + ['ident'], writes=k3 + k4)
                yield
                def tn_from_pt():
                    bt_ = gbank()
                    for u in range(2):
                        us = slice(u * 128, (u + 1) * 128)
                        op('pe', lambda e: e.transpose(out=ps[bt_][:, :].bitcast(FP16)[:, us], in_=B4[:, us], identity=IDH[:]),
                           reads=k4 + ['IDH'], writes=[f'ps{bt_}'], signal=(u == 1))
                    return bt_

                for lev in range(3):
                    bk_ = gbank()
                    mm2(bk_, 0, B2, B1, k1 + k2, last=False)
                    mm2(bk_, 256, B1, B2, k1 + k2)
                    yield
                    cp('act', XX[:, :], k1 + k2, bk_, 0, 512)
                    gfree(bk_)
                    yield
                    bk_ = gbank()
                    mm2(bk_, 0, B1, B4, k1 + k4)
                    yield
                    op('dve', lambda e: e.tensor_tensor(out=W_(B4), in0=ps[bk_][:, 0:256], in1=B4, op=ALU.add), reads=[f'ps{bk_}'] + k4, writes=k4)
                    gfree(bk_)
                    yield
                for kl in range(1, 4):
                    op('dve', lambda e, kl=kl: e.tensor_tensor(out=v2(W_(B5)), in0=v2(ps[bA][:, cs_]), in1=mskb(2 * kl), op=ALU.mult), reads=[f'ps{bA}', 'MSK'], writes=k5)
                    bt_ = tn_from_pt()
                    yield
                    op('act', lambda e: e.activation(out=B3, in_=ps[bt_][:, :].bitcast(FP16)[:, 0:256], func=AF.Copy), reads=[f'ps{bt_}'], writes=k3)
                    gfree(bt_)
                    bz = gbank()
                    mm2(bz, 0, B5, B4, k5 + k4)
                    yield
                    cp('act', B1, k1, bz, 0, 256)
                    gfree(bz)
                    yield
                    bz = gbank()
                    mm2(bz, 0, B3, B1, k3 + k1)
                    yield
                    if kl < 3:
                        op('dve', lambda e: e.tensor_tensor(out=W_(B4), in0=ps[bz][:, 0:256], in1=B4, op=ALU.add), reads=[f'ps{bz}'] + k4, writes=k4)
                    else:
                        op('dve', lambda e: e.tensor_tensor(out=Pb_, in0=ps[bz][:, 0:256], in1=B4, op=ALU.add), reads=[f'ps{bz}'] + k4, writes=kP)
                    gfree(bz)
                    yield
                done.append(1)
                if len(done) == 2:
                    reserved.discard(bA); reserved.discard(bAT)
                us2 = [2 * hf, 2 * hf + 1]
                bW = gbank()
                for j, u in enumerate(us2):
                    op('pe', lambda e: e.matmul(ps[bW][:, j * 64:(j + 1) * 64], lhsT=U44[:, u, 2, :], rhs=TM4[:, u, 3, fsl], start=True, stop=True),
                       reads=kU + ['TM'], writes=[f'ps{bW}'], signal=(j == 1))
                yield
                op('act', lambda e: e.activation(out=Wt_, in_=ps[bW][:, 0:128], func=AF.Copy), reads=[f'ps{bW}'], writes=kW)
                gfree(bW)
                yield
                bU = gbank()
                for j, u in enumerate(us2):
                    op('pe', lambda e: e.matmul(ps[bU][:, j * 128:j * 128 + 64], lhsT=Pb_[:, j * 128:(j + 1) * 128], rhs=TM4[:, u, 0, fsl], start=True, stop=True),
                       reads=kP + ['TM'], writes=[f'ps{bU}'], signal=False)
                    op('pe', lambda e: e.matmul(ps[bU][:, j * 128 + 64:(j + 1) * 128], lhsT=Pb_[:, j * 128:(j + 1) * 128], rhs=Wt_[:, j * 64:(j + 1) * 64], start=True, stop=True),
                       reads=kP + kW, writes=[f'ps{bU}'], signal=(j == 1))
                yield
                op('act', lambda e: e.activation(out=AU_, in_=ps[bU][:, 0:256], func=AF.Copy), reads=[f'ps{bU}'], writes=kAU)
                gfree(bU)
                yield
                bRY = gbank()
                for j, u in enumerate(us2):
                    op('pe', lambda e: e.matmul(ps[bRY][hsl, j * 128:(j + 1) * 128], lhsT=AU_[:, j * 128:j * 128 + 64], rhs=U44[:, u, 1, :], start=True, stop=True),
                       reads=kAU + kU, writes=[f'ps{bRY}'], signal=False)
                for j, u in enumerate(us2):
                    op('pe', lambda e: e.matmul(ps[bRY][hsl, 256 + j * 128:256 + (j + 1) * 128], lhsT=AU_[:, j * 128 + 64:(j + 1) * 128], rhs=U44[:, u, 1, :], start=True, stop=False),
                       reads=kAU + kU, writes=[f'ps{bRY}'], signal=False)
                    op('pe', lambda e: e.matmul(ps[bRY][hsl, 256 + j * 128:256 + (j + 1) * 128], lhsT=TM4[:, u, 3, fsl], rhs=U44[:, u, 3, :], start=False, stop=True),
                       reads=['TM'] + kU, writes=[f'ps{bRY}'], signal=(j == 1))
                yield
                tsl = slice(fc * 512 + hf * 256, fc * 512 + (hf + 1) * 256)
                op('dve', lambda e: e.tensor_tensor(out=RH[hsl, tsl].rearrange("p (c t) -> p c t", c=2),
                                                    in0=ps[bRY][hsl, 0:256].rearrange("p (c t) -> p c t", c=2),
                                                    in1=ARf[hsl, fc * 1024 + hf * 512:fc * 1024 + (hf + 1) * 512].rearrange("p (c k t) -> p c k t", c=2, k=2)[:, :, 1, :], op=ALU.add),
                   reads=[f'ps{bRY}'] + kAR, writes=[f'RH{fc}_{hs}_{hf}'])
                op('dve', lambda e: e.tensor_copy(out=F1[hsl, fc, hf * 256:(hf + 1) * 256], in_=ps[bRY][hsl, 256:512]), reads=[f'ps{bRY}'], writes=[f'F1_{fc}'])
                gfree(bRY)
                yield
                bGH = gbank()
                for j, u in enumerate(us2):
                    op('pe', lambda e: e.matmul(ps[bGH][hsl, j * 64:(j + 1) * 64], lhsT=AU_[:, j * 128:j * 128 + 64], rhs=TM4[:, u, 1, fsl], start=True, stop=True),
                       reads=kAU + ['TM'], writes=[f'ps{bGH}'], signal=False)
                for j, u in enumerate(us2):
                    op('pe', lambda e: e.matmul(ps[bGH][hsl, 128 + j * 64:128 + (j + 1) * 64], lhsT=TM4[:, u, 1, fsl], rhs=AU_[:, j * 128 + 64:(j + 1) * 128], start=True, stop=False),
                       reads=kAU + ['TM'], writes=[f'ps{bGH}'], signal=False)
                    op('pe', lambda e: e.matmul(ps[bGH][hsl, 128 + j * 64:128 + (j + 1) * 64], lhsT=TM4[:, u, 2, fsl], rhs=TM4[:, u, 3, fsl], start=False, stop=True),
                       reads=['TM'], writes=[f'ps{bGH}'], signal=(j == 1))
                yield
                gsl = slice(fc * 256 + hf * 128, fc * 256 + (hf + 1) * 128)
                op('dve', lambda e: e.tensor_tensor(out=GT[hsl, gsl], in0=ps[bGH][hsl, 0:128], in1=ID2[hsl, 0:128], op=ALU.add),
                   reads=[f'ps{bGH}', 'ID2'], writes=[f'GT{fc}_{hs}_{hf}'])
                op('dve', lambda e: e.tensor_tensor(out=HH[hsl, fc * 256 + hf * 128:fc * 256 + (hf + 1) * 128].rearrange("p (u i) -> p u i", u=2),
                                                    in0=ps[bGH][hsl, 128:256].rearrange("p (u i) -> p u i", u=2),
                                                    in1=GC[hsl, fc * 4 + 2 * hf:fc * 4 + 2 * hf + 2].rearrange("p (u o) -> p u o", o=1).to_broadcast([64, 2, 64]), op=ALU.mult),
                   reads=[f'ps{bGH}', 'GC'], writes=[f'HH{fc}_{hs}_{hf}'])
                gfree(bGH)
                yield

            fine = [f'{n}{fc}_{hs}_{hf}' for n in ('RH', 'GT', 'HH') for fc in range(8) for hs in range(2) for hf in range(2)]
            op('pool', lambda e: e.memset(SCR[:, 1:2], 0.0), reads=[], writes=f3k + fine)
            for fc in range(8):
                srcs = [lambda c, fc=fc: ARf[:, fc * 1024 + c * 256:fc * 1024 + c * 256 + 128],
                        lambda c, fc=fc: BH[:, 16 + fc, c * 128:(c + 1) * 128],
                        lambda c, fc=fc: BH[:, 24 + fc, c * 128:(c + 1) * 128],
                        lambda c, fc=fc: XN[:, fc, c * 128:(c + 1) * 128]]
                skeys = [[f'BH{2*fc}', f'BH{2*fc+1}'], [f'BH{16+fc}'], [f'BH{24+fc}'], [f'XN{fc}']]
                for half in range(2):
                    bk = nbank()
                    psb = ps[bk][:, :].bitcast(BF16)
                    for cl in range(2):
                        c = half * 2 + cl
                        for kind in range(4):
                            o = (cl * 4 + kind) * 128
                            op('pe', lambda e, c=c, kind=kind, o=o, psb=psb, srcs=srcs: e.transpose(out=psb[:, o:o + 128], in_=srcs[kind](c), identity=identb[:]),
                               reads=skeys[kind] + ['identb'], writes=[f'ps{bk}'], signal=(cl == 1 and kind == 3))
                    op('act', lambda e, half=half, psb=psb: e.activation(out=TM[:, half * 1024:(half + 1) * 1024], in_=psb, func=AF.Copy),
                       reads=[f'ps{bk}'], writes=['TM'])
                gens = []
                for hs in range(2):
                    bA_, bAT_ = head_prologue(fc, hs, sets[hs])
                    done = []
                    for hf in range(2):
                        gens.append(half_steps(fc, hs, sets[hs], hf, bA_, bAT_, done))
                if _os0.environ.get('SEQG', '0') == '1':
                    for g in gens:
                        for _ in g:
                            pass
                    gens = []
                _hs = int(_os0.environ.get('HSTOP', '999'))
                _rounds = 0
                while gens:
                    if _rounds >= _hs:
                        reserved.clear()
                        break
                    _rounds += 1
                    for g in list(gens):
                        try:
                            next(g)
                        except StopIteration:
                            gens.remove(g)
            op('pool', lambda e: e.memset(SCR[:, 2:3], 0.0), reads=[], writes=f3k + fine + [f'XNP{c}' for c in range(8)] + sets[1]['kS'] + hk1)
            if BSTOP[0] <= 2:
                return
            for c in range(4):
                bY = [nbank(), nbank()]
                bZ = nbank()
                for fc in range(8):
                    for hs in range(2):
                        hsl = slice(64 * hs, 64 * hs + 64)
                        op('pe', lambda e, fc=fc, hsl=hsl, c=c: e.matmul(ps[bY[fc // 4]][hsl, (fc % 4) * 128:(fc % 4 + 1) * 128], lhsT=STt[hsl, fc, :],
                                                                        rhs=RH[hsl, fc * 512 + c * 128:fc * 512 + (c + 1) * 128], start=True, stop=True),
                           reads=['STt'] + f3k, writes=[f'ps{bY[fc // 4]}'], signal=(fc % 4 == 3 and hs == 1))
                for fc in range(8):
                    for hs in range(2):
                        hsl = slice(64 * hs, 64 * hs + 64)
                        op('pe', lambda e, fc=fc, hsl=hsl, c=c: e.matmul(ps[bZ][hsl, fc * 64:(fc + 1) * 64], lhsT=GT[hsl, fc * 256 + c * 64:fc * 256 + (c + 1) * 64],
                                                                        rhs=STt[hsl, fc, :], start=True, stop=True),
                           reads=['STt'] + f3k, writes=[f'ps{bZ}'], signal=(fc == 7 and hs == 1))
                for half in range(2):
                    op('dve', lambda e, half=half, c=c: e.tensor_tensor(out=F1[:, half * 4:half * 4 + 4, c * 128:(c + 1) * 128],
                                                                        in0=ps[bY[half]][:, :].rearrange("p (f t) -> p f t", f=4),
                                                                        in1=F1[:, half * 4:half * 4 + 4, c * 128:(c + 1) * 128], op=ALU.add),
                       reads=[f'ps{bY[half]}'] + [f'F1_{f}' for f in range(half * 4, half * 4 + 4)], writes=[f'F1_{f}' for f in range(half * 4, half * 4 + 4)])
                for fc in range(8):
                    op('dve', lambda e, fc=fc, c=c: e.scalar_tensor_tensor(out=STt[:, fc, :], in0=ps[bZ][:, fc * 64:(fc + 1) * 64], scalar=GC[:, fc * 4 + c:fc * 4 + c + 1],
                                                                           in1=HH[:, fc * 256 + c * 64:fc * 256 + (c + 1) * 64], op0=ALU.mult, op1=ALU.add),
                       reads=[f'ps{bZ}', 'GC'] + f3k, writes=['STt'])
            if BSTOP[0] <= 3:
                return
            for m in range(8):
                op('act', lambda e, m=m: e.activation(out=PTb, in_=F1[:, m, :], func=AF.Copy), reads=[f'F1_{m}'], writes=['PTb'])
                b1 = nbank()
                op('pe', lambda e, b1=b1: e.matmul(ps[b1][:, :], lhsT=BO64[:], rhs=PTb, start=True, stop=True), reads=['BO64', 'PTb'], writes=[f'ps{b1}'])
                op('dve', lambda e, m=m, b1=b1: e.tensor_tensor(out=PT[0][:], in0=F1[:, m, :], in1=ps[b1][:, :], op=ALU.subtract), reads=[f'F1_{m}', f'ps{b1}'], writes=['PT0'])
                op('act', lambda e: e.activation(out=PTb, in_=PT[0][:], func=AF.Square), reads=['PT0', 'PTb'], writes=['PTb'])
                b2 = nbank()
                op('pe', lambda e, b2=b2: e.matmul(ps[b2][:, :], lhsT=BO64[:], rhs=PTb, start=True, stop=True), reads=['BO64', 'PTb'], writes=[f'ps{b2}'])
                op('act', lambda e, b2=b2: e.activation(out=PT[1][:], in_=ps[b2][:, :], func=AF.Ln, bias=64e-5), reads=[f'ps{b2}'], writes=['PT1'])
                op('act', lambda e: e.activation(out=PT[1][:], in_=PT[1][:], func=AF.Exp, scale=-0.5), reads=['PT1'], writes=['PT1'])
                op('dve', lambda e: e.tensor_tensor(out=PT[0][:], in0=PT[0][:], in1=PT[1][:], op=ALU.mult), reads=['PT0', 'PT1'], writes=['PT0'])
                op('dve', lambda e, m=m: e.tensor_scalar(out=PT[0][:], in0=PT[0][:], scalar1=vcol('b_gn_g', 0, m), scalar2=vcol('b_gn_b', 0, m), op0=ALU.mult, op1=ALU.add),
                   reads=['PT0', 'VT'], writes=['PT0'])
                op('pool', lambda e, m=m: e.tensor_tensor(out=PT[0][:], in0=PT[0][:], in1=F2[:, m, 4:T + 4], op=ALU.add), reads=['PT0', f'F2_{m}'], writes=['PT0'])
                op('dve', lambda e, m=m: e.tensor_tensor(out=BH[:, m, :], in0=PT[0][:], in1=SQ[:, m, :], op=ALU.mult), reads=['PT0', f'SQ{m}'], writes=[f'BH{m}'])
            proj('b_w_o', 2, lambda kc: BH[:, kc, :], lambda kc: f'BH{kc}', 8, evac_branch(None))
            post_norm(7)

        for b in range(NB):
            if nstage >= 2:
                mem_prep(b, [0, 1] if nstage >= 5 else [0])
            for i in range(NT):
                load_tile(b, i)
                if nstage >= 1:
                    stage_A(b, i)
                if nstage >= 2:
                    stage_C(0)
                if nstage >= 3:
                    stage_M(0)
                if nstage >= 4:
                    stage_B(b, i)
                if nstage >= 5:
                    stage_C(1)
                if nstage >= 6:
                    stage_M(1)
                store_tile(b, i)
        S_.finish('sp', ['y'])
        for k in ('xin0', 'ptio0', 'ptio1'):
            if k in S_.dsem:
                S_.ops['sp'].append(lambda e, semh=S_.dsem[k], v=S_.dcnt[k]: e.wait_ge(semh, v))
        S_.emit()
        build.nops = S_.nops
    return nc


def make_masks():
    t = np.arange(128)[:, None]
    s_ = np.arange(128)[None, :]
    low = t > s_
    ms = []
    m0 = low & (t // 16 == s_ // 16)
    ms += [m0, m0.T]
    for blk in (16, 32, 64):
        mk = (t // (2 * blk) == s_ // (2 * blk)) & ((t // blk) % 2 == 1) & ((s_ // blk) % 2 == 0)
        ms += [mk, mk.T]
    return np.ascontiguousarray(np.stack(ms, axis=1).astype(np.float32).reshape(128, 8 * 128))


def _prep_inputs(inp):
    vecs = pack_vecs(inp)
    wts = pack_weights(inp)
    return vecs, wts


def kernel(**inputs):
    NB, S = 4, 2048
    x = np.asarray(inputs['x'], np.float32)
    mem = np.asarray(inputs['mem'], np.float32)
    vecs, wts = _prep_inputs(inputs)
    nc = build(NB, S)
    in_maps = []
    for c in range(8):
        in_maps.append({"x": np.ascontiguousarray(x[c * NB:(c + 1) * NB].reshape(NB * S, D)),
                        "mem": np.ascontiguousarray(mem[c * NB:(c + 1) * NB].reshape(NB * MEM, D)),
                        "vecs": vecs, "wts": wts, "masks": make_masks()})
    res = run_bass_kernel_spmd(nc, in_maps, core_ids=list(range(8)))
    out = np.concatenate([r["y"].reshape(NB, S, D) for r in res.results], axis=0)
    return out.astype(np.float32)
```

```python
import numpy as np
from contextlib import ExitStack
import concourse.bass as bass
import concourse.mybir as mybir
from concourse.bass_utils import run_bass_kernel_spmd

F32 = mybir.dt.float32
BF16 = mybir.dt.bfloat16
FP16 = mybir.dt.float16
AF = mybir.ActivationFunctionType
ALU = mybir.AluOpType
AX = mybir.AxisListType
import os as _os0
TDT = mybir.dt.float32r if _os0.environ.get('TDT', 'r') == 'r' else mybir.dt.float32

D = 1024
T = 512
MEM = 256
PW = 4096
NRING = 3

VEC_ORDER = [('ln_gains', 12), ('mem_norm', 1), ('a_conv_w', 4), ('a_conv_b', 1), ('a_b_in', 2), ('a_gate_b', 2),
             ('a_lambda', 1), ('a_b_out', 1), ('b_mu', 6), ('b_w0', 1), ('b_a0', 1), ('b_k_k', 1), ('b_k_a', 1),
             ('b_r_k', 1), ('b_gn_g', 1), ('b_gn_b', 1)]
VOFF = {}
_o = 0
for _n, _c in VEC_ORDER:
    VOFF[_n] = _o
    _o += _c
NVEC = _o


def pack_vecs(inp):
    rows = [np.asarray(inp[n], np.float32).reshape(-1) for n, _ in VEC_ORDER]
    v = np.concatenate(rows)
    assert v.size == NVEC * D
    return np.ascontiguousarray(v.reshape(NVEC * 8, 128))


def _mat_pieces(W, MW):
    K, N = W.shape
    KC = K // 128
    NPc = N // MW
    a = W.reshape(KC, 128, NPc, MW).transpose(2, 1, 0, 3).reshape(NPc, 128, KC * MW)
    if KC * MW < PW:
        a = np.concatenate([a, np.zeros((NPc, 128, PW - KC * MW), np.float32)], axis=2)
    return a


PIECES = {}
PGAIN = []


def _layout():
    PIECES.clear()
    PGAIN.clear()

    def add(name, cnt, gain, KC, MW):
        PIECES[name] = (len(PGAIN), cnt)
        for _ in range(cnt):
            PGAIN.append((None if gain is None else [(0, MW, ('VT', gain))], KC, MW))

    g = VOFF['ln_gains']
    add('w_in', 4, g + 0, 8, 512)
    add('gates', 1, None, 16, 256)
    add('a_w_out', 2, None, 8, 512)
    for l in range(2):
        add(f'wq{l}', 2, g + 6 * l + 2, 8, 512)
        add(f'wkv{l}', 4, VOFF['mem_norm'], 8, 512)
        add(f'wo{l}', 2, None, 8, 512)
        add(f'up{l}', 8, g + 6 * l + 4, 8, 512)
        add(f'down{l}', 8, None, 32, 128)
    for var in range(2):
        PIECES['rkv' + 'AB'[var]] = (len(PGAIN), 6)
        for mix in (0, 0, 2, 2, 3, 3):
            PGAIN.append(([(0, 512, ('DV', 2 * mix + var))], 8, 512))
    for var in range(2):
        PIECES['loraA' + 'ab'[var]] = (len(PGAIN), 1)
        PGAIN.append(([(0, 64, ('DV', 2 * 1 + var)), (64, 128, ('DV', 2 * 4 + var)), (128, 256, ('DV', 2 * 5 + var))], 8, 256))
    add('loraB', 1, None, 3, 1024)
    add('b_w_o', 2, None, 8, 512)


_layout()
NPIECE = len(PGAIN)


def pack_weights(inp):
    f = lambda k: np.asarray(inp[k], np.float32)
    out = np.zeros((NPIECE, 128, PW), np.float32)

    def put(name, arr):
        i0, cnt = PIECES[name]
        assert arr.shape[0] == cnt, (name, arr.shape)
        out[i0:i0 + cnt] = arr

    put('w_in', _mat_pieces(f('a_w_in')[0], 512))
    gw = f('a_gate_w')[0].reshape(8, 2, 128, 256)
    put('gates', gw.transpose(2, 0, 1, 3).reshape(1, 128, 8 * 2 * 256))
    put('a_w_out', _mat_pieces(f('a_w_out')[0], 512))
    for l in range(2):
        put(f'wq{l}', _mat_pieces(f('c_w_q')[l], 512))
        put(f'wkv{l}', _mat_pieces(f('c_w_kv')[l], 512))
        put(f'wo{l}', _mat_pieces(f('c_w_o')[l], 512))
        put(f'up{l}', _mat_pieces(f('m_w_up')[l], 512))
        put(f'down{l}', _mat_pieces(f('m_w_down')[l], 128))
    rkv = f('b_w_rkv')[0]
    rk6 = np.concatenate([_mat_pieces(rkv[i], 512) for i in range(3)], axis=0)
    put('rkvA', rk6)
    put('rkvB', rk6)
    la = np.concatenate([f('b_w1')[0], f('b_a1')[0], f('b_g1')[0]], axis=1)
    put('loraAa', _mat_pieces(la, 256))
    put('loraAb', _mat_pieces(la, 256))
    lb = np.zeros((128, 3, 1024), np.float32)
    lb[:64, 0] = f('b_w2')[0]
    lb[64:, 1] = f('b_a2')[0]
    lb[:, 2] = f('b_g2')[0]
    put('loraB', np.concatenate([lb.reshape(1, 128, 3072), np.zeros((1, 128, PW - 3072), np.float32)], axis=2))
    put('b_w_o', _mat_pieces(f('b_w_o')[0], 512))
    return out


class _Rec:
    def __init__(self):
        self.name = None

    def __getattr__(self, name):
        def f(*args, **kwargs):
            self.name, self.args, self.kwargs = name, args, kwargs
            return self
        return f


class Sched:
    ENG = ('pe', 'act', 'dve', 'pool', 'sp')

    def __init__(self, nc, es):
        self.nc = nc
        self.es = es
        self.ops = {e: [] for e in self.ENG}
        self.sem = {e: es.enter_context(nc.semaphore('s_' + e)) for e in self.ENG}
        self.cnt = {e: 0 for e in self.ENG}
        self.known = {e: {} for e in self.ENG}
        self.last_w = {}
        self.reads = {}
        self.dsem = {}
        self.dcnt = {}
        self.nops = 0

    def _deps(self, eng, reads, writes):
        acc = {}

        def need(dep):
            s, v = dep
            if acc.get(s, 0) < v:
                acc[s] = v
        for b in reads:
            w = self.last_w.get(b)
            if w:
                need(w)
        for b in writes:
            w = self.last_w.get(b)
            if w:
                need(w)
            for r in self.reads.get(b, {}).items():
                need(r)
        for s, v in acc.items():
            if self.known[eng].get(s, 0) >= v:
                continue
            if s == eng and eng in ('pe', 'sp'):
                continue
            if s in self.cnt:
                assert v <= self.cnt[s], f"wait on unsignaled {s} {v} > {self.cnt[s]}"
                semh = self.sem[s]
            else:
                semh = self.dsem[s]
            self.known[eng][s] = v
            self.ops[eng].append(lambda e, semh=semh, v=v: e.wait_ge(semh, v))

    def _record(self, reads, writes, tag):
        for b in reads:
            d = self.reads.setdefault(b, {})
            if d.get(tag[0], 0) < tag[1]:
                d[tag[0]] = tag[1]
        for b in writes:
            self.last_w[b] = tag
            self.reads[b] = {}

    def op(self, eng, fn, reads=(), writes=(), signal=True):
        self.nops += 1
        self._deps(eng, reads, writes)
        val = self.cnt[eng] + 1
        rec = _Rec()
        fn(rec)
        assert rec.name is not None
        if signal:
            self.cnt[eng] += 1
            semh = self.sem[eng]
            self.ops[eng].append(lambda e, r=rec, semh=semh: getattr(e, r.name)(*r.args, **r.kwargs).then_inc(semh, 1))
        else:
            self.ops[eng].append(lambda e, r=rec: getattr(e, r.name)(*r.args, **r.kwargs))
        self._record(reads, writes, (eng, val))

    def dma(self, eng, out, in_, reads=(), writes=(), key=None):
        self.nops += 1
        self._deps(eng, reads, writes)
        if key not in self.dsem:
            self.dsem[key] = self.es.enter_context(self.nc.semaphore('d_' + key))
            self.dcnt[key] = 0
        self.dcnt[key] += 16
        semh = self.dsem[key]
        self.ops[eng].append(lambda e, out=out, in_=in_, semh=semh: e.dma_start(out=out, in_=in_).then_inc(semh, 16))
        self._record(reads, writes, (key, self.dcnt[key]))

    def barrier(self):
        snap = dict(self.cnt)
        dsnap = dict(self.dcnt)
        for eng in self.ENG:
            for s, v in snap.items():
                if s == eng or v == 0 or self.known[eng].get(s, 0) >= v:
                    continue
                self.known[eng][s] = v
                self.ops[eng].append(lambda e, semh=self.sem[s], v=v: e.wait_ge(semh, v))
            for s, v in dsnap.items():
                if self.known[eng].get(s, 0) >= v:
                    continue
                self.known[eng][s] = v
                self.ops[eng].append(lambda e, semh=self.dsem[s], v=v: e.wait_ge(semh, v))

    def finish(self, eng, keys):
        acc = {}
        for b in keys:
            w = self.last_w.get(b)
            if w and acc.get(w[0], 0) < w[1]:
                acc[w[0]] = w[1]
        for s, v in acc.items():
            semh = self.sem[s] if s in self.cnt else self.dsem[s]
            self.ops[eng].append(lambda e, semh=semh, v=v: e.wait_ge(semh, v))

    def emit(self):
        with self.nc.Block() as block:
            @block.tensor
            def _(e):
                for f in self.ops['pe']:
                    f(e)

            @block.scalar
            def _(e):
                for f in self.ops['act']:
                    f(e)

            @block.vector
            def _(e):
                for f in self.ops['dve']:
                    f(e)

            @block.gpsimd
            def _(e):
                for f in self.ops['pool']:
                    f(e)

            @block.sync
            def _(e):
                for f in self.ops['sp']:
                    f(e)


STAGES = ['load', 'A', 'C0', 'M0', 'B', 'C1', 'M1']
BSTOP = [9]


def build(NB, S, stop='M1', use_gelu=True):
    NT = S // T
    nstage = STAGES.index(stop)
    nc = bass.Bass("TRN2", target_bir_lowering=False)
    x_d = nc.dram_tensor("x", [NB * S, D], F32, kind="ExternalInput").ap()
    mem_d = nc.dram_tensor("mem", [NB * MEM, D], F32, kind="ExternalInput").ap()
    vec_d = nc.dram_tensor("vecs", [NVEC * 8, 128], F32, kind="ExternalInput").ap()
    wts_d = nc.dram_tensor("wts", [NPIECE, 128, PW], F32, kind="ExternalInput").ap()
    msk_d = nc.dram_tensor("masks", [128, 8 * 128], F32, kind="ExternalInput").ap()
    y_d = nc.dram_tensor("y", [NB * S, D], F32, kind="ExternalOutput").ap()
    wsc = nc.dram_tensor("wsc", [NPIECE, 128, PW], BF16, kind="Internal").ap()

    with ExitStack() as es:
        S_ = Sched(nc, es)
        op = S_.op

        def sb(name, shape, dt):
            return es.enter_context(nc.sbuf_tensor(name, shape, dt))

        VT = sb("VT", [128, NVEC * 8], F32)
        ident = sb("ident", [128, 128], F32)
        identb = sb("identb", [128, 128], BF16)
        onesb = sb("onesb", [128, 128], BF16)
        CV2 = sb("CV2", [128, 16], F32)
        DV = sb("DV", [128, 13 * 8], F32)
        ps = [es.enter_context(nc.psum_tensor(f"ps{i}", [128, 512], F32)) for i in range(8)]
        bank_ctr = [0]

        reserved = set()

        def nbank():
            for _ in range(17):
                b = bank_ctr[0] % 8
                bank_ctr[0] += 1
                if b not in reserved:
                    return b
            raise RuntimeError('no free PSUM bank')

        def vcol(name, idx=0, c=0):
            j = (VOFF[name] + idx) * 8 + c
            return VT[:, j:j + 1]

        op('pool', lambda e: e.memset(ident[:], 0.0), writes=['ident'])
        op('pool', lambda e: e.affine_select(out=ident[:], in_=ident[:], pattern=[[-1, 128]], base=0,
                                             channel_multiplier=1, compare_op=ALU.not_equal, fill=1.0),
           reads=['ident'], writes=['ident'])
        op('pool', lambda e: e.tensor_copy(out=identb[:], in_=ident[:]), reads=['ident'], writes=['identb'])
        op('pool', lambda e: e.memset(onesb[:], 1.0), writes=['onesb'])

        with ExitStack() as es0:
            def sb0(name, shape, dt):
                return es0.enter_context(nc.sbuf_tensor(name, shape, dt))
            vst = [sb0(f"vst{i}", [128, 128], F32) for i in range(3)]
            nrows = NVEC * 8
            for i in range(3):
                r0 = i * 128
                r1 = min(nrows, r0 + 128)
                n = r1 - r0
                S_.dma('sp', vst[i][0:n, :], vec_d[r0:r1, :], writes=[f'vst{i}'], key=f'vst{i}')
                b = nbank()
                op('pe', lambda e, i=i, n=n, b=b: e.transpose(out=ps[b][:, 0:n], in_=vst[i][0:n, :], identity=ident[0:n, 0:n]),
                   reads=[f'vst{i}', 'ident'], writes=[f'ps{b}'])
                op('act', lambda e, r0=r0, n=n, b=b: e.activation(out=VT[:, r0:r0 + n], in_=ps[b][:, 0:n], func=AF.Copy),
                   reads=[f'ps{b}'], writes=['VT'])
            lam = VT[:, VOFF['a_lambda'] * 8:VOFF['a_lambda'] * 8 + 8]
            op('act', lambda e: e.activation(out=CV2[:, 0:8], in_=lam, func=AF.Exp, scale=-1.0), reads=['VT'], writes=['CV2'])
            op('act', lambda e: e.activation(out=CV2[:, 0:8], in_=CV2[:, 0:8], func=AF.Ln, bias=1.0), reads=['CV2'], writes=['CV2'])
            op('act', lambda e: e.activation(out=CV2[:, 0:8], in_=CV2[:, 0:8], func=AF.Copy, scale=-8.0), reads=['CV2'], writes=['CV2'])

            g6 = VT[:, (VOFF['ln_gains'] + 6) * 8:(VOFF['ln_gains'] + 6) * 8 + 8]
            for mi in range(6):
                mu_i = VT[:, (VOFF['b_mu'] + mi) * 8:(VOFF['b_mu'] + mi) * 8 + 8]
                op('dve', lambda e, mi=mi, mu_i=mu_i: e.tensor_tensor(out=DV[:, (2 * mi + 1) * 8:(2 * mi + 2) * 8], in0=mu_i, in1=g6, op=ALU.mult),
                   reads=['VT'], writes=['DV'])
                op('dve', lambda e, mi=mi: e.tensor_tensor(out=DV[:, (2 * mi) * 8:(2 * mi + 1) * 8], in0=g6, in1=DV[:, (2 * mi + 1) * 8:(2 * mi + 2) * 8], op=ALU.subtract),
                   reads=['VT', 'DV'], writes=['DV'])
            ka = VT[:, VOFF['b_k_a'] * 8:VOFF['b_k_a'] * 8 + 8]
            op('dve', lambda e: e.tensor_scalar(out=DV[:, 96:104], in0=ka, scalar1=-1.0, scalar2=1.0, op0=ALU.mult, op1=ALU.add),
               reads=['VT'], writes=['DV'])

            NST = 3
            stf = [sb0(f"stf{i}", [128, PW], F32) for i in range(NST)]
            stb = [sb0(f"stb{i}", [128, PW], BF16) for i in range(NST)]
            for pi in range(NPIECE):
                k = pi % NST
                gain, KC, MW = PGAIN[pi]
                S_.dma('sp', stf[k][:], wts_d[pi], writes=[f'stf{k}'], key=f'stf{k}')
                eng = ('dve', 'pool')[pi % 2] if gain is not None else ('act', 'dve', 'pool')[pi % 3]
                if gain is None:
                    if eng == 'act':
                        op('act', lambda e, k=k: e.activation(out=stb[k][:], in_=stf[k][:], func=AF.Copy),
                           reads=[f'stf{k}'], writes=[f'stb{k}'])
                    else:
                        op(eng, lambda e, k=k: e.tensor_copy(out=stb[k][:], in_=stf[k][:]),
                           reads=[f'stf{k}'], writes=[f'stb{k}'])
                else:
                    for kc in range(KC):
                        for (c0, c1, (tab, gi)) in gain:
                            gc = (VT if tab == 'VT' else DV)[:, gi * 8 + kc:gi * 8 + kc + 1]
                            op(eng, lambda e, k=k, kc=kc, MW=MW, gc=gc, c0=c0, c1=c1: e.tensor_scalar(
                                out=stb[k][:, kc * MW + c0:kc * MW + c1], in0=stf[k][:, kc * MW + c0:kc * MW + c1],
                                scalar1=gc, scalar2=1.0, op0=ALU.mult, op1=ALU.mult),
                               reads=[f'stf{k}', 'VT', 'DV'], writes=[f'stb{k}'])
                S_.dma('act', wsc[pi], stb[k][:], reads=[f'stb{k}'], writes=['wsc'], key=f'wsc{k}')
        S_.barrier()

        import os as _os2
        _ex = int(_os2.environ.get('EXTRA_SBUF', '0'))
        if _ex:
            DUMMY = sb('DUMMY', [128, _ex * 256], F32)
            op('pool', lambda e: e.memset(DUMMY[:, _ex * 256 - 512:], 1.0), writes=['DUMMY'])
        X = sb("X", [128, 8, T], F32)
        XN = sb("XN", [128, 8, T], BF16)
        SQ = sb("SQ", [128, 8, T], BF16)
        RS = sb("RS", [128, T], F32)
        F1 = sb("F1", [128, 8, T], F32)
        F2 = sb("F2", [128, 8, T + 4], F32)
        F3 = sb("F3", [128, 8, T], F32)
        BH = sb("BH", [128, 32, T], BF16)
        PT = [sb(f"PT{i}", [128, T], F32) for i in range(4)]
        TB = [sb(f"TB{i}", [128, T], F32) for i in range(5)]
        ring = [sb(f"ring{i}", [128, PW], BF16) for i in range(NRING)]
        xin = [sb("xin0", [128, D], F32)] * 2
        KT = [sb(f"KT{l}", [128, 8, MEM], BF16) for l in range(2)]
        VV = [sb(f"VV{l}", [128, 2, D], BF16) for l in range(2)]
        HST = sb("HST", [128, 8], F32)
        SMX = sb("SMX", [128, 72], F32)


        XNP = sb("XNP", [128, 8, T + 8], BF16)
        RW = sb("RW", [128, 6400], BF16)
        STt = sb("STt", [128, 8, 64], BF16)
        GC = sb("GC", [128, 32], F32)
        XL = sb("XL", [128, 8], BF16)
        SCR = sb("SCR", [128, 8], F32)
        IDH = sb("IDH", [128, 128], FP16)
        MSK = sb("MSK", [128, 8, 128], BF16)
        MK2 = sb("MK2", [128, 512], BF16)
        ID2 = sb("ID2", [128, 256], BF16)
        BOb = sb("BOb", [128, 128], BF16)
        BO64 = sb("BO64", [128, 128], BF16)
        S_.dma('sp', xin[0][:], msk_d, writes=['xin0'], key='xin0')
        op('dve', lambda e: e.tensor_copy(out=MSK[:].rearrange("p a t -> p (a t)"), in_=xin[0][:]), reads=['xin0'], writes=['MSK'])
        op('dve', lambda e: e.tensor_copy(out=IDH[:], in_=ident[:]), reads=['ident'], writes=['IDH'])
        op('pool', lambda e: e.memset(MK2[:], 1.0), writes=['MK2'])
        for kind in range(4):
            op('pool', lambda e, kind=kind: e.affine_select(out=MK2[:, kind * 128:(kind + 1) * 128], in_=MK2[:, kind * 128:(kind + 1) * 128],
                                                            pattern=[[1, 128]], base=0, channel_multiplier=-1,
                                                            compare_op=(ALU.is_gt if kind % 2 == 0 else ALU.is_ge), fill=0.0),
               reads=['MK2'], writes=['MK2'])
        op('pool', lambda e: e.memset(ID2[:], 0.0), writes=['ID2'])
        for hs in range(2):
            op('pool', lambda e, hs=hs: e.affine_select(out=ID2[64 * hs:64 * hs + 64, :], in_=ID2[64 * hs:64 * hs + 64, :],
                                                        pattern=[[0, 4], [-1, 64]], base=0, channel_multiplier=1,
                                                        compare_op=ALU.not_equal, fill=1.0), reads=['ID2'], writes=['ID2'])
        op('pool', lambda e: e.memset(BOb[:], 0.0), writes=['BOb'])
        op('pool', lambda e: e.memset(BO64[:], 0.0), writes=['BO64'])
        for hs in range(2):
            op('pool', lambda e, hs=hs: e.memset(BOb[64 * hs:64 * hs + 64, 64 * hs:64 * hs + 64], 1.0), reads=['BOb'], writes=['BOb'])
            op('pool', lambda e, hs=hs: e.memset(BO64[64 * hs:64 * hs + 64, 64 * hs:64 * hs + 64], 1.0 / 64.0), reads=['BO64'], writes=['BO64'])

        def bhb(i):
            return BH[:, 8 * i:8 * (i + 1), :]

        BHF = BH[:].rearrange("p a t -> p (a t)").bitcast(F32)

        def bhf(i, c):
            o = (i * 8 + c) * T
            return BHF[:, o:o + T]

        def gk(i, c):
            k = i * 8 + c
            return [f'BH{2 * k}', f'BH{2 * k + 1}']

        seq = []
        for b in range(NB):
            if nstage >= 2:
                seq += [('wkv0', i) for i in range(4)]
            if nstage >= 5:
                seq += [('wkv1', i) for i in range(4)]
            for i in range(NT):
                if nstage >= 1:
                    seq += [('w_in', j) for j in range(4)] + [('gates', 0)] + [('a_w_out', j) for j in range(2)]
                if nstage >= 2:
                    seq += [('wq0', j) for j in range(2)] + [('wo0', j) for j in range(2)]
                if nstage >= 3:
                    seq += [('up0', j) for j in range(8)] + [('down0', j) for j in range(8)]
                if nstage >= 4:
                    seq += [('loraAa', 0), ('loraAb', 0), ('loraB', 0)]
                    for pj in (2, 3, 0, 1, 4, 5):
                        seq += [('rkvA', pj), ('rkvB', pj)]
                    seq += [('b_w_o', j) for j in range(2)]
                if nstage >= 5:
                    seq += [('wq1', j) for j in range(2)] + [('wo1', j) for j in range(2)]
                if nstage >= 6:
                    seq += [('up1', j) for j in range(8)] + [('down1', j) for j in range(8)]
        wstate = {'issued': 0, 'used': 0}

        def w_issue():
            k = wstate['issued']
            if k >= len(seq):
                return
            name, j = seq[k]
            pi = PIECES[name][0] + j
            slot = k % NRING
            S_.dma('sp', ring[slot][:], wsc[pi], writes=[f'ring{slot}'], key=f'ring{slot}')
            wstate['issued'] += 1

        def w_next(name, j):
            k = wstate['used']
            assert seq[k] == (name, j), (seq[k], name, j)
            prev_live = k >= 1 and seq[k - 1][0] in ('rkvA', 'loraAa') and seq[k][0] in ('rkvB', 'loraAb')
            retired = k - 2 if prev_live else k - 1
            while wstate['issued'] < min(len(seq), retired + NRING + 1):
                w_issue()
            wstate['used'] += 1
            slot = k % NRING
            return ring[slot], f'ring{slot}'

        def ones_norm(src_keys):
            b = nbank()
            for c in range(8):
                op('pe', lambda e, c=c, b=b: e.matmul(ps[b][:, :], lhsT=onesb[:], rhs=SQ[:, c, :], start=(c == 0), stop=(c == 7)),
                   reads=['onesb'] + [f'SQ{c}'], writes=[f'ps{b}'], signal=(c == 7))
            op('act', lambda e, b=b: e.activation(out=PT[3][:], in_=ps[b][:, :], func=AF.Ln, scale=1.0 / D, bias=1e-6),
               reads=[f'ps{b}'], writes=['PT3'])
            op('act', lambda e: e.activation(out=RS[:], in_=PT[3][:], func=AF.Exp, scale=-0.5), reads=['PT3'], writes=['RS'])

        def norm_in():
            op('act', lambda e: e.activation(out=SQ[:], in_=X[:], func=AF.Square),
               reads=[f'X{c}' for c in range(8)], writes=[f'SQ{c}' for c in range(8)])
            ones_norm(None)
            for c in range(8):
                eng = 'pool' if c % 3 == 2 else 'dve'
                op(eng, lambda e, c=c: e.tensor_tensor(out=XN[:, c, :], in0=X[:, c, :], in1=RS[:], op=ALU.mult),
                   reads=[f'X{c}', 'RS'], writes=[f'XN{c}'])

        def post_norm(gidx):
            ones_norm(None)
            for c in range(8):
                op('pool', lambda e, c=c: e.tensor_tensor(out=F1[:, c, :], in0=F1[:, c, :], in1=RS[:], op=ALU.mult),
                   reads=[f'F1_{c}', 'RS'], writes=[f'F1_{c}'])
                gc = vcol('ln_gains', gidx, c)
                op('dve', lambda e, c=c, gc=gc: e.scalar_tensor_tensor(out=X[:, c, :], in0=F1[:, c, :], scalar=gc, in1=X[:, c, :],
                                                                        op0=ALU.mult, op1=ALU.add),
                   reads=[f'F1_{c}', f'X{c}', 'VT'], writes=[f'X{c}'])

        def proj(wname, npieces, src, srckeys, KC, evac, mper=4, n=T):
            for pj in range(npieces):
                rg, rkey = w_next(wname, pj)
                MW = mper * 128
                for ml in range(mper):
                    m = pj * mper + ml
                    b = nbank()
                    for kc in range(KC):
                        op('pe', lambda e, rg=rg, kc=kc, ml=ml, b=b, MW=MW: e.matmul(
                            ps[b][:, 0:n], lhsT=rg[:, kc * MW + ml * 128:kc * MW + (ml + 1) * 128], rhs=src(kc),
                            start=(kc == 0), stop=(kc == KC - 1)),
                           reads=[rkey, srckeys(kc)], writes=[f'ps{b}'], signal=(kc == KC - 1))
                    evac(m, ps[b][:, 0:n], f'ps{b}')

        def evac_branch(bias_name):
            def ev(m, p, pk):
                if bias_name is None:
                    op('act', lambda e, m=m, p=p: e.activation(out=F1[:, m, :], in_=p, func=AF.Copy),
                       reads=[pk], writes=[f'F1_{m}'])
                    op('act', lambda e, m=m, p=p: e.activation(out=SQ[:, m, :], in_=p, func=AF.Square),
                       reads=[pk], writes=[f'SQ{m}'])
                else:
                    bc = vcol(bias_name, 0, m)
                    op('act', lambda e, m=m, p=p, bc=bc: e.activation(out=F1[:, m, :], in_=p, func=AF.Identity, bias=bc),
                       reads=[pk, 'VT'], writes=[f'F1_{m}'])
                    op('act', lambda e, m=m, p=p, bc=bc: e.activation(out=SQ[:, m, :], in_=p, func=AF.Square, bias=bc),
                       reads=[pk, 'VT'], writes=[f'SQ{m}'])
            return ev

        xkeys = [f'X{c}' for c in range(8)]

        def load_tile(b, i):
            for tb in range(4):
                r0 = b * S + i * T + tb * 128
                if tb % 2 == 0:
                    srcs = [xin[0][:, 0:512], xin[0][:, 512:1024]]
                    bkeys = ['xin0', 'xin0']
                    S_.dma('sp', xin[0][:], x_d[r0:r0 + 128, :], writes=['xin0'], key='xin0')
                else:
                    srcs = [PT[0][:], PT[1][:]]
                    bkeys = ['PT0', 'PT1']
                    for h_ in range(2):
                        S_.dma('sp', PT[h_][:], x_d[r0:r0 + 128, h_ * 512:(h_ + 1) * 512], writes=[bkeys[h_]], key=f'ptio{h_}')
                for half in range(2):
                    bk = nbank()
                    for cl in range(4):
                        c = half * 4 + cl
                        op('pe', lambda e: e.transpose(out=ps[bk][:, cl * 128:(cl + 1) * 128], in_=srcs[half][:, cl * 128:(cl + 1) * 128], identity=ident[:]),
                           reads=[bkeys[half], 'ident'], writes=[f'ps{bk}'], signal=(cl == 3))
                    op('act', lambda e: e.activation(
                        out=X[:, half * 4:half * 4 + 4, tb * 128:(tb + 1) * 128],
                        in_=ps[bk][:, :].rearrange("p (c t) -> p c t", c=4), func=AF.Copy),
                       reads=[f'ps{bk}'], writes=[f'X{c}' for c in range(half * 4, half * 4 + 4)])

        def store_tile(b, i):
            for tb in range(4):
                r0 = b * S + i * T + tb * 128
                if tb % 2 == 0:
                    dsts = [xin[0][:, 0:512], xin[0][:, 512:1024]]
                    bkeys = ['xin0', 'xin0']
                else:
                    dsts = [PT[0][:], PT[1][:]]
                    bkeys = ['PT0', 'PT1']
                for half in range(2):
                    bk = nbank()
                    for cl in range(4):
                        c = half * 4 + cl
                        op('pe', lambda e: e.transpose(out=ps[bk][:, cl * 128:(cl + 1) * 128], in_=X[:, c, tb * 128:(tb + 1) * 128], identity=ident[:]),
                           reads=[f'X{c}', 'ident'], writes=[f'ps{bk}'], signal=(cl == 3))
                    op('act', lambda e: e.activation(out=dsts[half], in_=ps[bk][:, :], func=AF.Copy),
                       reads=[f'ps{bk}'], writes=[bkeys[half]])
                if tb % 2 == 0:
                    S_.dma('sp', y_d[r0:r0 + 128, :], xin[0][:], reads=['xin0'], writes=['y'], key='xin0')
                else:
                    for h_ in range(2):
                        S_.dma('sp', y_d[r0:r0 + 128, h_ * 512:(h_ + 1) * 512], PT[h_][:], reads=[bkeys[h_]], writes=['y'], key=f'ptio{h_}')

        def stage_A(b, i):
            norm_in()
            vb = VOFF['a_b_in']

            def ev_in(m, p, pk):
                if m < 8:
                    bc = VT[:, vb * 8 + m:vb * 8 + m + 1]
                    if use_gelu:
                        op('act', lambda e, m=m, p=p, bc=bc: e.activation(out=F1[:, m, :], in_=p, func=AF.Gelu_apprx_tanh, bias=bc),
                           reads=[pk, 'VT'], writes=[f'F1_{m}'])
                    else:
                        op('act', lambda e, m=m, p=p, bc=bc: e.activation(out=F1[:, m, :], in_=p, func=AF.Identity, bias=bc),
                           reads=[pk, 'VT'], writes=[f'F1_{m}'])
                        op('pool', lambda e, m=m: e.tensor_tensor(out=PT[0][:], in0=F1[:, m, :], in1=F1[:, m, :], op=ALU.mult),
                           reads=[f'F1_{m}'], writes=['PT0'])
                        op('dve', lambda e: e.tensor_scalar(out=PT[0][:], in0=PT[0][:], scalar1=0.044715, scalar2=1.0, op0=ALU.mult, op1=ALU.add),
                           reads=['PT0'], writes=['PT0'])
                        op('pool', lambda e, m=m: e.tensor_tensor(out=PT[0][:], in0=PT[0][:], in1=F1[:, m, :], op=ALU.mult),
                           reads=[f'F1_{m}', 'PT0'], writes=['PT0'])
                        op('act', lambda e: e.activation(out=PT[0][:], in_=PT[0][:], func=AF.Sigmoid, scale=1.5957691216057308),
                           reads=['PT0'], writes=['PT0'])
                        op('dve', lambda e, m=m: e.tensor_tensor(out=F1[:, m, :], in0=F1[:, m, :], in1=PT[0][:], op=ALU.mult),
                           reads=[f'F1_{m}', 'PT0'], writes=[f'F1_{m}'])
                else:
                    c = m - 8
                    bc = VT[:, vb * 8 + m:vb * 8 + m + 1]
                    op('act', lambda e, c=c, p=p, bc=bc: e.activation(out=F2[:, c, 4:T + 4], in_=p, func=AF.Identity, bias=bc),
                       reads=[pk, 'VT'], writes=[f'F2_{c}'])
                    cw = [vcol('a_conv_w', k, c) for k in range(4)]
                    cb = vcol('a_conv_b', 0, c)
                    op('dve', lambda e, c=c, cw=cw, cb=cb: e.tensor_scalar(out=F3[:, c, :], in0=F2[:, c, 1:T + 1], scalar1=cw[0], scalar2=cb,
                                                                        op0=ALU.mult, op1=ALU.add),
                       reads=[f'F2_{c}', 'VT'], writes=[f'F3_{c}'])
                    for k in range(1, 4):
                        op('dve', lambda e, c=c, k=k, cw=cw: e.scalar_tensor_tensor(out=F3[:, c, :], in0=F2[:, c, 1 + k:T + 1 + k], scalar=cw[k],
                                                                                in1=F3[:, c, :], op0=ALU.mult, op1=ALU.add),
                           reads=[f'F2_{c}', f'F3_{c}', 'VT'], writes=[f'F3_{c}'])
                    op('pool', lambda e, c=c: e.tensor_copy(out=F2[:, c, 1:4], in_=F2[:, c, T + 1:T + 4]),
                       reads=[f'F2_{c}'], writes=[f'F2_{c}'])
                    op('pool', lambda e, c=c: e.tensor_copy(out=SQ[:, c, :], in_=F3[:, c, :]),
                       reads=[f'F3_{c}'], writes=[f'SQ{c}'])

            if i == 0:
                for c in range(8):
                    op('pool', lambda e, c=c: e.memset(F2[:, c, 0:4], 0.0), writes=[f'F2_{c}'])
                op('pool', lambda e: e.memset(HST[:], 0.0), writes=['HST'])
            proj('w_in', 4, lambda kc: XN[:, kc, :], lambda kc: f'XN{kc}', 8, ev_in)
            rg, rkey = w_next('gates', 0)
            gb = VOFF['a_gate_b']
            for gi in range(2):
                for c in range(8):
                    h, j = c // 2, c % 2
                    bk = nbank()
                    for kc in range(2):
                        o = ((gi * 4 + h) * 2 + kc) * 256 + j * 128
                        op('pe', lambda e, o=o, h=h, kc=kc, bk=bk: e.matmul(ps[bk][:, :], lhsT=rg[:, o:o + 128], rhs=SQ[:, 2 * h + kc, :],
                                                                          start=(kc == 0), stop=(kc == 1)),
                           reads=[rkey, f'SQ{2*h+kc}'], writes=[f'ps{bk}'], signal=(kc == 1))
                    bc = VT[:, (gb + gi) * 8 + c:(gb + gi) * 8 + c + 1]
                    op('act', lambda e, gi=gi, c=c, bk=bk, bc=bc: e.activation(out=bhf(gi, c), in_=ps[bk][:, :], func=AF.Sigmoid, bias=bc),
                       reads=[f'ps{bk}', 'VT'], writes=gk(gi, c))
            for c in range(8):
                cc = CV2[:, c:c + 1]
                op('act', lambda e, c=c, cc=cc: e.activation(out=bhf(0, c), in_=bhf(0, c), func=AF.Exp, scale=cc),
                   reads=gk(0, c) + ['CV2'], writes=gk(0, c))
                op('pool', lambda e, c=c: e.tensor_tensor(out=F2[:, c, 4:T + 4], in0=bhf(0, c), in1=bhf(0, c), op=ALU.mult),
                   reads=gk(0, c) + [f'F2_{c}'], writes=[f'F2_{c}'])
            for c in range(8):
                op('act', lambda e, c=c: e.activation(out=F2[:, c, 4:T + 4], in_=F2[:, c, 4:T + 4], func=AF.Sqrt, scale=-1.0, bias=1.0),
                   reads=[f'F2_{c}'], writes=[f'F2_{c}'])
            for c in range(8):
                op('dve', lambda e, c=c: e.tensor_tensor(out=bhf(1, c), in0=bhf(1, c), in1=F2[:, c, 4:T + 4], op=ALU.mult),
                   reads=gk(1, c) + [f'F2_{c}'], writes=gk(1, c))
                op('pool', lambda e, c=c: e.tensor_tensor(out=bhf(1, c), in0=bhf(1, c), in1=F3[:, c, :], op=ALU.mult),
                   reads=gk(1, c) + [f'F3_{c}'], writes=gk(1, c))
                op('dve', lambda e, c=c: e.tensor_tensor_scan(out=F3[:, c, :], data0=bhf(0, c), data1=bhf(1, c), initial=HST[:, c:c + 1],
                                                             op0=ALU.mult, op1=ALU.add),
                   reads=gk(0, c) + gk(1, c) + ['HST', f'F3_{c}'], writes=[f'F3_{c}'])
                op('pool', lambda e, c=c: e.tensor_copy(out=HST[:, c:c + 1], in_=F3[:, c, T - 1:T]),
                   reads=[f'F3_{c}'], writes=['HST'])
                op('pool', lambda e, c=c: e.tensor_tensor(out=XN[:, c, :], in0=F3[:, c, :], in1=F1[:, c, :], op=ALU.mult),
                   reads=[f'F3_{c}', f'F1_{c}'], writes=[f'XN{c}'])
            proj('a_w_out', 2, lambda kc: XN[:, kc, :], lambda kc: f'XN{kc}', 8, evac_branch('a_b_out'))
            post_norm(1)

        def mem_prep(b, layers):
            MT = F1[:].rearrange("p c t -> p (c t)")[:, 0:8 * MEM].rearrange("p (c t) -> p c t", c=8)
            MN = XN[:].rearrange("p c t -> p (c t)")[:, 0:8 * MEM].rearrange("p (c t) -> p c t", c=8)
            MSQ = SQ[:].rearrange("p c t -> p (c t)")[:, 0:8 * MEM].rearrange("p (c t) -> p c t", c=8)
            f1k = [f'F1_{c}' for c in range(8)]
            xnk = [f'XN{c}' for c in range(8)]
            sqk = [f'SQ{c}' for c in range(8)]
            for tb in range(2):
                r0 = b * MEM + tb * 128
                xb = xin[0]
                S_.dma('sp', xb[:], mem_d[r0:r0 + 128, :], writes=['xin0'], key='xin0')
                for half in range(2):
                    bk = nbank()
                    for cl in range(4):
                        c = half * 4 + cl
                        op('pe', lambda e, xb=xb, c=c, cl=cl, bk=bk: e.transpose(out=ps[bk][:, cl * 128:(cl + 1) * 128],
                                                                                in_=xb[:, c * 128:(c + 1) * 128], identity=ident[:]),
                           reads=['xin0', 'ident'], writes=[f'ps{bk}'], signal=(cl == 3))
                    op('act', lambda e, half=half, tb=tb, bk=bk: e.activation(
                        out=MT[:, half * 4:half * 4 + 4, tb * 128:(tb + 1) * 128],
                        in_=ps[bk][:, :].rearrange("p (c t) -> p c t", c=4), func=AF.Copy),
                       reads=[f'ps{bk}'], writes=f1k)
            op('act', lambda e: e.activation(out=MSQ, in_=MT, func=AF.Square), reads=f1k, writes=sqk)
            bk = nbank()
            for c in range(8):
                op('pe', lambda e, c=c, bk=bk: e.matmul(ps[bk][:, 0:MEM], lhsT=onesb[:], rhs=MSQ[:, c, :], start=(c == 0), stop=(c == 7)),
                   reads=['onesb'] + sqk, writes=[f'ps{bk}'], signal=(c == 7))
            op('act', lambda e, bk=bk: e.activation(out=PT[3][:, 0:MEM], in_=ps[bk][:, 0:MEM], func=AF.Ln, scale=1.0 / D, bias=1e-6),
               reads=[f'ps{bk}'], writes=['PT3'])
            op('act', lambda e: e.activation(out=RS[:, 0:MEM], in_=PT[3][:, 0:MEM], func=AF.Exp, scale=-0.5), reads=['PT3'], writes=['RS'])
            for c in range(8):
                op('dve', lambda e, c=c: e.tensor_tensor(out=MN[:, c, :], in0=MT[:, c, :], in1=RS[:, 0:MEM], op=ALU.mult),
                   reads=f1k + ['RS'], writes=xnk)
            for l in layers:
                for pj in range(2):
                    rg, rkey = w_next(f'wkv{l}', pj)
                    for ml in range(4):
                        m = pj * 4 + ml
                        bk = nbank()
                        for kc in range(8):
                            op('pe', lambda e, rg=rg, kc=kc, ml=ml, bk=bk: e.matmul(
                                ps[bk][:, 0:MEM], lhsT=rg[:, kc * 512 + ml * 128:kc * 512 + (ml + 1) * 128], rhs=MN[:, kc, :],
                                start=(kc == 0), stop=(kc == 7)),
                               reads=[rkey] + xnk, writes=[f'ps{bk}'], signal=(kc == 7))
                        op('act', lambda e, l=l, m=m, bk=bk: e.activation(out=KT[l][:, m, :], in_=ps[bk][:, 0:MEM], func=AF.Copy),
                           reads=[f'ps{bk}'], writes=[f'KT{l}'])
                for pj in range(2):
                    rg, rkey = w_next(f'wkv{l}', 2 + pj)
                    for mc in range(2):
                        bk = nbank()
                        for kc in range(8):
                            op('pe', lambda e, rg=rg, kc=kc, mc=mc, bk=bk: e.matmul(
                                ps[bk][:, :], lhsT=MN[:, kc, mc * 128:(mc + 1) * 128], rhs=rg[:, kc * 512:(kc + 1) * 512],
                                start=(kc == 0), stop=(kc == 7)),
                               reads=[rkey] + xnk, writes=[f'ps{bk}'], signal=(kc == 7))
                        op('act', lambda e, l=l, mc=mc, pj=pj, bk=bk: e.activation(out=VV[l][:, mc, pj * 512:(pj + 1) * 512], in_=ps[bk][:, :], func=AF.Copy),
                           reads=[f'ps{bk}'], writes=[f'VV{l}'])

        def stage_C(l):
            for c in range(8):
                eng = 'pool' if c % 3 == 2 else 'dve'
                op(eng, lambda e, c=c: e.tensor_copy(out=XN[:, c, :], in_=X[:, c, :]), reads=[f'X{c}'], writes=[f'XN{c}'])
            op('act', lambda e: e.activation(out=SQ[:], in_=X[:], func=AF.Square),
               reads=[f'X{c}' for c in range(8)], writes=[f'SQ{c}' for c in range(8)])
            bR = nbank()
            for tb in range(4):
                for c in range(8):
                    op('pe', lambda e, tb=tb, c=c: e.matmul(ps[bR][:, tb:tb + 1], lhsT=SQ[:, c, tb * 128:(tb + 1) * 128], rhs=onesb[:, 0:1],
                                                         start=(c == 0), stop=(c == 7)),
                       reads=['onesb', f'SQ{c}'], writes=[f'ps{bR}'], signal=(tb == 3 and c == 7))
            op('act', lambda e: e.activation(out=SMX[:, 48:52], in_=ps[bR][:, 0:4], func=AF.Ln, scale=1.0 / D, bias=1e-6),
               reads=[f'ps{bR}'], writes=['RSt'])
            op('act', lambda e: e.activation(out=SMX[:, 48:52], in_=SMX[:, 48:52], func=AF.Exp, scale=-0.5), reads=['RSt'], writes=['RSt'])
            QT = bhb(0)
            PN = bhb(1)
            PTt = bhb(2)
            OT = bhb(3)

            def ev_q(m, p, pk):
                op('act', lambda e, m=m, p=p: e.activation(out=QT[:, m, :], in_=p, func=AF.Copy, scale=1.0 / 16.0),
                   reads=[pk], writes=[f'BH{m}'])
            proj(f'wq{l}', 2, lambda kc: XN[:, kc, :], lambda kc: f'XN{kc}', 8, ev_q)
            for tb in range(4):
                pn = PN[:, 2 * tb:2 * tb + 2, :].rearrange("p a t -> p (a t)")
                pex = F3[:, 2 * tb:2 * tb + 2, :].rearrange("p a t -> p (a t)")
                banks = [nbank(), nbank()]
                for h in range(4):
                    bk = banks[h // 2]
                    for dc in range(2):
                        op('pe', lambda e, h=h, dc=dc, bk=bk, tb=tb: e.matmul(
                            ps[bk][:, (h % 2) * 256:(h % 2 + 1) * 256], lhsT=QT[:, 2 * h + dc, tb * 128:(tb + 1) * 128],
                            rhs=KT[l][:, 2 * h + dc, :], start=(dc == 0), stop=(dc == 1)),
                           reads=[f'BH{2*h+dc}', f'KT{l}'], writes=[f'ps{bk}'], signal=(dc == 1))
                for hb in range(2):
                    bk = banks[hb]
                    op('dve', lambda e, hb=hb, bk=bk, tb=tb: e.tensor_reduce(
                        out=SMX[:, tb * 4 + 2 * hb:tb * 4 + 2 * hb + 2], in_=ps[bk][:, :].rearrange("p (h k) -> p h k", h=2),
                        axis=AX.X, op=ALU.max, negate=True),
                       reads=[f'ps{bk}'], writes=[f'SMXm{tb}'])
                op('dve', lambda e, tb=tb: e.tensor_scalar(out=SMX[:, 52 + tb * 4:56 + tb * 4], in0=SMX[:, tb * 4:tb * 4 + 4],
                                                          scalar1=SMX[:, 48 + tb:49 + tb], scalar2=None, op0=ALU.mult),
                   reads=[f'SMXm{tb}', 'RSt'], writes=[f'SMXn{tb}'])
                for h in range(4):
                    bk = banks[h // 2]
                    op('act', lambda e, h=h, bk=bk, tb=tb, pex=pex: e.activation(
                        out=pex[:, h * 256:(h + 1) * 256], in_=ps[bk][:, (h % 2) * 256:(h % 2 + 1) * 256], func=AF.Exp,
                        scale=SMX[:, 48 + tb:49 + tb], bias=SMX[:, 52 + tb * 4 + h:53 + tb * 4 + h],
                        accum_out=SMX[:, 16 + tb * 4 + h:16 + tb * 4 + h + 1]),
                       reads=[f'ps{bk}', f'SMXn{tb}', 'RSt'], writes=[f'F3_{2*tb}', f'F3_{2*tb+1}', f'SMXs{tb}'])
                op('dve', lambda e, tb=tb: e.reciprocal(out=SMX[:, 32 + tb * 4:32 + tb * 4 + 4], in_=SMX[:, 16 + tb * 4:16 + tb * 4 + 4]),
                   reads=[f'SMXs{tb}'], writes=[f'SMXr{tb}'])
                for h in range(4):
                    op('dve', lambda e, h=h, tb=tb, pn=pn, pex=pex: e.tensor_scalar(
                        out=pn[:, h * 256:(h + 1) * 256], in0=pex[:, h * 256:(h + 1) * 256],
                        scalar1=SMX[:, 32 + tb * 4 + h:32 + tb * 4 + h + 1], scalar2=None, op0=ALU.mult),
                       reads=[f'F3_{2*tb}', f'F3_{2*tb+1}', f'SMXr{tb}'], writes=[f'BH{8+2*tb}', f'BH{9+2*tb}'])
                bk = nbank()
                psb = ps[bk][:, :].bitcast(BF16)
                for hm in range(8):
                    op('pe', lambda e, hm=hm, pn=pn, psb=psb: e.transpose(out=psb[:, hm * 128:(hm + 1) * 128], in_=pn[:, hm * 128:(hm + 1) * 128],
                                                                          identity=identb[:]),
                       reads=[f'BH{8+2*tb}', f'BH{9+2*tb}', 'identb'], writes=[f'ps{bk}'], signal=(hm == 7))
                op('act', lambda e, tb=tb, psb=psb: e.activation(out=PTt[:, :, tb * 128:(tb + 1) * 128],
                                                                 in_=psb.rearrange("p (a t) -> p a t", a=8), func=AF.Copy),
                   reads=[f'ps{bk}'], writes=[f'BH{16+a}' for a in range(8)])
            for m in range(8):
                h = m // 2
                bk = nbank()
                for mc in range(2):
                    op('pe', lambda e, m=m, h=h, mc=mc, bk=bk: e.matmul(ps[bk][:, :], lhsT=VV[l][:, mc, m * 128:(m + 1) * 128],
                                                                      rhs=PTt[:, 2 * h + mc, :], start=(mc == 0), stop=(mc == 1)),
                       reads=[f'VV{l}', f'BH{16+2*h+mc}'], writes=[f'ps{bk}'], signal=(mc == 1))
                op('act', lambda e, m=m, bk=bk: e.activation(out=OT[:, m, :], in_=ps[bk][:, :], func=AF.Copy),
                   reads=[f'ps{bk}'], writes=[f'BH{24+m}'])
            proj(f'wo{l}', 2, lambda kc: OT[:, kc, :], lambda kc: f'BH{24+kc}', 8, evac_branch(None))
            post_norm(6 * l + 3)

        def stage_M(l):
            for c in range(8):
                eng = 'pool' if c % 3 == 2 else 'dve'
                op(eng, lambda e, c=c: e.tensor_copy(out=XN[:, c, :], in_=X[:, c, :]), reads=[f'X{c}'], writes=[f'XN{c}'])
            op('act', lambda e: e.activation(out=SQ[:], in_=X[:], func=AF.Square),
               reads=[f'X{c}' for c in range(8)], writes=[f'SQ{c}' for c in range(8)])
            b = nbank()
            for c in range(8):
                op('pe', lambda e, c=c, b=b: e.matmul(ps[b][:, :], lhsT=onesb[:], rhs=SQ[:, c, :], start=(c == 0), stop=(c == 7)),
                   reads=['onesb'] + [f'SQ{c}'], writes=[f'ps{b}'], signal=(c == 7))
            op('act', lambda e, b=b: e.activation(out=PT[3][:], in_=ps[b][:, :], func=AF.Ln, scale=1.0 / D, bias=1e-6),
               reads=[f'ps{b}'], writes=['PT3'])
            op('act', lambda e: e.activation(out=RS[:], in_=PT[3][:], func=AF.Exp, scale=-1.0), reads=['PT3'], writes=['RS'])
            cnt = [0]

            def ev_down(m, p, pk):
                op('dve', lambda e, m=m, p=p: e.tensor_tensor(out=F1[:, m, :], in0=p, in1=RS[:], op=ALU.mult),
                   reads=[pk, 'RS'], writes=[f'F1_{m}'])
                op('act', lambda e, m=m: e.activation(out=SQ[:, m, :], in_=F1[:, m, :], func=AF.Square),
                   reads=[f'F1_{m}'], writes=[f'SQ{m}'])

            def ev_up(m, p, pk):
                k = cnt[0] % 2
                cnt[0] += 1
                op('act', lambda e, p=p, k=k: e.activation(out=PT[k][:], in_=p, func=AF.Square), reads=[pk], writes=[f'PT{k}'])
                op('dve', lambda e, m=m, p=p, k=k: e.scalar_tensor_tensor(out=BH[:, m, :], in0=p, scalar=0.0, in1=PT[k][:],
                                                                          op0=ALU.is_gt, op1=ALU.mult),
                   reads=[pk, f'PT{k}'], writes=[f'BH{m}'])
            proj(f'up{l}', 8, lambda kc: XN[:, kc, :], lambda kc: f'XN{kc}', 8, ev_up)
            proj(f'down{l}', 8, lambda kc: BH[:, kc, :], lambda kc: f'BH{kc}', 32, ev_down, mper=1)
            post_norm(6 * l + 5)

        _rw = [0]

        def rw_alloc(n):
            o = _rw[0]
            _rw[0] += n
            return RW[:, o:o + n]
        TM = rw_alloc(2048)
        U4 = rw_alloc(2048)
        Pb = rw_alloc(512)
        AU = rw_alloc(512)
        Wt = rw_alloc(256)
        LWA = rw_alloc(512)
        PTb = rw_alloc(512)
        LG = PTb
        F3B = F3[:].rearrange("p c t -> p (c t)").bitcast(BF16)
        RH = F3B[:, 0:4096]
        GT = F3B[:, 4096:6144]
        HH = F3B[:, 6144:8192]
        ARf = BH[:, 0:16, :].rearrange("p a t -> p (a t)")
        f3k = [f'F3_{c}' for c in range(8)]

        def ar_kind(fc, kind):
            return ARf[:, fc * 1024:(fc + 1) * 1024].rearrange("p (c k t) -> p c k t", c=4, k=2)[:, :, kind, :]

        def stage_B(b, i):
            for c in range(8):
                if i == 0:
                    op('pool', lambda e, c=c: e.memset(XNP[:, c, 0:8], 0.0), writes=[f'XNP{c}'])
                else:
                    op('pool', lambda e, c=c: e.tensor_copy(out=XNP[:, c, 7:8], in_=XL[:, c:c + 1]), reads=['XL', f'XNP{c}'], writes=[f'XNP{c}'])
            if i == 0:
                op('pool', lambda e: e.memset(STt[:], 0.0), writes=['STt'])
            op('act', lambda e: e.activation(out=SQ[:], in_=X[:], func=AF.Square), reads=xkeys, writes=[f'SQ{c}' for c in range(8)])
            ones_norm(None)
            for c in range(8):
                eng = 'pool' if c % 3 == 2 else 'dve'
                op(eng, lambda e, c=c: e.tensor_tensor(out=XNP[:, c, 8:T + 8], in0=X[:, c, :], in1=RS[:], op=ALU.mult),
                   reads=[f'X{c}', 'RS'], writes=[f'XNP{c}'])

            def proj2(nameA, nameB, pj, evac):
                rgA, kA = w_next(nameA, pj)
                rgB, kB = w_next(nameB, pj)
                return rgA, kA, rgB, kB

            def mm16(rgA, kA, rgB, kB, col0, ncol, bk, MW):
                for v_, (rg, rk, off) in enumerate(((rgA, kA, 8), (rgB, kB, 7))):
                    for kc in range(8):
                        op('pe', lambda e, rg=rg, kc=kc, off=off, v_=v_: e.matmul(
                            ps[bk][0:ncol, :], lhsT=rg[:, kc * MW + col0:kc * MW + col0 + ncol], rhs=XNP[:, kc, off:off + T],
                            start=(v_ == 0 and kc == 0), stop=(v_ == 1 and kc == 7)),
                           reads=[rk, f'XNP{kc}'], writes=[f'ps{bk}'], signal=(v_ == 1 and kc == 7))

            if BSTOP[0] <= 0.1:
                return
            rgA, kA = w_next('loraAa', 0)
            rgB, kB = w_next('loraAb', 0)
            bk = nbank()
            mm16(rgA, kA, rgB, kB, 0, 128, bk, 256)
            op('act', lambda e, bk=bk: e.activation(out=LWA[0:64, :], in_=ps[bk][0:64, :], func=AF.Tanh), reads=[f'ps{bk}'], writes=['LWA0'])
            op('act', lambda e, bk=bk: e.activation(out=LWA[64:128, :], in_=ps[bk][64:128, :], func=AF.Copy), reads=[f'ps{bk}'], writes=['LWA1'])
            bk = nbank()
            mm16(rgA, kA, rgB, kB, 128, 128, bk, 256)
            op('act', lambda e, bk=bk: e.activation(out=LG[:, :], in_=ps[bk][:, :], func=AF.Sigmoid), reads=[f'ps{bk}'], writes=['PTb'])
            if BSTOP[0] <= 0.3:
                return
            rgL, kL = w_next('loraB', 0)
            for m in range(8):
                bk = nbank()
                op('pe', lambda e, m=m, bk=bk: e.matmul(ps[bk][:, :], lhsT=rgL[0:64, m * 128:(m + 1) * 128], rhs=LWA[0:64, :], start=True, stop=True),
                   reads=[kL, 'LWA0'], writes=[f'ps{bk}'])
                op('act', lambda e, m=m, bk=bk: e.activation(out=PT[0][:], in_=ps[bk][:, :], func=AF.Sigmoid, bias=vcol('b_w0', 0, m)),
                   reads=[f'ps{bk}', 'VT'], writes=['PT0'])
                op('act', lambda e: e.activation(out=PT[0][:], in_=PT[0][:], func=AF.Copy, scale=-0.6065306597126334), reads=['PT0'], writes=['PT0'])
                for c in range(4):
                    op('dve', lambda e, m=m, c=c: e.tensor_tensor_scan(out=F1[:, m, c * 128:(c + 1) * 128], data0=onesb[:], data1=PT[0][:, c * 128:(c + 1) * 128],
                                                                       initial=0.0, op0=ALU.mult, op1=ALU.add),
                       reads=['PT0', 'onesb'], writes=[f'F1_{m}'])
                op('pool', lambda e, m=m: e.tensor_tensor(out=F3[:, m, :], in0=F1[:, m, :], in1=PT[0][:], op=ALU.subtract),
                   reads=[f'F1_{m}', 'PT0'], writes=[f'F3_{m}'])
                bk = nbank()
                op('pe', lambda e, m=m, bk=bk: e.matmul(ps[bk][:, :], lhsT=rgL[64:128, 1024 + m * 128:1024 + (m + 1) * 128], rhs=LWA[64:128, :], start=True, stop=True),
                   reads=[kL, 'LWA1'], writes=[f'ps{bk}'])
                op('act', lambda e, m=m, bk=bk: e.activation(out=F2[:, m, 4:T + 4], in_=ps[bk][:, :], func=AF.Sigmoid, bias=vcol('b_a0', 0, m)),
                   reads=[f'ps{bk}', 'VT'], writes=[f'F2_{m}'])
                bk = nbank()
                op('pe', lambda e, m=m, bk=bk: e.matmul(ps[bk][:, :], lhsT=rgL[:, 2048 + m * 128:2048 + (m + 1) * 128], rhs=LG[:, :], start=True, stop=True),
                   reads=[kL, 'PTb'], writes=[f'ps{bk}'])
                op('act', lambda e, m=m, bk=bk: e.activation(out=SQ[:, m, :], in_=ps[bk][:, :], func=AF.Copy), reads=[f'ps{bk}'], writes=[f'SQ{m}'])
            if BSTOP[0] <= 0.5:
                return
            op('act', lambda e: e.activation(out=GC[:].rearrange("p (a c) -> p a c", a=8),
                                             in_=F1[:].rearrange("p a (c t) -> p a c t", c=4)[:, :, :, 127], func=AF.Exp),
               reads=[f'F1_{c}' for c in range(8)], writes=['GC'])
            omk = lambda m: DV[:, 96 + m:97 + m]
            if BSTOP[0] <= 0.6:
                return
            TMf = RW[:, 0:2048].bitcast(F32)
            tsets = [(PT[0][:], PT[1][:], PT[2][:], PT[3][:], PTb, ['PT0'], ['PT1'], ['PT2'], ['PT3'], ['PTb']),
                     (xin[0][:, 0:512], xin[0][:, 512:1024], TMf[:, 0:512], TMf[:, 512:1024], LWA, ['xa'], ['xb'], ['tma'], ['tmb'], ['LWA0', 'LWA1'])]
            op('pool', lambda e: e.memset(SCR[:, 3:4], 0.0), reads=[], writes=['xin0', 'TM', 'xa', 'xb', 'tma', 'tmb'])
            for pj in (2, 3):
                rgA, kA = w_next('rkvA', pj)
                rgB, kB = w_next('rkvB', pj)
                for ml in range(4):
                    m = (pj - 2) * 4 + ml
                    bk = nbank()
                    mm16(rgA, kA, rgB, kB, ml * 128, 128, bk, 512)
                    pk = f'ps{bk}'
                    t0, t1, t2, t3, tq, k0, k1, k2, k3, kq = tsets[m % 2]
                    kkc = vcol('b_k_k', 0, m)
                    op('act', lambda e, bk=bk, kkc=kkc: e.activation(out=t0, in_=ps[bk][:, :], func=AF.Copy, scale=kkc), reads=[pk, 'VT'], writes=[*k0])
                    op('act', lambda e, bk=bk, kkc=kkc: e.activation(out=tq, in_=ps[bk][:, :], func=AF.Square, scale=kkc), reads=[pk, 'VT'], writes=[*kq])
                    b2 = nbank()
                    op('pe', lambda e, b2=b2: e.matmul(ps[b2][:, :], lhsT=BOb[:], rhs=tq, start=True, stop=True), reads=['BOb', *kq], writes=[f'ps{b2}'])
                    op('act', lambda e, b2=b2: e.activation(out=t1, in_=ps[b2][:, :], func=AF.Ln, bias=1e-24), reads=[f'ps{b2}'], writes=[*k1])
                    op('act', lambda e: e.activation(out=t1, in_=t1, func=AF.Exp, scale=-0.5), reads=[*k1], writes=[*k1])
                    op('dve', lambda e: e.tensor_tensor(out=t0, in0=t0, in1=t1, op=ALU.mult), reads=[*k0, *k1], writes=[*k0])
                    op('act', lambda e, m=m: e.activation(out=t2, in_=F3[:, m, :], func=AF.Exp), reads=[f'F3_{m}'], writes=[*k2])
                    op('dve', lambda e, m=m: e.scalar_tensor_tensor(out=ar_kind(m, 0), in0=t0.rearrange("p (c t) -> p c t", c=4), scalar=-1.0,
                                                                    in1=t2.rearrange("p (c t) -> p c t", c=4), op0=ALU.mult, op1=ALU.mult),
                       reads=[*k0, *k2], writes=[f'BH{2*m}', f'BH{2*m+1}'])
                    op('act', lambda e, m=m: e.activation(out=t3, in_=F1[:, m, :], func=AF.Exp, scale=-1.0), reads=[f'F1_{m}'], writes=[*k3])
                    op('pool', lambda e, m=m: e.tensor_tensor(out=t0, in0=t0, in1=F2[:, m, 4:T + 4], op=ALU.mult), reads=[*k0, f'F2_{m}'], writes=[*k0])
                    op('dve', lambda e, m=m: e.tensor_tensor(out=BH[:, 16 + m, :], in0=t0, in1=t3, op=ALU.mult), reads=[*k0, *k3], writes=[f'BH{16+m}'])
                    kac = vcol('b_k_a', 0, m)
                    op('dve', lambda e, m=m, kac=kac: e.tensor_scalar(out=t1, in0=F2[:, m, 4:T + 4], scalar1=kac, scalar2=omk(m), op0=ALU.mult, op1=ALU.add),
                       reads=[f'F2_{m}', 'VT', 'DV', *k1], writes=[*k1])
                    op('dve', lambda e, m=m, bk=bk: e.tensor_tensor(out=F2[:, m, 4:T + 4], in0=ps[bk][:, :], in1=t1, op=ALU.mult),
                       reads=[pk, *k1, f'F2_{m}'], writes=[f'F2_{m}'])
                    op('pool', lambda e, m=m: e.tensor_tensor(out=BH[:, 24 + m, :], in0=F2[:, m, 4:T + 4], in1=t3, op=ALU.mult),
                       reads=[f'F2_{m}', *k3], writes=[f'BH{24+m}'])
            if BSTOP[0] <= 0.7:
                return
            for pj in (0, 1):
                rgA, kA = w_next('rkvA', pj)
                rgB, kB = w_next('rkvB', pj)
                for ml in range(4):
                    m = pj * 4 + ml
                    bk = nbank()
                    mm16(rgA, kA, rgB, kB, ml * 128, 128, bk, 512)
                    pk = f'ps{bk}'
                    t0, t1, t2, t3, tq, k0, k1, k2, k3, kq = tsets[m % 2]
                    op('act', lambda e, m=m: e.activation(out=t2, in_=F1[:, m, :], func=AF.Exp), reads=[f'F1_{m}'], writes=[*k2])
                    op('dve', lambda e, m=m, bk=bk: e.tensor_tensor(out=ar_kind(m, 1), in0=ps[bk][:, :].rearrange("p (c t) -> p c t", c=4),
                                                                    in1=t2.rearrange("p (c t) -> p c t", c=4), op=ALU.mult),
                       reads=[pk, *k2], writes=[f'BH{2*m}', f'BH{2*m+1}'])
                    rkc = vcol('b_r_k', 0, m)
                    op('dve', lambda e, m=m, bk=bk, rkc=rkc: e.scalar_tensor_tensor(out=tq, in0=ps[bk][:, :], scalar=rkc, in1=F2[:, m, 4:T + 4],
                                                                                  op0=ALU.mult, op1=ALU.mult),
                       reads=[pk, 'VT', f'F2_{m}', *kq], writes=[*kq])
                    b2 = nbank()
                    op('pe', lambda e, b2=b2: e.matmul(ps[b2][:, :], lhsT=BOb[:], rhs=tq, start=True, stop=True), reads=['BOb', *kq], writes=[f'ps{b2}'])
                    op('act', lambda e, m=m, b2=b2: e.activation(out=F2[:, m, 4:T + 4], in_=ps[b2][:, :], func=AF.Copy), reads=[f'ps{b2}'], writes=[f'F2_{m}'])
            op('pool', lambda e: e.memset(SCR[:, 4:5], 0.0), reads=[], writes=['xin0', 'TM', 'xa', 'xb', 'tma', 'tmb'])
            if BSTOP[0] <= 0.8:
                return
            for pj in (4, 5):
                rgA, kA = w_next('rkvA', pj)
                rgB, kB = w_next('rkvB', pj)
                for ml in range(4):
                    m = (pj - 4) * 4 + ml
                    bk = nbank()
                    mm16(rgA, kA, rgB, kB, ml * 128, 128, bk, 512)
                    pk = f'ps{bk}'
                    import os as _os
                    _pm = int(_os.environ.get('P5MODE', '0'))
                    if _pm in (0, 1):
                        op('dve', lambda e, m=m, bk=bk: e.tensor_copy(out=XN[:, m, :], in_=ps[bk][:, :]), reads=[pk], writes=[f'XN{m}'])
                    if _pm in (0, 2):
                        op('dve', lambda e, m=m, bk=bk: e.tensor_tensor(out=F2[:, m, 4:T + 4], in0=ps[bk][:, :], in1=F2[:, m, 4:T + 4], op=ALU.mult),
                           reads=[pk, f'F2_{m}'], writes=[f'F2_{m}'])
            if BSTOP[0] <= 1:
                return
            TM4 = TM.rearrange("p (c k f) -> p c k f", c=4, k=4)
            op('pool', lambda e: e.tensor_copy(out=XL[:].rearrange("p (c o) -> p c o", o=1), in_=XNP[:, :, T + 7:T + 8]), reads=[f'XNP{c}' for c in range(8)], writes=['XL'])
            XNPf = XNP[:].rearrange("p c t -> p (c t)")
            sets = []
            for si in range(2):
                TBh = [TB[j][:].bitcast(FP16) for j in range(5)]
                if si == 0:
                    d = dict(Q=[TBh[0], TBh[1]], kQ=['TB0', 'TB1'], B5=TBh[4][:, 0:512], kB5='TB4s0',
                             U4=U4, Pb=Pb, AU=AU, Wt=Wt, kS=['U4', 'Pb', 'AU', 'Wt'])
                else:
                    d = dict(Q=[TBh[2], TBh[3]], kQ=['TB2', 'TB3'], B5=TBh[4][:, 512:1024], kB5='TB4s1',
                             U4=XNPf[:, 0:2048], Pb=XNPf[:, 2048:2560], AU=XNPf[:, 2560:3072], Wt=XNPf[:, 3072:3328], kS=['U4b', 'Pbb', 'AUb', 'Wtb'])
                sets.append(d)
            hk1 = [k + f'h{hf}' for k in sets[1]['kS'][1:] for hf in range(2)]
            op('pool', lambda e: e.memset(SCR[:, 0:1], 0.0), reads=[], writes=[f'XNP{c}' for c in range(8)] + sets[1]['kS'] + hk1)

            def head_prologue(fc, hs, st):
                hsl = slice(64 * hs, 64 * hs + 64)
                U4_ = st['U4']
                kU = [st['kS'][0]]
                At = lambda c: ARf[hsl, fc * 1024 + c * 256:fc * 1024 + c * 256 + 128]
                ARc = lambda c: ARf[hsl, fc * 1024 + c * 256:fc * 1024 + (c + 1) * 256]
                Bt = lambda c: BH[hsl, 16 + fc, c * 128:(c + 1) * 128]
                Kt = lambda c: BH[hsl, 24 + fc, c * 128:(c + 1) * 128]
                kAR = [f'BH{2*fc}', f'BH{2*fc+1}']
                bA = nbank(); reserved.add(bA)
                for c in range(4):
                    op('pe', lambda e, c=c: e.matmul(ps[bA][:, c * 128:(c + 1) * 128], lhsT=At(c), rhs=Bt(c), start=True, stop=True),
                       reads=kAR + [f'BH{16+fc}'], writes=[f'ps{bA}'], signal=(c == 3))
                bAT = nbank(); reserved.add(bAT)
                for c in range(4):
                    op('pe', lambda e, c=c: e.matmul(ps[bAT][:, c * 128:(c + 1) * 128], lhsT=Bt(c), rhs=At(c), start=True, stop=True),
                       reads=kAR + [f'BH{16+fc}'], writes=[f'ps{bAT}'], signal=(c == 3))
                for c in range(4):
                    bB = nbank()
                    op('pe', lambda e, c=c, bB=bB: e.matmul(ps[bB][:, 0:256], lhsT=Bt(c), rhs=ARc(c), start=True, stop=True),
                       reads=kAR + [f'BH{16+fc}'], writes=[f'ps{bB}'], signal=False)
                    op('pe', lambda e, c=c, bB=bB: e.matmul(ps[bB][:, 256:512], lhsT=Kt(c), rhs=ARc(c), start=True, stop=True),
                       reads=kAR + [f'BH{24+fc}'], writes=[f'ps{bB}'])
                    op('dve', lambda e, c=c, bB=bB: e.tensor_tensor(out=U4_[:, c * 512:(c + 1) * 512], in0=ps[bB][:, :], in1=MK2[:], op=ALU.mult),
                       reads=[f'ps{bB}', 'MK2'], writes=kU)
                return bA, bAT

            def half_steps(fc, hs, st, hf, bA, bAT, done):
                hsl = slice(64 * hs, 64 * hs + 64)
                fsl = slice(64 * hs, 64 * hs + 64)
                cs_ = slice(hf * 256, (hf + 1) * 256)
                QQ = st['Q'][hf]
                XX, TP = QQ[:, 0:512], QQ[:, 512:1024]
                B1, B2 = XX[:, 0:256], XX[:, 256:512]
                B3, B4 = TP[:, 0:256], TP[:, 256:512]
                B5 = st['B5'][:, cs_]
                kx = st['kQ'][hf]
                k1, k2, k3, k4, k5 = [kx + 'a'], [kx + 'b'], [kx + 'c'], [kx + 'd'], [st['kB5'] + f'h{hf}']
                dt_ = FP16
                idm = None
                U44 = st['U4'].rearrange("p (u k t) -> p u k t", u=4, k=4)
                Pb_ = st['Pb'][:, hf * 256:(hf + 1) * 256]
                AU_ = st['AU'][:, hf * 256:(hf + 1) * 256]
                Wt_ = st['Wt'][:, hf * 128:(hf + 1) * 128]
                kU = [st['kS'][0]]
                kP, kAU, kW = [[k + f'h{hf}'] for k in st['kS'][1:]]
                kAR = [f'BH{2*fc}', f'BH{2*fc+1}']
                v2 = lambda t: t.rearrange("p (u t) -> p u t", u=2)
                mskb = lambda j: MSK[:, j:j + 1, :].to_broadcast([128, 2, 128])
                idb = ident[:].rearrange("p (o t) -> p o t", o=1).to_broadcast([128, 2, 128])
                W_ = lambda t: t
                held = []

                def gbank():
                    bk_ = nbank()
                    reserved.add(bk_)
                    held.append(bk_)
                    return bk_

                def gfree(bk_):
                    reserved.discard(bk_)
                    held.remove(bk_)

                def mm2(bk_, col0, lhs, rhs, rk, acc=None, acck=None, last=True):
                    for u in range(2):
                        us = slice(u * 128, (u + 1) * 128)
                        os_ = slice(col0 + u * 128, col0 + (u + 1) * 128)
                        if acc is not None:
                            op('pe', lambda e: e.matmul(ps[bk_][:, os_], lhsT=W_(idm), rhs=W_(acc[:, us]), start=True, stop=False),
                               reads=acck + ['ident'], writes=[f'ps{bk_}'], signal=False)
                        op('pe', lambda e: e.matmul(ps[bk_][:, os_], lhsT=W_(lhs[:, us]), rhs=W_(rhs[:, us]), start=(acc is None), stop=True),
                           reads=rk, writes=[f'ps{bk_}'], signal=(last and u == 1))

                def cp(eng, dst, kd, bk_, col0, n):
                    if eng == 'act':
                        op('act', lambda e: e.activation(out=W_(dst), in_=ps[bk_][:, col0:col0 + n], func=AF.Copy), reads=[f'ps{bk_}'], writes=kd)
                    else:
                        op('dve', lambda e: e.tensor_copy(out=W_(dst), in_=ps[bk_][:, col0:col0 + n]), reads=[f'ps{bk_}'], writes=kd)

                op('dve', lambda e: e.tensor_tensor(out=v2(W_(B1)), in0=v2(ps[bA][:, cs_]), in1=mskb(0), op=ALU.mult), reads=[f'ps{bA}', 'MSK'], writes=k1)
                op('dve', lambda e: e.tensor_tensor(out=v2(W_(B2)), in0=v2(ps[bAT][:, cs_]), in1=mskb(1), op=ALU.mult), reads=[f'ps{bAT}', 'MSK'], writes=k2)
                e3 = 'pool'
                op(e3, lambda e: e.tensor_tensor(out=W_(TP[:, :]).rearrange("p (u t) -> p u t", u=4), in0=XX[:, :].rearrange("p (u t) -> p u t", u=4),
                                                 in1=ident[:].rearrange("p (o t) -> p o t", o=1).to_broadcast([128, 4, 128]), op=ALU.add),
                   reads=k1 + k2 + ['ident'], writes=k3 + k4)
                yield
                def tn_from_pt():
                    bt_ = gbank()
                    for u in range(2):
                        us = slice(u * 128, (u + 1) * 128)
                        op('pe', lambda e: e.transpose(out=ps[bt_][:, :].bitcast(FP16)[:, us], in_=B4[:, us], identity=IDH[:]),
                           reads=k4 + ['IDH'], writes=[f'ps{bt_}'], signal=(u == 1))
                    return bt_

                for lev in range(3):
                    bk_ = gbank()
                    mm2(bk_, 0, B2, B1, k1 + k2, last=False)
                    mm2(bk_, 256, B1, B2, k1 + k2)
                    yield
                    cp('act', XX[:, :], k1 + k2, bk_, 0, 512)
                    gfree(bk_)
                    yield
                    bk_ = gbank()
                    mm2(bk_, 0, B1, B4, k1 + k4)
                    yield
                    op('dve', lambda e: e.tensor_tensor(out=W_(B4), in0=ps[bk_][:, 0:256], in1=B4, op=ALU.add), reads=[f'ps{bk_}'] + k4, writes=k4)
                    gfree(bk_)
                    yield
                for kl in range(1, 4):
                    op('dve', lambda e, kl=kl: e.tensor_tensor(out=v2(W_(B5)), in0=v2(ps[bA][:, cs_]), in1=mskb(2 * kl), op=ALU.mult), reads=[f'ps{bA}', 'MSK'], writes=k5)
                    bt_ = tn_from_pt()
                    yield
                    op('act', lambda e: e.activation(out=B3, in_=ps[bt_][:, :].bitcast(FP16)[:, 0:256], func=AF.Copy), reads=[f'ps{bt_}'], writes=k3)
                    gfree(bt_)
                    bz = gbank()
                    mm2(bz, 0, B5, B4, k5 + k4)
                    yield
                    cp('act', B1, k1, bz, 0, 256)
                    gfree(bz)
                    yield
                    bz = gbank()
                    mm2(bz, 0, B3, B1, k3 + k1)
                    yield
                    if kl < 3:
                        op('dve', lambda e: e.tensor_tensor(out=W_(B4), in0=ps[bz][:, 0:256], in1=B4, op=ALU.add), reads=[f'ps{bz}'] + k4, writes=k4)
                    else:
                        op('dve', lambda e: e.tensor_tensor(out=Pb_, in0=ps[bz][:, 0:256], in1=B4, op=ALU.add), reads=[f'ps{bz}'] + k4, writes=kP)
                    gfree(bz)
                    yield
                done.append(1)
                if len(done) == 2:
                    reserved.discard(bA); reserved.discard(bAT)
                us2 = [2 * hf, 2 * hf + 1]
                bW = gbank()
                for j, u in enumerate(us2):
                    op('pe', lambda e: e.matmul(ps[bW][:, j * 64:(j + 1) * 64], lhsT=U44[:, u, 2, :], rhs=TM4[:, u, 3, fsl], start=True, stop=True),
                       reads=kU + ['TM'], writes=[f'ps{bW}'], signal=(j == 1))
                yield
                op('act', lambda e: e.activation(out=Wt_, in_=ps[bW][:, 0:128], func=AF.Copy), reads=[f'ps{bW}'], writes=kW)
                gfree(bW)
                yield
                bU = gbank()
                for j, u in enumerate(us2):
                    op('pe', lambda e: e.matmul(ps[bU][:, j * 128:j * 128 + 64], lhsT=Pb_[:, j * 128:(j + 1) * 128], rhs=TM4[:, u, 0, fsl], start=True, stop=True),
                       reads=kP + ['TM'], writes=[f'ps{bU}'], signal=False)
                    op('pe', lambda e: e.matmul(ps[bU][:, j * 128 + 64:(j + 1) * 128], lhsT=Pb_[:, j * 128:(j + 1) * 128], rhs=Wt_[:, j * 64:(j + 1) * 64], start=True, stop=True),
                       reads=kP + kW, writes=[f'ps{bU}'], signal=(j == 1))
                yield
                op('act', lambda e: e.activation(out=AU_, in_=ps[bU][:, 0:256], func=AF.Copy), reads=[f'ps{bU}'], writes=kAU)
                gfree(bU)
                yield
                bRY = gbank()
                for j, u in enumerate(us2):
                    op('pe', lambda e: e.matmul(ps[bRY][hsl, j * 128:(j + 1) * 128], lhsT=AU_[:, j * 128:j * 128 + 64], rhs=U44[:, u, 1, :], start=True, stop=True),
                       reads=kAU + kU, writes=[f'ps{bRY}'], signal=False)
                for j, u in enumerate(us2):
                    op('pe', lambda e: e.matmul(ps[bRY][hsl, 256 + j * 128:256 + (j + 1) * 128], lhsT=AU_[:, j * 128 + 64:(j + 1) * 128], rhs=U44[:, u, 1, :], start=True, stop=False),
                       reads=kAU + kU, writes=[f'ps{bRY}'], signal=False)
                    op('pe', lambda e: e.matmul(ps[bRY][hsl, 256 + j * 128:256 + (j + 1) * 128], lhsT=TM4[:, u, 3, fsl], rhs=U44[:, u, 3, :], start=False, stop=True),
                       reads=['TM'] + kU, writes=[f'ps{bRY}'], signal=(j == 1))
                yield
                tsl = slice(fc * 512 + hf * 256, fc * 512 + (hf + 1) * 256)
                op('dve', lambda e: e.tensor_tensor(out=RH[hsl, tsl].rearrange("p (c t) -> p c t", c=2),
                                                    in0=ps[bRY][hsl, 0:256].rearrange("p (c t) -> p c t", c=2),
                                                    in1=ARf[hsl, fc * 1024 + hf * 512:fc * 1024 + (hf + 1) * 512].rearrange("p (c k t) -> p c k t", c=2, k=2)[:, :, 1, :], op=ALU.add),
                   reads=[f'ps{bRY}'] + kAR, writes=[f'RH{fc}_{hs}_{hf}'])
                op('dve', lambda e: e.tensor_copy(out=F1[hsl, fc, hf * 256:(hf + 1) * 256], in_=ps[bRY][hsl, 256:512]), reads=[f'ps{bRY}'], writes=[f'F1_{fc}'])
                gfree(bRY)
                yield
                bGH = gbank()
                for j, u in enumerate(us2):
                    op('pe', lambda e: e.matmul(ps[bGH][hsl, j * 64:(j + 1) * 64], lhsT=AU_[:, j * 128:j * 128 + 64], rhs=TM4[:, u, 1, fsl], start=True, stop=True),
                       reads=kAU + ['TM'], writes=[f'ps{bGH}'], signal=False)
                for j, u in enumerate(us2):
                    op('pe', lambda e: e.matmul(ps[bGH][hsl, 128 + j * 64:128 + (j + 1) * 64], lhsT=TM4[:, u, 1, fsl], rhs=AU_[:, j * 128 + 64:(j + 1) * 128], start=True, stop=False),
                       reads=kAU + ['TM'], writes=[f'ps{bGH}'], signal=False)
                    op('pe', lambda e: e.matmul(ps[bGH][hsl, 128 + j * 64:128 + (j + 1) * 64], lhsT=TM4[:, u, 2, fsl], rhs=TM4[:, u, 3, fsl], start=False, stop=True),
                       reads=['TM'], writes=[f'ps{bGH}'], signal=(j == 1))
                yield
                gsl = slice(fc * 256 + hf * 128, fc * 256 + (hf + 1) * 128)
                op('dve', lambda e: e.tensor_tensor(out=GT[hsl, gsl], in0=ps[bGH][hsl, 0:128], in1=ID2[hsl, 0:128], op=ALU.add),
                   reads=[f'ps{bGH}', 'ID2'], writes=[f'GT{fc}_{hs}_{hf}'])
                op('dve', lambda e: e.tensor_tensor(out=HH[hsl, fc * 256 + hf * 128:fc * 256 + (hf + 1) * 128].rearrange("p (u i) -> p u i", u=2),
                                                    in0=ps[bGH][hsl, 128:256].rearrange("p (u i) -> p u i", u=2),
                                                    in1=GC[hsl, fc * 4 + 2 * hf:fc * 4 + 2 * hf + 2].rearrange("p (u o) -> p u o", o=1).to_broadcast([64, 2, 64]), op=ALU.mult),
                   reads=[f'ps{bGH}', 'GC'], writes=[f'HH{fc}_{hs}_{hf}'])
                gfree(bGH)
                yield

            fine = [f'{n}{fc}_{hs}_{hf}' for n in ('RH', 'GT', 'HH') for fc in range(8) for hs in range(2) for hf in range(2)]
            op('pool', lambda e: e.memset(SCR[:, 1:2], 0.0), reads=[], writes=f3k + fine)
            for fc in range(8):
                srcs = [lambda c, fc=fc: ARf[:, fc * 1024 + c * 256:fc * 1024 + c * 256 + 128],
                        lambda c, fc=fc: BH[:, 16 + fc, c * 128:(c + 1) * 128],
                        lambda c, fc=fc: BH[:, 24 + fc, c * 128:(c + 1) * 128],
                        lambda c, fc=fc: XN[:, fc, c * 128:(c + 1) * 128]]
                skeys = [[f'BH{2*fc}', f'BH{2*fc+1}'], [f'BH{16+fc}'], [f'BH{24+fc}'], [f'XN{fc}']]
                for half in range(2):
                    bk = nbank()
                    psb = ps[bk][:, :].bitcast(BF16)
                    for cl in range(2):
                        c = half * 2 + cl
                        for kind in range(4):
                            o = (cl * 4 + kind) * 128
                            op('pe', lambda e, c=c, kind=kind, o=o, psb=psb, srcs=srcs: e.transpose(out=psb[:, o:o + 128], in_=srcs[kind](c), identity=identb[:]),
                               reads=skeys[kind] + ['identb'], writes=[f'ps{bk}'], signal=(cl == 1 and kind == 3))
                    op('act', lambda e, half=half, psb=psb: e.activation(out=TM[:, half * 1024:(half + 1) * 1024], in_=psb, func=AF.Copy),
                       reads=[f'ps{bk}'], writes=['TM'])
                gens = []
                for hs in range(2):
                    bA_, bAT_ = head_prologue(fc, hs, sets[hs])
                    done = []
                    for hf in range(2):
                        gens.append(half_steps(fc, hs, sets[hs], hf, bA_, bAT_, done))
                if _os0.environ.get('SEQG', '0') == '1':
                    for g in gens:
                        for _ in g:
                            pass
                    gens = []
                _hs = int(_os0.environ.get('HSTOP', '999'))
                _rounds = 0
                while gens:
                    if _rounds >= _hs:
                        reserved.clear()
                        break
                    _rounds += 1
                    for g in list(gens):
                        try:
                            next(g)
                        except StopIteration:
                            gens.remove(g)
            op('pool', lambda e: e.memset(SCR[:, 2:3], 0.0), reads=[], writes=f3k + fine + [f'XNP{c}' for c in range(8)] + sets[1]['kS'] + hk1)
            if BSTOP[0] <= 2:
                return
            for c in range(4):
                bY = [nbank(), nbank()]
                bZ = nbank()
                for fc in range(8):
                    for hs in range(2):
                        hsl = slice(64 * hs, 64 * hs + 64)
                        op('pe', lambda e, fc=fc, hsl=hsl, c=c: e.matmul(ps[bY[fc // 4]][hsl, (fc % 4) * 128:(fc % 4 + 1) * 128], lhsT=STt[hsl, fc, :],
                                                                        rhs=RH[hsl, fc * 512 + c * 128:fc * 512 + (c + 1) * 128], start=True, stop=True),
                           reads=['STt'] + f3k, writes=[f'ps{bY[fc // 4]}'], signal=(fc % 4 == 3 and hs == 1))
                for fc in range(8):
                    for hs in range(2):
                        hsl = slice(64 * hs, 64 * hs + 64)
                        op('pe', lambda e, fc=fc, hsl=hsl, c=c: e.matmul(ps[bZ][hsl, fc * 64:(fc + 1) * 64], lhsT=GT[hsl, fc * 256 + c * 64:fc * 256 + (c + 1) * 64],
                                                                        rhs=STt[hsl, fc, :], start=True, stop=True),
                           reads=['STt'] + f3k, writes=[f'ps{bZ}'], signal=(fc == 7 and hs == 1))
                for half in range(2):
                    op('dve', lambda e, half=half, c=c: e.tensor_tensor(out=F1[:, half * 4:half * 4 + 4, c * 128:(c + 1) * 128],
                                                                        in0=ps[bY[half]][:, :].rearrange("p (f t) -> p f t", f=4),
                                                                        in1=F1[:, half * 4:half * 4 + 4, c * 128:(c + 1) * 128], op=ALU.add),
                       reads=[f'ps{bY[half]}'] + [f'F1_{f}' for f in range(half * 4, half * 4 + 4)], writes=[f'F1_{f}' for f in range(half * 4, half * 4 + 4)])
                for fc in range(8):
                    op('dve', lambda e, fc=fc, c=c: e.scalar_tensor_tensor(out=STt[:, fc, :], in0=ps[bZ][:, fc * 64:(fc + 1) * 64], scalar=GC[:, fc * 4 + c:fc * 4 + c + 1],
                                                                           in1=HH[:, fc * 256 + c * 64:fc * 256 + (c + 1) * 64], op0=ALU.mult, op1=ALU.add),
                       reads=[f'ps{bZ}', 'GC'] + f3k, writes=['STt'])
            if BSTOP[0] <= 3:
                return
            for m in range(8):
                op('act', lambda e, m=m: e.activation(out=PTb, in_=F1[:, m, :], func=AF.Copy), reads=[f'F1_{m}'], writes=['PTb'])
                b1 = nbank()
                op('pe', lambda e, b1=b1: e.matmul(ps[b1][:, :], lhsT=BO64[:], rhs=PTb, start=True, stop=True), reads=['BO64', 'PTb'], writes=[f'ps{b1}'])
                op('dve', lambda e, m=m, b1=b1: e.tensor_tensor(out=PT[0][:], in0=F1[:, m, :], in1=ps[b1][:, :], op=ALU.subtract), reads=[f'F1_{m}', f'ps{b1}'], writes=['PT0'])
                op('act', lambda e: e.activation(out=PTb, in_=PT[0][:], func=AF.Square), reads=['PT0', 'PTb'], writes=['PTb'])
                b2 = nbank()
                op('pe', lambda e, b2=b2: e.matmul(ps[b2][:, :], lhsT=BO64[:], rhs=PTb, start=True, stop=True), reads=['BO64', 'PTb'], writes=[f'ps{b2}'])
                op('act', lambda e, b2=b2: e.activation(out=PT[1][:], in_=ps[b2][:, :], func=AF.Ln, bias=64e-5), reads=[f'ps{b2}'], writes=['PT1'])
                op('act', lambda e: e.activation(out=PT[1][:], in_=PT[1][:], func=AF.Exp, scale=-0.5), reads=['PT1'], writes=['PT1'])
                op('dve', lambda e: e.tensor_tensor(out=PT[0][:], in0=PT[0][:], in1=PT[1][:], op=ALU.mult), reads=['PT0', 'PT1'], writes=['PT0'])
                op('dve', lambda e, m=m: e.tensor_scalar(out=PT[0][:], in0=PT[0][:], scalar1=vcol('b_gn_g', 0, m), scalar2=vcol('b_gn_b', 0, m), op0=ALU.mult, op1=ALU.add),
                   reads=['PT0', 'VT'], writes=['PT0'])
                op('pool', lambda e, m=m: e.tensor_tensor(out=PT[0][:], in0=PT[0][:], in1=F2[:, m, 4:T + 4], op=ALU.add), reads=['PT0', f'F2_{m}'], writes=['PT0'])
                op('dve', lambda e, m=m: e.tensor_tensor(out=BH[:, m, :], in0=PT[0][:], in1=SQ[:, m, :], op=ALU.mult), reads=['PT0', f'SQ{m}'], writes=[f'BH{m}'])
            proj('b_w_o', 2, lambda kc: BH[:, kc, :], lambda kc: f'BH{kc}', 8, evac_branch(None))
            post_norm(7)

        for b in range(NB):
            if nstage >= 2:
                mem_prep(b, [0, 1] if nstage >= 5 else [0])
            for i in range(NT):
                load_tile(b, i)
                if nstage >= 1:
                    stage_A(b, i)
                if nstage >= 2:
                    stage_C(0)
                if nstage >= 3:
                    stage_M(0)
                if nstage >= 4:
                    stage_B(b, i)
                if nstage >= 5:
                    stage_C(1)
                if nstage >= 6:
                    stage_M(1)
                store_tile(b, i)
        S_.finish('sp', ['y'])
        for k in ('xin0', 'ptio0', 'ptio1'):
            if k in S_.dsem:
                S_.ops['sp'].append(lambda e, semh=S_.dsem[k], v=S_.dcnt[k]: e.wait_ge(semh, v))
        S_.emit()
        build.nops = S_.nops
    return nc


def make_masks():
    t = np.arange(128)[:, None]
    s_ = np.arange(128)[None, :]
    low = t > s_
    ms = []
    m0 = low & (t // 16 == s_ // 16)
    ms += [m0, m0.T]
    for blk in (16, 32, 64):
        mk = (t // (2 * blk) == s_ // (2 * blk)) & ((t // blk) % 2 == 1) & ((s_ // blk) % 2 == 0)
        ms += [mk, mk.T]
    return np.ascontiguousarray(np.stack(ms, axis=1).astype(np.float32).reshape(128, 8 * 128))


def _prep_inputs(inp):
    vecs = pack_vecs(inp)
    wts = pack_weights(inp)
    return vecs, wts


def kernel(**inputs):
    NB, S = 4, 2048
    x = np.asarray(inputs['x'], np.float32)
    mem = np.asarray(inputs['mem'], np.float32)
    vecs, wts = _prep_inputs(inputs)
    nc = build(NB, S)
    in_maps = []
    for c in range(8):
        in_maps.append({"x": np.ascontiguousarray(x[c * NB:(c + 1) * NB].reshape(NB * S, D)),
                        "mem": np.ascontiguousarray(mem[c * NB:(c + 1) * NB].reshape(NB * MEM, D)),
                        "vecs": vecs, "wts": wts, "masks": make_masks()})
    res = run_bass_kernel_spmd(nc, in_maps, core_ids=list(range(8)))
    out = np.concatenate([r["y"].reshape(NB, S, D) for r in res.results], axis=0)
    return out.astype(np.float32)
```

```python
import numpy as np
from contextlib import ExitStack
import concourse.bass as bass
import concourse.mybir as mybir
from concourse.bass_utils import run_bass_kernel_spmd

F32 = mybir.dt.float32
BF16 = mybir.dt.bfloat16
FP16 = mybir.dt.float16
AF = mybir.ActivationFunctionType
ALU = mybir.AluOpType
AX = mybir.AxisListType
import os as _os0
TDT = mybir.dt.float32r if _os0.environ.get('TDT', 'r') == 'r' else mybir.dt.float32

D = 1024
T = 512
MEM = 256
PW = 4096
NRING = 3

VEC_ORDER = [('ln_gains', 12), ('mem_norm', 1), ('a_conv_w', 4), ('a_conv_b', 1), ('a_b_in', 2), ('a_gate_b', 2),
             ('a_lambda', 1), ('a_b_out', 1), ('b_mu', 6), ('b_w0', 1), ('b_a0', 1), ('b_k_k', 1), ('b_k_a', 1),
             ('b_r_k', 1), ('b_gn_g', 1), ('b_gn_b', 1)]
VOFF = {}
_o = 0
for _n, _c in VEC_ORDER:
    VOFF[_n] = _o
    _o += _c
NVEC = _o


def pack_vecs(inp):
    rows = [np.asarray(inp[n], np.float32).reshape(-1) for n, _ in VEC_ORDER]
    v = np.concatenate(rows)
    assert v.size == NVEC * D
    return np.ascontiguousarray(v.reshape(NVEC * 8, 128))


def _mat_pieces(W, MW):
    K, N = W.shape
    KC = K // 128
    NPc = N // MW
    a = W.reshape(KC, 128, NPc, MW).transpose(2, 1, 0, 3).reshape(NPc, 128, KC * MW)
    if KC * MW < PW:
        a = np.concatenate([a, np.zeros((NPc, 128, PW - KC * MW), np.float32)], axis=2)
    return a


PIECES = {}
PGAIN = []


def _layout():
    PIECES.clear()
    PGAIN.clear()

    def add(name, cnt, gain, KC, MW):
        PIECES[name] = (len(PGAIN), cnt)
        for _ in range(cnt):
            PGAIN.append((None if gain is None else [(0, MW, ('VT', gain))], KC, MW))

    g = VOFF['ln_gains']
    add('w_in', 4, g + 0, 8, 512)
    add('gates', 1, None, 16, 256)
    add('a_w_out', 2, None, 8, 512)
    for l in range(2):
        add(f'wq{l}', 2, g + 6 * l + 2, 8, 512)
        add(f'wkv{l}', 4, VOFF['mem_norm'], 8, 512)
        add(f'wo{l}', 2, None, 8, 512)
        add(f'up{l}', 8, g + 6 * l + 4, 8, 512)
        add(f'down{l}', 8, None, 32, 128)
    for var in range(2):
        PIECES['rkv' + 'AB'[var]] = (len(PGAIN), 6)
        for mix in (0, 0, 2, 2, 3, 3):
            PGAIN.append(([(0, 512, ('DV', 2 * mix + var))], 8, 512))
    for var in range(2):
        PIECES['loraA' + 'ab'[var]] = (len(PGAIN), 1)
        PGAIN.append(([(0, 64, ('DV', 2 * 1 + var)), (64, 128, ('DV', 2 * 4 + var)), (128, 256, ('DV', 2 * 5 + var))], 8, 256))
    add('loraB', 1, None, 3, 1024)
    add('b_w_o', 2, None, 8, 512)


_layout()
NPIECE = len(PGAIN)


def pack_weights(inp):
    f = lambda k: np.asarray(inp[k], np.float32)
    out = np.zeros((NPIECE, 128, PW), np.float32)

    def put(name, arr):
        i0, cnt = PIECES[name]
        assert arr.shape[0] == cnt, (name, arr.shape)
        out[i0:i0 + cnt] = arr

    put('w_in', _mat_pieces(f('a_w_in')[0], 512))
    gw = f('a_gate_w')[0].reshape(8, 2, 128, 256)
    put('gates', gw.transpose(2, 0, 1, 3).reshape(1, 128, 8 * 2 * 256))
    put('a_w_out', _mat_pieces(f('a_w_out')[0], 512))
    for l in range(2):
        put(f'wq{l}', _mat_pieces(f('c_w_q')[l], 512))
        put(f'wkv{l}', _mat_pieces(f('c_w_kv')[l], 512))
        put(f'wo{l}', _mat_pieces(f('c_w_o')[l], 512))
        put(f'up{l}', _mat_pieces(f('m_w_up')[l], 512))
        put(f'down{l}', _mat_pieces(f('m_w_down')[l], 128))
    rkv = f('b_w_rkv')[0]
    rk6 = np.concatenate([_mat_pieces(rkv[i], 512) for i in range(3)], axis=0)
    put('rkvA', rk6)
    put('rkvB', rk6)
    la = np.concatenate([f('b_w1')[0], f('b_a1')[0], f('b_g1')[0]], axis=1)
    put('loraAa', _mat_pieces(la, 256))
    put('loraAb', _mat_pieces(la, 256))
    lb = np.zeros((128, 3, 1024), np.float32)
    lb[:64, 0] = f('b_w2')[0]
    lb[64:, 1] = f('b_a2')[0]
    lb[:, 2] = f('b_g2')[0]
    put('loraB', np.concatenate([lb.reshape(1, 128, 3072), np.zeros((1, 128, PW - 3072), np.float32)], axis=2))
    put('b_w_o', _mat_pieces(f('b_w_o')[0], 512))
    return out


class _Rec:
    def __init__(self):
        self.name = None

    def __getattr__(self, name):
        def f(*args, **kwargs):
            self.name, self.args, self.kwargs = name, args, kwargs
            return self
        return f


class Sched:
    ENG = ('pe', 'act', 'dve', 'pool', 'sp')

    def __init__(self, nc, es):
        self.nc = nc
        self.es = es
        self.ops = {e: [] for e in self.ENG}
        self.sem = {e: es.enter_context(nc.semaphore('s_' + e)) for e in self.ENG}
        self.cnt = {e: 0 for e in self.ENG}
        self.known = {e: {} for e in self.ENG}
        self.last_w = {}
        self.reads = {}
        self.dsem = {}
        self.dcnt = {}
        self.nops = 0

    def _deps(self, eng, reads, writes):
        acc = {}

        def need(dep):
            s, v = dep
            if acc.get(s, 0) < v:
                acc[s] = v
        for b in reads:
            w = self.last_w.get(b)
            if w:
                need(w)
        for b in writes:
            w = self.last_w.get(b)
            if w:
                need(w)
            for r in self.reads.get(b, {}).items():
                need(r)
        for s, v in acc.items():
            if self.known[eng].get(s, 0) >= v:
                continue
            if s == eng and eng in ('pe', 'sp'):
                continue
            if s in self.cnt:
                assert v <= self.cnt[s], f"wait on unsignaled {s} {v} > {self.cnt[s]}"
                semh = self.sem[s]
            else:
                semh = self.dsem[s]
            self.known[eng][s] = v
            self.ops[eng].append(lambda e, semh=semh, v=v: e.wait_ge(semh, v))

    def _record(self, reads, writes, tag):
        for b in reads:
            d = self.reads.setdefault(b, {})
            if d.get(tag[0], 0) < tag[1]:
                d[tag[0]] = tag[1]
        for b in writes:
            self.last_w[b] = tag
            self.reads[b] = {}

    def op(self, eng, fn, reads=(), writes=(), signal=True):
        self.nops += 1
        self._deps(eng, reads, writes)
        val = self.cnt[eng] + 1
        rec = _Rec()
        fn(rec)
        assert rec.name is not None
        if signal:
            self.cnt[eng] += 1
            semh = self.sem[eng]
            self.ops[eng].append(lambda e, r=rec, semh=semh: getattr(e, r.name)(*r.args, **r.kwargs).then_inc(semh, 1))
        else:
            self.ops[eng].append(lambda e, r=rec: getattr(e, r.name)(*r.args, **r.kwargs))
        self._record(reads, writes, (eng, val))

    def dma(self, eng, out, in_, reads=(), writes=(), key=None):
        self.nops += 1
        self._deps(eng, reads, writes)
        if key not in self.dsem:
            self.dsem[key] = self.es.enter_context(self.nc.semaphore('d_' + key))
            self.dcnt[key] = 0
        self.dcnt[key] += 16
        semh = self.dsem[key]
        self.ops[eng].append(lambda e, out=out, in_=in_, semh=semh: e.dma_start(out=out, in_=in_).then_inc(semh, 16))
        self._record(reads, writes, (key, self.dcnt[key]))

    def barrier(self):
        snap = dict(self.cnt)
        dsnap = dict(self.dcnt)
        for eng in self.ENG:
            for s, v in snap.items():
                if s == eng or v == 0 or self.known[eng].get(s, 0) >= v:
                    continue
                self.known[eng][s] = v
                self.ops[eng].append(lambda e, semh=self.sem[s], v=v: e.wait_ge(semh, v))
            for s, v in dsnap.items():
                if self.known[eng].get(s, 0) >= v:
                    continue
                self.known[eng][s] = v
                self.ops[eng].append(lambda e, semh=self.dsem[s], v=v: e.wait_ge(semh, v))

    def finish(self, eng, keys):
        acc = {}
        for b in keys:
            w = self.last_w.get(b)
            if w and acc.get(w[0], 0) < w[1]:
                acc[w[0]] = w[1]
        for s, v in acc.items():
            semh = self.sem[s] if s in self.cnt else self.dsem[s]
            self.ops[eng].append(lambda e, semh=semh, v=v: e.wait_ge(semh, v))

    def emit(self):
        with self.nc.Block() as block:
            @block.tensor
            def _(e):
                for f in self.ops['pe']:
                    f(e)

            @block.scalar
            def _(e):
                for f in self.ops['act']:
                    f(e)

            @block.vector
            def _(e):
                for f in self.ops['dve']:
                    f(e)

            @block.gpsimd
            def _(e):
                for f in self.ops['pool']:
                    f(e)

            @block.sync
            def _(e):
                for f in self.ops['sp']:
                    f(e)


STAGES = ['load', 'A', 'C0', 'M0', 'B', 'C1', 'M1']
BSTOP = [9]


def build(NB, S, stop='M1', use_gelu=True):
    NT = S // T
    nstage = STAGES.index(stop)
    nc = bass.Bass("TRN2", target_bir_lowering=False)
    x_d = nc.dram_tensor("x", [NB * S, D], F32, kind="ExternalInput").ap()
    mem_d = nc.dram_tensor("mem", [NB * MEM, D], F32, kind="ExternalInput").ap()
    vec_d = nc.dram_tensor("vecs", [NVEC * 8, 128], F32, kind="ExternalInput").ap()
    wts_d = nc.dram_tensor("wts", [NPIECE, 128, PW], F32, kind="ExternalInput").ap()
    msk_d = nc.dram_tensor("masks", [128, 8 * 128], F32, kind="ExternalInput").ap()
    y_d = nc.dram_tensor("y", [NB * S, D], F32, kind="ExternalOutput").ap()
    wsc = nc.dram_tensor("wsc", [NPIECE, 128, PW], BF16, kind="Internal").ap()

    with ExitStack() as es:
        S_ = Sched(nc, es)
        op = S_.op

        def sb(name, shape, dt):
            return es.enter_context(nc.sbuf_tensor(name, shape, dt))

        VT = sb("VT", [128, NVEC * 8], F32)
        ident = sb("ident", [128, 128], F32)
        identb = sb("identb", [128, 128], BF16)
        onesb = sb("onesb", [128, 128], BF16)
        CV2 = sb("CV2", [128, 16], F32)
        DV = sb("DV", [128, 13 * 8], F32)
        ps = [es.enter_context(nc.psum_tensor(f"ps{i}", [128, 512], F32)) for i in range(8)]
        bank_ctr = [0]

        reserved = set()

        def nbank():
            for _ in range(17):
                b = bank_ctr[0] % 8
                bank_ctr[0] += 1
                if b not in reserved:
                    return b
            raise RuntimeError('no free PSUM bank')

        def vcol(name, idx=0, c=0):
            j = (VOFF[name] + idx) * 8 + c
            return VT[:, j:j + 1]

        op('pool', lambda e: e.memset(ident[:], 0.0), writes=['ident'])
        op('pool', lambda e: e.affine_select(out=ident[:], in_=ident[:], pattern=[[-1, 128]], base=0,
                                             channel_multiplier=1, compare_op=ALU.not_equal, fill=1.0),
           reads=['ident'], writes=['ident'])
        op('pool', lambda e: e.tensor_copy(out=identb[:], in_=ident[:]), reads=['ident'], writes=['identb'])
        op('pool', lambda e: e.memset(onesb[:], 1.0), writes=['onesb'])

        with ExitStack() as es0:
            def sb0(name, shape, dt):
                return es0.enter_context(nc.sbuf_tensor(name, shape, dt))
            vst = [sb0(f"vst{i}", [128, 128], F32) for i in range(3)]
            nrows = NVEC * 8
            for i in range(3):
                r0 = i * 128
                r1 = min(nrows, r0 + 128)
                n = r1 - r0
                S_.dma('sp', vst[i][0:n, :], vec_d[r0:r1, :], writes=[f'vst{i}'], key=f'vst{i}')
                b = nbank()
                op('pe', lambda e, i=i, n=n, b=b: e.transpose(out=ps[b][:, 0:n], in_=vst[i][0:n, :], identity=ident[0:n, 0:n]),
                   reads=[f'vst{i}', 'ident'], writes=[f'ps{b}'])
                op('act', lambda e, r0=r0, n=n, b=b: e.activation(out=VT[:, r0:r0 + n], in_=ps[b][:, 0:n], func=AF.Copy),
                   reads=[f'ps{b}'], writes=['VT'])
            lam = VT[:, VOFF['a_lambda'] * 8:VOFF['a_lambda'] * 8 + 8]
            op('act', lambda e: e.activation(out=CV2[:, 0:8], in_=lam, func=AF.Exp, scale=-1.0), reads=['VT'], writes=['CV2'])
            op('act', lambda e: e.activation(out=CV2[:, 0:8], in_=CV2[:, 0:8], func=AF.Ln, bias=1.0), reads=['CV2'], writes=['CV2'])
            op('act', lambda e: e.activation(out=CV2[:, 0:8], in_=CV2[:, 0:8], func=AF.Copy, scale=-8.0), reads=['CV2'], writes=['CV2'])

            g6 = VT[:, (VOFF['ln_gains'] + 6) * 8:(VOFF['ln_gains'] + 6) * 8 + 8]
            for mi in range(6):
                mu_i = VT[:, (VOFF['b_mu'] + mi) * 8:(VOFF['b_mu'] + mi) * 8 + 8]
                op('dve', lambda e, mi=mi, mu_i=mu_i: e.tensor_tensor(out=DV[:, (2 * mi + 1) * 8:(2 * mi + 2) * 8], in0=mu_i, in1=g6, op=ALU.mult),
                   reads=['VT'], writes=['DV'])
                op('dve', lambda e, mi=mi: e.tensor_tensor(out=DV[:, (2 * mi) * 8:(2 * mi + 1) * 8], in0=g6, in1=DV[:, (2 * mi + 1) * 8:(2 * mi + 2) * 8], op=ALU.subtract),
                   reads=['VT', 'DV'], writes=['DV'])
            ka = VT[:, VOFF['b_k_a'] * 8:VOFF['b_k_a'] * 8 + 8]
            op('dve', lambda e: e.tensor_scalar(out=DV[:, 96:104], in0=ka, scalar1=-1.0, scalar2=1.0, op0=ALU.mult, op1=ALU.add),
               reads=['VT'], writes=['DV'])

            NST = 3
            stf = [sb0(f"stf{i}", [128, PW], F32) for i in range(NST)]
            stb = [sb0(f"stb{i}", [128, PW], BF16) for i in range(NST)]
            for pi in range(NPIECE):
                k = pi % NST
                gain, KC, MW = PGAIN[pi]
                S_.dma('sp', stf[k][:], wts_d[pi], writes=[f'stf{k}'], key=f'stf{k}')
                eng = ('dve', 'pool')[pi % 2] if gain is not None else ('act', 'dve', 'pool')[pi % 3]
                if gain is None:
                    if eng == 'act':
                        op('act', lambda e, k=k: e.activation(out=stb[k][:], in_=stf[k][:], func=AF.Copy),
                           reads=[f'stf{k}'], writes=[f'stb{k}'])
                    else:
                        op(eng, lambda e, k=k: e.tensor_copy(out=stb[k][:], in_=stf[k][:]),
                           reads=[f'stf{k}'], writes=[f'stb{k}'])
                else:
                    for kc in range(KC):
                        for (c0, c1, (tab, gi)) in gain:
                            gc = (VT if tab == 'VT' else DV)[:, gi * 8 + kc:gi * 8 + kc + 1]
                            op(eng, lambda e, k=k, kc=kc, MW=MW, gc=gc, c0=c0, c1=c1: e.tensor_scalar(
                                out=stb[k][:, kc * MW + c0:kc * MW + c1], in0=stf[k][:, kc * MW + c0:kc * MW + c1],
                                scalar1=gc, scalar2=1.0, op0=ALU.mult, op1=ALU.mult),
                               reads=[f'stf{k}', 'VT', 'DV'], writes=[f'stb{k}'])
                S_.dma('act', wsc[pi], stb[k][:], reads=[f'stb{k}'], writes=['wsc'], key=f'wsc{k}')
        S_.barrier()

        import os as _os2
        _ex = int(_os2.environ.get('EXTRA_SBUF', '0'))
        if _ex:
            DUMMY = sb('DUMMY', [128, _ex * 256], F32)
            op('pool', lambda e: e.memset(DUMMY[:, _ex * 256 - 512:], 1.0), writes=['DUMMY'])
        X = sb("X", [128, 8, T], F32)
        XN = sb("XN", [128, 8, T], BF16)
        SQ = sb("SQ", [128, 8, T], BF16)
        RS = sb("RS", [128, T], F32)
        F1 = sb("F1", [128, 8, T], F32)
        F2 = sb("F2", [128, 8, T + 4], F32)
        F3 = sb("F3", [128, 8, T], F32)
        BH = sb("BH", [128, 32, T], BF16)
        PT = [sb(f"PT{i}", [128, T], F32) for i in range(4)]
        TB = [sb(f"TB{i}", [128, T], F32) for i in range(5)]
        ring = [sb(f"ring{i}", [128, PW], BF16) for i in range(NRING)]
        xin = [sb("xin0", [128, D], F32)] * 2
        KT = [sb(f"KT{l}", [128, 8, MEM], BF16) for l in range(2)]
        VV = [sb(f"VV{l}", [128, 2, D], BF16) for l in range(2)]
        HST = sb("HST", [128, 8], F32)
        SMX = sb("SMX", [128, 72], F32)


        XNP = sb("XNP", [128, 8, T + 8], BF16)
        RW = sb("RW", [128, 6400], BF16)
        STt = sb("STt", [128, 8, 64], BF16)
        GC = sb("GC", [128, 32], F32)
        XL = sb("XL", [128, 8], BF16)
        SCR = sb("SCR", [128, 8], F32)
        IDH = sb("IDH", [128, 128], FP16)
        MSK = sb("MSK", [128, 8, 128], BF16)
        MK2 = sb("MK2", [128, 512], BF16)
        ID2 = sb("ID2", [128, 256], BF16)
        BOb = sb("BOb", [128, 128], BF16)
        BO64 = sb("BO64", [128, 128], BF16)
        S_.dma('sp', xin[0][:], msk_d, writes=['xin0'], key='xin0')
        op('dve', lambda e: e.tensor_copy(out=MSK[:].rearrange("p a t -> p (a t)"), in_=xin[0][:]), reads=['xin0'], writes=['MSK'])
        op('dve', lambda e: e.tensor_copy(out=IDH[:], in_=ident[:]), reads=['ident'], writes=['IDH'])
        op('pool', lambda e: e.memset(MK2[:], 1.0), writes=['MK2'])
        for kind in range(4):
            op('pool', lambda e, kind=kind: e.affine_select(out=MK2[:, kind * 128:(kind + 1) * 128], in_=MK2[:, kind * 128:(kind + 1) * 128],
                                                            pattern=[[1, 128]], base=0, channel_multiplier=-1,
                                                            compare_op=(ALU.is_gt if kind % 2 == 0 else ALU.is_ge), fill=0.0),
               reads=['MK2'], writes=['MK2'])
        op('pool', lambda e: e.memset(ID2[:], 0.0), writes=['ID2'])
        for hs in range(2):
            op('pool', lambda e, hs=hs: e.affine_select(out=ID2[64 * hs:64 * hs + 64, :], in_=ID2[64 * hs:64 * hs + 64, :],
                                                        pattern=[[0, 4], [-1, 64]], base=0, channel_multiplier=1,
                                                        compare_op=ALU.not_equal, fill=1.0), reads=['ID2'], writes=['ID2'])
        op('pool', lambda e: e.memset(BOb[:], 0.0), writes=['BOb'])
        op('pool', lambda e: e.memset(BO64[:], 0.0), writes=['BO64'])
        for hs in range(2):
            op('pool', lambda e, hs=hs: e.memset(BOb[64 * hs:64 * hs + 64, 64 * hs:64 * hs + 64], 1.0), reads=['BOb'], writes=['BOb'])
            op('pool', lambda e, hs=hs: e.memset(BO64[64 * hs:64 * hs + 64, 64 * hs:64 * hs + 64], 1.0 / 64.0), reads=['BO64'], writes=['BO64'])

        def bhb(i):
            return BH[:, 8 * i:8 * (i + 1), :]

        BHF = BH[:].rearrange("p a t -> p (a t)").bitcast(F32)

        def bhf(i, c):
            o = (i * 8 + c) * T
            return BHF[:, o:o + T]

        def gk(i, c):
            k = i * 8 + c
            return [f'BH{2 * k}', f'BH{2 * k + 1}']

        seq = []
        for b in range(NB):
            if nstage >= 2:
                seq += [('wkv0', i) for i in range(4)]
            if nstage >= 5:
                seq += [('wkv1', i) for i in range(4)]
            for i in range(NT):
                if nstage >= 1:
                    seq += [('w_in', j) for j in range(4)] + [('gates', 0)] + [('a_w_out', j) for j in range(2)]
                if nstage >= 2:
                    seq += [('wq0', j) for j in range(2)] + [('wo0', j) for j in range(2)]
                if nstage >= 3:
                    seq += [('up0', j) for j in range(8)] + [('down0', j) for j in range(8)]
                if nstage >= 4:
                    seq += [('loraAa', 0), ('loraAb', 0), ('loraB', 0)]
                    for pj in (2, 3, 0, 1, 4, 5):
                        seq += [('rkvA', pj), ('rkvB', pj)]
                    seq += [('b_w_o', j) for j in range(2)]
                if nstage >= 5:
                    seq += [('wq1', j) for j in range(2)] + [('wo1', j) for j in range(2)]
                if nstage >= 6:
                    seq += [('up1', j) for j in range(8)] + [('down1', j) for j in range(8)]
        wstate = {'issued': 0, 'used': 0}

        def w_issue():
            k = wstate['issued']
            if k >= len(seq):
                return
            name, j = seq[k]
            pi = PIECES[name][0] + j
            slot = k % NRING
            S_.dma('sp', ring[slot][:], wsc[pi], writes=[f'ring{slot}'], key=f'ring{slot}')
            wstate['issued'] += 1

        def w_next(name, j):
            k = wstate['used']
            assert seq[k] == (name, j), (seq[k], name, j)
            prev_live = k >= 1 and seq[k - 1][0] in ('rkvA', 'loraAa') and seq[k][0] in ('rkvB', 'loraAb')
            retired = k - 2 if prev_live else k - 1
            while wstate['issued'] < min(len(seq), retired + NRING + 1):
                w_issue()
            wstate['used'] += 1
            slot = k % NRING
            return ring[slot], f'ring{slot}'

        def ones_norm(src_keys):
            b = nbank()
            for c in range(8):
                op('pe', lambda e, c=c, b=b: e.matmul(ps[b][:, :], lhsT=onesb[:], rhs=SQ[:, c, :], start=(c == 0), stop=(c == 7)),
                   reads=['onesb'] + [f'SQ{c}'], writes=[f'ps{b}'], signal=(c == 7))
            op('act', lambda e, b=b: e.activation(out=PT[3][:], in_=ps[b][:, :], func=AF.Ln, scale=1.0 / D, bias=1e-6),
               reads=[f'ps{b}'], writes=['PT3'])
            op('act', lambda e: e.activation(out=RS[:], in_=PT[3][:], func=AF.Exp, scale=-0.5), reads=['PT3'], writes=['RS'])

        def norm_in():
            op('act', lambda e: e.activation(out=SQ[:], in_=X[:], func=AF.Square),
               reads=[f'X{c}' for c in range(8)], writes=[f'SQ{c}' for c in range(8)])
            ones_norm(None)
            for c in range(8):
                eng = 'pool' if c % 3 == 2 else 'dve'
                op(eng, lambda e, c=c: e.tensor_tensor(out=XN[:, c, :], in0=X[:, c, :], in1=RS[:], op=ALU.mult),
                   reads=[f'X{c}', 'RS'], writes=[f'XN{c}'])

        def post_norm(gidx):
            ones_norm(None)
            for c in range(8):
                op('pool', lambda e, c=c: e.tensor_tensor(out=F1[:, c, :], in0=F1[:, c, :], in1=RS[:], op=ALU.mult),
                   reads=[f'F1_{c}', 'RS'], writes=[f'F1_{c}'])
                gc = vcol('ln_gains', gidx, c)
                op('dve', lambda e, c=c, gc=gc: e.scalar_tensor_tensor(out=X[:, c, :], in0=F1[:, c, :], scalar=gc, in1=X[:, c, :],
                                                                        op0=ALU.mult, op1=ALU.add),
                   reads=[f'F1_{c}', f'X{c}', 'VT'], writes=[f'X{c}'])

        def proj(wname, npieces, src, srckeys, KC, evac, mper=4, n=T):
            for pj in range(npieces):
                rg, rkey = w_next(wname, pj)
                MW = mper * 128
                for ml in range(mper):
                    m = pj * mper + ml
                    b = nbank()
                    for kc in range(KC):
                        op('pe', lambda e, rg=rg, kc=kc, ml=ml, b=b, MW=MW: e.matmul(
                            ps[b][:, 0:n], lhsT=rg[:, kc * MW + ml * 128:kc * MW + (ml + 1) * 128], rhs=src(kc),
                            start=(kc == 0), stop=(kc == KC - 1)),
                           reads=[rkey, srckeys(kc)], writes=[f'ps{b}'], signal=(kc == KC - 1))
                    evac(m, ps[b][:, 0:n], f'ps{b}')

        def evac_branch(bias_name):
            def ev(m, p, pk):
                if bias_name is None:
                    op('act', lambda e, m=m, p=p: e.activation(out=F1[:, m, :], in_=p, func=AF.Copy),
                       reads=[pk], writes=[f'F1_{m}'])
                    op('act', lambda e, m=m, p=p: e.activation(out=SQ[:, m, :], in_=p, func=AF.Square),
                       reads=[pk], writes=[f'SQ{m}'])
                else:
                    bc = vcol(bias_name, 0, m)
                    op('act', lambda e, m=m, p=p, bc=bc: e.activation(out=F1[:, m, :], in_=p, func=AF.Identity, bias=bc),
                       reads=[pk, 'VT'], writes=[f'F1_{m}'])
                    op('act', lambda e, m=m, p=p, bc=bc: e.activation(out=SQ[:, m, :], in_=p, func=AF.Square, bias=bc),
                       reads=[pk, 'VT'], writes=[f'SQ{m}'])
            return ev

        xkeys = [f'X{c}' for c in range(8)]

        def load_tile(b, i):
            for tb in range(4):
                r0 = b * S + i * T + tb * 128
                if tb % 2 == 0:
                    srcs = [xin[0][:, 0:512], xin[0][:, 512:1024]]
                    bkeys = ['xin0', 'xin0']
                    S_.dma('sp', xin[0][:], x_d[r0:r0 + 128, :], writes=['xin0'], key='xin0')
                else:
                    srcs = [PT[0][:], PT[1][:]]
                    bkeys = ['PT0', 'PT1']
                    for h_ in range(2):
                        S_.dma('sp', PT[h_][:], x_d[r0:r0 + 128, h_ * 512:(h_ + 1) * 512], writes=[bkeys[h_]], key=f'ptio{h_}')
                for half in range(2):
                    bk = nbank()
                    for cl in range(4):
                        c = half * 4 + cl
                        op('pe', lambda e: e.transpose(out=ps[bk][:, cl * 128:(cl + 1) * 128], in_=srcs[half][:, cl * 128:(cl + 1) * 128], identity=ident[:]),
                           reads=[bkeys[half], 'ident'], writes=[f'ps{bk}'], signal=(cl == 3))
                    op('act', lambda e: e.activation(
                        out=X[:, half * 4:half * 4 + 4, tb * 128:(tb + 1) * 128],
                        in_=ps[bk][:, :].rearrange("p (c t) -> p c t", c=4), func=AF.Copy),
                       reads=[f'ps{bk}'], writes=[f'X{c}' for c in range(half * 4, half * 4 + 4)])

        def store_tile(b, i):
            for tb in range(4):
                r0 = b * S + i * T + tb * 128
                if tb % 2 == 0:
                    dsts = [xin[0][:, 0:512], xin[0][:, 512:1024]]
                    bkeys = ['xin0', 'xin0']
                else:
                    dsts = [PT[0][:], PT[1][:]]
                    bkeys = ['PT0', 'PT1']
                for half in range(2):
                    bk = nbank()
                    for cl in range(4):
                        c = half * 4 + cl
                        op('pe', lambda e: e.transpose(out=ps[bk][:, cl * 128:(cl + 1) * 128], in_=X[:, c, tb * 128:(tb + 1) * 128], identity=ident[:]),
                           reads=[f'X{c}', 'ident'], writes=[f'ps{bk}'], signal=(cl == 3))
                    op('act', lambda e: e.activation(out=dsts[half], in_=ps[bk][:, :], func=AF.Copy),
                       reads=[f'ps{bk}'], writes=[bkeys[half]])
                if tb % 2 == 0:
                    S_.dma('sp', y_d[r0:r0 + 128, :], xin[0][:], reads=['xin0'], writes=['y'], key='xin0')
                else:
                    for h_ in range(2):
                        S_.dma('sp', y_d[r0:r0 + 128, h_ * 512:(h_ + 1) * 512], PT[h_][:], reads=[bkeys[h_]], writes=['y'], key=f'ptio{h_}')

        def stage_A(b, i):
            norm_in()
            vb = VOFF['a_b_in']

            def ev_in(m, p, pk):
                if m < 8:
                    bc = VT[:, vb * 8 + m:vb * 8 + m + 1]
                    if use_gelu:
                        op('act', lambda e, m=m, p=p, bc=bc: e.activation(out=F1[:, m, :], in_=p, func=AF.Gelu_apprx_tanh, bias=bc),
                           reads=[pk, 'VT'], writes=[f'F1_{m}'])
                    else:
                        op('act', lambda e, m=m, p=p, bc=bc: e.activation(out=F1[:, m, :], in_=p, func=AF.Identity, bias=bc),
                           reads=[pk, 'VT'], writes=[f'F1_{m}'])
                        op('pool', lambda e, m=m: e.tensor_tensor(out=PT[0][:], in0=F1[:, m, :], in1=F1[:, m, :], op=ALU.mult),
                           reads=[f'F1_{m}'], writes=['PT0'])
                        op('dve', lambda e: e.tensor_scalar(out=PT[0][:], in0=PT[0][:], scalar1=0.044715, scalar2=1.0, op0=ALU.mult, op1=ALU.add),
                           reads=['PT0'], writes=['PT0'])
                        op('pool', lambda e, m=m: e.tensor_tensor(out=PT[0][:], in0=PT[0][:], in1=F1[:, m, :], op=ALU.mult),
                           reads=[f'F1_{m}', 'PT0'], writes=['PT0'])
                        op('act', lambda e: e.activation(out=PT[0][:], in_=PT[0][:], func=AF.Sigmoid, scale=1.5957691216057308),
                           reads=['PT0'], writes=['PT0'])
                        op('dve', lambda e, m=m: e.tensor_tensor(out=F1[:, m, :], in0=F1[:, m, :], in1=PT[0][:], op=ALU.mult),
                           reads=[f'F1_{m}', 'PT0'], writes=[f'F1_{m}'])
                else:
                    c = m - 8
                    bc = VT[:, vb * 8 + m:vb * 8 + m + 1]
                    op('act', lambda e, c=c, p=p, bc=bc: e.activation(out=F2[:, c, 4:T + 4], in_=p, func=AF.Identity, bias=bc),
                       reads=[pk, 'VT'], writes=[f'F2_{c}'])
                    cw = [vcol('a_conv_w', k, c) for k in range(4)]
                    cb = vcol('a_conv_b', 0, c)
                    op('dve', lambda e, c=c, cw=cw, cb=cb: e.tensor_scalar(out=F3[:, c, :], in0=F2[:, c, 1:T + 1], scalar1=cw[0], scalar2=cb,
                                                                        op0=ALU.mult, op1=ALU.add),
                       reads=[f'F2_{c}', 'VT'], writes=[f'F3_{c}'])
                    for k in range(1, 4):
                        op('dve', lambda e, c=c, k=k, cw=cw: e.scalar_tensor_tensor(out=F3[:, c, :], in0=F2[:, c, 1 + k:T + 1 + k], scalar=cw[k],
                                                                                in1=F3[:, c, :], op0=ALU.mult, op1=ALU.add),
                           reads=[f'F2_{c}', f'F3_{c}', 'VT'], writes=[f'F3_{c}'])
                    op('pool', lambda e, c=c: e.tensor_copy(out=F2[:, c, 1:4], in_=F2[:, c, T + 1:T + 4]),
                       reads=[f'F2_{c}'], writes=[f'F2_{c}'])
                    op('pool', lambda e, c=c: e.tensor_copy(out=SQ[:, c, :], in_=F3[:, c, :]),
                       reads=[f'F3_{c}'], writes=[f'SQ{c}'])

            if i == 0:
                for c in range(8):
                    op('pool', lambda e, c=c: e.memset(F2[:, c, 0:4], 0.0), writes=[f'F2_{c}'])
                op('pool', lambda e: e.memset(HST[:], 0.0), writes=['HST'])
            proj('w_in', 4, lambda kc: XN[:, kc, :], lambda kc: f'XN{kc}', 8, ev_in)
            rg, rkey = w_next('gates', 0)
            gb = VOFF['a_gate_b']
            for gi in range(2):
                for c in range(8):
                    h, j = c // 2, c % 2
                    bk = nbank()
                    for kc in range(2):
                        o = ((gi * 4 + h) * 2 + kc) * 256 + j * 128
                        op('pe', lambda e, o=o, h=h, kc=kc, bk=bk: e.matmul(ps[bk][:, :], lhsT=rg[:, o:o + 128], rhs=SQ[:, 2 * h + kc, :],
                                                                          start=(kc == 0), stop=(kc == 1)),
                           reads=[rkey, f'SQ{2*h+kc}'], writes=[f'ps{bk}'], signal=(kc == 1))
                    bc = VT[:, (gb + gi) * 8 + c:(gb + gi) * 8 + c + 1]
                    op('act', lambda e, gi=gi, c=c, bk=bk, bc=bc: e.activation(out=bhf(gi, c), in_=ps[bk][:, :], func=AF.Sigmoid, bias=bc),
                       reads=[f'ps{bk}', 'VT'], writes=gk(gi, c))
            for c in range(8):
                cc = CV2[:, c:c + 1]
                op('act', lambda e, c=c, cc=cc: e.activation(out=bhf(0, c), in_=bhf(0, c), func=AF.Exp, scale=cc),
                   reads=gk(0, c) + ['CV2'], writes=gk(0, c))
                op('pool', lambda e, c=c: e.tensor_tensor(out=F2[:, c, 4:T + 4], in0=bhf(0, c), in1=bhf(0, c), op=ALU.mult),
                   reads=gk(0, c) + [f'F2_{c}'], writes=[f'F2_{c}'])
            for c in range(8):
                op('act', lambda e, c=c: e.activation(out=F2[:, c, 4:T + 4], in_=F2[:, c, 4:T + 4], func=AF.Sqrt, scale=-1.0, bias=1.0),
                   reads=[f'F2_{c}'], writes=[f'F2_{c}'])
            for c in range(8):
                op('dve', lambda e, c=c: e.tensor_tensor(out=bhf(1, c), in0=bhf(1, c), in1=F2[:, c, 4:T + 4], op=ALU.mult),
                   reads=gk(1, c) + [f'F2_{c}'], writes=gk(1, c))
                op('pool', lambda e, c=c: e.tensor_tensor(out=bhf(1, c), in0=bhf(1, c), in1=F3[:, c, :], op=ALU.mult),
                   reads=gk(1, c) + [f'F3_{c}'], writes=gk(1, c))
                op('dve', lambda e, c=c: e.tensor_tensor_scan(out=F3[:, c, :], data0=bhf(0, c), data1=bhf(1, c), initial=HST[:, c:c + 1],
                                                             op0=ALU.mult, op1=ALU.add),
                   reads=gk(0, c) + gk(1, c) + ['HST', f'F3_{c}'], writes=[f'F3_{c}'])
                op('pool', lambda e, c=c: e.tensor_copy(out=HST[:, c:c + 1], in_=F3[:, c, T - 1:T]),
                   reads=[f'F3_{c}'], writes=['HST'])
                op('pool', lambda e, c=c: e.tensor_tensor(out=XN[:, c, :], in0=F3[:, c, :], in1=F1[:, c, :], op=ALU.mult),
                   reads=[f'F3_{c}', f'F1_{c}'], writes=[f'XN{c}'])
            proj('a_w_out', 2, lambda kc: XN[:, kc, :], lambda kc: f'XN{kc}', 8, evac_branch('a_b_out'))
            post_norm(1)

        def mem_prep(b, layers):
            MT = F1[:].rearrange("p c t -> p (c t)")[:, 0:8 * MEM].rearrange("p (c t) -> p c t", c=8)
            MN = XN[:].rearrange("p c t -> p (c t)")[:, 0:8 * MEM].rearrange("p (c t) -> p c t", c=8)
            MSQ = SQ[:].rearrange("p c t -> p (c t)")[:, 0:8 * MEM].rearrange("p (c t) -> p c t", c=8)
            f1k = [f'F1_{c}' for c in range(8)]
            xnk = [f'XN{c}' for c in range(8)]
            sqk = [f'SQ{c}' for c in range(8)]
            for tb in range(2):
                r0 = b * MEM + tb * 128
                xb = xin[0]
                S_.dma('sp', xb[:], mem_d[r0:r0 + 128, :], writes=['xin0'], key='xin0')
                for half in range(2):
                    bk = nbank()
                    for cl in range(4):
                        c = half * 4 + cl
                        op('pe', lambda e, xb=xb, c=c, cl=cl, bk=bk: e.transpose(out=ps[bk][:, cl * 128:(cl + 1) * 128],
                                                                                in_=xb[:, c * 128:(c + 1) * 128], identity=ident[:]),
                           reads=['xin0', 'ident'], writes=[f'ps{bk}'], signal=(cl == 3))
                    op('act', lambda e, half=half, tb=tb, bk=bk: e.activation(
                        out=MT[:, half * 4:half * 4 + 4, tb * 128:(tb + 1) * 128],
                        in_=ps[bk][:, :].rearrange("p (c t) -> p c t", c=4), func=AF.Copy),
                       reads=[f'ps{bk}'], writes=f1k)
            op('act', lambda e: e.activation(out=MSQ, in_=MT, func=AF.Square), reads=f1k, writes=sqk)
            bk = nbank()
            for c in range(8):
                op('pe', lambda e, c=c, bk=bk: e.matmul(ps[bk][:, 0:MEM], lhsT=onesb[:], rhs=MSQ[:, c, :], start=(c == 0), stop=(c == 7)),
                   reads=['onesb'] + sqk, writes=[f'ps{bk}'], signal=(c == 7))
            op('act', lambda e, bk=bk: e.activation(out=PT[3][:, 0:MEM], in_=ps[bk][:, 0:MEM], func=AF.Ln, scale=1.0 / D, bias=1e-6),
               reads=[f'ps{bk}'], writes=['PT3'])
            op('act', lambda e: e.activation(out=RS[:, 0:MEM], in_=PT[3][:, 0:MEM], func=AF.Exp, scale=-0.5), reads=['PT3'], writes=['RS'])
            for c in range(8):
                op('dve', lambda e, c=c: e.tensor_tensor(out=MN[:, c, :], in0=MT[:, c, :], in1=RS[:, 0:MEM], op=ALU.mult),
                   reads=f1k + ['RS'], writes=xnk)
            for l in layers:
                for pj in range(2):
                    rg, rkey = w_next(f'wkv{l}', pj)
                    for ml in range(4):
                        m = pj * 4 + ml
                        bk = nbank()
                        for kc in range(8):
                            op('pe', lambda e, rg=rg, kc=kc, ml=ml, bk=bk: e.matmul(
                                ps[bk][:, 0:MEM], lhsT=rg[:, kc * 512 + ml * 128:kc * 512 + (ml + 1) * 128], rhs=MN[:, kc, :],
                                start=(kc == 0), stop=(kc == 7)),
                               reads=[rkey] + xnk, writes=[f'ps{bk}'], signal=(kc == 7))
                        op('act', lambda e, l=l, m=m, bk=bk: e.activation(out=KT[l][:, m, :], in_=ps[bk][:, 0:MEM], func=AF.Copy),
                           reads=[f'ps{bk}'], writes=[f'KT{l}'])
                for pj in range(2):
                    rg, rkey = w_next(f'wkv{l}', 2 + pj)
                    for mc in range(2):
                        bk = nbank()
                        for kc in range(8):
                            op('pe', lambda e, rg=rg, kc=kc, mc=mc, bk=bk: e.matmul(
                                ps[bk][:, :], lhsT=MN[:, kc, mc * 128:(mc + 1) * 128], rhs=rg[:, kc * 512:(kc + 1) * 512],
                                start=(kc == 0), stop=(kc == 7)),
                               reads=[rkey] + xnk, writes=[f'ps{bk}'], signal=(kc == 7))
                        op('act', lambda e, l=l, mc=mc, pj=pj, bk=bk: e.activation(out=VV[l][:, mc, pj * 512:(pj + 1) * 512], in_=ps[bk][:, :], func=AF.Copy),
                           reads=[f'ps{bk}'], writes=[f'VV{l}'])

        def stage_C(l):
            for c in range(8):
                eng = 'pool' if c % 3 == 2 else 'dve'
                op(eng, lambda e, c=c: e.tensor_copy(out=XN[:, c, :], in_=X[:, c, :]), reads=[f'X{c}'], writes=[f'XN{c}'])
            op('act', lambda e: e.activation(out=SQ[:], in_=X[:], func=AF.Square),
               reads=[f'X{c}' for c in range(8)], writes=[f'SQ{c}' for c in range(8)])
            bR = nbank()
            for tb in range(4):
                for c in range(8):
                    op('pe', lambda e, tb=tb, c=c: e.matmul(ps[bR][:, tb:tb + 1], lhsT=SQ[:, c, tb * 128:(tb + 1) * 128], rhs=onesb[:, 0:1],
                                                         start=(c == 0), stop=(c == 7)),
                       reads=['onesb', f'SQ{c}'], writes=[f'ps{bR}'], signal=(tb == 3 and c == 7))
            op('act', lambda e: e.activation(out=SMX[:, 48:52], in_=ps[bR][:, 0:4], func=AF.Ln, scale=1.0 / D, bias=1e-6),
               reads=[f'ps{bR}'], writes=['RSt'])
            op('act', lambda e: e.activation(out=SMX[:, 48:52], in_=SMX[:, 48:52], func=AF.Exp, scale=-0.5), reads=['RSt'], writes=['RSt'])
            QT = bhb(0)
            PN = bhb(1)
            PTt = bhb(2)
            OT = bhb(3)

            def ev_q(m, p, pk):
                op('act', lambda e, m=m, p=p: e.activation(out=QT[:, m, :], in_=p, func=AF.Copy, scale=1.0 / 16.0),
                   reads=[pk], writes=[f'BH{m}'])
            proj(f'wq{l}', 2, lambda kc: XN[:, kc, :], lambda kc: f'XN{kc}', 8, ev_q)
            for tb in range(4):
                pn = PN[:, 2 * tb:2 * tb + 2, :].rearrange("p a t -> p (a t)")
                pex = F3[:, 2 * tb:2 * tb + 2, :].rearrange("p a t -> p (a t)")
                banks = [nbank(), nbank()]
                for h in range(4):
                    bk = banks[h // 2]
                    for dc in range(2):
                        op('pe', lambda e, h=h, dc=dc, bk=bk, tb=tb: e.matmul(
                            ps[bk][:, (h % 2) * 256:(h % 2 + 1) * 256], lhsT=QT[:, 2 * h + dc, tb * 128:(tb + 1) * 128],
                            rhs=KT[l][:, 2 * h + dc, :], start=(dc == 0), stop=(dc == 1)),
                           reads=[f'BH{2*h+dc}', f'KT{l}'], writes=[f'ps{bk}'], signal=(dc == 1))
                for hb in range(2):
                    bk = banks[hb]
                    op('dve', lambda e, hb=hb, bk=bk, tb=tb: e.tensor_reduce(
                        out=SMX[:, tb * 4 + 2 * hb:tb * 4 + 2 * hb + 2], in_=ps[bk][:, :].rearrange("p (h k) -> p h k", h=2),
                        axis=AX.X, op=ALU.max, negate=True),
                       reads=[f'ps{bk}'], writes=[f'SMXm{tb}'])
                op('dve', lambda e, tb=tb: e.tensor_scalar(out=SMX[:, 52 + tb * 4:56 + tb * 4], in0=SMX[:, tb * 4:tb * 4 + 4],
                                                          scalar1=SMX[:, 48 + tb:49 + tb], scalar2=None, op0=ALU.mult),
                   reads=[f'SMXm{tb}', 'RSt'], writes=[f'SMXn{tb}'])
                for h in range(4):
                    bk = banks[h // 2]
                    op('act', lambda e, h=h, bk=bk, tb=tb, pex=pex: e.activation(
                        out=pex[:, h * 256:(h + 1) * 256], in_=ps[bk][:, (h % 2) * 256:(h % 2 + 1) * 256], func=AF.Exp,
                        scale=SMX[:, 48 + tb:49 + tb], bias=SMX[:, 52 + tb * 4 + h:53 + tb * 4 + h],
                        accum_out=SMX[:, 16 + tb * 4 + h:16 + tb * 4 + h + 1]),
                       reads=[f'ps{bk}', f'SMXn{tb}', 'RSt'], writes=[f'F3_{2*tb}', f'F3_{2*tb+1}', f'SMXs{tb}'])
                op('dve', lambda e, tb=tb: e.reciprocal(out=SMX[:, 32 + tb * 4:32 + tb * 4 + 4], in_=SMX[:, 16 + tb * 4:16 + tb * 4 + 4]),
                   reads=[f'SMXs{tb}'], writes=[f'SMXr{tb}'])
                for h in range(4):
                    op('dve', lambda e, h=h, tb=tb, pn=pn, pex=pex: e.tensor_scalar(
                        out=pn[:, h * 256:(h + 1) * 256], in0=pex[:, h * 256:(h + 1) * 256],
                        scalar1=SMX[:, 32 + tb * 4 + h:32 + tb * 4 + h + 1], scalar2=None, op0=ALU.mult),
                       reads=[f'F3_{2*tb}', f'F3_{2*tb+1}', f'SMXr{tb}'], writes=[f'BH{8+2*tb}', f'BH{9+2*tb}'])
                bk = nbank()
                psb = ps[bk][:, :].bitcast(BF16)
                for hm in range(8):
                    op('pe', lambda e, hm=hm, pn=pn, psb=psb: e.transpose(out=psb[:, hm * 128:(hm + 1) * 128], in_=pn[:, hm * 128:(hm + 1) * 128],
                                                                          identity=identb[:]),
                       reads=[f'BH{8+2*tb}', f'BH{9+2*tb}', 'identb'], writes=[f'ps{bk}'], signal=(hm == 7))
                op('act', lambda e, tb=tb, psb=psb: e.activation(out=PTt[:, :, tb * 128:(tb + 1) * 128],
                                                                 in_=psb.rearrange("p (a t) -> p a t", a=8), func=AF.Copy),
                   reads=[f'ps{bk}'], writes=[f'BH{16+a}' for a in range(8)])
            for m in range(8):
                h = m // 2
                bk = nbank()
                for mc in range(2):
                    op('pe', lambda e, m=m, h=h, mc=mc, bk=bk: e.matmul(ps[bk][:, :], lhsT=VV[l][:, mc, m * 128:(m + 1) * 128],
                                                                      rhs=PTt[:, 2 * h + mc, :], start=(mc == 0), stop=(mc == 1)),
                       reads=[f'VV{l}', f'BH{16+2*h+mc}'], writes=[f'ps{bk}'], signal=(mc == 1))
                op('act', lambda e, m=m, bk=bk: e.activation(out=OT[:, m, :], in_=ps[bk][:, :], func=AF.Copy),
                   reads=[f'ps{bk}'], writes=[f'BH{24+m}'])
            proj(f'wo{l}', 2, lambda kc: OT[:, kc, :], lambda kc: f'BH{24+kc}', 8, evac_branch(None))
            post_norm(6 * l + 3)

        def stage_M(l):
            for c in range(8):
                eng = 'pool' if c % 3 == 2 else 'dve'
                op(eng, lambda e, c=c: e.tensor_copy(out=XN[:, c, :], in_=X[:, c, :]), reads=[f'X{c}'], writes=[f'XN{c}'])
            op('act', lambda e: e.activation(out=SQ[:], in_=X[:], func=AF.Square),
               reads=[f'X{c}' for c in range(8)], writes=[f'SQ{c}' for c in range(8)])
            b = nbank()
            for c in range(8):
                op('pe', lambda e, c=c, b=b: e.matmul(ps[b][:, :], lhsT=onesb[:], rhs=SQ[:, c, :], start=(c == 0), stop=(c == 7)),
                   reads=['onesb'] + [f'SQ{c}'], writes=[f'ps{b}'], signal=(c == 7))
            op('act', lambda e, b=b: e.activation(out=PT[3][:], in_=ps[b][:, :], func=AF.Ln, scale=1.0 / D, bias=1e-6),
               reads=[f'ps{b}'], writes=['PT3'])
            op('act', lambda e: e.activation(out=RS[:], in_=PT[3][:], func=AF.Exp, scale=-1.0), reads=['PT3'], writes=['RS'])
            cnt = [0]

            def ev_down(m, p, pk):
                op('dve', lambda e, m=m, p=p: e.tensor_tensor(out=F1[:, m, :], in0=p, in1=RS[:], op=ALU.mult),
                   reads=[pk, 'RS'], writes=[f'F1_{m}'])
                op('act', lambda e, m=m: e.activation(out=SQ[:, m, :], in_=F1[:, m, :], func=AF.Square),
                   reads=[f'F1_{m}'], writes=[f'SQ{m}'])

            def ev_up(m, p, pk):
                k = cnt[0] % 2
                cnt[0] += 1
                op('act', lambda e, p=p, k=k: e.activation(out=PT[k][:], in_=p, func=AF.Square), reads=[pk], writes=[f'PT{k}'])
                op('dve', lambda e, m=m, p=p, k=k: e.scalar_tensor_tensor(out=BH[:, m, :], in0=p, scalar=0.0, in1=PT[k][:],
                                                                          op0=ALU.is_gt, op1=ALU.mult),
                   reads=[pk, f'PT{k}'], writes=[f'BH{m}'])
            proj(f'up{l}', 8, lambda kc: XN[:, kc, :], lambda kc: f'XN{kc}', 8, ev_up)
            proj(f'down{l}', 8, lambda kc: BH[:, kc, :], lambda kc: f'BH{kc}', 32, ev_down, mper=1)
            post_norm(6 * l + 5)

        _rw = [0]

        def rw_alloc(n):
            o = _rw[0]
            _rw[0] += n
            return RW[:, o:o + n]
        TM = rw_alloc(2048)
        U4 = rw_alloc(2048)
        Pb = rw_alloc(512)
        AU = rw_alloc(512)
        Wt = rw_alloc(256)
        LWA = rw_alloc(512)
        PTb = rw_alloc(512)
        LG = PTb
        F3B = F3[:].rearrange("p c t -> p (c t)").bitcast(BF16)
        RH = F3B[:, 0:4096]
        GT = F3B[:, 4096:6144]
        HH = F3B[:, 6144:8192]
        ARf = BH[:, 0:16, :].rearrange("p a t -> p (a t)")
        f3k = [f'F3_{c}' for c in range(8)]

        def ar_kind(fc, kind):
            return ARf[:, fc * 1024:(fc + 1) * 1024].rearrange("p (c k t) -> p c k t", c=4, k=2)[:, :, kind, :]

        def stage_B(b, i):
            for c in range(8):
                if i == 0:
                    op('pool', lambda e, c=c: e.memset(XNP[:, c, 0:8], 0.0), writes=[f'XNP{c}'])
                else:
                    op('pool', lambda e, c=c: e.tensor_copy(out=XNP[:, c, 7:8], in_=XL[:, c:c + 1]), reads=['XL', f'XNP{c}'], writes=[f'XNP{c}'])
            if i == 0:
                op('pool', lambda e: e.memset(STt[:], 0.0), writes=['STt'])
            op('act', lambda e: e.activation(out=SQ[:], in_=X[:], func=AF.Square), reads=xkeys, writes=[f'SQ{c}' for c in range(8)])
            ones_norm(None)
            for c in range(8):
                eng = 'pool' if c % 3 == 2 else 'dve'
                op(eng, lambda e, c=c: e.tensor_tensor(out=XNP[:, c, 8:T + 8], in0=X[:, c, :], in1=RS[:], op=ALU.mult),
                   reads=[f'X{c}', 'RS'], writes=[f'XNP{c}'])

            def proj2(nameA, nameB, pj, evac):
                rgA, kA = w_next(nameA, pj)
                rgB, kB = w_next(nameB, pj)
                return rgA, kA, rgB, kB

            def mm16(rgA, kA, rgB, kB, col0, ncol, bk, MW):
                for v_, (rg, rk, off) in enumerate(((rgA, kA, 8), (rgB, kB, 7))):
                    for kc in range(8):
                        op('pe', lambda e, rg=rg, kc=kc, off=off, v_=v_: e.matmul(
                            ps[bk][0:ncol, :], lhsT=rg[:, kc * MW + col0:kc * MW + col0 + ncol], rhs=XNP[:, kc, off:off + T],
                            start=(v_ == 0 and kc == 0), stop=(v_ == 1 and kc == 7)),
                           reads=[rk, f'XNP{kc}'], writes=[f'ps{bk}'], signal=(v_ == 1 and kc == 7))

            if BSTOP[0] <= 0.1:
                return
            rgA, kA = w_next('loraAa', 0)
            rgB, kB = w_next('loraAb', 0)
            bk = nbank()
            mm16(rgA, kA, rgB, kB, 0, 128, bk, 256)
            op('act', lambda e, bk=bk: e.activation(out=LWA[0:64, :], in_=ps[bk][0:64, :], func=AF.Tanh), reads=[f'ps{bk}'], writes=['LWA0'])
            op('act', lambda e, bk=bk: e.activation(out=LWA[64:128, :], in_=ps[bk][64:128, :], func=AF.Copy), reads=[f'ps{bk}'], writes=['LWA1'])
            bk = nbank()
            mm16(rgA, kA, rgB, kB, 128, 128, bk, 256)
            op('act', lambda e, bk=bk: e.activation(out=LG[:, :], in_=ps[bk][:, :], func=AF.Sigmoid), reads=[f'ps{bk}'], writes=['PTb'])
            if BSTOP[0] <= 0.3:
                return
            rgL, kL = w_next('loraB', 0)
            for m in range(8):
                bk = nbank()
                op('pe', lambda e, m=m, bk=bk: e.matmul(ps[bk][:, :], lhsT=rgL[0:64, m * 128:(m + 1) * 128], rhs=LWA[0:64, :], start=True, stop=True),
                   reads=[kL, 'LWA0'], writes=[f'ps{bk}'])
                op('act', lambda e, m=m, bk=bk: e.activation(out=PT[0][:], in_=ps[bk][:, :], func=AF.Sigmoid, bias=vcol('b_w0', 0, m)),
                   reads=[f'ps{bk}', 'VT'], writes=['PT0'])
                op('act', lambda e: e.activation(out=PT[0][:], in_=PT[0][:], func=AF.Copy, scale=-0.6065306597126334), reads=['PT0'], writes=['PT0'])
                for c in range(4):
                    op('dve', lambda e, m=m, c=c: e.tensor_tensor_scan(out=F1[:, m, c * 128:(c + 1) * 128], data0=onesb[:], data1=PT[0][:, c * 128:(c + 1) * 128],
                                                                       initial=0.0, op0=ALU.mult, op1=ALU.add),
                       reads=['PT0', 'onesb'], writes=[f'F1_{m}'])
                op('pool', lambda e, m=m: e.tensor_tensor(out=F3[:, m, :], in0=F1[:, m, :], in1=PT[0][:], op=ALU.subtract),
                   reads=[f'F1_{m}', 'PT0'], writes=[f'F3_{m}'])
                bk = nbank()
                op('pe', lambda e, m=m, bk=bk: e.matmul(ps[bk][:, :], lhsT=rgL[64:128, 1024 + m * 128:1024 + (m + 1) * 128], rhs=LWA[64:128, :], start=True, stop=True),
                   reads=[kL, 'LWA1'], writes=[f'ps{bk}'])
                op('act', lambda e, m=m, bk=bk: e.activation(out=F2[:, m, 4:T + 4], in_=ps[bk][:, :], func=AF.Sigmoid, bias=vcol('b_a0', 0, m)),
                   reads=[f'ps{bk}', 'VT'], writes=[f'F2_{m}'])
                bk = nbank()
                op('pe', lambda e, m=m, bk=bk: e.matmul(ps[bk][:, :], lhsT=rgL[:, 2048 + m * 128:2048 + (m + 1) * 128], rhs=LG[:, :], start=True, stop=True),
                   reads=[kL, 'PTb'], writes=[f'ps{bk}'])
                op('act', lambda e, m=m, bk=bk: e.activation(out=SQ[:, m, :], in_=ps[bk][:, :], func=AF.Copy), reads=[f'ps{bk}'], writes=[f'SQ{m}'])
            if BSTOP[0] <= 0.5:
                return
            op('act', lambda e: e.activation(out=GC[:].rearrange("p (a c) -> p a c", a=8),
                                             in_=F1[:].rearrange("p a (c t) -> p a c t", c=4)[:, :, :, 127], func=AF.Exp),
               reads=[f'F1_{c}' for c in range(8)], writes=['GC'])
            omk = lambda m: DV[:, 96 + m:97 + m]
            if BSTOP[0] <= 0.6:
                return
            TMf = RW[:, 0:2048].bitcast(F32)
            tsets = [(PT[0][:], PT[1][:], PT[2][:], PT[3][:], PTb, ['PT0'], ['PT1'], ['PT2'], ['PT3'], ['PTb']),
                     (xin[0][:, 0:512], xin[0][:, 512:1024], TMf[:, 0:512], TMf[:, 512:1024], LWA, ['xa'], ['xb'], ['tma'], ['tmb'], ['LWA0', 'LWA1'])]
            op('pool', lambda e: e.memset(SCR[:, 3:4], 0.0), reads=[], writes=['xin0', 'TM', 'xa', 'xb', 'tma', 'tmb'])
            def rr(gens):
                while gens:
                    for g in list(gens):
                        try:
                            next(g)
                        except StopIteration:
                            gens.remove(g)

            def k_chain(m, bk):
                pk = f'ps{bk}'
                t0, t1, t2, t3, tq, k0, k1, k2, k3, kq = tsets[m % 2]
                kkc = vcol('b_k_k', 0, m)
                op('act', lambda e: e.activation(out=t0, in_=ps[bk][:, :], func=AF.Copy, scale=kkc), reads=[pk, 'VT'], writes=k0)
                op('act', lambda e: e.activation(out=tq, in_=ps[bk][:, :], func=AF.Square, scale=kkc), reads=[pk, 'VT'], writes=kq)
                yield
                b2 = nbank()
                op('pe', lambda e: e.matmul(ps[b2][:, :], lhsT=BOb[:], rhs=tq, start=True, stop=True), reads=['BOb'] + kq, writes=[f'ps{b2}'])
                op('act', lambda e: e.activation(out=t2, in_=F3[:, m, :], func=AF.Exp), reads=[f'F3_{m}'], writes=k2)
                yield
                op('act', lambda e: e.activation(out=t1, in_=ps[b2][:, :], func=AF.Ln, bias=1e-24), reads=[f'ps{b2}'], writes=k1)
                yield
                op('act', lambda e: e.activation(out=t1, in_=t1, func=AF.Exp, scale=-0.5), reads=k1, writes=k1)
                op('act', lambda e: e.activation(out=t3, in_=F1[:, m, :], func=AF.Exp, scale=-1.0), reads=[f'F1_{m}'], writes=k3)
                yield
                op('dve', lambda e: e.tensor_tensor(out=t0, in0=t0, in1=t1, op=ALU.mult), reads=k0 + k1, writes=k0)
                yield
                op('dve', lambda e: e.scalar_tensor_tensor(out=ar_kind(m, 0), in0=t0.rearrange("p (c t) -> p c t", c=4), scalar=-1.0,
                                                           in1=t2.rearrange("p (c t) -> p c t", c=4), op0=ALU.mult, op1=ALU.mult),
                   reads=k0 + k2, writes=[f'BH{2*m}', f'BH{2*m+1}'])
                op('dve', lambda e: e.tensor_scalar(out=t1, in0=F2[:, m, 4:T + 4], scalar1=vcol('b_k_a', 0, m), scalar2=omk(m), op0=ALU.mult, op1=ALU.add),
                   reads=[f'F2_{m}', 'VT', 'DV'] + k1, writes=k1)
                yield
                op('pool', lambda e: e.tensor_tensor(out=t0, in0=t0, in1=F2[:, m, 4:T + 4], op=ALU.mult), reads=k0 + [f'F2_{m}'], writes=k0)
                yield
                op('dve', lambda e: e.tensor_tensor(out=BH[:, 16 + m, :], in0=t0, in1=t3, op=ALU.mult), reads=k0 + k3, writes=[f'BH{16+m}'])
                op('dve', lambda e: e.tensor_tensor(out=F2[:, m, 4:T + 4], in0=ps[bk][:, :], in1=t1, op=ALU.mult),
                   reads=[pk, f'F2_{m}'] + k1, writes=[f'F2_{m}'])
                yield
                op('pool', lambda e: e.tensor_tensor(out=BH[:, 24 + m, :], in0=F2[:, m, 4:T + 4], in1=t3, op=ALU.mult),
                   reads=[f'F2_{m}'] + k3, writes=[f'BH{24+m}'])
                yield

            def r_chain(m, bk):
                pk = f'ps{bk}'
                t0, t1, t2, t3, tq, k0, k1, k2, k3, kq = tsets[m % 2]
                op('act', lambda e: e.activation(out=t2, in_=F1[:, m, :], func=AF.Exp), reads=[f'F1_{m}'], writes=k2)
                yield
                op('dve', lambda e: e.tensor_tensor(out=ar_kind(m, 1), in0=ps[bk][:, :].rearrange("p (c t) -> p c t", c=4),
                                                    in1=t2.rearrange("p (c t) -> p c t", c=4), op=ALU.mult),
                   reads=[pk] + k2, writes=[f'BH{2*m}', f'BH{2*m+1}'])
                op('dve', lambda e: e.scalar_tensor_tensor(out=tq, in0=ps[bk][:, :], scalar=vcol('b_r_k', 0, m), in1=F2[:, m, 4:T + 4],
                                                           op0=ALU.mult, op1=ALU.mult),
                   reads=[pk, 'VT', f'F2_{m}'] + kq, writes=kq)
                yield
                b2 = nbank()
                op('pe', lambda e: e.matmul(ps[b2][:, :], lhsT=BOb[:], rhs=tq, start=True, stop=True), reads=['BOb'] + kq, writes=[f'ps{b2}'])
                yield
                op('act', lambda e: e.activation(out=F2[:, m, 4:T + 4], in_=ps[b2][:, :], func=AF.Copy), reads=[f'ps{b2}'], writes=[f'F2_{m}'])
                yield

            for pj in (2, 3):
                rgA, kA = w_next('rkvA', pj)
                rgB, kB = w_next('rkvB', pj)
                for pr in range(2):
                    gens = []
                    for ml in (2 * pr, 2 * pr + 1):
                        m = (pj - 2) * 4 + ml
                        bk = nbank()
                        mm16(rgA, kA, rgB, kB, ml * 128, 128, bk, 512)
                        gens.append(k_chain(m, bk))
                    rr(gens)
            if BSTOP[0] <= 0.7:
                return
            for pj in (0, 1):
                rgA, kA = w_next('rkvA', pj)
                rgB, kB = w_next('rkvB', pj)
                for pr in range(2):
                    gens = []
                    for ml in (2 * pr, 2 * pr + 1):
                        m = pj * 4 + ml
                        bk = nbank()
                        mm16(rgA, kA, rgB, kB, ml * 128, 128, bk, 512)
                        gens.append(r_chain(m, bk))
                    rr(gens)
            op('pool', lambda e: e.memset(SCR[:, 4:5], 0.0), reads=[], writes=['xin0', 'TM', 'xa', 'xb', 'tma', 'tmb'])
            if BSTOP[0] <= 0.8:
                return
            for pj in (4, 5):
                rgA, kA = w_next('rkvA', pj)
                rgB, kB = w_next('rkvB', pj)
                for ml in range(4):
                    m = (pj - 4) * 4 + ml
                    bk = nbank()
                    mm16(rgA, kA, rgB, kB, ml * 128, 128, bk, 512)
                    pk = f'ps{bk}'
                    import os as _os
                    _pm = int(_os.environ.get('P5MODE', '0'))
                    if _pm in (0, 1):
                        op('dve', lambda e, m=m, bk=bk: e.tensor_copy(out=XN[:, m, :], in_=ps[bk][:, :]), reads=[pk], writes=[f'XN{m}'])
                    if _pm in (0, 2):
                        op('dve', lambda e, m=m, bk=bk: e.tensor_tensor(out=F2[:, m, 4:T + 4], in0=ps[bk][:, :], in1=F2[:, m, 4:T + 4], op=ALU.mult),
                           reads=[pk, f'F2_{m}'], writes=[f'F2_{m}'])
            if BSTOP[0] <= 1:
                return
            TM4 = TM.rearrange("p (c k f) -> p c k f", c=4, k=4)
            op('pool', lambda e: e.tensor_copy(out=XL[:].rearrange("p (c o) -> p c o", o=1), in_=XNP[:, :, T + 7:T + 8]), reads=[f'XNP{c}' for c in range(8)], writes=['XL'])
            XNPf = XNP[:].rearrange("p c t -> p (c t)")
            sets = []
            for si in range(2):
                TBh = [TB[j][:].bitcast(FP16) for j in range(5)]
                if si == 0:
                    d = dict(Q=[TBh[0], TBh[1]], kQ=['TB0', 'TB1'], B5=TBh[4][:, 0:512], kB5='TB4s0',
                             U4=U4, Pb=Pb, AU=AU, Wt=Wt, kS=['U4', 'Pb', 'AU', 'Wt'])
                else:
                    d = dict(Q=[TBh[2], TBh[3]], kQ=['TB2', 'TB3'], B5=TBh[4][:, 512:1024], kB5='TB4s1',
                             U4=XNPf[:, 0:2048], Pb=XNPf[:, 2048:2560], AU=XNPf[:, 2560:3072], Wt=XNPf[:, 3072:3328], kS=['U4b', 'Pbb', 'AUb', 'Wtb'])
                sets.append(d)
            hk1 = [k + f'h{hf}' for k in sets[1]['kS'][1:] for hf in range(2)]
            op('pool', lambda e: e.memset(SCR[:, 0:1], 0.0), reads=[], writes=[f'XNP{c}' for c in range(8)] + sets[1]['kS'] + hk1)

            def head_prologue(fc, hs, st):
                hsl = slice(64 * hs, 64 * hs + 64)
                U4_ = st['U4']
                kU = [st['kS'][0]]
                At = lambda c: ARf[hsl, fc * 1024 + c * 256:fc * 1024 + c * 256 + 128]
                ARc = lambda c: ARf[hsl, fc * 1024 + c * 256:fc * 1024 + (c + 1) * 256]
                Bt = lambda c: BH[hsl, 16 + fc, c * 128:(c + 1) * 128]
                Kt = lambda c: BH[hsl, 24 + fc, c * 128:(c + 1) * 128]
                kAR = [f'BH{2*fc}', f'BH{2*fc+1}']
                bA = nbank(); reserved.add(bA)
                for c in range(4):
                    op('pe', lambda e, c=c: e.matmul(ps[bA][:, c * 128:(c + 1) * 128], lhsT=At(c), rhs=Bt(c), start=True, stop=True),
                       reads=kAR + [f'BH{16+fc}'], writes=[f'ps{bA}'], signal=(c == 3))
                bAT = nbank(); reserved.add(bAT)
                for c in range(4):
                    op('pe', lambda e, c=c: e.matmul(ps[bAT][:, c * 128:(c + 1) * 128], lhsT=Bt(c), rhs=At(c), start=True, stop=True),
                       reads=kAR + [f'BH{16+fc}'], writes=[f'ps{bAT}'], signal=(c == 3))
                for c in range(4):
                    bB = nbank()
                    op('pe', lambda e, c=c, bB=bB: e.matmul(ps[bB][:, 0:256], lhsT=Bt(c), rhs=ARc(c), start=True, stop=True),
                       reads=kAR + [f'BH{16+fc}'], writes=[f'ps{bB}'], signal=False)
                    op('pe', lambda e, c=c, bB=bB: e.matmul(ps[bB][:, 256:512], lhsT=Kt(c), rhs=ARc(c), start=True, stop=True),
                       reads=kAR + [f'BH{24+fc}'], writes=[f'ps{bB}'])
                    op('dve', lambda e, c=c, bB=bB: e.tensor_tensor(out=U4_[:, c * 512:(c + 1) * 512], in0=ps[bB][:, :], in1=MK2[:], op=ALU.mult),
                       reads=[f'ps{bB}', 'MK2'], writes=kU)
                return bA, bAT

            def half_steps(fc, hs, st, hf, bA, bAT, done):
                hsl = slice(64 * hs, 64 * hs + 64)
                fsl = slice(64 * hs, 64 * hs + 64)
                cs_ = slice(hf * 256, (hf + 1) * 256)
                QQ = st['Q'][hf]
                XX, TP = QQ[:, 0:512], QQ[:, 512:1024]
                B1, B2 = XX[:, 0:256], XX[:, 256:512]
                B3, B4 = TP[:, 0:256], TP[:, 256:512]
                B5 = st['B5'][:, cs_]
                kx = st['kQ'][hf]
                k1, k2, k3, k4, k5 = [kx + 'a'], [kx + 'b'], [kx + 'c'], [kx + 'd'], [st['kB5'] + f'h{hf}']
                dt_ = FP16
                idm = None
                U44 = st['U4'].rearrange("p (u k t) -> p u k t", u=4, k=4)
                Pb_ = st['Pb'][:, hf * 256:(hf + 1) * 256]
                AU_ = st['AU'][:, hf * 256:(hf + 1) * 256]
                Wt_ = st['Wt'][:, hf * 128:(hf + 1) * 128]
                kU = [st['kS'][0]]
                kP, kAU, kW = [[k + f'h{hf}'] for k in st['kS'][1:]]
                kAR = [f'BH{2*fc}', f'BH{2*fc+1}']
                v2 = lambda t: t.rearrange("p (u t) -> p u t", u=2)
                mskb = lambda j: MSK[:, j:j + 1, :].to_broadcast([128, 2, 128])
                idb = ident[:].rearrange("p (o t) -> p o t", o=1).to_broadcast([128, 2, 128])
                W_ = lambda t: t
                held = []

                def gbank():
                    bk_ = nbank()
                    reserved.add(bk_)
                    held.append(bk_)
                    return bk_

                def gfree(bk_):
                    reserved.discard(bk_)
                    held.remove(bk_)

                def mm2(bk_, col0, lhs, rhs, rk, acc=None, acck=None, last=True):
                    for u in range(2):
                        us = slice(u * 128, (u + 1) * 128)
                        os_ = slice(col0 + u * 128, col0 + (u + 1) * 128)
                        if acc is not None:
                            op('pe', lambda e: e.matmul(ps[bk_][:, os_], lhsT=W_(idm), rhs=W_(acc[:, us]), start=True, stop=False),
                               reads=acck + ['ident'], writes=[f'ps{bk_}'], signal=False)
                        op('pe', lambda e: e.matmul(ps[bk_][:, os_], lhsT=W_(lhs[:, us]), rhs=W_(rhs[:, us]), start=(acc is None), stop=True),
                           reads=rk, writes=[f'ps{bk_}'], signal=(last and u == 1))

                def cp(eng, dst, kd, bk_, col0, n):
                    if eng == 'act':
                        op('act', lambda e: e.activation(out=W_(dst), in_=ps[bk_][:, col0:col0 + n], func=AF.Copy), reads=[f'ps{bk_}'], writes=kd)
                    else:
                        op('dve', lambda e: e.tensor_copy(out=W_(dst), in_=ps[bk_][:, col0:col0 + n]), reads=[f'ps{bk_}'], writes=kd)

                op('dve', lambda e: e.tensor_tensor(out=v2(W_(B1)), in0=v2(ps[bA][:, cs_]), in1=mskb(0), op=ALU.mult), reads=[f'ps{bA}', 'MSK'], writes=k1)
                op('dve', lambda e: e.tensor_tensor(out=v2(W_(B2)), in0=v2(ps[bAT][:, cs_]), in1=mskb(1), op=ALU.mult), reads=[f'ps{bAT}', 'MSK'], writes=k2)
                e3 = 'pool'
                op(e3, lambda e: e.tensor_tensor(out=W_(TP[:, :]).rearrange("p (u t) -> p u t", u=4), in0=XX[:, :].rearrange("p (u t) -> p u t", u=4),
                                                 in1=ident[:].rearrange("p (o t) -> p o t", o=1).to_broadcast([128, 4, 128]), op=ALU.add),
                   reads=k1 + k2 + ['ident'], writes=k3 + k4)
                yield
                def tn_from_pt():
                    bt_ = gbank()
                    for u in range(2):
                        us = slice(u * 128, (u + 1) * 128)
                        op('pe', lambda e: e.transpose(out=ps[bt_][:, :].bitcast(FP16)[:, us], in_=B4[:, us], identity=IDH[:]),
                           reads=k4 + ['IDH'], writes=[f'ps{bt_}'], signal=(u == 1))
                    return bt_

                for lev in range(3):
                    bk_ = gbank()
                    mm2(bk_, 0, B2, B1, k1 + k2, last=False)
                    mm2(bk_, 256, B1, B2, k1 + k2)
                    yield
                    cp('act', XX[:, :], k1 + k2, bk_, 0, 512)
                    gfree(bk_)
                    yield
                    bk_ = gbank()
                    mm2(bk_, 0, B1, B4, k1 + k4)
                    yield
                    op('dve', lambda e: e.tensor_tensor(out=W_(B4), in0=ps[bk_][:, 0:256], in1=B4, op=ALU.add), reads=[f'ps{bk_}'] + k4, writes=k4)
                    gfree(bk_)
                    yield
                for kl in range(1, 4):
                    op('dve', lambda e, kl=kl: e.tensor_tensor(out=v2(W_(B5)), in0=v2(ps[bA][:, cs_]), in1=mskb(2 * kl), op=ALU.mult), reads=[f'ps{bA}', 'MSK'], writes=k5)
                    bt_ = tn_from_pt()
                    yield
                    op('act', lambda e: e.activation(out=B3, in_=ps[bt_][:, :].bitcast(FP16)[:, 0:256], func=AF.Copy), reads=[f'ps{bt_}'], writes=k3)
                    gfree(bt_)
                    bz = gbank()
                    mm2(bz, 0, B5, B4, k5 + k4)
                    yield
                    cp('act', B1, k1, bz, 0, 256)
                    gfree(bz)
                    yield
                    bz = gbank()
                    mm2(bz, 0, B3, B1, k3 + k1)
                    yield
                    if kl < 3:
                        op('dve', lambda e: e.tensor_tensor(out=W_(B4), in0=ps[bz][:, 0:256], in1=B4, op=ALU.add), reads=[f'ps{bz}'] + k4, writes=k4)
                    else:
                        op('dve', lambda e: e.tensor_tensor(out=Pb_, in0=ps[bz][:, 0:256], in1=B4, op=ALU.add), reads=[f'ps{bz}'] + k4, writes=kP)
                    gfree(bz)
                    yield
                done.append(1)
                if len(done) == 2:
                    reserved.discard(bA); reserved.discard(bAT)
                us2 = [2 * hf, 2 * hf + 1]
                bW = gbank()
                for j, u in enumerate(us2):
                    op('pe', lambda e: e.matmul(ps[bW][:, j * 64:(j + 1) * 64], lhsT=U44[:, u, 2, :], rhs=TM4[:, u, 3, fsl], start=True, stop=True),
                       reads=kU + ['TM'], writes=[f'ps{bW}'], signal=(j == 1))
                yield
                op('act', lambda e: e.activation(out=Wt_, in_=ps[bW][:, 0:128], func=AF.Copy), reads=[f'ps{bW}'], writes=kW)
                gfree(bW)
                yield
                bU = gbank()
                for j, u in enumerate(us2):
                    op('pe', lambda e: e.matmul(ps[bU][:, j * 128:j * 128 + 64], lhsT=Pb_[:, j * 128:(j + 1) * 128], rhs=TM4[:, u, 0, fsl], start=True, stop=True),
                       reads=kP + ['TM'], writes=[f'ps{bU}'], signal=False)
                    op('pe', lambda e: e.matmul(ps[bU][:, j * 128 + 64:(j + 1) * 128], lhsT=Pb_[:, j * 128:(j + 1) * 128], rhs=Wt_[:, j * 64:(j + 1) * 64], start=True, stop=True),
                       reads=kP + kW, writes=[f'ps{bU}'], signal=(j == 1))
                yield
                op('act', lambda e: e.activation(out=AU_, in_=ps[bU][:, 0:256], func=AF.Copy), reads=[f'ps{bU}'], writes=kAU)
                gfree(bU)
                yield
                bRY = gbank()
                for j, u in enumerate(us2):
                    op('pe', lambda e: e.matmul(ps[bRY][hsl, j * 128:(j + 1) * 128], lhsT=AU_[:, j * 128:j * 128 + 64], rhs=U44[:, u, 1, :], start=True, stop=True),
                       reads=kAU + kU, writes=[f'ps{bRY}'], signal=False)
                for j, u in enumerate(us2):
                    op('pe', lambda e: e.matmul(ps[bRY][hsl, 256 + j * 128:256 + (j + 1) * 128], lhsT=AU_[:, j * 128 + 64:(j + 1) * 128], rhs=U44[:, u, 1, :], start=True, stop=False),
                       reads=kAU + kU, writes=[f'ps{bRY}'], signal=False)
                    op('pe', lambda e: e.matmul(ps[bRY][hsl, 256 + j * 128:256 + (j + 1) * 128], lhsT=TM4[:, u, 3, fsl], rhs=U44[:, u, 3, :], start=False, stop=True),
                       reads=['TM'] + kU, writes=[f'ps{bRY}'], signal=(j == 1))
                yield
                tsl = slice(fc * 512 + hf * 256, fc * 512 + (hf + 1) * 256)
                op('dve', lambda e: e.tensor_tensor(out=RH[hsl, tsl].rearrange("p (c t) -> p c t", c=2),
                                                    in0=ps[bRY][hsl, 0:256].rearrange("p (c t) -> p c t", c=2),
                                                    in1=ARf[hsl, fc * 1024 + hf * 512:fc * 1024 + (hf + 1) * 512].rearrange("p (c k t) -> p c k t", c=2, k=2)[:, :, 1, :], op=ALU.add),
                   reads=[f'ps{bRY}'] + kAR, writes=[f'RH{fc}_{hs}_{hf}'])
                op('dve', lambda e: e.tensor_copy(out=F1[hsl, fc, hf * 256:(hf + 1) * 256], in_=ps[bRY][hsl, 256:512]), reads=[f'ps{bRY}'], writes=[f'F1_{fc}'])
                gfree(bRY)
                yield
                bGH = gbank()
                for j, u in enumerate(us2):
                    op('pe', lambda e: e.matmul(ps[bGH][hsl, j * 64:(j + 1) * 64], lhsT=AU_[:, j * 128:j * 128 + 64], rhs=TM4[:, u, 1, fsl], start=True, stop=True),
                       reads=kAU + ['TM'], writes=[f'ps{bGH}'], signal=False)
                for j, u in enumerate(us2):
                    op('pe', lambda e: e.matmul(ps[bGH][hsl, 128 + j * 64:128 + (j + 1) * 64], lhsT=TM4[:, u, 1, fsl], rhs=AU_[:, j * 128 + 64:(j + 1) * 128], start=True, stop=False),
                       reads=kAU + ['TM'], writes=[f'ps{bGH}'], signal=False)
                    op('pe', lambda e: e.matmul(ps[bGH][hsl, 128 + j * 64:128 + (j + 1) * 64], lhsT=TM4[:, u, 2, fsl], rhs=TM4[:, u, 3, fsl], start=False, stop=True),
                       reads=['TM'], writes=[f'ps{bGH}'], signal=(j == 1))
                yield
                gsl = slice(fc * 256 + hf * 128, fc * 256 + (hf + 1) * 128)
                op('dve', lambda e: e.tensor_tensor(out=GT[hsl, gsl], in0=ps[bGH][hsl, 0:128], in1=ID2[hsl, 0:128], op=ALU.add),
                   reads=[f'ps{bGH}', 'ID2'], writes=[f'GT{fc}_{hs}_{hf}'])
                op('dve', lambda e: e.tensor_tensor(out=HH[hsl, fc * 256 + hf * 128:fc * 256 + (hf + 1) * 128].rearrange("p (u i) -> p u i", u=2),
                                                    in0=ps[bGH][hsl, 128:256].rearrange("p (u i) -> p u i", u=2),
                                                    in1=GC[hsl, fc * 4 + 2 * hf:fc * 4 + 2 * hf + 2].rearrange("p (u o) -> p u o", o=1).to_broadcast([64, 2, 64]), op=ALU.mult),
                   reads=[f'ps{bGH}', 'GC'], writes=[f'HH{fc}_{hs}_{hf}'])
                gfree(bGH)
                yield

            fine = [f'{n}{fc}_{hs}_{hf}' for n in ('RH', 'GT', 'HH') for fc in range(8) for hs in range(2) for hf in range(2)]
            op('pool', lambda e: e.memset(SCR[:, 1:2], 0.0), reads=[], writes=f3k + fine)
            for fc in range(8):
                srcs = [lambda c, fc=fc: ARf[:, fc * 1024 + c * 256:fc * 1024 + c * 256 + 128],
                        lambda c, fc=fc: BH[:, 16 + fc, c * 128:(c + 1) * 128],
                        lambda c, fc=fc: BH[:, 24 + fc, c * 128:(c + 1) * 128],
                        lambda c, fc=fc: XN[:, fc, c * 128:(c + 1) * 128]]
                skeys = [[f'BH{2*fc}', f'BH{2*fc+1}'], [f'BH{16+fc}'], [f'BH{24+fc}'], [f'XN{fc}']]
                for half in range(2):
                    bk = nbank()
                    psb = ps[bk][:, :].bitcast(BF16)
                    for cl in range(2):
                        c = half * 2 + cl
                        for kind in range(4):
                            o = (cl * 4 + kind) * 128
                            op('pe', lambda e, c=c, kind=kind, o=o, psb=psb, srcs=srcs: e.transpose(out=psb[:, o:o + 128], in_=srcs[kind](c), identity=identb[:]),
                               reads=skeys[kind] + ['identb'], writes=[f'ps{bk}'], signal=(cl == 1 and kind == 3))
                    op('act', lambda e, half=half, psb=psb: e.activation(out=TM[:, half * 1024:(half + 1) * 1024], in_=psb, func=AF.Copy),
                       reads=[f'ps{bk}'], writes=['TM'])
                gens = []
                for hs in range(2):
                    bA_, bAT_ = head_prologue(fc, hs, sets[hs])
                    done = []
                    for hf in range(2):
                        gens.append(half_steps(fc, hs, sets[hs], hf, bA_, bAT_, done))
                if _os0.environ.get('SEQG', '0') == '1':
                    for g in gens:
                        for _ in g:
                            pass
                    gens = []
                _hs = int(_os0.environ.get('HSTOP', '999'))
                _rounds = 0
                while gens:
                    if _rounds >= _hs:
                        reserved.clear()
                        break
                    _rounds += 1
                    for g in list(gens):
                        try:
                            next(g)
                        except StopIteration:
                            gens.remove(g)
            op('pool', lambda e: e.memset(SCR[:, 2:3], 0.0), reads=[], writes=f3k + fine + [f'XNP{c}' for c in range(8)] + sets[1]['kS'] + hk1)
            if BSTOP[0] <= 2:
                return
            for c in range(4):
                bY = [nbank(), nbank()]
                bZ = nbank()
                for fc in range(8):
                    for hs in range(2):
                        hsl = slice(64 * hs, 64 * hs + 64)
                        op('pe', lambda e, fc=fc, hsl=hsl, c=c: e.matmul(ps[bY[fc // 4]][hsl, (fc % 4) * 128:(fc % 4 + 1) * 128], lhsT=STt[hsl, fc, :],
                                                                        rhs=RH[hsl, fc * 512 + c * 128:fc * 512 + (c + 1) * 128], start=True, stop=True),
                           reads=['STt'] + f3k, writes=[f'ps{bY[fc // 4]}'], signal=(fc % 4 == 3 and hs == 1))
                for fc in range(8):
                    for hs in range(2):
                        hsl = slice(64 * hs, 64 * hs + 64)
                        op('pe', lambda e, fc=fc, hsl=hsl, c=c: e.matmul(ps[bZ][hsl, fc * 64:(fc + 1) * 64], lhsT=GT[hsl, fc * 256 + c * 64:fc * 256 + (c + 1) * 64],
                                                                        rhs=STt[hsl, fc, :], start=True, stop=True),
                           reads=['STt'] + f3k, writes=[f'ps{bZ}'], signal=(fc == 7 and hs == 1))
                for half in range(2):
                    op('dve', lambda e, half=half, c=c: e.tensor_tensor(out=F1[:, half * 4:half * 4 + 4, c * 128:(c + 1) * 128],
                                                                        in0=ps[bY[half]][:, :].rearrange("p (f t) -> p f t", f=4),
                                                                        in1=F1[:, half * 4:half * 4 + 4, c * 128:(c + 1) * 128], op=ALU.add),
                       reads=[f'ps{bY[half]}'] + [f'F1_{f}' for f in range(half * 4, half * 4 + 4)], writes=[f'F1_{f}' for f in range(half * 4, half * 4 + 4)])
                for fc in range(8):
                    op('dve', lambda e, fc=fc, c=c: e.scalar_tensor_tensor(out=STt[:, fc, :], in0=ps[bZ][:, fc * 64:(fc + 1) * 64], scalar=GC[:, fc * 4 + c:fc * 4 + c + 1],
                                                                           in1=HH[:, fc * 256 + c * 64:fc * 256 + (c + 1) * 64], op0=ALU.mult, op1=ALU.add),
                       reads=[f'ps{bZ}', 'GC'] + f3k, writes=['STt'])
            if BSTOP[0] <= 3:
                return
            def gn_chain(m):
                ta, tb_, tq = (PT[0], PT[1], PTb) if m % 2 == 0 else (PT[2], PT[3], LWA)
                ka, kb, kq = (['PT0'], ['PT1'], ['PTb']) if m % 2 == 0 else (['PT2'], ['PT3'], ['LWA0', 'LWA1'])
                op('act', lambda e: e.activation(out=tq, in_=F1[:, m, :], func=AF.Copy), reads=[f'F1_{m}'], writes=kq)
                yield
                b1 = nbank()
                op('pe', lambda e: e.matmul(ps[b1][:, :], lhsT=BO64[:], rhs=tq, start=True, stop=True), reads=['BO64'] + kq, writes=[f'ps{b1}'])
                yield
                op('dve', lambda e: e.tensor_tensor(out=ta[:], in0=F1[:, m, :], in1=ps[b1][:, :], op=ALU.subtract), reads=[f'F1_{m}', f'ps{b1}'], writes=ka)
                yield
                op('act', lambda e: e.activation(out=tq, in_=ta[:], func=AF.Square), reads=ka + kq, writes=kq)
                yield
                b2 = nbank()
                op('pe', lambda e: e.matmul(ps[b2][:, :], lhsT=BO64[:], rhs=tq, start=True, stop=True), reads=['BO64'] + kq, writes=[f'ps{b2}'])
                yield
                op('act', lambda e: e.activation(out=tb_[:], in_=ps[b2][:, :], func=AF.Ln, bias=64e-5), reads=[f'ps{b2}'], writes=kb)
                yield
                op('act', lambda e: e.activation(out=tb_[:], in_=tb_[:], func=AF.Exp, scale=-0.5), reads=kb, writes=kb)
                yield
                op('dve', lambda e: e.scalar_tensor_tensor(out=ta[:], in0=ta[:], scalar=vcol('b_gn_g', 0, m), in1=tb_[:], op0=ALU.mult, op1=ALU.mult),
                   reads=ka + kb + ['VT'], writes=ka)
                yield
                op('dve', lambda e: e.scalar_tensor_tensor(out=ta[:], in0=ta[:], scalar=vcol('b_gn_b', 0, m), in1=F2[:, m, 4:T + 4], op0=ALU.add, op1=ALU.add),
                   reads=ka + ['VT', f'F2_{m}'], writes=ka)
                yield
                op('dve', lambda e: e.tensor_tensor(out=BH[:, m, :], in0=ta[:], in1=SQ[:, m, :], op=ALU.mult), reads=ka + [f'SQ{m}'], writes=[f'BH{m}'])
                yield

            for m0 in range(0, 8, 2):
                gens = [gn_chain(m0), gn_chain(m0 + 1)]
                while gens:
                    for g in list(gens):
                        try:
                            next(g)
                        except StopIteration:
                            gens.remove(g)
            proj('b_w_o', 2, lambda kc: BH[:, kc, :], lambda kc: f'BH{kc}', 8, evac_branch(None))
            post_norm(7)

        for b in range(NB):
            if nstage >= 2:
                mem_prep(b, [0, 1] if nstage >= 5 else [0])
            for i in range(NT):
                load_tile(b, i)
                if nstage >= 1:
                    stage_A(b, i)
                if nstage >= 2:
                    stage_C(0)
                if nstage >= 3:
                    stage_M(0)
                if nstage >= 4:
                    stage_B(b, i)
                if nstage >= 5:
                    stage_C(1)
                if nstage >= 6:
                    stage_M(1)
                store_tile(b, i)
        S_.finish('sp', ['y'])
        for k in ('xin0', 'ptio0', 'ptio1'):
            if k in S_.dsem:
                S_.ops['sp'].append(lambda e, semh=S_.dsem[k], v=S_.dcnt[k]: e.wait_ge(semh, v))
        S_.emit()
        build.nops = S_.nops
    return nc


def make_masks():
    t = np.arange(128)[:, None]
    s_ = np.arange(128)[None, :]
    low = t > s_
    ms = []
    m0 = low & (t // 16 == s_ // 16)
    ms += [m0, m0.T]
    for blk in (16, 32, 64):
        mk = (t // (2 * blk) == s_ // (2 * blk)) & ((t // blk) % 2 == 1) & ((s_ // blk) % 2 == 0)
        ms += [mk, mk.T]
    return np.ascontiguousarray(np.stack(ms, axis=1).astype(np.float32).reshape(128, 8 * 128))


def _prep_inputs(inp):
    vecs = pack_vecs(inp)
    wts = pack_weights(inp)
    return vecs, wts


def kernel(**inputs):
    NB, S = 4, 2048
    x = np.asarray(inputs['x'], np.float32)
    mem = np.asarray(inputs['mem'], np.float32)
    vecs, wts = _prep_inputs(inputs)
    nc = build(NB, S)
    in_maps = []
    for c in range(8):
        in_maps.append({"x": np.ascontiguousarray(x[c * NB:(c + 1) * NB].reshape(NB * S, D)),
                        "mem": np.ascontiguousarray(mem[c * NB:(c + 1) * NB].reshape(NB * MEM, D)),
                        "vecs": vecs, "wts": wts, "masks": make_masks()})
    res = run_bass_kernel_spmd(nc, in_maps, core_ids=list(range(8)))
    out = np.concatenate([r["y"].reshape(NB, S, D) for r in res.results], axis=0)
    return out.astype(np.float32)
```

```python
import numpy as np
from contextlib import ExitStack
import concourse.bass as bass
import concourse.mybir as mybir
from concourse.bass_utils import run_bass_kernel_spmd

F32 = mybir.dt.float32
BF16 = mybir.dt.bfloat16
FP16 = mybir.dt.float16
AF = mybir.ActivationFunctionType
ALU = mybir.AluOpType
AX = mybir.AxisListType
import os as _os0
TDT = mybir.dt.float32r if _os0.environ.get('TDT', 'r') == 'r' else mybir.dt.float32

D = 1024
T = 512
MEM = 256
PW = 4096
NRING = 3

VEC_ORDER = [('ln_gains', 12), ('mem_norm', 1), ('a_conv_w', 4), ('a_conv_b', 1), ('a_b_in', 2), ('a_gate_b', 2),
             ('a_lambda', 1), ('a_b_out', 1), ('b_mu', 6), ('b_w0', 1), ('b_a0', 1), ('b_k_k', 1), ('b_k_a', 1),
             ('b_r_k', 1), ('b_gn_g', 1), ('b_gn_b', 1)]
VOFF = {}
_o = 0
for _n, _c in VEC_ORDER:
    VOFF[_n] = _o
    _o += _c
NVEC = _o


def pack_vecs(inp):
    rows = [np.asarray(inp[n], np.float32).reshape(-1) for n, _ in VEC_ORDER]
    v = np.concatenate(rows)
    assert v.size == NVEC * D
    return np.ascontiguousarray(v.reshape(NVEC * 8, 128))


def _mat_pieces(W, MW):
    K, N = W.shape
    KC = K // 128
    NPc = N // MW
    a = W.reshape(KC, 128, NPc, MW).transpose(2, 1, 0, 3).reshape(NPc, 128, KC * MW)
    if KC * MW < PW:
        a = np.concatenate([a, np.zeros((NPc, 128, PW - KC * MW), np.float32)], axis=2)
    return a


PIECES = {}
PGAIN = []


def _layout():
    PIECES.clear()
    PGAIN.clear()

    def add(name, cnt, gain, KC, MW):
        PIECES[name] = (len(PGAIN), cnt)
        for _ in range(cnt):
            PGAIN.append((None if gain is None else [(0, MW, ('VT', gain))], KC, MW))

    g = VOFF['ln_gains']
    add('w_in', 4, g + 0, 8, 512)
    add('gates', 1, None, 16, 256)
    add('a_w_out', 2, None, 8, 512)
    for l in range(2):
        add(f'wq{l}', 2, g + 6 * l + 2, 8, 512)
        add(f'wkv{l}', 4, VOFF['mem_norm'], 8, 512)
        add(f'wo{l}', 2, None, 8, 512)
        add(f'up{l}', 8, g + 6 * l + 4, 8, 512)
        add(f'down{l}', 8, None, 32, 128)
    for var in range(2):
        PIECES['rkv' + 'AB'[var]] = (len(PGAIN), 6)
        for mix in (0, 0, 2, 2, 3, 3):
            PGAIN.append(([(0, 512, ('DV', 2 * mix + var))], 8, 512))
    for var in range(2):
        PIECES['loraA' + 'ab'[var]] = (len(PGAIN), 1)
        PGAIN.append(([(0, 64, ('DV', 2 * 1 + var)), (64, 128, ('DV', 2 * 4 + var)), (128, 256, ('DV', 2 * 5 + var))], 8, 256))
    add('loraB', 1, None, 3, 1024)
    add('b_w_o', 2, None, 8, 512)


_layout()
NPIECE = len(PGAIN)


def pack_weights(inp):
    f = lambda k: np.asarray(inp[k], np.float32)
    out = np.zeros((NPIECE, 128, PW), np.float32)

    def put(name, arr):
        i0, cnt = PIECES[name]
        assert arr.shape[0] == cnt, (name, arr.shape)
        out[i0:i0 + cnt] = arr

    put('w_in', _mat_pieces(f('a_w_in')[0], 512))
    gw = f('a_gate_w')[0].reshape(8, 2, 128, 256)
    put('gates', gw.transpose(2, 0, 1, 3).reshape(1, 128, 8 * 2 * 256))
    put('a_w_out', _mat_pieces(f('a_w_out')[0], 512))
    for l in range(2):
        put(f'wq{l}', _mat_pieces(f('c_w_q')[l], 512))
        put(f'wkv{l}', _mat_pieces(f('c_w_kv')[l], 512))
        put(f'wo{l}', _mat_pieces(f('c_w_o')[l], 512))
        put(f'up{l}', _mat_pieces(f('m_w_up')[l], 512))
        put(f'down{l}', _mat_pieces(f('m_w_down')[l], 128))
    rkv = f('b_w_rkv')[0]
    rk6 = np.concatenate([_mat_pieces(rkv[i], 512) for i in range(3)], axis=0)
    put('rkvA', rk6)
    put('rkvB', rk6)
    la = np.concatenate([f('b_w1')[0], f('b_a1')[0], f('b_g1')[0]], axis=1)
    put('loraAa', _mat_pieces(la, 256))
    put('loraAb', _mat_pieces(la, 256))
    lb = np.zeros((128, 3, 1024), np.float32)
    lb[:64, 0] = f('b_w2')[0]
    lb[64:, 1] = f('b_a2')[0]
    lb[:, 2] = f('b_g2')[0]
    put('loraB', np.concatenate([lb.reshape(1, 128, 3072), np.zeros((1, 128, PW - 3072), np.float32)], axis=2))
    put('b_w_o', _mat_pieces(f('b_w_o')[0], 512))
    return out


class _Rec:
    def __init__(self):
        self.name = None

    def __getattr__(self, name):
        def f(*args, **kwargs):
            self.name, self.args, self.kwargs = name, args, kwargs
            return self
        return f


class Sched:
    ENG = ('pe', 'act', 'dve', 'pool', 'sp')

    def __init__(self, nc, es):
        self.nc = nc
        self.es = es
        self.ops = {e: [] for e in self.ENG}
        self.sem = {e: es.enter_context(nc.semaphore('s_' + e)) for e in self.ENG}
        self.cnt = {e: 0 for e in self.ENG}
        self.known = {e: {} for e in self.ENG}
        self.last_w = {}
        self.reads = {}
        self.dsem = {}
        self.dcnt = {}
        self.nops = 0

    def _deps(self, eng, reads, writes):
        acc = {}

        def need(dep):
            s, v = dep
            if acc.get(s, 0) < v:
                acc[s] = v
        for b in reads:
            w = self.last_w.get(b)
            if w:
                need(w)
        for b in writes:
            w = self.last_w.get(b)
            if w:
                need(w)
            for r in self.reads.get(b, {}).items():
                need(r)
        for s, v in acc.items():
            if self.known[eng].get(s, 0) >= v:
                continue
            if s == eng and eng in ('pe', 'sp'):
                continue
            if s in self.cnt:
                assert v <= self.cnt[s], f"wait on unsignaled {s} {v} > {self.cnt[s]}"
                semh = self.sem[s]
            else:
                semh = self.dsem[s]
            self.known[eng][s] = v
            self.ops[eng].append(lambda e, semh=semh, v=v: e.wait_ge(semh, v))

    def _record(self, reads, writes, tag):
        for b in reads:
            d = self.reads.setdefault(b, {})
            if d.get(tag[0], 0) < tag[1]:
                d[tag[0]] = tag[1]
        for b in writes:
            self.last_w[b] = tag
            self.reads[b] = {}

    def op(self, eng, fn, reads=(), writes=(), signal=True):
        self.nops += 1
        self._deps(eng, reads, writes)
        val = self.cnt[eng] + 1
        rec = _Rec()
        fn(rec)
        assert rec.name is not None
        if signal:
            self.cnt[eng] += 1
            semh = self.sem[eng]
            self.ops[eng].append(lambda e, r=rec, semh=semh: getattr(e, r.name)(*r.args, **r.kwargs).then_inc(semh, 1))
        else:
            self.ops[eng].append(lambda e, r=rec: getattr(e, r.name)(*r.args, **r.kwargs))
        self._record(reads, writes, (eng, val))

    def dma(self, eng, out, in_, reads=(), writes=(), key=None):
        self.nops += 1
        self._deps(eng, reads, writes)
        if key not in self.dsem:
            self.dsem[key] = self.es.enter_context(self.nc.semaphore('d_' + key))
            self.dcnt[key] = 0
        self.dcnt[key] += 16
        semh = self.dsem[key]
        self.ops[eng].append(lambda e, out=out, in_=in_, semh=semh: e.dma_start(out=out, in_=in_).then_inc(semh, 16))
        self._record(reads, writes, (key, self.dcnt[key]))

    def barrier(self):
        snap = dict(self.cnt)
        dsnap = dict(self.dcnt)
        for eng in self.ENG:
            for s, v in snap.items():
                if s == eng or v == 0 or self.known[eng].get(s, 0) >= v:
                    continue
                self.known[eng][s] = v
                self.ops[eng].append(lambda e, semh=self.sem[s], v=v: e.wait_ge(semh, v))
            for s, v in dsnap.items():
                if self.known[eng].get(s, 0) >= v:
                    continue
                self.known[eng][s] = v
                self.ops[eng].append(lambda e, semh=self.dsem[s], v=v: e.wait_ge(semh, v))

    def finish(self, eng, keys):
        acc = {}
        for b in keys:
            w = self.last_w.get(b)
            if w and acc.get(w[0], 0) < w[1]:
                acc[w[0]] = w[1]
        for s, v in acc.items():
            semh = self.sem[s] if s in self.cnt else self.dsem[s]
            self.ops[eng].append(lambda e, semh=semh, v=v: e.wait_ge(semh, v))

    def emit(self):
        with self.nc.Block() as block:
            @block.tensor
            def _(e):
                for f in self.ops['pe']:
                    f(e)

            @block.scalar
            def _(e):
                for f in self.ops['act']:
                    f(e)

            @block.vector
            def _(e):
                for f in self.ops['dve']:
                    f(e)

            @block.gpsimd
            def _(e):
                for f in self.ops['pool']:
                    f(e)

            @block.sync
            def _(e):
                for f in self.ops['sp']:
                    f(e)


STAGES = ['load', 'A', 'C0', 'M0', 'B', 'C1', 'M1']
BSTOP = [9]


def build(NB, S, stop='M1', use_gelu=True):
    NT = S // T
    nstage = STAGES.index(stop)
    nc = bass.Bass("TRN2", target_bir_lowering=False)
    x_d = nc.dram_tensor("x", [NB * S, D], F32, kind="ExternalInput").ap()
    mem_d = nc.dram_tensor("mem", [NB * MEM, D], F32, kind="ExternalInput").ap()
    vec_d = nc.dram_tensor("vecs", [NVEC * 8, 128], F32, kind="ExternalInput").ap()
    wts_d = nc.dram_tensor("wts", [NPIECE, 128, PW], F32, kind="ExternalInput").ap()
    msk_d = nc.dram_tensor("masks", [128, 8 * 128], F32, kind="ExternalInput").ap()
    y_d = nc.dram_tensor("y", [NB * S, D], F32, kind="ExternalOutput").ap()
    wsc = nc.dram_tensor("wsc", [NPIECE, 128, PW], BF16, kind="Internal").ap()

    with ExitStack() as es:
        S_ = Sched(nc, es)
        op = S_.op

        def sb(name, shape, dt):
            return es.enter_context(nc.sbuf_tensor(name, shape, dt))

        VT = sb("VT", [128, NVEC * 8], F32)
        ident = sb("ident", [128, 128], F32)
        identb = sb("identb", [128, 128], BF16)
        onesb = sb("onesb", [128, 128], BF16)
        CV2 = sb("CV2", [128, 16], F32)
        DV = sb("DV", [128, 13 * 8], F32)
        ps = [es.enter_context(nc.psum_tensor(f"ps{i}", [128, 512], F32)) for i in range(8)]
        bank_ctr = [0]

        reserved = set()

        def nbank():
            for _ in range(17):
                b = bank_ctr[0] % 8
                bank_ctr[0] += 1
                if b not in reserved:
                    return b
            raise RuntimeError('no free PSUM bank')

        def vcol(name, idx=0, c=0):
            j = (VOFF[name] + idx) * 8 + c
            return VT[:, j:j + 1]

        op('pool', lambda e: e.memset(ident[:], 0.0), writes=['ident'])
        op('pool', lambda e: e.affine_select(out=ident[:], in_=ident[:], pattern=[[-1, 128]], base=0,
                                             channel_multiplier=1, compare_op=ALU.not_equal, fill=1.0),
           reads=['ident'], writes=['ident'])
        op('pool', lambda e: e.tensor_copy(out=identb[:], in_=ident[:]), reads=['ident'], writes=['identb'])
        op('pool', lambda e: e.memset(onesb[:], 1.0), writes=['onesb'])

        with ExitStack() as es0:
            def sb0(name, shape, dt):
                return es0.enter_context(nc.sbuf_tensor(name, shape, dt))
            vst = [sb0(f"vst{i}", [128, 128], F32) for i in range(3)]
            nrows = NVEC * 8
            for i in range(3):
                r0 = i * 128
                r1 = min(nrows, r0 + 128)
                n = r1 - r0
                S_.dma('sp', vst[i][0:n, :], vec_d[r0:r1, :], writes=[f'vst{i}'], key=f'vst{i}')
                b = nbank()
                op('pe', lambda e, i=i, n=n, b=b: e.transpose(out=ps[b][:, 0:n], in_=vst[i][0:n, :], identity=ident[0:n, 0:n]),
                   reads=[f'vst{i}', 'ident'], writes=[f'ps{b}'])
                op('act', lambda e, r0=r0, n=n, b=b: e.activation(out=VT[:, r0:r0 + n], in_=ps[b][:, 0:n], func=AF.Copy),
                   reads=[f'ps{b}'], writes=['VT'])
            lam = VT[:, VOFF['a_lambda'] * 8:VOFF['a_lambda'] * 8 + 8]
            op('act', lambda e: e.activation(out=CV2[:, 0:8], in_=lam, func=AF.Exp, scale=-1.0), reads=['VT'], writes=['CV2'])
            op('act', lambda e: e.activation(out=CV2[:, 0:8], in_=CV2[:, 0:8], func=AF.Ln, bias=1.0), reads=['CV2'], writes=['CV2'])
            op('act', lambda e: e.activation(out=CV2[:, 0:8], in_=CV2[:, 0:8], func=AF.Copy, scale=-8.0), reads=['CV2'], writes=['CV2'])

            g6 = VT[:, (VOFF['ln_gains'] + 6) * 8:(VOFF['ln_gains'] + 6) * 8 + 8]
            for mi in range(6):
                mu_i = VT[:, (VOFF['b_mu'] + mi) * 8:(VOFF['b_mu'] + mi) * 8 + 8]
                op('dve', lambda e, mi=mi, mu_i=mu_i: e.tensor_tensor(out=DV[:, (2 * mi + 1) * 8:(2 * mi + 2) * 8], in0=mu_i, in1=g6, op=ALU.mult),
                   reads=['VT'], writes=['DV'])
                op('dve', lambda e, mi=mi: e.tensor_tensor(out=DV[:, (2 * mi) * 8:(2 * mi + 1) * 8], in0=g6, in1=DV[:, (2 * mi + 1) * 8:(2 * mi + 2) * 8], op=ALU.subtract),
                   reads=['VT', 'DV'], writes=['DV'])
            ka = VT[:, VOFF['b_k_a'] * 8:VOFF['b_k_a'] * 8 + 8]
            op('dve', lambda e: e.tensor_scalar(out=DV[:, 96:104], in0=ka, scalar1=-1.0, scalar2=1.0, op0=ALU.mult, op1=ALU.add),
               reads=['VT'], writes=['DV'])

            NST = 3
            stf = [sb0(f"stf{i}", [128, PW], F32) for i in range(NST)]
            stb = [sb0(f"stb{i}", [128, PW], BF16) for i in range(NST)]
            for pi in range(NPIECE):
                k = pi % NST
                gain, KC, MW = PGAIN[pi]
                S_.dma('sp', stf[k][:], wts_d[pi], writes=[f'stf{k}'], key=f'stf{k}')
                eng = ('dve', 'pool')[pi % 2] if gain is not None else ('act', 'dve', 'pool')[pi % 3]
                if gain is None:
                    if eng == 'act':
                        op('act', lambda e, k=k: e.activation(out=stb[k][:], in_=stf[k][:], func=AF.Copy),
                           reads=[f'stf{k}'], writes=[f'stb{k}'])
                    else:
                        op(eng, lambda e, k=k: e.tensor_copy(out=stb[k][:], in_=stf[k][:]),
                           reads=[f'stf{k}'], writes=[f'stb{k}'])
                else:
                    for kc in range(KC):
                        for (c0, c1, (tab, gi)) in gain:
                            gc = (VT if tab == 'VT' else DV)[:, gi * 8 + kc:gi * 8 + kc + 1]
                            op(eng, lambda e, k=k, kc=kc, MW=MW, gc=gc, c0=c0, c1=c1: e.tensor_scalar(
                                out=stb[k][:, kc * MW + c0:kc * MW + c1], in0=stf[k][:, kc * MW + c0:kc * MW + c1],
                                scalar1=gc, scalar2=1.0, op0=ALU.mult, op1=ALU.mult),
                               reads=[f'stf{k}', 'VT', 'DV'], writes=[f'stb{k}'])
                S_.dma('act', wsc[pi], stb[k][:], reads=[f'stb{k}'], writes=['wsc'], key=f'wsc{k}')
        S_.barrier()

        import os as _os2
        _ex = int(_os2.environ.get('EXTRA_SBUF', '0'))
        if _ex:
            DUMMY = sb('DUMMY', [128, _ex * 256], F32)
            op('pool', lambda e: e.memset(DUMMY[:, _ex * 256 - 512:], 1.0), writes=['DUMMY'])
        X = sb("X", [128, 8, T], F32)
        XN = sb("XN", [128, 8, T], BF16)
        SQ = sb("SQ", [128, 8, T], BF16)
        RS = sb("RS", [128, T], F32)
        F1 = sb("F1", [128, 8, T], F32)
        F2 = sb("F2", [128, 8, T + 4], F32)
        F3 = sb("F3", [128, 8, T], F32)
        BH = sb("BH", [128, 32, T], BF16)
        PT = [sb(f"PT{i}", [128, T], F32) for i in range(4)]
        TB = [sb(f"TB{i}", [128, T], F32) for i in range(5)]
        ring = [sb(f"ring{i}", [128, PW], BF16) for i in range(NRING)]
        xin = [sb("xin0", [128, D], F32)] * 2
        KT = [sb(f"KT{l}", [128, 8, MEM], BF16) for l in range(2)]
        VV = [sb(f"VV{l}", [128, 2, D], BF16) for l in range(2)]
        HST = sb("HST", [128, 8], F32)
        SMX = sb("SMX", [128, 72], F32)


        XNP = sb("XNP", [128, 8, T + 8], BF16)
        RW = sb("RW", [128, 6400], BF16)
        STt = sb("STt", [128, 8, 64], BF16)
        GC = sb("GC", [128, 32], F32)
        XL = sb("XL", [128, 8], BF16)
        SCR = sb("SCR", [128, 8], F32)
        IDH = sb("IDH", [128, 128], FP16)
        MSK = sb("MSK", [128, 8, 128], BF16)
        MK2 = sb("MK2", [128, 512], BF16)
        ID2 = sb("ID2", [128, 256], BF16)
        BOb = sb("BOb", [128, 128], BF16)
        BO64 = sb("BO64", [128, 128], BF16)
        S_.dma('sp', xin[0][:], msk_d, writes=['xin0'], key='xin0')
        op('dve', lambda e: e.tensor_copy(out=MSK[:].rearrange("p a t -> p (a t)"), in_=xin[0][:]), reads=['xin0'], writes=['MSK'])
        op('dve', lambda e: e.tensor_copy(out=IDH[:], in_=ident[:]), reads=['ident'], writes=['IDH'])
        op('pool', lambda e: e.memset(MK2[:], 1.0), writes=['MK2'])
        for kind in range(4):
            op('pool', lambda e, kind=kind: e.affine_select(out=MK2[:, kind * 128:(kind + 1) * 128], in_=MK2[:, kind * 128:(kind + 1) * 128],
                                                            pattern=[[1, 128]], base=0, channel_multiplier=-1,
                                                            compare_op=(ALU.is_gt if kind % 2 == 0 else ALU.is_ge), fill=0.0),
               reads=['MK2'], writes=['MK2'])
        op('pool', lambda e: e.memset(ID2[:], 0.0), writes=['ID2'])
        for hs in range(2):
            op('pool', lambda e, hs=hs: e.affine_select(out=ID2[64 * hs:64 * hs + 64, :], in_=ID2[64 * hs:64 * hs + 64, :],
                                                        pattern=[[0, 4], [-1, 64]], base=0, channel_multiplier=1,
                                                        compare_op=ALU.not_equal, fill=1.0), reads=['ID2'], writes=['ID2'])
        op('pool', lambda e: e.memset(BOb[:], 0.0), writes=['BOb'])
        op('pool', lambda e: e.memset(BO64[:], 0.0), writes=['BO64'])
        for hs in range(2):
            op('pool', lambda e, hs=hs: e.memset(BOb[64 * hs:64 * hs + 64, 64 * hs:64 * hs + 64], 1.0), reads=['BOb'], writes=['BOb'])
            op('pool', lambda e, hs=hs: e.memset(BO64[64 * hs:64 * hs + 64, 64 * hs:64 * hs + 64], 1.0 / 64.0), reads=['BO64'], writes=['BO64'])

        def bhb(i):
            return BH[:, 8 * i:8 * (i + 1), :]

        BHF = BH[:].rearrange("p a t -> p (a t)").bitcast(F32)

        def bhf(i, c):
            o = (i * 8 + c) * T
            return BHF[:, o:o + T]

        def gk(i, c):
            k = i * 8 + c
            return [f'BH{2 * k}', f'BH{2 * k + 1}']

        seq = []
        for b in range(NB):
            if nstage >= 2:
                seq += [('wkv0', i) for i in range(4)]
            if nstage >= 5:
                seq += [('wkv1', i) for i in range(4)]
            for i in range(NT):
                if nstage >= 1:
                    seq += [('w_in', j) for j in range(4)] + [('gates', 0)] + [('a_w_out', j) for j in range(2)]
                if nstage >= 2:
                    seq += [('wq0', j) for j in range(2)] + [('wo0', j) for j in range(2)]
                if nstage >= 3:
                    seq += [('up0', j) for j in range(8)] + [('down0', j) for j in range(8)]
                if nstage >= 4:
                    seq += [('loraAa', 0), ('loraAb', 0), ('loraB', 0)]
                    for pj in (2, 3, 0, 1, 4, 5):
                        seq += [('rkvA', pj), ('rkvB', pj)]
                    seq += [('b_w_o', j) for j in range(2)]
                if nstage >= 5:
                    seq += [('wq1', j) for j in range(2)] + [('wo1', j) for j in range(2)]
                if nstage >= 6:
                    seq += [('up1', j) for j in range(8)] + [('down1', j) for j in range(8)]
        wstate = {'issued': 0, 'used': 0}

        def w_issue():
            k = wstate['issued']
            if k >= len(seq):
                return
            name, j = seq[k]
            pi = PIECES[name][0] + j
            slot = k % NRING
            S_.dma('sp', ring[slot][:], wsc[pi], writes=[f'ring{slot}'], key=f'ring{slot}')
            wstate['issued'] += 1

        def w_next(name, j):
            k = wstate['used']
            assert seq[k] == (name, j), (seq[k], name, j)
            prev_live = k >= 1 and seq[k - 1][0] in ('rkvA', 'loraAa') and seq[k][0] in ('rkvB', 'loraAb')
            retired = k - 2 if prev_live else k - 1
            while wstate['issued'] < min(len(seq), retired + NRING + 1):
                w_issue()
            wstate['used'] += 1
            slot = k % NRING
            return ring[slot], f'ring{slot}'

        def ones_norm(src_keys):
            b = nbank()
            for c in range(8):
                op('pe', lambda e, c=c, b=b: e.matmul(ps[b][:, :], lhsT=onesb[:], rhs=SQ[:, c, :], start=(c == 0), stop=(c == 7)),
                   reads=['onesb'] + [f'SQ{c}'], writes=[f'ps{b}'], signal=(c == 7))
            op('act', lambda e, b=b: e.activation(out=PT[3][:], in_=ps[b][:, :], func=AF.Ln, scale=1.0 / D, bias=1e-6),
               reads=[f'ps{b}'], writes=['PT3'])
            op('act', lambda e: e.activation(out=RS[:], in_=PT[3][:], func=AF.Exp, scale=-0.5), reads=['PT3'], writes=['RS'])

        def norm_in():
            op('act', lambda e: e.activation(out=SQ[:], in_=X[:], func=AF.Square),
               reads=[f'X{c}' for c in range(8)], writes=[f'SQ{c}' for c in range(8)])
            ones_norm(None)
            for c in range(8):
                eng = 'pool' if c % 3 == 2 else 'dve'
                op(eng, lambda e, c=c: e.tensor_tensor(out=XN[:, c, :], in0=X[:, c, :], in1=RS[:], op=ALU.mult),
                   reads=[f'X{c}', 'RS'], writes=[f'XN{c}'])

        def post_norm(gidx):
            ones_norm(None)
            for c in range(8):
                op('pool', lambda e, c=c: e.tensor_tensor(out=F1[:, c, :], in0=F1[:, c, :], in1=RS[:], op=ALU.mult),
                   reads=[f'F1_{c}', 'RS'], writes=[f'F1_{c}'])
                gc = vcol('ln_gains', gidx, c)
                op('dve', lambda e, c=c, gc=gc: e.scalar_tensor_tensor(out=X[:, c, :], in0=F1[:, c, :], scalar=gc, in1=X[:, c, :],
                                                                        op0=ALU.mult, op1=ALU.add),
                   reads=[f'F1_{c}', f'X{c}', 'VT'], writes=[f'X{c}'])

        def proj(wname, npieces, src, srckeys, KC, evac, mper=4, n=T):
            for pj in range(npieces):
                rg, rkey = w_next(wname, pj)
                MW = mper * 128
                for ml in range(mper):
                    m = pj * mper + ml
                    b = nbank()
                    for kc in range(KC):
                        op('pe', lambda e, rg=rg, kc=kc, ml=ml, b=b, MW=MW: e.matmul(
                            ps[b][:, 0:n], lhsT=rg[:, kc * MW + ml * 128:kc * MW + (ml + 1) * 128], rhs=src(kc),
                            start=(kc == 0), stop=(kc == KC - 1)),
                           reads=[rkey, srckeys(kc)], writes=[f'ps{b}'], signal=(kc == KC - 1))
                    evac(m, ps[b][:, 0:n], f'ps{b}')

        def evac_branch(bias_name):
            def ev(m, p, pk):
                if bias_name is None:
                    op('act', lambda e, m=m, p=p: e.activation(out=F1[:, m, :], in_=p, func=AF.Copy),
                       reads=[pk], writes=[f'F1_{m}'])
                    op('act', lambda e, m=m, p=p: e.activation(out=SQ[:, m, :], in_=p, func=AF.Square),
                       reads=[pk], writes=[f'SQ{m}'])
                else:
                    bc = vcol(bias_name, 0, m)
                    op('act', lambda e, m=m, p=p, bc=bc: e.activation(out=F1[:, m, :], in_=p, func=AF.Identity, bias=bc),
                       reads=[pk, 'VT'], writes=[f'F1_{m}'])
                    op('act', lambda e, m=m, p=p, bc=bc: e.activation(out=SQ[:, m, :], in_=p, func=AF.Square, bias=bc),
                       reads=[pk, 'VT'], writes=[f'SQ{m}'])
            return ev

        xkeys = [f'X{c}' for c in range(8)]

        def load_tile(b, i):
            for tb in range(4):
                r0 = b * S + i * T + tb * 128
                if tb % 2 == 0:
                    srcs = [xin[0][:, 0:512], xin[0][:, 512:1024]]
                    bkeys = ['xin0', 'xin0']
                    S_.dma('sp', xin[0][:], x_d[r0:r0 + 128, :], writes=['xin0'], key='xin0')
                else:
                    srcs = [PT[0][:], PT[1][:]]
                    bkeys = ['PT0', 'PT1']
                    for h_ in range(2):
                        S_.dma('sp', PT[h_][:], x_d[r0:r0 + 128, h_ * 512:(h_ + 1) * 512], writes=[bkeys[h_]], key=f'ptio{h_}')
                for half in range(2):
                    bk = nbank()
                    for cl in range(4):
                        c = half * 4 + cl
                        op('pe', lambda e: e.transpose(out=ps[bk][:, cl * 128:(cl + 1) * 128], in_=srcs[half][:, cl * 128:(cl + 1) * 128], identity=ident[:]),
                           reads=[bkeys[half], 'ident'], writes=[f'ps{bk}'], signal=(cl == 3))
                    op('act', lambda e: e.activation(
                        out=X[:, half * 4:half * 4 + 4, tb * 128:(tb + 1) * 128],
                        in_=ps[bk][:, :].rearrange("p (c t) -> p c t", c=4), func=AF.Copy),
                       reads=[f'ps{bk}'], writes=[f'X{c}' for c in range(half * 4, half * 4 + 4)])

        def store_tile(b, i):
            for tb in range(4):
                r0 = b * S + i * T + tb * 128
                if tb % 2 == 0:
                    dsts = [xin[0][:, 0:512], xin[0][:, 512:1024]]
                    bkeys = ['xin0', 'xin0']
                else:
                    dsts = [PT[0][:], PT[1][:]]
                    bkeys = ['PT0', 'PT1']
                for half in range(2):
                    bk = nbank()
                    for cl in range(4):
                        c = half * 4 + cl
                        op('pe', lambda e: e.transpose(out=ps[bk][:, cl * 128:(cl + 1) * 128], in_=X[:, c, tb * 128:(tb + 1) * 128], identity=ident[:]),
                           reads=[f'X{c}', 'ident'], writes=[f'ps{bk}'], signal=(cl == 3))
                    op('act', lambda e: e.activation(out=dsts[half], in_=ps[bk][:, :], func=AF.Copy),
                       reads=[f'ps{bk}'], writes=[bkeys[half]])
                if tb % 2 == 0:
                    S_.dma('sp', y_d[r0:r0 + 128, :], xin[0][:], reads=['xin0'], writes=['y'], key='xin0')
                else:
                    for h_ in range(2):
                        S_.dma('sp', y_d[r0:r0 + 128, h_ * 512:(h_ + 1) * 512], PT[h_][:], reads=[bkeys[h_]], writes=['y'], key=f'ptio{h_}')

        def stage_A(b, i):
            norm_in()
            vb = VOFF['a_b_in']

            def ev_in(m, p, pk):
                if m < 8:
                    bc = VT[:, vb * 8 + m:vb * 8 + m + 1]
                    if use_gelu:
                        op('act', lambda e, m=m, p=p, bc=bc: e.activation(out=F1[:, m, :], in_=p, func=AF.Gelu_apprx_tanh, bias=bc),
                           reads=[pk, 'VT'], writes=[f'F1_{m}'])
                    else:
                        op('act', lambda e, m=m, p=p, bc=bc: e.activation(out=F1[:, m, :], in_=p, func=AF.Identity, bias=bc),
                           reads=[pk, 'VT'], writes=[f'F1_{m}'])
                        op('pool', lambda e, m=m: e.tensor_tensor(out=PT[0][:], in0=F1[:, m, :], in1=F1[:, m, :], op=ALU.mult),
                           reads=[f'F1_{m}'], writes=['PT0'])
                        op('dve', lambda e: e.tensor_scalar(out=PT[0][:], in0=PT[0][:], scalar1=0.044715, scalar2=1.0, op0=ALU.mult, op1=ALU.add),
                           reads=['PT0'], writes=['PT0'])
                        op('pool', lambda e, m=m: e.tensor_tensor(out=PT[0][:], in0=PT[0][:], in1=F1[:, m, :], op=ALU.mult),
                           reads=[f'F1_{m}', 'PT0'], writes=['PT0'])
                        op('act', lambda e: e.activation(out=PT[0][:], in_=PT[0][:], func=AF.Sigmoid, scale=1.5957691216057308),
                           reads=['PT0'], writes=['PT0'])
                        op('dve', lambda e, m=m: e.tensor_tensor(out=F1[:, m, :], in0=F1[:, m, :], in1=PT[0][:], op=ALU.mult),
                           reads=[f'F1_{m}', 'PT0'], writes=[f'F1_{m}'])
                else:
                    c = m - 8
                    bc = VT[:, vb * 8 + m:vb * 8 + m + 1]
                    op('act', lambda e, c=c, p=p, bc=bc: e.activation(out=F2[:, c, 4:T + 4], in_=p, func=AF.Identity, bias=bc),
                       reads=[pk, 'VT'], writes=[f'F2_{c}'])
                    cw = [vcol('a_conv_w', k, c) for k in range(4)]
                    cb = vcol('a_conv_b', 0, c)
                    op('dve', lambda e, c=c, cw=cw, cb=cb: e.tensor_scalar(out=F3[:, c, :], in0=F2[:, c, 1:T + 1], scalar1=cw[0], scalar2=cb,
                                                                        op0=ALU.mult, op1=ALU.add),
                       reads=[f'F2_{c}', 'VT'], writes=[f'F3_{c}'])
                    for k in range(1, 4):
                        op('dve', lambda e, c=c, k=k, cw=cw: e.scalar_tensor_tensor(out=F3[:, c, :], in0=F2[:, c, 1 + k:T + 1 + k], scalar=cw[k],
                                                                                in1=F3[:, c, :], op0=ALU.mult, op1=ALU.add),
                           reads=[f'F2_{c}', f'F3_{c}', 'VT'], writes=[f'F3_{c}'])
                    op('pool', lambda e, c=c: e.tensor_copy(out=F2[:, c, 1:4], in_=F2[:, c, T + 1:T + 4]),
                       reads=[f'F2_{c}'], writes=[f'F2_{c}'])
                    op('pool', lambda e, c=c: e.tensor_copy(out=SQ[:, c, :], in_=F3[:, c, :]),
                       reads=[f'F3_{c}'], writes=[f'SQ{c}'])

            if i == 0:
                for c in range(8):
                    op('pool', lambda e, c=c: e.memset(F2[:, c, 0:4], 0.0), writes=[f'F2_{c}'])
                op('pool', lambda e: e.memset(HST[:], 0.0), writes=['HST'])
            proj('w_in', 4, lambda kc: XN[:, kc, :], lambda kc: f'XN{kc}', 8, ev_in)
            rg, rkey = w_next('gates', 0)
            gb = VOFF['a_gate_b']
            for gi in range(2):
                for c in range(8):
                    h, j = c // 2, c % 2
                    bk = nbank()
                    for kc in range(2):
                        o = ((gi * 4 + h) * 2 + kc) * 256 + j * 128
                        op('pe', lambda e, o=o, h=h, kc=kc, bk=bk: e.matmul(ps[bk][:, :], lhsT=rg[:, o:o + 128], rhs=SQ[:, 2 * h + kc, :],
                                                                          start=(kc == 0), stop=(kc == 1)),
                           reads=[rkey, f'SQ{2*h+kc}'], writes=[f'ps{bk}'], signal=(kc == 1))
                    bc = VT[:, (gb + gi) * 8 + c:(gb + gi) * 8 + c + 1]
                    op('act', lambda e, gi=gi, c=c, bk=bk, bc=bc: e.activation(out=bhf(gi, c), in_=ps[bk][:, :], func=AF.Sigmoid, bias=bc),
                       reads=[f'ps{bk}', 'VT'], writes=gk(gi, c))
            for c in range(8):
                cc = CV2[:, c:c + 1]
                op('act', lambda e, c=c, cc=cc: e.activation(out=bhf(0, c), in_=bhf(0, c), func=AF.Exp, scale=cc),
                   reads=gk(0, c) + ['CV2'], writes=gk(0, c))
                op('pool', lambda e, c=c: e.tensor_tensor(out=F2[:, c, 4:T + 4], in0=bhf(0, c), in1=bhf(0, c), op=ALU.mult),
                   reads=gk(0, c) + [f'F2_{c}'], writes=[f'F2_{c}'])
            for c in range(8):
                op('act', lambda e, c=c: e.activation(out=F2[:, c, 4:T + 4], in_=F2[:, c, 4:T + 4], func=AF.Sqrt, scale=-1.0, bias=1.0),
                   reads=[f'F2_{c}'], writes=[f'F2_{c}'])
            for c in range(8):
                op('dve', lambda e, c=c: e.tensor_tensor(out=bhf(1, c), in0=bhf(1, c), in1=F2[:, c, 4:T + 4], op=ALU.mult),
                   reads=gk(1, c) + [f'F2_{c}'], writes=gk(1, c))
                op('pool', lambda e, c=c: e.tensor_tensor(out=bhf(1, c), in0=bhf(1, c), in1=F3[:, c, :], op=ALU.mult),
                   reads=gk(1, c) + [f'F3_{c}'], writes=gk(1, c))
                op('dve', lambda e, c=c: e.tensor_tensor_scan(out=F3[:, c, :], data0=bhf(0, c), data1=bhf(1, c), initial=HST[:, c:c + 1],
                                                             op0=ALU.mult, op1=ALU.add),
                   reads=gk(0, c) + gk(1, c) + ['HST', f'F3_{c}'], writes=[f'F3_{c}'])
                op('pool', lambda e, c=c: e.tensor_copy(out=HST[:, c:c + 1], in_=F3[:, c, T - 1:T]),
                   reads=[f'F3_{c}'], writes=['HST'])
                op('pool', lambda e, c=c: e.tensor_tensor(out=XN[:, c, :], in0=F3[:, c, :], in1=F1[:, c, :], op=ALU.mult),
                   reads=[f'F3_{c}', f'F1_{c}'], writes=[f'XN{c}'])
            proj('a_w_out', 2, lambda kc: XN[:, kc, :], lambda kc: f'XN{kc}', 8, evac_branch('a_b_out'))
            post_norm(1)

        def mem_prep(b, layers):
            MT = F1[:].rearrange("p c t -> p (c t)")[:, 0:8 * MEM].rearrange("p (c t) -> p c t", c=8)
            MN = XN[:].rearrange("p c t -> p (c t)")[:, 0:8 * MEM].rearrange("p (c t) -> p c t", c=8)
            MSQ = SQ[:].rearrange("p c t -> p (c t)")[:, 0:8 * MEM].rearrange("p (c t) -> p c t", c=8)
            f1k = [f'F1_{c}' for c in range(8)]
            xnk = [f'XN{c}' for c in range(8)]
            sqk = [f'SQ{c}' for c in range(8)]
            for tb in range(2):
                r0 = b * MEM + tb * 128
                xb = xin[0]
                S_.dma('sp', xb[:], mem_d[r0:r0 + 128, :], writes=['xin0'], key='xin0')
                for half in range(2):
                    bk = nbank()
                    for cl in range(4):
                        c = half * 4 + cl
                        op('pe', lambda e, xb=xb, c=c, cl=cl, bk=bk: e.transpose(out=ps[bk][:, cl * 128:(cl + 1) * 128],
                                                                                in_=xb[:, c * 128:(c + 1) * 128], identity=ident[:]),
                           reads=['xin0', 'ident'], writes=[f'ps{bk}'], signal=(cl == 3))
                    op('act', lambda e, half=half, tb=tb, bk=bk: e.activation(
                        out=MT[:, half * 4:half * 4 + 4, tb * 128:(tb + 1) * 128],
                        in_=ps[bk][:, :].rearrange("p (c t) -> p c t", c=4), func=AF.Copy),
                       reads=[f'ps{bk}'], writes=f1k)
            op('act', lambda e: e.activation(out=MSQ, in_=MT, func=AF.Square), reads=f1k, writes=sqk)
            bk = nbank()
            for c in range(8):
                op('pe', lambda e, c=c, bk=bk: e.matmul(ps[bk][:, 0:MEM], lhsT=onesb[:], rhs=MSQ[:, c, :], start=(c == 0), stop=(c == 7)),
                   reads=['onesb'] + sqk, writes=[f'ps{bk}'], signal=(c == 7))
            op('act', lambda e, bk=bk: e.activation(out=PT[3][:, 0:MEM], in_=ps[bk][:, 0:MEM], func=AF.Ln, scale=1.0 / D, bias=1e-6),
               reads=[f'ps{bk}'], writes=['PT3'])
            op('act', lambda e: e.activation(out=RS[:, 0:MEM], in_=PT[3][:, 0:MEM], func=AF.Exp, scale=-0.5), reads=['PT3'], writes=['RS'])
            for c in range(8):
                op('dve', lambda e, c=c: e.tensor_tensor(out=MN[:, c, :], in0=MT[:, c, :], in1=RS[:, 0:MEM], op=ALU.mult),
                   reads=f1k + ['RS'], writes=xnk)
            for l in layers:
                for pj in range(2):
                    rg, rkey = w_next(f'wkv{l}', pj)
                    for ml in range(4):
                        m = pj * 4 + ml
                        bk = nbank()
                        for kc in range(8):
                            op('pe', lambda e, rg=rg, kc=kc, ml=ml, bk=bk: e.matmul(
                                ps[bk][:, 0:MEM], lhsT=rg[:, kc * 512 + ml * 128:kc * 512 + (ml + 1) * 128], rhs=MN[:, kc, :],
                                start=(kc == 0), stop=(kc == 7)),
                               reads=[rkey] + xnk, writes=[f'ps{bk}'], signal=(kc == 7))
                        op('act', lambda e, l=l, m=m, bk=bk: e.activation(out=KT[l][:, m, :], in_=ps[bk][:, 0:MEM], func=AF.Copy),
                           reads=[f'ps{bk}'], writes=[f'KT{l}'])
                for pj in range(2):
                    rg, rkey = w_next(f'wkv{l}', 2 + pj)
                    for mc in range(2):
                        bk = nbank()
                        for kc in range(8):
                            op('pe', lambda e, rg=rg, kc=kc, mc=mc, bk=bk: e.matmul(
                                ps[bk][:, :], lhsT=MN[:, kc, mc * 128:(mc + 1) * 128], rhs=rg[:, kc * 512:(kc + 1) * 512],
                                start=(kc == 0), stop=(kc == 7)),
                               reads=[rkey] + xnk, writes=[f'ps{bk}'], signal=(kc == 7))
                        op('act', lambda e, l=l, mc=mc, pj=pj, bk=bk: e.activation(out=VV[l][:, mc, pj * 512:(pj + 1) * 512], in_=ps[bk][:, :], func=AF.Copy),
                           reads=[f'ps{bk}'], writes=[f'VV{l}'])

        def stage_C(l):
            for c in range(8):
                eng = 'pool' if c % 3 == 2 else 'dve'
                op(eng, lambda e, c=c: e.tensor_copy(out=XN[:, c, :], in_=X[:, c, :]), reads=[f'X{c}'], writes=[f'XN{c}'])
            op('act', lambda e: e.activation(out=SQ[:], in_=X[:], func=AF.Square),
               reads=[f'X{c}' for c in range(8)], writes=[f'SQ{c}' for c in range(8)])
            bR = nbank()
            for tb in range(4):
                for c in range(8):
                    op('pe', lambda e, tb=tb, c=c: e.matmul(ps[bR][:, tb:tb + 1], lhsT=SQ[:, c, tb * 128:(tb + 1) * 128], rhs=onesb[:, 0:1],
                                                         start=(c == 0), stop=(c == 7)),
                       reads=['onesb', f'SQ{c}'], writes=[f'ps{bR}'], signal=(tb == 3 and c == 7))
            op('act', lambda e: e.activation(out=SMX[:, 48:52], in_=ps[bR][:, 0:4], func=AF.Ln, scale=1.0 / D, bias=1e-6),
               reads=[f'ps{bR}'], writes=['RSt'])
            op('act', lambda e: e.activation(out=SMX[:, 48:52], in_=SMX[:, 48:52], func=AF.Exp, scale=-0.5), reads=['RSt'], writes=['RSt'])
            QT = bhb(0)
            PN = bhb(1)
            PTt = bhb(2)
            OT = bhb(3)

            def ev_q(m, p, pk):
                op('act', lambda e, m=m, p=p: e.activation(out=QT[:, m, :], in_=p, func=AF.Copy, scale=1.0 / 16.0),
                   reads=[pk], writes=[f'BH{m}'])
            proj(f'wq{l}', 2, lambda kc: XN[:, kc, :], lambda kc: f'XN{kc}', 8, ev_q)
            def sm_chain(tb):
                pn = PN[:, 2 * tb:2 * tb + 2, :].rearrange("p a t -> p (a t)")
                pex = F3[:, 2 * tb:2 * tb + 2, :].rearrange("p a t -> p (a t)")
                banks = [nbank(), nbank()]
                for h in range(4):
                    bk = banks[h // 2]
                    for dc in range(2):
                        op('pe', lambda e: e.matmul(
                            ps[bk][:, (h % 2) * 256:(h % 2 + 1) * 256], lhsT=QT[:, 2 * h + dc, tb * 128:(tb + 1) * 128],
                            rhs=KT[l][:, 2 * h + dc, :], start=(dc == 0), stop=(dc == 1)),
                           reads=[f'BH{2*h+dc}', f'KT{l}'], writes=[f'ps{bk}'], signal=(dc == 1))
                yield
                for hb in range(2):
                    bk = banks[hb]
                    op('dve', lambda e: e.tensor_reduce(
                        out=SMX[:, tb * 4 + 2 * hb:tb * 4 + 2 * hb + 2], in_=ps[bk][:, :].rearrange("p (h k) -> p h k", h=2),
                        axis=AX.X, op=ALU.max, negate=True),
                       reads=[f'ps{bk}'], writes=[f'SMXm{tb}'])
                op('dve', lambda e: e.tensor_scalar(out=SMX[:, 52 + tb * 4:56 + tb * 4], in0=SMX[:, tb * 4:tb * 4 + 4],
                                                   scalar1=SMX[:, 48 + tb:49 + tb], scalar2=None, op0=ALU.mult),
                   reads=[f'SMXm{tb}', 'RSt'], writes=[f'SMXn{tb}'])
                yield
                for h in range(4):
                    bk = banks[h // 2]
                    op('act', lambda e: e.activation(
                        out=pex[:, h * 256:(h + 1) * 256], in_=ps[bk][:, (h % 2) * 256:(h % 2 + 1) * 256], func=AF.Exp,
                        scale=SMX[:, 48 + tb:49 + tb], bias=SMX[:, 52 + tb * 4 + h:53 + tb * 4 + h],
                        accum_out=SMX[:, 16 + tb * 4 + h:16 + tb * 4 + h + 1]),
                       reads=[f'ps{bk}', f'SMXn{tb}', 'RSt'], writes=[f'F3_{2*tb}', f'F3_{2*tb+1}', f'SMXs{tb}'])
                yield
                op('dve', lambda e: e.reciprocal(out=SMX[:, 32 + tb * 4:32 + tb * 4 + 4], in_=SMX[:, 16 + tb * 4:16 + tb * 4 + 4]),
                   reads=[f'SMXs{tb}'], writes=[f'SMXr{tb}'])
                for h in range(4):
                    op('dve', lambda e: e.tensor_scalar(
                        out=pn[:, h * 256:(h + 1) * 256], in0=pex[:, h * 256:(h + 1) * 256],
                        scalar1=SMX[:, 32 + tb * 4 + h:32 + tb * 4 + h + 1], scalar2=None, op0=ALU.mult),
                       reads=[f'F3_{2*tb}', f'F3_{2*tb+1}', f'SMXr{tb}'], writes=[f'BH{8+2*tb}', f'BH{9+2*tb}'])
                yield
                bk = nbank()
                psb = ps[bk][:, :].bitcast(BF16)
                for hm in range(8):
                    op('pe', lambda e: e.transpose(out=psb[:, hm * 128:(hm + 1) * 128], in_=pn[:, hm * 128:(hm + 1) * 128], identity=identb[:]),
                       reads=[f'BH{8+2*tb}', f'BH{9+2*tb}', 'identb'], writes=[f'ps{bk}'], signal=(hm == 7))
                yield
                op('act', lambda e: e.activation(out=PTt[:, :, tb * 128:(tb + 1) * 128],
                                                 in_=psb.rearrange("p (a t) -> p a t", a=8), func=AF.Copy),
                   reads=[f'ps{bk}'], writes=[f'BH{16+a}' for a in range(8)])
                yield

            gens = [sm_chain(tb) for tb in range(4)]
            while gens:
                for g in list(gens):
                    try:
                        next(g)
                    except StopIteration:
                        gens.remove(g)
            for m in range(8):
                h = m // 2
                bk = nbank()
                for mc in range(2):
                    op('pe', lambda e, m=m, h=h, mc=mc, bk=bk: e.matmul(ps[bk][:, :], lhsT=VV[l][:, mc, m * 128:(m + 1) * 128],
                                                                      rhs=PTt[:, 2 * h + mc, :], start=(mc == 0), stop=(mc == 1)),
                       reads=[f'VV{l}', f'BH{16+2*h+mc}'], writes=[f'ps{bk}'], signal=(mc == 1))
                op('act', lambda e, m=m, bk=bk: e.activation(out=OT[:, m, :], in_=ps[bk][:, :], func=AF.Copy),
                   reads=[f'ps{bk}'], writes=[f'BH{24+m}'])
            proj(f'wo{l}', 2, lambda kc: OT[:, kc, :], lambda kc: f'BH{24+kc}', 8, evac_branch(None))
            post_norm(6 * l + 3)

        def stage_M(l):
            for c in range(8):
                eng = 'pool' if c % 3 == 2 else 'dve'
                op(eng, lambda e, c=c: e.tensor_copy(out=XN[:, c, :], in_=X[:, c, :]), reads=[f'X{c}'], writes=[f'XN{c}'])
            op('act', lambda e: e.activation(out=SQ[:], in_=X[:], func=AF.Square),
               reads=[f'X{c}' for c in range(8)], writes=[f'SQ{c}' for c in range(8)])
            b = nbank()
            for c in range(8):
                op('pe', lambda e, c=c, b=b: e.matmul(ps[b][:, :], lhsT=onesb[:], rhs=SQ[:, c, :], start=(c == 0), stop=(c == 7)),
                   reads=['onesb'] + [f'SQ{c}'], writes=[f'ps{b}'], signal=(c == 7))
            op('act', lambda e, b=b: e.activation(out=PT[3][:], in_=ps[b][:, :], func=AF.Ln, scale=1.0 / D, bias=1e-6),
               reads=[f'ps{b}'], writes=['PT3'])
            op('act', lambda e: e.activation(out=RS[:], in_=PT[3][:], func=AF.Exp, scale=-1.0), reads=['PT3'], writes=['RS'])
            cnt = [0]

            def ev_down(m, p, pk):
                op('dve', lambda e, m=m, p=p: e.tensor_tensor(out=F1[:, m, :], in0=p, in1=RS[:], op=ALU.mult),
                   reads=[pk, 'RS'], writes=[f'F1_{m}'])
                op('act', lambda e, m=m: e.activation(out=SQ[:, m, :], in_=F1[:, m, :], func=AF.Square),
                   reads=[f'F1_{m}'], writes=[f'SQ{m}'])

            def ev_up(m, p, pk):
                k = cnt[0] % 2
                cnt[0] += 1
                op('act', lambda e, p=p, k=k: e.activation(out=PT[k][:], in_=p, func=AF.Square), reads=[pk], writes=[f'PT{k}'])
                op('dve', lambda e, m=m, p=p, k=k: e.scalar_tensor_tensor(out=BH[:, m, :], in0=p, scalar=0.0, in1=PT[k][:],
                                                                          op0=ALU.is_gt, op1=ALU.mult),
                   reads=[pk, f'PT{k}'], writes=[f'BH{m}'])
            proj(f'up{l}', 8, lambda kc: XN[:, kc, :], lambda kc: f'XN{kc}', 8, ev_up)
            proj(f'down{l}', 8, lambda kc: BH[:, kc, :], lambda kc: f'BH{kc}', 32, ev_down, mper=1)
            post_norm(6 * l + 5)

        _rw = [0]

        def rw_alloc(n):
            o = _rw[0]
            _rw[0] += n
            return RW[:, o:o + n]
        TM = rw_alloc(2048)
        U4 = rw_alloc(2048)
        Pb = rw_alloc(512)
        AU = rw_alloc(512)
        Wt = rw_alloc(256)
        LWA = rw_alloc(512)
        PTb = rw_alloc(512)
        LG = PTb
        F3B = F3[:].rearrange("p c t -> p (c t)").bitcast(BF16)
        RH = F3B[:, 0:4096]
        GT = F3B[:, 4096:6144]
        HH = F3B[:, 6144:8192]
        ARf = BH[:, 0:16, :].rearrange("p a t -> p (a t)")
        f3k = [f'F3_{c}' for c in range(8)]

        def ar_kind(fc, kind):
            return ARf[:, fc * 1024:(fc + 1) * 1024].rearrange("p (c k t) -> p c k t", c=4, k=2)[:, :, kind, :]

        def stage_B(b, i):
            for c in range(8):
                if i == 0:
                    op('pool', lambda e, c=c: e.memset(XNP[:, c, 0:8], 0.0), writes=[f'XNP{c}'])
                else:
                    op('pool', lambda e, c=c: e.tensor_copy(out=XNP[:, c, 7:8], in_=XL[:, c:c + 1]), reads=['XL', f'XNP{c}'], writes=[f'XNP{c}'])
            if i == 0:
                op('pool', lambda e: e.memset(STt[:], 0.0), writes=['STt'])
            op('act', lambda e: e.activation(out=SQ[:], in_=X[:], func=AF.Square), reads=xkeys, writes=[f'SQ{c}' for c in range(8)])
            ones_norm(None)
            for c in range(8):
                eng = 'pool' if c % 3 == 2 else 'dve'
                op(eng, lambda e, c=c: e.tensor_tensor(out=XNP[:, c, 8:T + 8], in0=X[:, c, :], in1=RS[:], op=ALU.mult),
                   reads=[f'X{c}', 'RS'], writes=[f'XNP{c}'])

            def proj2(nameA, nameB, pj, evac):
                rgA, kA = w_next(nameA, pj)
                rgB, kB = w_next(nameB, pj)
                return rgA, kA, rgB, kB

            def mm16(rgA, kA, rgB, kB, col0, ncol, bk, MW):
                for v_, (rg, rk, off) in enumerate(((rgA, kA, 8), (rgB, kB, 7))):
                    for kc in range(8):
                        op('pe', lambda e, rg=rg, kc=kc, off=off, v_=v_: e.matmul(
                            ps[bk][0:ncol, :], lhsT=rg[:, kc * MW + col0:kc * MW + col0 + ncol], rhs=XNP[:, kc, off:off + T],
                            start=(v_ == 0 and kc == 0), stop=(v_ == 1 and kc == 7)),
                           reads=[rk, f'XNP{kc}'], writes=[f'ps{bk}'], signal=(v_ == 1 and kc == 7))

            if BSTOP[0] <= 0.1:
                return
            rgA, kA = w_next('loraAa', 0)
            rgB, kB = w_next('loraAb', 0)
            bk = nbank()
            mm16(rgA, kA, rgB, kB, 0, 128, bk, 256)
            op('act', lambda e, bk=bk: e.activation(out=LWA[0:64, :], in_=ps[bk][0:64, :], func=AF.Tanh), reads=[f'ps{bk}'], writes=['LWA0'])
            op('act', lambda e, bk=bk: e.activation(out=LWA[64:128, :], in_=ps[bk][64:128, :], func=AF.Copy), reads=[f'ps{bk}'], writes=['LWA1'])
            bk = nbank()
            mm16(rgA, kA, rgB, kB, 128, 128, bk, 256)
            op('act', lambda e, bk=bk: e.activation(out=LG[:, :], in_=ps[bk][:, :], func=AF.Sigmoid), reads=[f'ps{bk}'], writes=['PTb'])
            if BSTOP[0] <= 0.3:
                return
            rgL, kL = w_next('loraB', 0)
            def lo_chain(m):
                tt, kt_ = (PT[0], ['PT0']) if m % 2 == 0 else (PT[2], ['PT2'])
                bw, ba, bg = nbank(), nbank(), nbank()
                op('pe', lambda e: e.matmul(ps[bw][:, :], lhsT=rgL[0:64, m * 128:(m + 1) * 128], rhs=LWA[0:64, :], start=True, stop=True),
                   reads=[kL, 'LWA0'], writes=[f'ps{bw}'])
                op('pe', lambda e: e.matmul(ps[ba][:, :], lhsT=rgL[64:128, 1024 + m * 128:1024 + (m + 1) * 128], rhs=LWA[64:128, :], start=True, stop=True),
                   reads=[kL, 'LWA1'], writes=[f'ps{ba}'])
                op('pe', lambda e: e.matmul(ps[bg][:, :], lhsT=rgL[:, 2048 + m * 128:2048 + (m + 1) * 128], rhs=LG[:, :], start=True, stop=True),
                   reads=[kL, 'PTb'], writes=[f'ps{bg}'])
                yield
                op('act', lambda e: e.activation(out=tt[:], in_=ps[bw][:, :], func=AF.Sigmoid, bias=vcol('b_w0', 0, m)),
                   reads=[f'ps{bw}', 'VT'], writes=kt_)
                yield
                for c in range(4):
                    op('dve', lambda e, c=c: e.tensor_tensor_scan(out=F1[:, m, c * 128:(c + 1) * 128], data0=onesb[:], data1=tt[:, c * 128:(c + 1) * 128],
                                                                 initial=0.0, op0=ALU.mult, op1=ALU.add),
                       reads=kt_ + ['onesb'], writes=[f'F1_{m}'])
                op('act', lambda e: e.activation(out=F2[:, m, 4:T + 4], in_=ps[ba][:, :], func=AF.Sigmoid, bias=vcol('b_a0', 0, m)),
                   reads=[f'ps{ba}', 'VT'], writes=[f'F2_{m}'])
                yield
                op('pool', lambda e: e.tensor_tensor(out=F3[:, m, :], in0=F1[:, m, :], in1=tt[:], op=ALU.subtract),
                   reads=[f'F1_{m}'] + kt_, writes=[f'F3_{m}'])
                op('act', lambda e: e.activation(out=SQ[:, m, :], in_=ps[bg][:, :], func=AF.Copy), reads=[f'ps{bg}'], writes=[f'SQ{m}'])
                yield

            for m0 in range(0, 8, 2):
                gens = [lo_chain(m0), lo_chain(m0 + 1)]
                while gens:
                    for g in list(gens):
                        try:
                            next(g)
                        except StopIteration:
                            gens.remove(g)
            if BSTOP[0] <= 0.5:
                return
            op('act', lambda e: e.activation(out=GC[:].rearrange("p (a c) -> p a c", a=8),
                                             in_=F1[:].rearrange("p a (c t) -> p a c t", c=4)[:, :, :, 127], func=AF.Exp, scale=-0.6065306597126334),
               reads=[f'F1_{c}' for c in range(8)], writes=['GC'])
            omk = lambda m: DV[:, 96 + m:97 + m]
            if BSTOP[0] <= 0.6:
                return
            TMf = RW[:, 0:2048].bitcast(F32)
            tsets = [(PT[0][:], PT[1][:], PT[2][:], PT[3][:], PTb, ['PT0'], ['PT1'], ['PT2'], ['PT3'], ['PTb']),
                     (xin[0][:, 0:512], xin[0][:, 512:1024], TMf[:, 0:512], TMf[:, 512:1024], LWA, ['xa'], ['xb'], ['tma'], ['tmb'], ['LWA0', 'LWA1'])]
            op('pool', lambda e: e.memset(SCR[:, 3:4], 0.0), reads=[], writes=['xin0', 'TM', 'xa', 'xb', 'tma', 'tmb'])
            def rr(gens):
                while gens:
                    for g in list(gens):
                        try:
                            next(g)
                        except StopIteration:
                            gens.remove(g)

            def k_chain(m, bk):
                pk = f'ps{bk}'
                t0, t1, t2, t3, tq, k0, k1, k2, k3, kq = tsets[m % 2]
                kkc = vcol('b_k_k', 0, m)
                op('act', lambda e: e.activation(out=t0, in_=ps[bk][:, :], func=AF.Copy, scale=kkc), reads=[pk, 'VT'], writes=k0)
                op('act', lambda e: e.activation(out=tq, in_=ps[bk][:, :], func=AF.Square, scale=kkc), reads=[pk, 'VT'], writes=kq)
                yield
                b2 = nbank()
                op('pe', lambda e: e.matmul(ps[b2][:, :], lhsT=BOb[:], rhs=tq, start=True, stop=True), reads=['BOb'] + kq, writes=[f'ps{b2}'])
                op('act', lambda e: e.activation(out=t2, in_=F3[:, m, :], func=AF.Exp, scale=-0.6065306597126334), reads=[f'F3_{m}'], writes=k2)
                yield
                op('act', lambda e: e.activation(out=t1, in_=ps[b2][:, :], func=AF.Ln, bias=1e-24), reads=[f'ps{b2}'], writes=k1)
                yield
                op('act', lambda e: e.activation(out=t1, in_=t1, func=AF.Exp, scale=-0.5), reads=k1, writes=k1)
                op('act', lambda e: e.activation(out=t3, in_=F1[:, m, :], func=AF.Exp, scale=0.6065306597126334), reads=[f'F1_{m}'], writes=k3)
                yield
                op('dve', lambda e: e.tensor_tensor(out=t0, in0=t0, in1=t1, op=ALU.mult), reads=k0 + k1, writes=k0)
                yield
                op('dve', lambda e: e.scalar_tensor_tensor(out=ar_kind(m, 0), in0=t0.rearrange("p (c t) -> p c t", c=4), scalar=-1.0,
                                                           in1=t2.rearrange("p (c t) -> p c t", c=4), op0=ALU.mult, op1=ALU.mult),
                   reads=k0 + k2, writes=[f'BH{2*m}', f'BH{2*m+1}'])
                op('dve', lambda e: e.tensor_scalar(out=t1, in0=F2[:, m, 4:T + 4], scalar1=vcol('b_k_a', 0, m), scalar2=omk(m), op0=ALU.mult, op1=ALU.add),
                   reads=[f'F2_{m}', 'VT', 'DV'] + k1, writes=k1)
                yield
                op('pool', lambda e: e.tensor_tensor(out=t0, in0=t0, in1=F2[:, m, 4:T + 4], op=ALU.mult), reads=k0 + [f'F2_{m}'], writes=k0)
                yield
                op('dve', lambda e: e.tensor_tensor(out=BH[:, 16 + m, :], in0=t0, in1=t3, op=ALU.mult), reads=k0 + k3, writes=[f'BH{16+m}'])
                op('dve', lambda e: e.tensor_tensor(out=F2[:, m, 4:T + 4], in0=ps[bk][:, :], in1=t1, op=ALU.mult),
                   reads=[pk, f'F2_{m}'] + k1, writes=[f'F2_{m}'])
                yield
                op('pool', lambda e: e.tensor_tensor(out=BH[:, 24 + m, :], in0=F2[:, m, 4:T + 4], in1=t3, op=ALU.mult),
                   reads=[f'F2_{m}'] + k3, writes=[f'BH{24+m}'])
                yield

            def r_chain(m, bk):
                pk = f'ps{bk}'
                t0, t1, t2, t3, tq, k0, k1, k2, k3, kq = tsets[m % 2]
                op('act', lambda e: e.activation(out=t2, in_=F1[:, m, :], func=AF.Exp, scale=-0.6065306597126334), reads=[f'F1_{m}'], writes=k2)
                yield
                op('dve', lambda e: e.tensor_tensor(out=ar_kind(m, 1), in0=ps[bk][:, :].rearrange("p (c t) -> p c t", c=4),
                                                    in1=t2.rearrange("p (c t) -> p c t", c=4), op=ALU.mult),
                   reads=[pk] + k2, writes=[f'BH{2*m}', f'BH{2*m+1}'])
                op('dve', lambda e: e.scalar_tensor_tensor(out=tq, in0=ps[bk][:, :], scalar=vcol('b_r_k', 0, m), in1=F2[:, m, 4:T + 4],
                                                           op0=ALU.mult, op1=ALU.mult),
                   reads=[pk, 'VT', f'F2_{m}'] + kq, writes=kq)
                yield
                b2 = nbank()
                op('pe', lambda e: e.matmul(ps[b2][:, :], lhsT=BOb[:], rhs=tq, start=True, stop=True), reads=['BOb'] + kq, writes=[f'ps{b2}'])
                yield
                op('act', lambda e: e.activation(out=F2[:, m, 4:T + 4], in_=ps[b2][:, :], func=AF.Copy), reads=[f'ps{b2}'], writes=[f'F2_{m}'])
                yield

            for pj in (2, 3):
                rgA, kA = w_next('rkvA', pj)
                rgB, kB = w_next('rkvB', pj)
                for pr in range(2):
                    gens = []
                    for ml in (2 * pr, 2 * pr + 1):
                        m = (pj - 2) * 4 + ml
                        bk = nbank()
                        mm16(rgA, kA, rgB, kB, ml * 128, 128, bk, 512)
                        gens.append(k_chain(m, bk))
                    rr(gens)
            if BSTOP[0] <= 0.7:
                return
            for pj in (0, 1):
                rgA, kA = w_next('rkvA', pj)
                rgB, kB = w_next('rkvB', pj)
                for pr in range(2):
                    gens = []
                    for ml in (2 * pr, 2 * pr + 1):
                        m = pj * 4 + ml
                        bk = nbank()
                        mm16(rgA, kA, rgB, kB, ml * 128, 128, bk, 512)
                        gens.append(r_chain(m, bk))
                    rr(gens)
            op('pool', lambda e: e.memset(SCR[:, 4:5], 0.0), reads=[], writes=['xin0', 'TM', 'xa', 'xb', 'tma', 'tmb'])
            if BSTOP[0] <= 0.8:
                return
            for pj in (4, 5):
                rgA, kA = w_next('rkvA', pj)
                rgB, kB = w_next('rkvB', pj)
                for ml in range(4):
                    m = (pj - 4) * 4 + ml
                    bk = nbank()
                    mm16(rgA, kA, rgB, kB, ml * 128, 128, bk, 512)
                    pk = f'ps{bk}'
                    import os as _os
                    _pm = int(_os.environ.get('P5MODE', '0'))
                    if _pm in (0, 1):
                        op('dve', lambda e, m=m, bk=bk: e.tensor_copy(out=XN[:, m, :], in_=ps[bk][:, :]), reads=[pk], writes=[f'XN{m}'])
                    if _pm in (0, 2):
                        op('dve', lambda e, m=m, bk=bk: e.tensor_tensor(out=F2[:, m, 4:T + 4], in0=ps[bk][:, :], in1=F2[:, m, 4:T + 4], op=ALU.mult),
                           reads=[pk, f'F2_{m}'], writes=[f'F2_{m}'])
            if BSTOP[0] <= 1:
                return
            TM4 = TM.rearrange("p (c k f) -> p c k f", c=4, k=4)
            op('pool', lambda e: e.tensor_copy(out=XL[:].rearrange("p (c o) -> p c o", o=1), in_=XNP[:, :, T + 7:T + 8]), reads=[f'XNP{c}' for c in range(8)], writes=['XL'])
            XNPf = XNP[:].rearrange("p c t -> p (c t)")
            sets = []
            for si in range(2):
                TBh = [TB[j][:].bitcast(FP16) for j in range(5)]
                if si == 0:
                    d = dict(Q=[TBh[0], TBh[1]], kQ=['TB0', 'TB1'], B5=TBh[4][:, 0:512], kB5='TB4s0',
                             U4=U4, Pb=Pb, AU=AU, Wt=Wt, kS=['U4', 'Pb', 'AU', 'Wt'])
                else:
                    d = dict(Q=[TBh[2], TBh[3]], kQ=['TB2', 'TB3'], B5=TBh[4][:, 512:1024], kB5='TB4s1',
                             U4=XNPf[:, 0:2048], Pb=XNPf[:, 2048:2560], AU=XNPf[:, 2560:3072], Wt=XNPf[:, 3072:3328], kS=['U4b', 'Pbb', 'AUb', 'Wtb'])
                sets.append(d)
            hk1 = [k + f'h{hf}' for k in sets[1]['kS'][1:] for hf in range(2)]
            op('pool', lambda e: e.memset(SCR[:, 0:1], 0.0), reads=[], writes=[f'XNP{c}' for c in range(8)] + sets[1]['kS'] + hk1)

            def head_prologue(fc, hs, st):
                hsl = slice(64 * hs, 64 * hs + 64)
                U4_ = st['U4']
                kU = [st['kS'][0]]
                At = lambda c: ARf[hsl, fc * 1024 + c * 256:fc * 1024 + c * 256 + 128]
                ARc = lambda c: ARf[hsl, fc * 1024 + c * 256:fc * 1024 + (c + 1) * 256]
                Bt = lambda c: BH[hsl, 16 + fc, c * 128:(c + 1) * 128]
                Kt = lambda c: BH[hsl, 24 + fc, c * 128:(c + 1) * 128]
                kAR = [f'BH{2*fc}', f'BH{2*fc+1}']
                bA = nbank(); reserved.add(bA)
                for c in range(4):
                    op('pe', lambda e, c=c: e.matmul(ps[bA][:, c * 128:(c + 1) * 128], lhsT=At(c), rhs=Bt(c), start=True, stop=True),
                       reads=kAR + [f'BH{16+fc}'], writes=[f'ps{bA}'], signal=(c == 3))
                bAT = nbank(); reserved.add(bAT)
                for c in range(4):
                    op('pe', lambda e, c=c: e.matmul(ps[bAT][:, c * 128:(c + 1) * 128], lhsT=Bt(c), rhs=At(c), start=True, stop=True),
                       reads=kAR + [f'BH{16+fc}'], writes=[f'ps{bAT}'], signal=(c == 3))
                for c in range(4):
                    bB = nbank()
                    op('pe', lambda e, c=c, bB=bB: e.matmul(ps[bB][:, 0:256], lhsT=Bt(c), rhs=ARc(c), start=True, stop=True),
                       reads=kAR + [f'BH{16+fc}'], writes=[f'ps{bB}'], signal=False)
                    op('pe', lambda e, c=c, bB=bB: e.matmul(ps[bB][:, 256:512], lhsT=Kt(c), rhs=ARc(c), start=True, stop=True),
                       reads=kAR + [f'BH{24+fc}'], writes=[f'ps{bB}'])
                    op('dve', lambda e, c=c, bB=bB: e.tensor_tensor(out=U4_[:, c * 512:(c + 1) * 512], in0=ps[bB][:, :], in1=MK2[:], op=ALU.mult),
                       reads=[f'ps{bB}', 'MK2'], writes=kU)
                return bA, bAT

            def half_steps(fc, hs, st, hf, bA, bAT, done):
                hsl = slice(64 * hs, 64 * hs + 64)
                fsl = slice(64 * hs, 64 * hs + 64)
                cs_ = slice(hf * 256, (hf + 1) * 256)
                QQ = st['Q'][hf]
                XX, TP = QQ[:, 0:512], QQ[:, 512:1024]
                B1, B2 = XX[:, 0:256], XX[:, 256:512]
                B3, B4 = TP[:, 0:256], TP[:, 256:512]
                B5 = st['B5'][:, cs_]
                kx = st['kQ'][hf]
                k1, k2, k3, k4, k5 = [kx + 'a'], [kx + 'b'], [kx + 'c'], [kx + 'd'], [st['kB5'] + f'h{hf}']
                dt_ = FP16
                idm = None
                U44 = st['U4'].rearrange("p (u k t) -> p u k t", u=4, k=4)
                Pb_ = st['Pb'][:, hf * 256:(hf + 1) * 256]
                AU_ = st['AU'][:, hf * 256:(hf + 1) * 256]
                Wt_ = st['Wt'][:, hf * 128:(hf + 1) * 128]
                kU = [st['kS'][0]]
                kP, kAU, kW = [[k + f'h{hf}'] for k in st['kS'][1:]]
                kAR = [f'BH{2*fc}', f'BH{2*fc+1}']
                v2 = lambda t: t.rearrange("p (u t) -> p u t", u=2)
                mskb = lambda j: MSK[:, j:j + 1, :].to_broadcast([128, 2, 128])
                idb = ident[:].rearrange("p (o t) -> p o t", o=1).to_broadcast([128, 2, 128])
                W_ = lambda t: t
                held = []

                def gbank():
                    bk_ = nbank()
                    reserved.add(bk_)
                    held.append(bk_)
                    return bk_

                def gfree(bk_):
                    reserved.discard(bk_)
                    held.remove(bk_)

                def mm2(bk_, col0, lhs, rhs, rk, acc=None, acck=None, last=True):
                    for u in range(2):
                        us = slice(u * 128, (u + 1) * 128)
                        os_ = slice(col0 + u * 128, col0 + (u + 1) * 128)
                        if acc is not None:
                            op('pe', lambda e: e.matmul(ps[bk_][:, os_], lhsT=W_(idm), rhs=W_(acc[:, us]), start=True, stop=False),
                               reads=acck + ['ident'], writes=[f'ps{bk_}'], signal=False)
                        op('pe', lambda e: e.matmul(ps[bk_][:, os_], lhsT=W_(lhs[:, us]), rhs=W_(rhs[:, us]), start=(acc is None), stop=True),
                           reads=rk, writes=[f'ps{bk_}'], signal=(last and u == 1))

                def cp(eng, dst, kd, bk_, col0, n):
                    if eng == 'act':
                        op('act', lambda e: e.activation(out=W_(dst), in_=ps[bk_][:, col0:col0 + n], func=AF.Copy), reads=[f'ps{bk_}'], writes=kd)
                    else:
                        op('dve', lambda e: e.tensor_copy(out=W_(dst), in_=ps[bk_][:, col0:col0 + n]), reads=[f'ps{bk_}'], writes=kd)

                op('dve', lambda e: e.tensor_tensor(out=v2(W_(B1)), in0=v2(ps[bA][:, cs_]), in1=mskb(0), op=ALU.mult), reads=[f'ps{bA}', 'MSK'], writes=k1)
                op('dve', lambda e: e.tensor_tensor(out=v2(W_(B2)), in0=v2(ps[bAT][:, cs_]), in1=mskb(1), op=ALU.mult), reads=[f'ps{bAT}', 'MSK'], writes=k2)
                e3 = 'pool'
                op(e3, lambda e: e.tensor_tensor(out=W_(TP[:, :]).rearrange("p (u t) -> p u t", u=4), in0=XX[:, :].rearrange("p (u t) -> p u t", u=4),
                                                 in1=ident[:].rearrange("p (o t) -> p o t", o=1).to_broadcast([128, 4, 128]), op=ALU.add),
                   reads=k1 + k2 + ['ident'], writes=k3 + k4)
                yield
                def tn_from_pt():
                    bt_ = gbank()
                    for u in range(2):
                        us = slice(u * 128, (u + 1) * 128)
                        op('pe', lambda e: e.transpose(out=ps[bt_][:, :].bitcast(FP16)[:, us], in_=B4[:, us], identity=IDH[:]),
                           reads=k4 + ['IDH'], writes=[f'ps{bt_}'], signal=(u == 1))
                    return bt_

                for lev in range(3):
                    bk_ = gbank()
                    mm2(bk_, 0, B2, B1, k1 + k2, last=False)
                    mm2(bk_, 256, B1, B2, k1 + k2)
                    yield
                    cp('act', XX[:, :], k1 + k2, bk_, 0, 512)
                    gfree(bk_)
                    yield
                    bk_ = gbank()
                    mm2(bk_, 0, B1, B4, k1 + k4)
                    yield
                    op('dve', lambda e: e.tensor_tensor(out=W_(B4), in0=ps[bk_][:, 0:256], in1=B4, op=ALU.add), reads=[f'ps{bk_}'] + k4, writes=k4)
                    gfree(bk_)
                    yield
                for kl in range(1, 4):
                    op('dve', lambda e, kl=kl: e.tensor_tensor(out=v2(W_(B5)), in0=v2(ps[bA][:, cs_]), in1=mskb(2 * kl), op=ALU.mult), reads=[f'ps{bA}', 'MSK'], writes=k5)
                    bt_ = tn_from_pt()
                    yield
                    op('act', lambda e: e.activation(out=B3, in_=ps[bt_][:, :].bitcast(FP16)[:, 0:256], func=AF.Copy), reads=[f'ps{bt_}'], writes=k3)
                    gfree(bt_)
                    bz = gbank()
                    mm2(bz, 0, B5, B4, k5 + k4)
                    yield
                    cp('act', B1, k1, bz, 0, 256)
                    gfree(bz)
                    yield
                    bz = gbank()
                    mm2(bz, 0, B3, B1, k3 + k1)
                    yield
                    if kl < 3:
                        op('dve', lambda e: e.tensor_tensor(out=W_(B4), in0=ps[bz][:, 0:256], in1=B4, op=ALU.add), reads=[f'ps{bz}'] + k4, writes=k4)
                    else:
                        op('dve', lambda e: e.tensor_tensor(out=Pb_, in0=ps[bz][:, 0:256], in1=B4, op=ALU.add), reads=[f'ps{bz}'] + k4, writes=kP)
                    gfree(bz)
                    yield
                done.append(1)
                if len(done) == 2:
                    reserved.discard(bA); reserved.discard(bAT)
                us2 = [2 * hf, 2 * hf + 1]
                bW = gbank()
                for j, u in enumerate(us2):
                    op('pe', lambda e: e.matmul(ps[bW][:, j * 64:(j + 1) * 64], lhsT=U44[:, u, 2, :], rhs=TM4[:, u, 3, fsl], start=True, stop=True),
                       reads=kU + ['TM'], writes=[f'ps{bW}'], signal=(j == 1))
                yield
                op('act', lambda e: e.activation(out=Wt_, in_=ps[bW][:, 0:128], func=AF.Copy), reads=[f'ps{bW}'], writes=kW)
                gfree(bW)
                yield
                bU = gbank()
                for j, u in enumerate(us2):
                    op('pe', lambda e: e.matmul(ps[bU][:, j * 128:j * 128 + 64], lhsT=Pb_[:, j * 128:(j + 1) * 128], rhs=TM4[:, u, 0, fsl], start=True, stop=True),
                       reads=kP + ['TM'], writes=[f'ps{bU}'], signal=False)
                    op('pe', lambda e: e.matmul(ps[bU][:, j * 128 + 64:(j + 1) * 128], lhsT=Pb_[:, j * 128:(j + 1) * 128], rhs=Wt_[:, j * 64:(j + 1) * 64], start=True, stop=True),
                       reads=kP + kW, writes=[f'ps{bU}'], signal=(j == 1))
                yield
                op('act', lambda e: e.activation(out=AU_, in_=ps[bU][:, 0:256], func=AF.Copy), reads=[f'ps{bU}'], writes=kAU)
                gfree(bU)
                yield
                bRY = gbank()
                for j, u in enumerate(us2):
                    op('pe', lambda e: e.matmul(ps[bRY][hsl, j * 128:(j + 1) * 128], lhsT=AU_[:, j * 128:j * 128 + 64], rhs=U44[:, u, 1, :], start=True, stop=True),
                       reads=kAU + kU, writes=[f'ps{bRY}'], signal=False)
                for j, u in enumerate(us2):
                    op('pe', lambda e: e.matmul(ps[bRY][hsl, 256 + j * 128:256 + (j + 1) * 128], lhsT=AU_[:, j * 128 + 64:(j + 1) * 128], rhs=U44[:, u, 1, :], start=True, stop=False),
                       reads=kAU + kU, writes=[f'ps{bRY}'], signal=False)
                    op('pe', lambda e: e.matmul(ps[bRY][hsl, 256 + j * 128:256 + (j + 1) * 128], lhsT=TM4[:, u, 3, fsl], rhs=U44[:, u, 3, :], start=False, stop=True),
                       reads=['TM'] + kU, writes=[f'ps{bRY}'], signal=(j == 1))
                yield
                tsl = slice(fc * 512 + hf * 256, fc * 512 + (hf + 1) * 256)
                op('dve', lambda e: e.tensor_tensor(out=RH[hsl, tsl].rearrange("p (c t) -> p c t", c=2),
                                                    in0=ps[bRY][hsl, 0:256].rearrange("p (c t) -> p c t", c=2),
                                                    in1=ARf[hsl, fc * 1024 + hf * 512:fc * 1024 + (hf + 1) * 512].rearrange("p (c k t) -> p c k t", c=2, k=2)[:, :, 1, :], op=ALU.add),
                   reads=[f'ps{bRY}'] + kAR, writes=[f'RH{fc}_{hs}_{hf}'])
                op('dve', lambda e: e.tensor_copy(out=F1[hsl, fc, hf * 256:(hf + 1) * 256], in_=ps[bRY][hsl, 256:512]), reads=[f'ps{bRY}'], writes=[f'F1_{fc}'])
                gfree(bRY)
                yield
                bGH = gbank()
                for j, u in enumerate(us2):
                    op('pe', lambda e: e.matmul(ps[bGH][hsl, j * 64:(j + 1) * 64], lhsT=AU_[:, j * 128:j * 128 + 64], rhs=TM4[:, u, 1, fsl], start=True, stop=True),
                       reads=kAU + ['TM'], writes=[f'ps{bGH}'], signal=False)
                for j, u in enumerate(us2):
                    op('pe', lambda e: e.matmul(ps[bGH][hsl, 128 + j * 64:128 + (j + 1) * 64], lhsT=TM4[:, u, 1, fsl], rhs=AU_[:, j * 128 + 64:(j + 1) * 128], start=True, stop=False),
                       reads=kAU + ['TM'], writes=[f'ps{bGH}'], signal=False)
                    op('pe', lambda e: e.matmul(ps[bGH][hsl, 128 + j * 64:128 + (j + 1) * 64], lhsT=TM4[:, u, 2, fsl], rhs=TM4[:, u, 3, fsl], start=False, stop=True),
                       reads=['TM'], writes=[f'ps{bGH}'], signal=(j == 1))
                yield
                gsl = slice(fc * 256 + hf * 128, fc * 256 + (hf + 1) * 128)
                op('dve', lambda e: e.tensor_tensor(out=GT[hsl, gsl], in0=ps[bGH][hsl, 0:128], in1=ID2[hsl, 0:128], op=ALU.add),
                   reads=[f'ps{bGH}', 'ID2'], writes=[f'GT{fc}_{hs}_{hf}'])
                op('dve', lambda e: e.tensor_tensor(out=HH[hsl, fc * 256 + hf * 128:fc * 256 + (hf + 1) * 128].rearrange("p (u i) -> p u i", u=2),
                                                    in0=ps[bGH][hsl, 128:256].rearrange("p (u i) -> p u i", u=2),
                                                    in1=GC[hsl, fc * 4 + 2 * hf:fc * 4 + 2 * hf + 2].rearrange("p (u o) -> p u o", o=1).to_broadcast([64, 2, 64]), op=ALU.mult),
                   reads=[f'ps{bGH}', 'GC'], writes=[f'HH{fc}_{hs}_{hf}'])
                gfree(bGH)
                yield

            fine = [f'{n}{fc}_{hs}_{hf}' for n in ('RH', 'GT', 'HH') for fc in range(8) for hs in range(2) for hf in range(2)]
            op('pool', lambda e: e.memset(SCR[:, 1:2], 0.0), reads=[], writes=f3k + fine)
            for fc in range(8):
                srcs = [lambda c, fc=fc: ARf[:, fc * 1024 + c * 256:fc * 1024 + c * 256 + 128],
                        lambda c, fc=fc: BH[:, 16 + fc, c * 128:(c + 1) * 128],
                        lambda c, fc=fc: BH[:, 24 + fc, c * 128:(c + 1) * 128],
                        lambda c, fc=fc: XN[:, fc, c * 128:(c + 1) * 128]]
                skeys = [[f'BH{2*fc}', f'BH{2*fc+1}'], [f'BH{16+fc}'], [f'BH{24+fc}'], [f'XN{fc}']]
                for half in range(2):
                    bk = nbank()
                    psb = ps[bk][:, :].bitcast(BF16)
                    for cl in range(2):
                        c = half * 2 + cl
                        for kind in range(4):
                            o = (cl * 4 + kind) * 128
                            op('pe', lambda e, c=c, kind=kind, o=o, psb=psb, srcs=srcs: e.transpose(out=psb[:, o:o + 128], in_=srcs[kind](c), identity=identb[:]),
                               reads=skeys[kind] + ['identb'], writes=[f'ps{bk}'], signal=(cl == 1 and kind == 3))
                    op('act', lambda e, half=half, psb=psb: e.activation(out=TM[:, half * 1024:(half + 1) * 1024], in_=psb, func=AF.Copy),
                       reads=[f'ps{bk}'], writes=['TM'])
                gens = []
                for hs in range(2):
                    bA_, bAT_ = head_prologue(fc, hs, sets[hs])
                    done = []
                    for hf in range(2):
                        gens.append(half_steps(fc, hs, sets[hs], hf, bA_, bAT_, done))
                if _os0.environ.get('SEQG', '0') == '1':
                    for g in gens:
                        for _ in g:
                            pass
                    gens = []
                _hs = int(_os0.environ.get('HSTOP', '999'))
                _rounds = 0
                while gens:
                    if _rounds >= _hs:
                        reserved.clear()
                        break
                    _rounds += 1
                    for g in list(gens):
                        try:
                            next(g)
                        except StopIteration:
                            gens.remove(g)
            op('pool', lambda e: e.memset(SCR[:, 2:3], 0.0), reads=[], writes=f3k + fine + [f'XNP{c}' for c in range(8)] + sets[1]['kS'] + hk1)
            if BSTOP[0] <= 2:
                return
            for c in range(4):
                bY = [nbank(), nbank()]
                bZ = nbank()
                for fc in range(8):
                    for hs in range(2):
                        hsl = slice(64 * hs, 64 * hs + 64)
                        op('pe', lambda e, fc=fc, hsl=hsl, c=c: e.matmul(ps[bY[fc // 4]][hsl, (fc % 4) * 128:(fc % 4 + 1) * 128], lhsT=STt[hsl, fc, :],
                                                                        rhs=RH[hsl, fc * 512 + c * 128:fc * 512 + (c + 1) * 128], start=True, stop=True),
                           reads=['STt'] + f3k, writes=[f'ps{bY[fc // 4]}'], signal=(fc % 4 == 3 and hs == 1))
                for fc in range(8):
                    for hs in range(2):
                        hsl = slice(64 * hs, 64 * hs + 64)
                        op('pe', lambda e, fc=fc, hsl=hsl, c=c: e.matmul(ps[bZ][hsl, fc * 64:(fc + 1) * 64], lhsT=GT[hsl, fc * 256 + c * 64:fc * 256 + (c + 1) * 64],
                                                                        rhs=STt[hsl, fc, :], start=True, stop=True),
                           reads=['STt'] + f3k, writes=[f'ps{bZ}'], signal=(fc == 7 and hs == 1))
                for half in range(2):
                    op('dve', lambda e, half=half, c=c: e.tensor_tensor(out=F1[:, half * 4:half * 4 + 4, c * 128:(c + 1) * 128],
                                                                        in0=ps[bY[half]][:, :].rearrange("p (f t) -> p f t", f=4),
                                                                        in1=F1[:, half * 4:half * 4 + 4, c * 128:(c + 1) * 128], op=ALU.add),
                       reads=[f'ps{bY[half]}'] + [f'F1_{f}' for f in range(half * 4, half * 4 + 4)], writes=[f'F1_{f}' for f in range(half * 4, half * 4 + 4)])
                for fc in range(8):
                    op('dve', lambda e, fc=fc, c=c: e.scalar_tensor_tensor(out=STt[:, fc, :], in0=ps[bZ][:, fc * 64:(fc + 1) * 64], scalar=GC[:, fc * 4 + c:fc * 4 + c + 1],
                                                                           in1=HH[:, fc * 256 + c * 64:fc * 256 + (c + 1) * 64], op0=ALU.mult, op1=ALU.add),
                       reads=[f'ps{bZ}', 'GC'] + f3k, writes=['STt'])
            if BSTOP[0] <= 3:
                return
            def gn_chain(m):
                ta, tb_, tq = (PT[0], PT[1], PTb) if m % 2 == 0 else (PT[2], PT[3], LWA)
                ka, kb, kq = (['PT0'], ['PT1'], ['PTb']) if m % 2 == 0 else (['PT2'], ['PT3'], ['LWA0', 'LWA1'])
                op('act', lambda e: e.activation(out=tq, in_=F1[:, m, :], func=AF.Copy), reads=[f'F1_{m}'], writes=kq)
                yield
                b1 = nbank()
                op('pe', lambda e: e.matmul(ps[b1][:, :], lhsT=BO64[:], rhs=tq, start=True, stop=True), reads=['BO64'] + kq, writes=[f'ps{b1}'])
                yield
                op('dve', lambda e: e.tensor_tensor(out=ta[:], in0=F1[:, m, :], in1=ps[b1][:, :], op=ALU.subtract), reads=[f'F1_{m}', f'ps{b1}'], writes=ka)
                yield
                op('act', lambda e: e.activation(out=tq, in_=ta[:], func=AF.Square), reads=ka + kq, writes=kq)
                yield
                b2 = nbank()
                op('pe', lambda e: e.matmul(ps[b2][:, :], lhsT=BO64[:], rhs=tq, start=True, stop=True), reads=['BO64'] + kq, writes=[f'ps{b2}'])
                yield
                op('act', lambda e: e.activation(out=tb_[:], in_=ps[b2][:, :], func=AF.Ln, bias=64e-5), reads=[f'ps{b2}'], writes=kb)
                yield
                op('act', lambda e: e.activation(out=tb_[:], in_=tb_[:], func=AF.Exp, scale=-0.5), reads=kb, writes=kb)
                yield
                op('dve', lambda e: e.scalar_tensor_tensor(out=ta[:], in0=ta[:], scalar=vcol('b_gn_g', 0, m), in1=tb_[:], op0=ALU.mult, op1=ALU.mult),
                   reads=ka + kb + ['VT'], writes=ka)
                yield
                op('dve', lambda e: e.scalar_tensor_tensor(out=ta[:], in0=ta[:], scalar=vcol('b_gn_b', 0, m), in1=F2[:, m, 4:T + 4], op0=ALU.add, op1=ALU.add),
                   reads=ka + ['VT', f'F2_{m}'], writes=ka)
                yield
                op('dve', lambda e: e.tensor_tensor(out=BH[:, m, :], in0=ta[:], in1=SQ[:, m, :], op=ALU.mult), reads=ka + [f'SQ{m}'], writes=[f'BH{m}'])
                yield

            for m0 in range(0, 8, 2):
                gens = [gn_chain(m0), gn_chain(m0 + 1)]
                while gens:
                    for g in list(gens):
                        try:
                            next(g)
                        except StopIteration:
                            gens.remove(g)
            proj('b_w_o', 2, lambda kc: BH[:, kc, :], lambda kc: f'BH{kc}', 8, evac_branch(None))
            post_norm(7)

        for b in range(NB):
            if nstage >= 2:
                mem_prep(b, [0, 1] if nstage >= 5 else [0])
            for i in range(NT):
                load_tile(b, i)
                if nstage >= 1:
                    stage_A(b, i)
                if nstage >= 2:
                    stage_C(0)
                if nstage >= 3:
                    stage_M(0)
                if nstage >= 4:
                    stage_B(b, i)
                if nstage >= 5:
                    stage_C(1)
                if nstage >= 6:
                    stage_M(1)
                store_tile(b, i)
        S_.finish('sp', ['y'])
        for k in ('xin0', 'ptio0', 'ptio1'):
            if k in S_.dsem:
                S_.ops['sp'].append(lambda e, semh=S_.dsem[k], v=S_.dcnt[k]: e.wait_ge(semh, v))
        S_.emit()
        build.nops = S_.nops
    return nc


def make_masks():
    t = np.arange(128)[:, None]
    s_ = np.arange(128)[None, :]
    low = t > s_
    ms = []
    m0 = low & (t // 16 == s_ // 16)
    ms += [m0, m0.T]
    for blk in (16, 32, 64):
        mk = (t // (2 * blk) == s_ // (2 * blk)) & ((t // blk) % 2 == 1) & ((s_ // blk) % 2 == 0)
        ms += [mk, mk.T]
    return np.ascontiguousarray(np.stack(ms, axis=1).astype(np.float32).reshape(128, 8 * 128))


def _prep_inputs(inp):
    vecs = pack_vecs(inp)
    wts = pack_weights(inp)
    return vecs, wts


def kernel(**inputs):
    NB, S = 4, 2048
    x = np.asarray(inputs['x'], np.float32)
    mem = np.asarray(inputs['mem'], np.float32)
    vecs, wts = _prep_inputs(inputs)
    nc = build(NB, S)
    in_maps = []
    for c in range(8):
        in_maps.append({"x": np.ascontiguousarray(x[c * NB:(c + 1) * NB].reshape(NB * S, D)),
                        "mem": np.ascontiguousarray(mem[c * NB:(c + 1) * NB].reshape(NB * MEM, D)),
                        "vecs": vecs, "wts": wts, "masks": make_masks()})
    res = run_bass_kernel_spmd(nc, in_maps, core_ids=list(range(8)))
    out = np.concatenate([r["y"].reshape(NB, S, D) for r in res.results], axis=0)
    return out.astype(np.float32)
```

```python
import numpy as np
from contextlib import ExitStack
import concourse.bass as bass
import concourse.mybir as mybir
from concourse.bass_utils import run_bass_kernel_spmd

F32 = mybir.dt.float32
BF16 = mybir.dt.bfloat16
FP16 = mybir.dt.float16
AF = mybir.ActivationFunctionType
ALU = mybir.AluOpType
AX = mybir.AxisListType
import os as _os0
TDT = mybir.dt.float32r if _os0.environ.get('TDT', 'r') == 'r' else mybir.dt.float32

D = 1024
T = 512
MEM = 256
PW = 4096
NRING = 3

VEC_ORDER = [('ln_gains', 12), ('mem_norm', 1), ('a_conv_w', 4), ('a_conv_b', 1), ('a_b_in', 2), ('a_gate_b', 2),
             ('a_lambda', 1), ('a_b_out', 1), ('b_mu', 6), ('b_w0', 1), ('b_a0', 1), ('b_k_k', 1), ('b_k_a', 1),
             ('b_r_k', 1), ('b_gn_g', 1), ('b_gn_b', 1)]
VOFF = {}
_o = 0
for _n, _c in VEC_ORDER:
    VOFF[_n] = _o
    _o += _c
NVEC = _o


def pack_vecs(inp):
    rows = [np.asarray(inp[n], np.float32).reshape(-1) for n, _ in VEC_ORDER]
    v = np.concatenate(rows)
    assert v.size == NVEC * D
    return np.ascontiguousarray(v.reshape(NVEC * 8, 128))


def _mat_pieces(W, MW):
    K, N = W.shape
    KC = K // 128
    NPc = N // MW
    a = W.reshape(KC, 128, NPc, MW).transpose(2, 1, 0, 3).reshape(NPc, 128, KC * MW)
    if KC * MW < PW:
        a = np.concatenate([a, np.zeros((NPc, 128, PW - KC * MW), np.float32)], axis=2)
    return a


PIECES = {}
PGAIN = []


def _layout():
    PIECES.clear()
    PGAIN.clear()

    def add(name, cnt, gain, KC, MW):
        PIECES[name] = (len(PGAIN), cnt)
        for _ in range(cnt):
            PGAIN.append((None if gain is None else [(0, MW, ('VT', gain))], KC, MW))

    g = VOFF['ln_gains']
    add('w_in', 4, g + 0, 8, 512)
    add('gates', 1, None, 16, 256)
    add('a_w_out', 2, None, 8, 512)
    for l in range(2):
        add(f'wq{l}', 2, g + 6 * l + 2, 8, 512)
        add(f'wkv{l}', 4, VOFF['mem_norm'], 8, 512)
        add(f'wo{l}', 2, None, 8, 512)
        add(f'up{l}', 8, g + 6 * l + 4, 8, 512)
        add(f'down{l}', 8, None, 32, 128)
    for var in range(2):
        PIECES['rkv' + 'AB'[var]] = (len(PGAIN), 6)
        for mix in (0, 0, 2, 2, 3, 3):
            PGAIN.append(([(0, 512, ('DV', 2 * mix + var))], 8, 512))
    for var in range(2):
        PIECES['loraA' + 'ab'[var]] = (len(PGAIN), 1)
        PGAIN.append(([(0, 64, ('DV', 2 * 1 + var)), (64, 128, ('DV', 2 * 4 + var)), (128, 256, ('DV', 2 * 5 + var))], 8, 256))
    add('loraB', 1, None, 3, 1024)
    add('b_w_o', 2, None, 8, 512)


_layout()
NPIECE = len(PGAIN)


def pack_weights(inp):
    f = lambda k: np.asarray(inp[k], np.float32)
    out = np.zeros((NPIECE, 128, PW), np.float32)

    def put(name, arr):
        i0, cnt = PIECES[name]
        assert arr.shape[0] == cnt, (name, arr.shape)
        out[i0:i0 + cnt] = arr

    put('w_in', _mat_pieces(f('a_w_in')[0], 512))
    gw = f('a_gate_w')[0].reshape(8, 2, 128, 256)
    put('gates', gw.transpose(2, 0, 1, 3).reshape(1, 128, 8 * 2 * 256))
    put('a_w_out', _mat_pieces(f('a_w_out')[0], 512))
    for l in range(2):
        put(f'wq{l}', _mat_pieces(f('c_w_q')[l], 512))
        put(f'wkv{l}', _mat_pieces(f('c_w_kv')[l], 512))
        put(f'wo{l}', _mat_pieces(f('c_w_o')[l], 512))
        put(f'up{l}', _mat_pieces(f('m_w_up')[l], 512))
        put(f'down{l}', _mat_pieces(f('m_w_down')[l], 128))
    rkv = f('b_w_rkv')[0]
    rk6 = np.concatenate([_mat_pieces(rkv[i], 512) for i in range(3)], axis=0)
    put('rkvA', rk6)
    put('rkvB', rk6)
    la = np.concatenate([f('b_w1')[0], f('b_a1')[0], f('b_g1')[0]], axis=1)
    put('loraAa', _mat_pieces(la, 256))
    put('loraAb', _mat_pieces(la, 256))
    lb = np.zeros((128, 3, 1024), np.float32)
    lb[:64, 0] = f('b_w2')[0]
    lb[64:, 1] = f('b_a2')[0]
    lb[:, 2] = f('b_g2')[0]
    put('loraB', np.concatenate([lb.reshape(1, 128, 3072), np.zeros((1, 128, PW - 3072), np.float32)], axis=2))
    put('b_w_o', _mat_pieces(f('b_w_o')[0], 512))
    return out


class _Rec:
    def __init__(self):
        self.name = None

    def __getattr__(self, name):
        def f(*args, **kwargs):
            self.name, self.args, self.kwargs = name, args, kwargs
            return self
        return f


class Sched:
    ENG = ('pe', 'act', 'dve', 'pool', 'sp')

    def __init__(self, nc, es):
        self.nc = nc
        self.es = es
        self.ops = {e: [] for e in self.ENG}
        self.sem = {e: es.enter_context(nc.semaphore('s_' + e)) for e in self.ENG}
        self.cnt = {e: 0 for e in self.ENG}
        self.known = {e: {} for e in self.ENG}
        self.last_w = {}
        self.reads = {}
        self.dsem = {}
        self.dcnt = {}
        self.nops = 0

    def _deps(self, eng, reads, writes):
        acc = {}

        def need(dep):
            s, v = dep
            if acc.get(s, 0) < v:
                acc[s] = v
        for b in reads:
            w = self.last_w.get(b)
            if w:
                need(w)
        for b in writes:
            w = self.last_w.get(b)
            if w:
                need(w)
            for r in self.reads.get(b, {}).items():
                need(r)
        for s, v in acc.items():
            if self.known[eng].get(s, 0) >= v:
                continue
            if s == eng and eng in ('pe', 'sp'):
                continue
            if s in self.cnt:
                assert v <= self.cnt[s], f"wait on unsignaled {s} {v} > {self.cnt[s]}"
                semh = self.sem[s]
            else:
                semh = self.dsem[s]
            self.known[eng][s] = v
            self.ops[eng].append(lambda e, semh=semh, v=v: e.wait_ge(semh, v))

    def _record(self, reads, writes, tag):
        for b in reads:
            d = self.reads.setdefault(b, {})
            if d.get(tag[0], 0) < tag[1]:
                d[tag[0]] = tag[1]
        for b in writes:
            self.last_w[b] = tag
            self.reads[b] = {}

    def op(self, eng, fn, reads=(), writes=(), signal=True):
        self.nops += 1
        self._deps(eng, reads, writes)
        val = self.cnt[eng] + 1
        rec = _Rec()
        fn(rec)
        assert rec.name is not None
        if signal:
            self.cnt[eng] += 1
            semh = self.sem[eng]
            self.ops[eng].append(lambda e, r=rec, semh=semh: getattr(e, r.name)(*r.args, **r.kwargs).then_inc(semh, 1))
        else:
            self.ops[eng].append(lambda e, r=rec: getattr(e, r.name)(*r.args, **r.kwargs))
        self._record(reads, writes, (eng, val))

    def dma(self, eng, out, in_, reads=(), writes=(), key=None):
        self.nops += 1
        self._deps(eng, reads, writes)
        if key not in self.dsem:
            self.dsem[key] = self.es.enter_context(self.nc.semaphore('d_' + key))
            self.dcnt[key] = 0
        self.dcnt[key] += 16
        semh = self.dsem[key]
        self.ops[eng].append(lambda e, out=out, in_=in_, semh=semh: e.dma_start(out=out, in_=in_).then_inc(semh, 16))
        self._record(reads, writes, (key, self.dcnt[key]))

    def barrier(self):
        snap = dict(self.cnt)
        dsnap = dict(self.dcnt)
        for eng in self.ENG:
            for s, v in snap.items():
                if s == eng or v == 0 or self.known[eng].get(s, 0) >= v:
                    continue
                self.known[eng][s] = v
                self.ops[eng].append(lambda e, semh=self.sem[s], v=v: e.wait_ge(semh, v))
            for s, v in dsnap.items():
                if self.known[eng].get(s, 0) >= v:
                    continue
                self.known[eng][s] = v
                self.ops[eng].append(lambda e, semh=self.dsem[s], v=v: e.wait_ge(semh, v))

    def finish(self, eng, keys):
        acc = {}
        for b in keys:
            w = self.last_w.get(b)
            if w and acc.get(w[0], 0) < w[1]:
                acc[w[0]] = w[1]
        for s, v in acc.items():
            semh = self.sem[s] if s in self.cnt else self.dsem[s]
            self.ops[eng].append(lambda e, semh=semh, v=v: e.wait_ge(semh, v))

    def emit(self):
        with self.nc.Block() as block:
            @block.tensor
            def _(e):
                for f in self.ops['pe']:
                    f(e)

            @block.scalar
            def _(e):
                for f in self.ops['act']:
                    f(e)

            @block.vector
            def _(e):
                for f in self.ops['dve']:
                    f(e)

            @block.gpsimd
            def _(e):
                for f in self.ops['pool']:
                    f(e)

            @block.sync
            def _(e):
                for f in self.ops['sp']:
                    f(e)


STAGES = ['load', 'A', 'C0', 'M0', 'B', 'C1', 'M1']
BSTOP = [9]


def build(NB, S, stop='M1', use_gelu=True):
    NT = S // T
    nstage = STAGES.index(stop)
    nc = bass.Bass("TRN2", target_bir_lowering=False)
    x_d = nc.dram_tensor("x", [NB * S, D], F32, kind="ExternalInput").ap()
    mem_d = nc.dram_tensor("mem", [NB * MEM, D], F32, kind="ExternalInput").ap()
    vec_d = nc.dram_tensor("vecs", [NVEC * 8, 128], F32, kind="ExternalInput").ap()
    wts_d = nc.dram_tensor("wts", [NPIECE, 128, PW], F32, kind="ExternalInput").ap()
    msk_d = nc.dram_tensor("masks", [128, 8 * 128], F32, kind="ExternalInput").ap()
    y_d = nc.dram_tensor("y", [NB * S, D], F32, kind="ExternalOutput").ap()
    wsc = nc.dram_tensor("wsc", [NPIECE, 128, PW], BF16, kind="Internal").ap()

    with ExitStack() as es:
        S_ = Sched(nc, es)
        op = S_.op

        def sb(name, shape, dt):
            return es.enter_context(nc.sbuf_tensor(name, shape, dt))

        VT = sb("VT", [128, NVEC * 8], F32)
        ident = sb("ident", [128, 128], F32)
        identb = sb("identb", [128, 128], BF16)
        onesb = sb("onesb", [128, 128], BF16)
        CV2 = sb("CV2", [128, 16], F32)
        DV = sb("DV", [128, 13 * 8], F32)
        ps = [es.enter_context(nc.psum_tensor(f"ps{i}", [128, 512], F32)) for i in range(8)]
        bank_ctr = [0]

        reserved = set()

        def nbank():
            for _ in range(17):
                b = bank_ctr[0] % 8
                bank_ctr[0] += 1
                if b not in reserved:
                    return b
            raise RuntimeError('no free PSUM bank')

        def vcol(name, idx=0, c=0):
            j = (VOFF[name] + idx) * 8 + c
            return VT[:, j:j + 1]

        op('pool', lambda e: e.memset(ident[:], 0.0), writes=['ident'])
        op('pool', lambda e: e.affine_select(out=ident[:], in_=ident[:], pattern=[[-1, 128]], base=0,
                                             channel_multiplier=1, compare_op=ALU.not_equal, fill=1.0),
           reads=['ident'], writes=['ident'])
        op('pool', lambda e: e.tensor_copy(out=identb[:], in_=ident[:]), reads=['ident'], writes=['identb'])
        op('pool', lambda e: e.memset(onesb[:], 1.0), writes=['onesb'])

        with ExitStack() as es0:
            def sb0(name, shape, dt):
                return es0.enter_context(nc.sbuf_tensor(name, shape, dt))
            vst = [sb0(f"vst{i}", [128, 128], F32) for i in range(3)]
            nrows = NVEC * 8
            for i in range(3):
                r0 = i * 128
                r1 = min(nrows, r0 + 128)
                n = r1 - r0
                S_.dma('sp', vst[i][0:n, :], vec_d[r0:r1, :], writes=[f'vst{i}'], key=f'vst{i}')
                b = nbank()
                op('pe', lambda e, i=i, n=n, b=b: e.transpose(out=ps[b][:, 0:n], in_=vst[i][0:n, :], identity=ident[0:n, 0:n]),
                   reads=[f'vst{i}', 'ident'], writes=[f'ps{b}'])
                op('act', lambda e, r0=r0, n=n, b=b: e.activation(out=VT[:, r0:r0 + n], in_=ps[b][:, 0:n], func=AF.Copy),
                   reads=[f'ps{b}'], writes=['VT'])
            lam = VT[:, VOFF['a_lambda'] * 8:VOFF['a_lambda'] * 8 + 8]
            op('act', lambda e: e.activation(out=CV2[:, 0:8], in_=lam, func=AF.Exp, scale=-1.0), reads=['VT'], writes=['CV2'])
            op('act', lambda e: e.activation(out=CV2[:, 0:8], in_=CV2[:, 0:8], func=AF.Ln, bias=1.0), reads=['CV2'], writes=['CV2'])
            op('act', lambda e: e.activation(out=CV2[:, 0:8], in_=CV2[:, 0:8], func=AF.Copy, scale=-8.0), reads=['CV2'], writes=['CV2'])

            g6 = VT[:, (VOFF['ln_gains'] + 6) * 8:(VOFF['ln_gains'] + 6) * 8 + 8]
            for mi in range(6):
                mu_i = VT[:, (VOFF['b_mu'] + mi) * 8:(VOFF['b_mu'] + mi) * 8 + 8]
                op('dve', lambda e, mi=mi, mu_i=mu_i: e.tensor_tensor(out=DV[:, (2 * mi + 1) * 8:(2 * mi + 2) * 8], in0=mu_i, in1=g6, op=ALU.mult),
                   reads=['VT'], writes=['DV'])
                op('dve', lambda e, mi=mi: e.tensor_tensor(out=DV[:, (2 * mi) * 8:(2 * mi + 1) * 8], in0=g6, in1=DV[:, (2 * mi + 1) * 8:(2 * mi + 2) * 8], op=ALU.subtract),
                   reads=['VT', 'DV'], writes=['DV'])
            ka = VT[:, VOFF['b_k_a'] * 8:VOFF['b_k_a'] * 8 + 8]
            op('dve', lambda e: e.tensor_scalar(out=DV[:, 96:104], in0=ka, scalar1=-1.0, scalar2=1.0, op0=ALU.mult, op1=ALU.add),
               reads=['VT'], writes=['DV'])

            NST = 3
            stf = [sb0(f"stf{i}", [128, PW], F32) for i in range(NST)]
            stb = [sb0(f"stb{i}", [128, PW], BF16) for i in range(NST)]
            for pi in range(NPIECE):
                k = pi % NST
                gain, KC, MW = PGAIN[pi]
                S_.dma('sp', stf[k][:], wts_d[pi], writes=[f'stf{k}'], key=f'stf{k}')
                eng = ('dve', 'pool')[pi % 2] if gain is not None else ('act', 'dve', 'pool')[pi % 3]
                if gain is None:
                    if eng == 'act':
                        op('act', lambda e, k=k: e.activation(out=stb[k][:], in_=stf[k][:], func=AF.Copy),
                           reads=[f'stf{k}'], writes=[f'stb{k}'])
                    else:
                        op(eng, lambda e, k=k: e.tensor_copy(out=stb[k][:], in_=stf[k][:]),
                           reads=[f'stf{k}'], writes=[f'stb{k}'])
                else:
                    for kc in range(KC):
                        for (c0, c1, (tab, gi)) in gain:
                            gc = (VT if tab == 'VT' else DV)[:, gi * 8 + kc:gi * 8 + kc + 1]
                            op(eng, lambda e, k=k, kc=kc, MW=MW, gc=gc, c0=c0, c1=c1: e.tensor_scalar(
                                out=stb[k][:, kc * MW + c0:kc * MW + c1], in0=stf[k][:, kc * MW + c0:kc * MW + c1],
                                scalar1=gc, scalar2=1.0, op0=ALU.mult, op1=ALU.mult),
                               reads=[f'stf{k}', 'VT', 'DV'], writes=[f'stb{k}'])
                S_.dma('act', wsc[pi], stb[k][:], reads=[f'stb{k}'], writes=['wsc'], key=f'wsc{k}')
        S_.barrier()

        import os as _os2
        _ex = int(_os2.environ.get('EXTRA_SBUF', '0'))
        if _ex:
            DUMMY = sb('DUMMY', [128, _ex * 256], F32)
            op('pool', lambda e: e.memset(DUMMY[:, _ex * 256 - 512:], 1.0), writes=['DUMMY'])
        X = sb("X", [128, 8, T], F32)
        XN = sb("XN", [128, 8, T], BF16)
        SQ = sb("SQ", [128, 8, T], BF16)
        RS = sb("RS", [128, T], F32)
        F1 = sb("F1", [128, 8, T], F32)
        F2 = sb("F2", [128, 8, T + 4], F32)
        F3 = sb("F3", [128, 8, T], F32)
        BH = sb("BH", [128, 32, T], BF16)
        PT = [sb(f"PT{i}", [128, T], F32) for i in range(4)]
        TB = [sb(f"TB{i}", [128, T], F32) for i in range(5)]
        ring = [sb(f"ring{i}", [128, PW], BF16) for i in range(NRING)]
        xin = [sb("xin0", [128, D], F32)] * 2
        KT = [sb(f"KT{l}", [128, 8, MEM], BF16) for l in range(2)]
        VV = [sb(f"VV{l}", [128, 2, D], BF16) for l in range(2)]
        HST = sb("HST", [128, 8], F32)
        SMX = sb("SMX", [128, 72], F32)


        XNP = sb("XNP", [128, 8, T + 8], BF16)
        RW = sb("RW", [128, 6400], BF16)
        STt = sb("STt", [128, 8, 64], BF16)
        GC = sb("GC", [128, 32], F32)
        XL = sb("XL", [128, 8], BF16)
        SCR = sb("SCR", [128, 8], F32)
        IDH = sb("IDH", [128, 128], FP16)
        MSK = sb("MSK", [128, 8, 128], BF16)
        MK2 = sb("MK2", [128, 512], BF16)
        ID2 = sb("ID2", [128, 256], BF16)
        BOb = sb("BOb", [128, 128], BF16)
        BO64 = sb("BO64", [128, 128], BF16)
        S_.dma('sp', xin[0][:], msk_d, writes=['xin0'], key='xin0')
        op('dve', lambda e: e.tensor_copy(out=MSK[:].rearrange("p a t -> p (a t)"), in_=xin[0][:]), reads=['xin0'], writes=['MSK'])
        op('dve', lambda e: e.tensor_copy(out=IDH[:], in_=ident[:]), reads=['ident'], writes=['IDH'])
        op('pool', lambda e: e.memset(MK2[:], 1.0), writes=['MK2'])
        for kind in range(4):
            op('pool', lambda e, kind=kind: e.affine_select(out=MK2[:, kind * 128:(kind + 1) * 128], in_=MK2[:, kind * 128:(kind + 1) * 128],
                                                            pattern=[[1, 128]], base=0, channel_multiplier=-1,
                                                            compare_op=(ALU.is_gt if kind % 2 == 0 else ALU.is_ge), fill=0.0),
               reads=['MK2'], writes=['MK2'])
        op('pool', lambda e: e.memset(ID2[:], 0.0), writes=['ID2'])
        for hs in range(2):
            op('pool', lambda e, hs=hs: e.affine_select(out=ID2[64 * hs:64 * hs + 64, :], in_=ID2[64 * hs:64 * hs + 64, :],
                                                        pattern=[[0, 4], [-1, 64]], base=0, channel_multiplier=1,
                                                        compare_op=ALU.not_equal, fill=1.0), reads=['ID2'], writes=['ID2'])
        op('pool', lambda e: e.memset(BOb[:], 0.0), writes=['BOb'])
        op('pool', lambda e: e.memset(BO64[:], 0.0), writes=['BO64'])
        for hs in range(2):
            op('pool', lambda e, hs=hs: e.memset(BOb[64 * hs:64 * hs + 64, 64 * hs:64 * hs + 64], 1.0), reads=['BOb'], writes=['BOb'])
            op('pool', lambda e, hs=hs: e.memset(BO64[64 * hs:64 * hs + 64, 64 * hs:64 * hs + 64], 1.0 / 64.0), reads=['BO64'], writes=['BO64'])

        def bhb(i):
            return BH[:, 8 * i:8 * (i + 1), :]

        BHF = BH[:].rearrange("p a t -> p (a t)").bitcast(F32)

        def bhf(i, c):
            o = (i * 8 + c) * T
            return BHF[:, o:o + T]

        def gk(i, c):
            k = i * 8 + c
            return [f'BH{2 * k}', f'BH{2 * k + 1}']

        seq = []
        for b in range(NB):
            if nstage >= 2:
                seq += [('wkv0', i) for i in range(4)]
            if nstage >= 5:
                seq += [('wkv1', i) for i in range(4)]
            for i in range(NT):
                if nstage >= 1:
                    seq += [('w_in', j) for j in (2, 3, 0, 1)] + [('gates', 0)] + [('a_w_out', j) for j in range(2)]
                if nstage >= 2:
                    seq += [('wq0', j) for j in range(2)] + [('wo0', j) for j in range(2)]
                if nstage >= 3:
                    seq += [('up0', j) for j in range(8)] + [('down0', j) for j in range(8)]
                if nstage >= 4:
                    seq += [('loraAa', 0), ('loraAb', 0), ('loraB', 0)]
                    for pj in (2, 3, 0, 1, 4, 5):
                        seq += [('rkvA', pj), ('rkvB', pj)]
                    seq += [('b_w_o', j) for j in range(2)]
                if nstage >= 5:
                    seq += [('wq1', j) for j in range(2)] + [('wo1', j) for j in range(2)]
                if nstage >= 6:
                    seq += [('up1', j) for j in range(8)] + [('down1', j) for j in range(8)]
        wstate = {'issued': 0, 'used': 0}

        def w_issue():
            k = wstate['issued']
            if k >= len(seq):
                return
            name, j = seq[k]
            pi = PIECES[name][0] + j
            slot = k % NRING
            S_.dma('sp', ring[slot][:], wsc[pi], writes=[f'ring{slot}'], key=f'ring{slot}')
            wstate['issued'] += 1

        def w_next(name, j):
            k = wstate['used']
            assert seq[k] == (name, j), (seq[k], name, j)
            prev_live = k >= 1 and seq[k - 1][0] in ('rkvA', 'loraAa') and seq[k][0] in ('rkvB', 'loraAb')
            retired = k - 2 if prev_live else k - 1
            while wstate['issued'] < min(len(seq), retired + NRING + 1):
                w_issue()
            wstate['used'] += 1
            slot = k % NRING
            return ring[slot], f'ring{slot}'

        def ones_norm(src_keys):
            b = nbank()
            for c in range(8):
                op('pe', lambda e, c=c, b=b: e.matmul(ps[b][:, :], lhsT=onesb[:], rhs=SQ[:, c, :], start=(c == 0), stop=(c == 7)),
                   reads=['onesb'] + [f'SQ{c}'], writes=[f'ps{b}'], signal=(c == 7))
            op('act', lambda e, b=b: e.activation(out=PT[3][:], in_=ps[b][:, :], func=AF.Ln, scale=1.0 / D, bias=1e-6),
               reads=[f'ps{b}'], writes=['PT3'])
            op('act', lambda e: e.activation(out=RS[:], in_=PT[3][:], func=AF.Exp, scale=-0.5), reads=['PT3'], writes=['RS'])

        def norm_in():
            op('act', lambda e: e.activation(out=SQ[:], in_=X[:], func=AF.Square),
               reads=[f'X{c}' for c in range(8)], writes=[f'SQ{c}' for c in range(8)])
            ones_norm(None)
            for c in range(8):
                eng = 'pool' if c % 3 == 2 else 'dve'
                op(eng, lambda e, c=c: e.tensor_tensor(out=XN[:, c, :], in0=X[:, c, :], in1=RS[:], op=ALU.mult),
                   reads=[f'X{c}', 'RS'], writes=[f'XN{c}'])

        def post_norm(gidx):
            ones_norm(None)
            for c in range(8):
                op('pool', lambda e, c=c: e.tensor_tensor(out=F1[:, c, :], in0=F1[:, c, :], in1=RS[:], op=ALU.mult),
                   reads=[f'F1_{c}', 'RS'], writes=[f'F1_{c}'])
                gc = vcol('ln_gains', gidx, c)
                op('dve', lambda e, c=c, gc=gc: e.scalar_tensor_tensor(out=X[:, c, :], in0=F1[:, c, :], scalar=gc, in1=X[:, c, :],
                                                                        op0=ALU.mult, op1=ALU.add),
                   reads=[f'F1_{c}', f'X{c}', 'VT'], writes=[f'X{c}'])

        def proj(wname, npieces, src, srckeys, KC, evac, mper=4, n=T, order=None):
            for pj in (order if order is not None else range(npieces)):
                rg, rkey = w_next(wname, pj)
                MW = mper * 128
                for ml in range(mper):
                    m = pj * mper + ml
                    b = nbank()
                    for kc in range(KC):
                        op('pe', lambda e, rg=rg, kc=kc, ml=ml, b=b, MW=MW: e.matmul(
                            ps[b][:, 0:n], lhsT=rg[:, kc * MW + ml * 128:kc * MW + (ml + 1) * 128], rhs=src(kc),
                            start=(kc == 0), stop=(kc == KC - 1)),
                           reads=[rkey, srckeys(kc)], writes=[f'ps{b}'], signal=(kc == KC - 1))
                    evac(m, ps[b][:, 0:n], f'ps{b}')

        def evac_branch(bias_name):
            def ev(m, p, pk):
                if bias_name is None:
                    op('act', lambda e, m=m, p=p: e.activation(out=F1[:, m, :], in_=p, func=AF.Copy),
                       reads=[pk], writes=[f'F1_{m}'])
                    op('act', lambda e, m=m, p=p: e.activation(out=SQ[:, m, :], in_=p, func=AF.Square),
                       reads=[pk], writes=[f'SQ{m}'])
                else:
                    bc = vcol(bias_name, 0, m)
                    op('act', lambda e, m=m, p=p, bc=bc: e.activation(out=F1[:, m, :], in_=p, func=AF.Identity, bias=bc),
                       reads=[pk, 'VT'], writes=[f'F1_{m}'])
                    op('act', lambda e, m=m, p=p, bc=bc: e.activation(out=SQ[:, m, :], in_=p, func=AF.Square, bias=bc),
                       reads=[pk, 'VT'], writes=[f'SQ{m}'])
            return ev

        xkeys = [f'X{c}' for c in range(8)]

        def load_tile(b, i):
            for tb in range(4):
                r0 = b * S + i * T + tb * 128
                if tb % 2 == 0:
                    srcs = [xin[0][:, 0:512], xin[0][:, 512:1024]]
                    bkeys = ['xin0', 'xin0']
                    S_.dma('sp', xin[0][:], x_d[r0:r0 + 128, :], writes=['xin0'], key='xin0')
                else:
                    srcs = [PT[0][:], PT[1][:]]
                    bkeys = ['PT0', 'PT1']
                    for h_ in range(2):
                        S_.dma('sp', PT[h_][:], x_d[r0:r0 + 128, h_ * 512:(h_ + 1) * 512], writes=[bkeys[h_]], key=f'ptio{h_}')
                for half in range(2):
                    bk = nbank()
                    for cl in range(4):
                        c = half * 4 + cl
                        op('pe', lambda e: e.transpose(out=ps[bk][:, cl * 128:(cl + 1) * 128], in_=srcs[half][:, cl * 128:(cl + 1) * 128], identity=ident[:]),
                           reads=[bkeys[half], 'ident'], writes=[f'ps{bk}'], signal=(cl == 3))
                    op('act', lambda e: e.activation(
                        out=X[:, half * 4:half * 4 + 4, tb * 128:(tb + 1) * 128],
                        in_=ps[bk][:, :].rearrange("p (c t) -> p c t", c=4), func=AF.Copy),
                       reads=[f'ps{bk}'], writes=[f'X{c}' for c in range(half * 4, half * 4 + 4)])

        def store_tile(b, i):
            for tb in range(4):
                r0 = b * S + i * T + tb * 128
                if tb % 2 == 0:
                    dsts = [xin[0][:, 0:512], xin[0][:, 512:1024]]
                    bkeys = ['xin0', 'xin0']
                else:
                    dsts = [PT[0][:], PT[1][:]]
                    bkeys = ['PT0', 'PT1']
                for half in range(2):
                    bk = nbank()
                    for cl in range(4):
                        c = half * 4 + cl
                        op('pe', lambda e: e.transpose(out=ps[bk][:, cl * 128:(cl + 1) * 128], in_=X[:, c, tb * 128:(tb + 1) * 128], identity=ident[:]),
                           reads=[f'X{c}', 'ident'], writes=[f'ps{bk}'], signal=(cl == 3))
                    op('act', lambda e: e.activation(out=dsts[half], in_=ps[bk][:, :], func=AF.Copy),
                       reads=[f'ps{bk}'], writes=[bkeys[half]])
                if tb % 2 == 0:
                    S_.dma('sp', y_d[r0:r0 + 128, :], xin[0][:], reads=['xin0'], writes=['y'], key='xin0')
                else:
                    for h_ in range(2):
                        S_.dma('sp', y_d[r0:r0 + 128, h_ * 512:(h_ + 1) * 512], PT[h_][:], reads=[bkeys[h_]], writes=['y'], key=f'ptio{h_}')

        def stage_A(b, i):
            norm_in()
            vb = VOFF['a_b_in']

            def ev_in(m, p, pk):
                if m < 8:
                    bc = VT[:, vb * 8 + m:vb * 8 + m + 1]
                    if use_gelu:
                        op('act', lambda e, m=m, p=p, bc=bc: e.activation(out=F1[:, m, :], in_=p, func=AF.Gelu_apprx_tanh, bias=bc),
                           reads=[pk, 'VT'], writes=[f'F1_{m}'])
                    else:
                        op('act', lambda e, m=m, p=p, bc=bc: e.activation(out=F1[:, m, :], in_=p, func=AF.Identity, bias=bc),
                           reads=[pk, 'VT'], writes=[f'F1_{m}'])
                        op('pool', lambda e, m=m: e.tensor_tensor(out=PT[0][:], in0=F1[:, m, :], in1=F1[:, m, :], op=ALU.mult),
                           reads=[f'F1_{m}'], writes=['PT0'])
                        op('dve', lambda e: e.tensor_scalar(out=PT[0][:], in0=PT[0][:], scalar1=0.044715, scalar2=1.0, op0=ALU.mult, op1=ALU.add),
                           reads=['PT0'], writes=['PT0'])
                        op('pool', lambda e, m=m: e.tensor_tensor(out=PT[0][:], in0=PT[0][:], in1=F1[:, m, :], op=ALU.mult),
                           reads=[f'F1_{m}', 'PT0'], writes=['PT0'])
                        op('act', lambda e: e.activation(out=PT[0][:], in_=PT[0][:], func=AF.Sigmoid, scale=1.5957691216057308),
                           reads=['PT0'], writes=['PT0'])
                        op('dve', lambda e, m=m: e.tensor_tensor(out=F1[:, m, :], in0=F1[:, m, :], in1=PT[0][:], op=ALU.mult),
                           reads=[f'F1_{m}', 'PT0'], writes=[f'F1_{m}'])
                else:
                    c = m - 8
                    bc = VT[:, vb * 8 + m:vb * 8 + m + 1]
                    op('act', lambda e, c=c, p=p, bc=bc: e.activation(out=F2[:, c, 4:T + 4], in_=p, func=AF.Identity, bias=bc),
                       reads=[pk, 'VT'], writes=[f'F2_{c}'])
                    cw = [vcol('a_conv_w', k, c) for k in range(4)]
                    cb = vcol('a_conv_b', 0, c)
                    op('dve', lambda e, c=c, cw=cw, cb=cb: e.tensor_scalar(out=F3[:, c, :], in0=F2[:, c, 1:T + 1], scalar1=cw[0], scalar2=cb,
                                                                        op0=ALU.mult, op1=ALU.add),
                       reads=[f'F2_{c}', 'VT'], writes=[f'F3_{c}'])
                    for k in range(1, 4):
                        op('dve', lambda e, c=c, k=k, cw=cw: e.scalar_tensor_tensor(out=F3[:, c, :], in0=F2[:, c, 1 + k:T + 1 + k], scalar=cw[k],
                                                                                in1=F3[:, c, :], op0=ALU.mult, op1=ALU.add),
                           reads=[f'F2_{c}', f'F3_{c}', 'VT'], writes=[f'F3_{c}'])
                    op('pool', lambda e, c=c: e.tensor_copy(out=F2[:, c, 1:4], in_=F2[:, c, T + 1:T + 4]),
                       reads=[f'F2_{c}'], writes=[f'F2_{c}'])
                    op('pool', lambda e, c=c: e.tensor_copy(out=SQ[:, c, :], in_=F3[:, c, :]),
                       reads=[f'F3_{c}'], writes=[f'SQ{c}'])

            if i == 0:
                for c in range(8):
                    op('pool', lambda e, c=c: e.memset(F2[:, c, 0:4], 0.0), writes=[f'F2_{c}'])
                op('pool', lambda e: e.memset(HST[:], 0.0), writes=['HST'])
            proj('w_in', 4, lambda kc: XN[:, kc, :], lambda kc: f'XN{kc}', 8, ev_in, order=(2, 3, 0, 1))
            rg, rkey = w_next('gates', 0)
            gb = VOFF['a_gate_b']
            for gi in range(2):
                for c in range(8):
                    h, j = c // 2, c % 2
                    bk = nbank()
                    for kc in range(2):
                        o = ((gi * 4 + h) * 2 + kc) * 256 + j * 128
                        op('pe', lambda e, o=o, h=h, kc=kc, bk=bk: e.matmul(ps[bk][:, :], lhsT=rg[:, o:o + 128], rhs=SQ[:, 2 * h + kc, :],
                                                                          start=(kc == 0), stop=(kc == 1)),
                           reads=[rkey, f'SQ{2*h+kc}'], writes=[f'ps{bk}'], signal=(kc == 1))
                    bc = VT[:, (gb + gi) * 8 + c:(gb + gi) * 8 + c + 1]
                    op('act', lambda e, gi=gi, c=c, bk=bk, bc=bc: e.activation(out=bhf(gi, c), in_=ps[bk][:, :], func=AF.Sigmoid, bias=bc),
                       reads=[f'ps{bk}', 'VT'], writes=gk(gi, c))
            for c in range(8):
                cc = CV2[:, c:c + 1]
                op('act', lambda e, c=c, cc=cc: e.activation(out=bhf(0, c), in_=bhf(0, c), func=AF.Exp, scale=cc),
                   reads=gk(0, c) + ['CV2'], writes=gk(0, c))
                op('pool', lambda e, c=c: e.tensor_tensor(out=F2[:, c, 4:T + 4], in0=bhf(0, c), in1=bhf(0, c), op=ALU.mult),
                   reads=gk(0, c) + [f'F2_{c}'], writes=[f'F2_{c}'])
            for c in range(8):
                op('act', lambda e, c=c: e.activation(out=F2[:, c, 4:T + 4], in_=F2[:, c, 4:T + 4], func=AF.Sqrt, scale=-1.0, bias=1.0),
                   reads=[f'F2_{c}'], writes=[f'F2_{c}'])
            for c in range(8):
                op('dve', lambda e, c=c: e.tensor_tensor(out=bhf(1, c), in0=bhf(1, c), in1=F2[:, c, 4:T + 4], op=ALU.mult),
                   reads=gk(1, c) + [f'F2_{c}'], writes=gk(1, c))
                op('pool', lambda e, c=c: e.tensor_tensor(out=bhf(1, c), in0=bhf(1, c), in1=F3[:, c, :], op=ALU.mult),
                   reads=gk(1, c) + [f'F3_{c}'], writes=gk(1, c))
                op('dve', lambda e, c=c: e.tensor_tensor_scan(out=F3[:, c, :], data0=bhf(0, c), data1=bhf(1, c), initial=HST[:, c:c + 1],
                                                             op0=ALU.mult, op1=ALU.add),
                   reads=gk(0, c) + gk(1, c) + ['HST', f'F3_{c}'], writes=[f'F3_{c}'])
                op('pool', lambda e, c=c: e.tensor_copy(out=HST[:, c:c + 1], in_=F3[:, c, T - 1:T]),
                   reads=[f'F3_{c}'], writes=['HST'])
                op('pool', lambda e, c=c: e.tensor_tensor(out=XN[:, c, :], in0=F3[:, c, :], in1=F1[:, c, :], op=ALU.mult),
                   reads=[f'F3_{c}', f'F1_{c}'], writes=[f'XN{c}'])
            proj('a_w_out', 2, lambda kc: XN[:, kc, :], lambda kc: f'XN{kc}', 8, evac_branch('a_b_out'))
            post_norm(1)

        def mem_prep(b, layers):
            MT = F1[:].rearrange("p c t -> p (c t)")[:, 0:8 * MEM].rearrange("p (c t) -> p c t", c=8)
            MN = XN[:].rearrange("p c t -> p (c t)")[:, 0:8 * MEM].rearrange("p (c t) -> p c t", c=8)
            MSQ = SQ[:].rearrange("p c t -> p (c t)")[:, 0:8 * MEM].rearrange("p (c t) -> p c t", c=8)
            f1k = [f'F1_{c}' for c in range(8)]
            xnk = [f'XN{c}' for c in range(8)]
            sqk = [f'SQ{c}' for c in range(8)]
            for tb in range(2):
                r0 = b * MEM + tb * 128
                xb = xin[0]
                S_.dma('sp', xb[:], mem_d[r0:r0 + 128, :], writes=['xin0'], key='xin0')
                for half in range(2):
                    bk = nbank()
                    for cl in range(4):
                        c = half * 4 + cl
                        op('pe', lambda e, xb=xb, c=c, cl=cl, bk=bk: e.transpose(out=ps[bk][:, cl * 128:(cl + 1) * 128],
                                                                                in_=xb[:, c * 128:(c + 1) * 128], identity=ident[:]),
                           reads=['xin0', 'ident'], writes=[f'ps{bk}'], signal=(cl == 3))
                    op('act', lambda e, half=half, tb=tb, bk=bk: e.activation(
                        out=MT[:, half * 4:half * 4 + 4, tb * 128:(tb + 1) * 128],
                        in_=ps[bk][:, :].rearrange("p (c t) -> p c t", c=4), func=AF.Copy),
                       reads=[f'ps{bk}'], writes=f1k)
            op('act', lambda e: e.activation(out=MSQ, in_=MT, func=AF.Square), reads=f1k, writes=sqk)
            bk = nbank()
            for c in range(8):
                op('pe', lambda e, c=c, bk=bk: e.matmul(ps[bk][:, 0:MEM], lhsT=onesb[:], rhs=MSQ[:, c, :], start=(c == 0), stop=(c == 7)),
                   reads=['onesb'] + sqk, writes=[f'ps{bk}'], signal=(c == 7))
            op('act', lambda e, bk=bk: e.activation(out=PT[3][:, 0:MEM], in_=ps[bk][:, 0:MEM], func=AF.Ln, scale=1.0 / D, bias=1e-6),
               reads=[f'ps{bk}'], writes=['PT3'])
            op('act', lambda e: e.activation(out=RS[:, 0:MEM], in_=PT[3][:, 0:MEM], func=AF.Exp, scale=-0.5), reads=['PT3'], writes=['RS'])
            for c in range(8):
                op('dve', lambda e, c=c: e.tensor_tensor(out=MN[:, c, :], in0=MT[:, c, :], in1=RS[:, 0:MEM], op=ALU.mult),
                   reads=f1k + ['RS'], writes=xnk)
            for l in layers:
                for pj in range(2):
                    rg, rkey = w_next(f'wkv{l}', pj)
                    for ml in range(4):
                        m = pj * 4 + ml
                        bk = nbank()
                        for kc in range(8):
                            op('pe', lambda e, rg=rg, kc=kc, ml=ml, bk=bk: e.matmul(
                                ps[bk][:, 0:MEM], lhsT=rg[:, kc * 512 + ml * 128:kc * 512 + (ml + 1) * 128], rhs=MN[:, kc, :],
                                start=(kc == 0), stop=(kc == 7)),
                               reads=[rkey] + xnk, writes=[f'ps{bk}'], signal=(kc == 7))
                        op('act', lambda e, l=l, m=m, bk=bk: e.activation(out=KT[l][:, m, :], in_=ps[bk][:, 0:MEM], func=AF.Copy),
                           reads=[f'ps{bk}'], writes=[f'KT{l}'])
                for pj in range(2):
                    rg, rkey = w_next(f'wkv{l}', 2 + pj)
                    for mc in range(2):
                        bk = nbank()
                        for kc in range(8):
                            op('pe', lambda e, rg=rg, kc=kc, mc=mc, bk=bk: e.matmul(
                                ps[bk][:, :], lhsT=MN[:, kc, mc * 128:(mc + 1) * 128], rhs=rg[:, kc * 512:(kc + 1) * 512],
                                start=(kc == 0), stop=(kc == 7)),
                               reads=[rkey] + xnk, writes=[f'ps{bk}'], signal=(kc == 7))
                        op('act', lambda e, l=l, mc=mc, pj=pj, bk=bk: e.activation(out=VV[l][:, mc, pj * 512:(pj + 1) * 512], in_=ps[bk][:, :], func=AF.Copy),
                           reads=[f'ps{bk}'], writes=[f'VV{l}'])

        def stage_C(l):
            for c in range(8):
                eng = 'pool' if c % 3 == 2 else 'dve'
                op(eng, lambda e, c=c: e.tensor_copy(out=XN[:, c, :], in_=X[:, c, :]), reads=[f'X{c}'], writes=[f'XN{c}'])
            op('act', lambda e: e.activation(out=SQ[:], in_=X[:], func=AF.Square),
               reads=[f'X{c}' for c in range(8)], writes=[f'SQ{c}' for c in range(8)])
            bR = nbank()
            for tb in range(4):
                for c in range(8):
                    op('pe', lambda e, tb=tb, c=c: e.matmul(ps[bR][:, tb:tb + 1], lhsT=SQ[:, c, tb * 128:(tb + 1) * 128], rhs=onesb[:, 0:1],
                                                         start=(c == 0), stop=(c == 7)),
                       reads=['onesb', f'SQ{c}'], writes=[f'ps{bR}'], signal=(tb == 3 and c == 7))
            op('act', lambda e: e.activation(out=SMX[:, 48:52], in_=ps[bR][:, 0:4], func=AF.Ln, scale=1.0 / D, bias=1e-6),
               reads=[f'ps{bR}'], writes=['RSt'])
            op('act', lambda e: e.activation(out=SMX[:, 48:52], in_=SMX[:, 48:52], func=AF.Exp, scale=-0.5), reads=['RSt'], writes=['RSt'])
            QT = bhb(0)
            PN = bhb(1)
            PTt = bhb(2)
            OT = bhb(3)

            def ev_q(m, p, pk):
                op('act', lambda e, m=m, p=p: e.activation(out=QT[:, m, :], in_=p, func=AF.Copy, scale=1.0 / 16.0),
                   reads=[pk], writes=[f'BH{m}'])
            proj(f'wq{l}', 2, lambda kc: XN[:, kc, :], lambda kc: f'XN{kc}', 8, ev_q)
            def sm_chain(tb):
                pn = PN[:, 2 * tb:2 * tb + 2, :].rearrange("p a t -> p (a t)")
                pex = F3[:, 2 * tb:2 * tb + 2, :].rearrange("p a t -> p (a t)")
                banks = [nbank(), nbank()]
                for h in range(4):
                    bk = banks[h // 2]
                    for dc in range(2):
                        op('pe', lambda e: e.matmul(
                            ps[bk][:, (h % 2) * 256:(h % 2 + 1) * 256], lhsT=QT[:, 2 * h + dc, tb * 128:(tb + 1) * 128],
                            rhs=KT[l][:, 2 * h + dc, :], start=(dc == 0), stop=(dc == 1)),
                           reads=[f'BH{2*h+dc}', f'KT{l}'], writes=[f'ps{bk}'], signal=(dc == 1))
                yield
                for hb in range(2):
                    bk = banks[hb]
                    op('dve', lambda e: e.tensor_reduce(
                        out=SMX[:, tb * 4 + 2 * hb:tb * 4 + 2 * hb + 2], in_=ps[bk][:, :].rearrange("p (h k) -> p h k", h=2),
                        axis=AX.X, op=ALU.max, negate=True),
                       reads=[f'ps{bk}'], writes=[f'SMXm{tb}'])
                op('dve', lambda e: e.tensor_scalar(out=SMX[:, 52 + tb * 4:56 + tb * 4], in0=SMX[:, tb * 4:tb * 4 + 4],
                                                   scalar1=SMX[:, 48 + tb:49 + tb], scalar2=None, op0=ALU.mult),
                   reads=[f'SMXm{tb}', 'RSt'], writes=[f'SMXn{tb}'])
                yield
                for h in range(4):
                    bk = banks[h // 2]
                    op('act', lambda e: e.activation(
                        out=pex[:, h * 256:(h + 1) * 256], in_=ps[bk][:, (h % 2) * 256:(h % 2 + 1) * 256], func=AF.Exp,
                        scale=SMX[:, 48 + tb:49 + tb], bias=SMX[:, 52 + tb * 4 + h:53 + tb * 4 + h],
                        accum_out=SMX[:, 16 + tb * 4 + h:16 + tb * 4 + h + 1]),
                       reads=[f'ps{bk}', f'SMXn{tb}', 'RSt'], writes=[f'F3_{2*tb}', f'F3_{2*tb+1}', f'SMXs{tb}'])
                yield
                op('dve', lambda e: e.reciprocal(out=SMX[:, 32 + tb * 4:32 + tb * 4 + 4], in_=SMX[:, 16 + tb * 4:16 + tb * 4 + 4]),
                   reads=[f'SMXs{tb}'], writes=[f'SMXr{tb}'])
                for h in range(4):
                    op('dve', lambda e: e.tensor_scalar(
                        out=pn[:, h * 256:(h + 1) * 256], in0=pex[:, h * 256:(h + 1) * 256],
                        scalar1=SMX[:, 32 + tb * 4 + h:32 + tb * 4 + h + 1], scalar2=None, op0=ALU.mult),
                       reads=[f'F3_{2*tb}', f'F3_{2*tb+1}', f'SMXr{tb}'], writes=[f'BH{8+2*tb}', f'BH{9+2*tb}'])
                yield
                bk = nbank()
                psb = ps[bk][:, :].bitcast(BF16)
                for hm in range(8):
                    op('pe', lambda e: e.transpose(out=psb[:, hm * 128:(hm + 1) * 128], in_=pn[:, hm * 128:(hm + 1) * 128], identity=identb[:]),
                       reads=[f'BH{8+2*tb}', f'BH{9+2*tb}', 'identb'], writes=[f'ps{bk}'], signal=(hm == 7))
                yield
                op('act', lambda e: e.activation(out=PTt[:, :, tb * 128:(tb + 1) * 128],
                                                 in_=psb.rearrange("p (a t) -> p a t", a=8), func=AF.Copy),
                   reads=[f'ps{bk}'], writes=[f'BH{16+a}' for a in range(8)])
                yield

            gens = [sm_chain(tb) for tb in range(4)]
            while gens:
                for g in list(gens):
                    try:
                        next(g)
                    except StopIteration:
                        gens.remove(g)
            for m in range(8):
                h = m // 2
                bk = nbank()
                for mc in range(2):
                    op('pe', lambda e, m=m, h=h, mc=mc, bk=bk: e.matmul(ps[bk][:, :], lhsT=VV[l][:, mc, m * 128:(m + 1) * 128],
                                                                      rhs=PTt[:, 2 * h + mc, :], start=(mc == 0), stop=(mc == 1)),
                       reads=[f'VV{l}', f'BH{16+2*h+mc}'], writes=[f'ps{bk}'], signal=(mc == 1))
                op('act', lambda e, m=m, bk=bk: e.activation(out=OT[:, m, :], in_=ps[bk][:, :], func=AF.Copy),
                   reads=[f'ps{bk}'], writes=[f'BH{24+m}'])
            proj(f'wo{l}', 2, lambda kc: OT[:, kc, :], lambda kc: f'BH{24+kc}', 8, evac_branch(None))
            post_norm(6 * l + 3)

        def stage_M(l):
            for c in range(8):
                eng = 'pool' if c % 3 == 2 else 'dve'
                op(eng, lambda e, c=c: e.tensor_copy(out=XN[:, c, :], in_=X[:, c, :]), reads=[f'X{c}'], writes=[f'XN{c}'])
            op('act', lambda e: e.activation(out=SQ[:], in_=X[:], func=AF.Square),
               reads=[f'X{c}' for c in range(8)], writes=[f'SQ{c}' for c in range(8)])
            b = nbank()
            for c in range(8):
                op('pe', lambda e, c=c, b=b: e.matmul(ps[b][:, :], lhsT=onesb[:], rhs=SQ[:, c, :], start=(c == 0), stop=(c == 7)),
                   reads=['onesb'] + [f'SQ{c}'], writes=[f'ps{b}'], signal=(c == 7))
            op('act', lambda e, b=b: e.activation(out=PT[3][:], in_=ps[b][:, :], func=AF.Ln, scale=1.0 / D, bias=1e-6),
               reads=[f'ps{b}'], writes=['PT3'])
            op('act', lambda e: e.activation(out=RS[:], in_=PT[3][:], func=AF.Exp, scale=-1.0), reads=['PT3'], writes=['RS'])
            cnt = [0]

            def ev_down(m, p, pk):
                op('dve', lambda e, m=m, p=p: e.tensor_tensor(out=F1[:, m, :], in0=p, in1=RS[:], op=ALU.mult),
                   reads=[pk, 'RS'], writes=[f'F1_{m}'])
                op('act', lambda e, m=m: e.activation(out=SQ[:, m, :], in_=F1[:, m, :], func=AF.Square),
                   reads=[f'F1_{m}'], writes=[f'SQ{m}'])

            def ev_up(m, p, pk):
                k = cnt[0] % 2
                cnt[0] += 1
                op('act', lambda e, p=p, k=k: e.activation(out=PT[k][:], in_=p, func=AF.Square), reads=[pk], writes=[f'PT{k}'])
                op('dve', lambda e, m=m, p=p, k=k: e.scalar_tensor_tensor(out=BH[:, m, :], in0=p, scalar=0.0, in1=PT[k][:],
                                                                          op0=ALU.is_gt, op1=ALU.mult),
                   reads=[pk, f'PT{k}'], writes=[f'BH{m}'])
            proj(f'up{l}', 8, lambda kc: XN[:, kc, :], lambda kc: f'XN{kc}', 8, ev_up)
            proj(f'down{l}', 8, lambda kc: BH[:, kc, :], lambda kc: f'BH{kc}', 32, ev_down, mper=1)
            post_norm(6 * l + 5)

        _rw = [0]

        def rw_alloc(n):
            o = _rw[0]
            _rw[0] += n
            return RW[:, o:o + n]
        TM = rw_alloc(2048)
        U4 = rw_alloc(2048)
        Pb = rw_alloc(512)
        AU = rw_alloc(512)
        Wt = rw_alloc(256)
        LWA = rw_alloc(512)
        PTb = rw_alloc(512)
        LG = PTb
        F3B = F3[:].rearrange("p c t -> p (c t)").bitcast(BF16)
        RH = F3B[:, 0:4096]
        GT = F3B[:, 4096:6144]
        HH = F3B[:, 6144:8192]
        ARf = BH[:, 0:16, :].rearrange("p a t -> p (a t)")
        f3k = [f'F3_{c}' for c in range(8)]

        def ar_kind(fc, kind):
            return ARf[:, fc * 1024:(fc + 1) * 1024].rearrange("p (c k t) -> p c k t", c=4, k=2)[:, :, kind, :]

        def stage_B(b, i):
            for c in range(8):
                if i == 0:
                    op('pool', lambda e, c=c: e.memset(XNP[:, c, 0:8], 0.0), writes=[f'XNP{c}'])
                else:
                    op('pool', lambda e, c=c: e.tensor_copy(out=XNP[:, c, 7:8], in_=XL[:, c:c + 1]), reads=['XL', f'XNP{c}'], writes=[f'XNP{c}'])
            if i == 0:
                op('pool', lambda e: e.memset(STt[:], 0.0), writes=['STt'])
            op('act', lambda e: e.activation(out=SQ[:], in_=X[:], func=AF.Square), reads=xkeys, writes=[f'SQ{c}' for c in range(8)])
            ones_norm(None)
            for c in range(8):
                eng = 'pool' if c % 3 == 2 else 'dve'
                op(eng, lambda e, c=c: e.tensor_tensor(out=XNP[:, c, 8:T + 8], in0=X[:, c, :], in1=RS[:], op=ALU.mult),
                   reads=[f'X{c}', 'RS'], writes=[f'XNP{c}'])

            def proj2(nameA, nameB, pj, evac):
                rgA, kA = w_next(nameA, pj)
                rgB, kB = w_next(nameB, pj)
                return rgA, kA, rgB, kB

            def mm16(rgA, kA, rgB, kB, col0, ncol, bk, MW):
                for v_, (rg, rk, off) in enumerate(((rgA, kA, 8), (rgB, kB, 7))):
                    for kc in range(8):
                        op('pe', lambda e, rg=rg, kc=kc, off=off, v_=v_: e.matmul(
                            ps[bk][0:ncol, :], lhsT=rg[:, kc * MW + col0:kc * MW + col0 + ncol], rhs=XNP[:, kc, off:off + T],
                            start=(v_ == 0 and kc == 0), stop=(v_ == 1 and kc == 7)),
                           reads=[rk, f'XNP{kc}'], writes=[f'ps{bk}'], signal=(v_ == 1 and kc == 7))

            if BSTOP[0] <= 0.1:
                return
            rgA, kA = w_next('loraAa', 0)
            rgB, kB = w_next('loraAb', 0)
            bk = nbank()
            mm16(rgA, kA, rgB, kB, 0, 128, bk, 256)
            op('act', lambda e, bk=bk: e.activation(out=LWA[0:64, :], in_=ps[bk][0:64, :], func=AF.Tanh), reads=[f'ps{bk}'], writes=['LWA0'])
            op('act', lambda e, bk=bk: e.activation(out=LWA[64:128, :], in_=ps[bk][64:128, :], func=AF.Copy), reads=[f'ps{bk}'], writes=['LWA1'])
            bk = nbank()
            mm16(rgA, kA, rgB, kB, 128, 128, bk, 256)
            op('act', lambda e, bk=bk: e.activation(out=LG[:, :], in_=ps[bk][:, :], func=AF.Sigmoid), reads=[f'ps{bk}'], writes=['PTb'])
            if BSTOP[0] <= 0.3:
                return
            rgL, kL = w_next('loraB', 0)
            def lo_chain(m):
                tt, kt_ = (PT[0], ['PT0']) if m % 2 == 0 else (PT[2], ['PT2'])
                bw, ba, bg = nbank(), nbank(), nbank()
                op('pe', lambda e: e.matmul(ps[bw][:, :], lhsT=rgL[0:64, m * 128:(m + 1) * 128], rhs=LWA[0:64, :], start=True, stop=True),
                   reads=[kL, 'LWA0'], writes=[f'ps{bw}'])
                op('pe', lambda e: e.matmul(ps[ba][:, :], lhsT=rgL[64:128, 1024 + m * 128:1024 + (m + 1) * 128], rhs=LWA[64:128, :], start=True, stop=True),
                   reads=[kL, 'LWA1'], writes=[f'ps{ba}'])
                op('pe', lambda e: e.matmul(ps[bg][:, :], lhsT=rgL[:, 2048 + m * 128:2048 + (m + 1) * 128], rhs=LG[:, :], start=True, stop=True),
                   reads=[kL, 'PTb'], writes=[f'ps{bg}'])
                yield
                op('act', lambda e: e.activation(out=tt[:], in_=ps[bw][:, :], func=AF.Sigmoid, bias=vcol('b_w0', 0, m)),
                   reads=[f'ps{bw}', 'VT'], writes=kt_)
                yield
                for c in range(4):
                    op('dve', lambda e, c=c: e.tensor_tensor_scan(out=F1[:, m, c * 128:(c + 1) * 128], data0=onesb[:], data1=tt[:, c * 128:(c + 1) * 128],
                                                                 initial=0.0, op0=ALU.mult, op1=ALU.add),
                       reads=kt_ + ['onesb'], writes=[f'F1_{m}'])
                op('act', lambda e: e.activation(out=F2[:, m, 4:T + 4], in_=ps[ba][:, :], func=AF.Sigmoid, bias=vcol('b_a0', 0, m)),
                   reads=[f'ps{ba}', 'VT'], writes=[f'F2_{m}'])
                yield
                op('pool', lambda e: e.tensor_tensor(out=F3[:, m, :], in0=F1[:, m, :], in1=tt[:], op=ALU.subtract),
                   reads=[f'F1_{m}'] + kt_, writes=[f'F3_{m}'])
                op('act', lambda e: e.activation(out=SQ[:, m, :], in_=ps[bg][:, :], func=AF.Copy), reads=[f'ps{bg}'], writes=[f'SQ{m}'])
                yield

            for m0 in range(0, 8, 2):
                gens = [lo_chain(m0), lo_chain(m0 + 1)]
                while gens:
                    for g in list(gens):
                        try:
                            next(g)
                        except StopIteration:
                            gens.remove(g)
            if BSTOP[0] <= 0.5:
                return
            op('act', lambda e: e.activation(out=GC[:].rearrange("p (a c) -> p a c", a=8),
                                             in_=F1[:].rearrange("p a (c t) -> p a c t", c=4)[:, :, :, 127], func=AF.Exp, scale=-0.6065306597126334),
               reads=[f'F1_{c}' for c in range(8)], writes=['GC'])
            omk = lambda m: DV[:, 96 + m:97 + m]
            if BSTOP[0] <= 0.6:
                return
            TMf = RW[:, 0:2048].bitcast(F32)
            tsets = [(PT[0][:], PT[1][:], PT[2][:], PT[3][:], PTb, ['PT0'], ['PT1'], ['PT2'], ['PT3'], ['PTb']),
                     (xin[0][:, 0:512], xin[0][:, 512:1024], TMf[:, 0:512], TMf[:, 512:1024], LWA, ['xa'], ['xb'], ['tma'], ['tmb'], ['LWA0', 'LWA1'])]
            op('pool', lambda e: e.memset(SCR[:, 3:4], 0.0), reads=[], writes=['xin0', 'TM', 'xa', 'xb', 'tma', 'tmb'])
            def rr(gens):
                while gens:
                    for g in list(gens):
                        try:
                            next(g)
                        except StopIteration:
                            gens.remove(g)

            def k_chain(m, bk):
                pk = f'ps{bk}'
                t0, t1, t2, t3, tq, k0, k1, k2, k3, kq = tsets[m % 2]
                kkc = vcol('b_k_k', 0, m)
                op('act', lambda e: e.activation(out=t0, in_=ps[bk][:, :], func=AF.Copy, scale=kkc), reads=[pk, 'VT'], writes=k0)
                op('act', lambda e: e.activation(out=tq, in_=ps[bk][:, :], func=AF.Square, scale=kkc), reads=[pk, 'VT'], writes=kq)
                yield
                b2 = nbank()
                op('pe', lambda e: e.matmul(ps[b2][:, :], lhsT=BOb[:], rhs=tq, start=True, stop=True), reads=['BOb'] + kq, writes=[f'ps{b2}'])
                op('act', lambda e: e.activation(out=t2, in_=F3[:, m, :], func=AF.Exp, scale=-0.6065306597126334), reads=[f'F3_{m}'], writes=k2)
                yield
                op('act', lambda e: e.activation(out=t1, in_=ps[b2][:, :], func=AF.Ln, bias=1e-24), reads=[f'ps{b2}'], writes=k1)
                yield
                op('act', lambda e: e.activation(out=t1, in_=t1, func=AF.Exp, scale=-0.5), reads=k1, writes=k1)
                op('act', lambda e: e.activation(out=t3, in_=F1[:, m, :], func=AF.Exp, scale=0.6065306597126334), reads=[f'F1_{m}'], writes=k3)
                yield
                op('dve', lambda e: e.tensor_tensor(out=t0, in0=t0, in1=t1, op=ALU.mult), reads=k0 + k1, writes=k0)
                yield
                op('dve', lambda e: e.scalar_tensor_tensor(out=ar_kind(m, 0), in0=t0.rearrange("p (c t) -> p c t", c=4), scalar=-1.0,
                                                           in1=t2.rearrange("p (c t) -> p c t", c=4), op0=ALU.mult, op1=ALU.mult),
                   reads=k0 + k2, writes=[f'BH{2*m}', f'BH{2*m+1}'])
                op('dve', lambda e: e.tensor_scalar(out=t1, in0=F2[:, m, 4:T + 4], scalar1=vcol('b_k_a', 0, m), scalar2=omk(m), op0=ALU.mult, op1=ALU.add),
                   reads=[f'F2_{m}', 'VT', 'DV'] + k1, writes=k1)
                yield
                op('pool', lambda e: e.tensor_tensor(out=t0, in0=t0, in1=F2[:, m, 4:T + 4], op=ALU.mult), reads=k0 + [f'F2_{m}'], writes=k0)
                yield
                op('dve', lambda e: e.tensor_tensor(out=BH[:, 16 + m, :], in0=t0, in1=t3, op=ALU.mult), reads=k0 + k3, writes=[f'BH{16+m}'])
                op('dve', lambda e: e.tensor_tensor(out=F2[:, m, 4:T + 4], in0=ps[bk][:, :], in1=t1, op=ALU.mult),
                   reads=[pk, f'F2_{m}'] + k1, writes=[f'F2_{m}'])
                yield
                op('pool', lambda e: e.tensor_tensor(out=BH[:, 24 + m, :], in0=F2[:, m, 4:T + 4], in1=t3, op=ALU.mult),
                   reads=[f'F2_{m}'] + k3, writes=[f'BH{24+m}'])
                yield

            def r_chain(m, bk):
                pk = f'ps{bk}'
                t0, t1, t2, t3, tq, k0, k1, k2, k3, kq = tsets[m % 2]
                op('act', lambda e: e.activation(out=t2, in_=F1[:, m, :], func=AF.Exp, scale=-0.6065306597126334), reads=[f'F1_{m}'], writes=k2)
                yield
                op('dve', lambda e: e.tensor_tensor(out=ar_kind(m, 1), in0=ps[bk][:, :].rearrange("p (c t) -> p c t", c=4),
                                                    in1=t2.rearrange("p (c t) -> p c t", c=4), op=ALU.mult),
                   reads=[pk] + k2, writes=[f'BH{2*m}', f'BH{2*m+1}'])
                op('dve', lambda e: e.scalar_tensor_tensor(out=tq, in0=ps[bk][:, :], scalar=vcol('b_r_k', 0, m), in1=F2[:, m, 4:T + 4],
                                                           op0=ALU.mult, op1=ALU.mult),
                   reads=[pk, 'VT', f'F2_{m}'] + kq, writes=kq)
                yield
                b2 = nbank()
                op('pe', lambda e: e.matmul(ps[b2][:, :], lhsT=BOb[:], rhs=tq, start=True, stop=True), reads=['BOb'] + kq, writes=[f'ps{b2}'])
                yield
                op('act', lambda e: e.activation(out=F2[:, m, 4:T + 4], in_=ps[b2][:, :], func=AF.Copy), reads=[f'ps{b2}'], writes=[f'F2_{m}'])
                yield

            for pj in (2, 3):
                rgA, kA = w_next('rkvA', pj)
                rgB, kB = w_next('rkvB', pj)
                for pr in range(2):
                    gens = []
                    for ml in (2 * pr, 2 * pr + 1):
                        m = (pj - 2) * 4 + ml
                        bk = nbank()
                        mm16(rgA, kA, rgB, kB, ml * 128, 128, bk, 512)
                        gens.append(k_chain(m, bk))
                    rr(gens)
            if BSTOP[0] <= 0.7:
                return
            for pj in (0, 1):
                rgA, kA = w_next('rkvA', pj)
                rgB, kB = w_next('rkvB', pj)
                for pr in range(2):
                    gens = []
                    for ml in (2 * pr, 2 * pr + 1):
                        m = pj * 4 + ml
                        bk = nbank()
                        mm16(rgA, kA, rgB, kB, ml * 128, 128, bk, 512)
                        gens.append(r_chain(m, bk))
                    rr(gens)
            op('pool', lambda e: e.memset(SCR[:, 4:5], 0.0), reads=[], writes=['xin0', 'TM', 'xa', 'xb', 'tma', 'tmb'])
            if BSTOP[0] <= 0.8:
                return
            for pj in (4, 5):
                rgA, kA = w_next('rkvA', pj)
                rgB, kB = w_next('rkvB', pj)
                for ml in range(4):
                    m = (pj - 4) * 4 + ml
                    bk = nbank()
                    mm16(rgA, kA, rgB, kB, ml * 128, 128, bk, 512)
                    pk = f'ps{bk}'
                    import os as _os
                    _pm = int(_os.environ.get('P5MODE', '0'))
                    if _pm in (0, 1):
                        op('dve', lambda e, m=m, bk=bk: e.tensor_copy(out=XN[:, m, :], in_=ps[bk][:, :]), reads=[pk], writes=[f'XN{m}'])
                    if _pm in (0, 2):
                        op('dve', lambda e, m=m, bk=bk: e.tensor_tensor(out=F2[:, m, 4:T + 4], in0=ps[bk][:, :], in1=F2[:, m, 4:T + 4], op=ALU.mult),
                           reads=[pk, f'F2_{m}'], writes=[f'F2_{m}'])
            if BSTOP[0] <= 1:
                return
            TM4 = TM.rearrange("p (c k f) -> p c k f", c=4, k=4)
            op('pool', lambda e: e.tensor_copy(out=XL[:].rearrange("p (c o) -> p c o", o=1), in_=XNP[:, :, T + 7:T + 8]), reads=[f'XNP{c}' for c in range(8)], writes=['XL'])
            XNPf = XNP[:].rearrange("p c t -> p (c t)")
            sets = []
            for si in range(2):
                TBh = [TB[j][:].bitcast(FP16) for j in range(5)]
                if si == 0:
                    d = dict(Q=[TBh[0], TBh[1]], kQ=['TB0', 'TB1'], B5=TBh[4][:, 0:512], kB5='TB4s0',
                             U4=U4, Pb=Pb, AU=AU, Wt=Wt, kS=['U4', 'Pb', 'AU', 'Wt'])
                else:
                    d = dict(Q=[TBh[2], TBh[3]], kQ=['TB2', 'TB3'], B5=TBh[4][:, 512:1024], kB5='TB4s1',
                             U4=XNPf[:, 0:2048], Pb=XNPf[:, 2048:2560], AU=XNPf[:, 2560:3072], Wt=XNPf[:, 3072:3328], kS=['U4b', 'Pbb', 'AUb', 'Wtb'])
                sets.append(d)
            hk1 = [k + f'h{hf}' for k in sets[1]['kS'][1:] for hf in range(2)]
            op('pool', lambda e: e.memset(SCR[:, 0:1], 0.0), reads=[], writes=[f'XNP{c}' for c in range(8)] + sets[1]['kS'] + hk1)

            def head_prologue(fc, hs, st):
                hsl = slice(64 * hs, 64 * hs + 64)
                U4_ = st['U4']
                kU = [st['kS'][0]]
                At = lambda c: ARf[hsl, fc * 1024 + c * 256:fc * 1024 + c * 256 + 128]
                ARc = lambda c: ARf[hsl, fc * 1024 + c * 256:fc * 1024 + (c + 1) * 256]
                Bt = lambda c: BH[hsl, 16 + fc, c * 128:(c + 1) * 128]
                Kt = lambda c: BH[hsl, 24 + fc, c * 128:(c + 1) * 128]
                kAR = [f'BH{2*fc}', f'BH{2*fc+1}']
                bA = nbank(); reserved.add(bA)
                for c in range(4):
                    op('pe', lambda e, c=c: e.matmul(ps[bA][:, c * 128:(c + 1) * 128], lhsT=At(c), rhs=Bt(c), start=True, stop=True),
                       reads=kAR + [f'BH{16+fc}'], writes=[f'ps{bA}'], signal=(c == 3))
                bAT = nbank(); reserved.add(bAT)
                for c in range(4):
                    op('pe', lambda e, c=c: e.matmul(ps[bAT][:, c * 128:(c + 1) * 128], lhsT=Bt(c), rhs=At(c), start=True, stop=True),
                       reads=kAR + [f'BH{16+fc}'], writes=[f'ps{bAT}'], signal=(c == 3))
                for c in range(4):
                    bB = nbank()
                    op('pe', lambda e, c=c, bB=bB: e.matmul(ps[bB][:, 128:256], lhsT=Bt(c), rhs=ARc(c)[:, 128:256], start=True, stop=True),
                       reads=kAR + [f'BH{16+fc}'], writes=[f'ps{bB}'], signal=False)
                    op('pe', lambda e, c=c, bB=bB: e.matmul(ps[bB][:, 256:512], lhsT=Kt(c), rhs=ARc(c), start=True, stop=True),
                       reads=kAR + [f'BH{24+fc}'], writes=[f'ps{bB}'])
                    op('dve', lambda e, c=c, bB=bB: e.tensor_tensor(out=U4_[:, c * 512 + 128:(c + 1) * 512], in0=ps[bB][:, 128:512], in1=MK2[:, 128:512], op=ALU.mult),
                       reads=[f'ps{bB}', 'MK2'], writes=kU)
                return bA, bAT

            def half_steps(fc, hs, st, hf, bA, bAT, done):
                hsl = slice(64 * hs, 64 * hs + 64)
                fsl = slice(64 * hs, 64 * hs + 64)
                cs_ = slice(hf * 256, (hf + 1) * 256)
                QQ = st['Q'][hf]
                XX, TP = QQ[:, 0:512], QQ[:, 512:1024]
                B1, B2 = XX[:, 0:256], XX[:, 256:512]
                B3, B4 = TP[:, 0:256], TP[:, 256:512]
                B5 = st['B5'][:, cs_]
                kx = st['kQ'][hf]
                k1, k2, k3, k4, k5 = [kx + 'a'], [kx + 'b'], [kx + 'c'], [kx + 'd'], [st['kB5'] + f'h{hf}']
                dt_ = FP16
                idm = None
                U44 = st['U4'].rearrange("p (u k t) -> p u k t", u=4, k=4)
                Pb_ = st['Pb'][:, hf * 256:(hf + 1) * 256]
                AU_ = st['AU'][:, hf * 256:(hf + 1) * 256]
                Wt_ = st['Wt'][:, hf * 128:(hf + 1) * 128]
                kU = [st['kS'][0]]
                kP, kAU, kW = [[k + f'h{hf}'] for k in st['kS'][1:]]
                kAR = [f'BH{2*fc}', f'BH{2*fc+1}']
                v2 = lambda t: t.rearrange("p (u t) -> p u t", u=2)
                mskb = lambda j: MSK[:, j:j + 1, :].to_broadcast([128, 2, 128])
                idb = ident[:].rearrange("p (o t) -> p o t", o=1).to_broadcast([128, 2, 128])
                W_ = lambda t: t
                held = []

                def gbank():
                    bk_ = nbank()
                    reserved.add(bk_)
                    held.append(bk_)
                    return bk_

                def gfree(bk_):
                    reserved.discard(bk_)
                    held.remove(bk_)

                def mm2(bk_, col0, lhs, rhs, rk, acc=None, acck=None, last=True):
                    for u in range(2):
                        us = slice(u * 128, (u + 1) * 128)
                        os_ = slice(col0 + u * 128, col0 + (u + 1) * 128)
                        if acc is not None:
                            op('pe', lambda e: e.matmul(ps[bk_][:, os_], lhsT=W_(idm), rhs=W_(acc[:, us]), start=True, stop=False),
                               reads=acck + ['ident'], writes=[f'ps{bk_}'], signal=False)
                        op('pe', lambda e: e.matmul(ps[bk_][:, os_], lhsT=W_(lhs[:, us]), rhs=W_(rhs[:, us]), start=(acc is None), stop=True),
                           reads=rk, writes=[f'ps{bk_}'], signal=(last and u == 1))

                def cp(eng, dst, kd, bk_, col0, n):
                    if eng == 'act':
                        op('act', lambda e: e.activation(out=W_(dst), in_=ps[bk_][:, col0:col0 + n], func=AF.Copy), reads=[f'ps{bk_}'], writes=kd)
                    else:
                        op('dve', lambda e: e.tensor_copy(out=W_(dst), in_=ps[bk_][:, col0:col0 + n]), reads=[f'ps{bk_}'], writes=kd)

                op('dve', lambda e: e.tensor_tensor(out=v2(W_(B1)), in0=v2(ps[bA][:, cs_]), in1=mskb(0), op=ALU.mult), reads=[f'ps{bA}', 'MSK'], writes=k1)
                op('dve', lambda e: e.tensor_tensor(out=v2(W_(B2)), in0=v2(ps[bAT][:, cs_]), in1=mskb(1), op=ALU.mult), reads=[f'ps{bAT}', 'MSK'], writes=k2)
                e3 = 'pool'
                op(e3, lambda e: e.tensor_tensor(out=W_(TP[:, :]).rearrange("p (u t) -> p u t", u=4), in0=XX[:, :].rearrange("p (u t) -> p u t", u=4),
                                                 in1=ident[:].rearrange("p (o t) -> p o t", o=1).to_broadcast([128, 4, 128]), op=ALU.add),
                   reads=k1 + k2 + ['ident'], writes=k3 + k4)
                yield
                def tn_from_pt():
                    bt_ = gbank()
                    for u in range(2):
                        us = slice(u * 128, (u + 1) * 128)
                        op('pe', lambda e: e.transpose(out=ps[bt_][:, :].bitcast(FP16)[:, us], in_=B4[:, us], identity=IDH[:]),
                           reads=k4 + ['IDH'], writes=[f'ps{bt_}'], signal=(u == 1))
                    return bt_

                for lev in range(3):
                    bk_ = gbank()
                    mm2(bk_, 0, B2, B1, k1 + k2, last=False)
                    mm2(bk_, 256, B1, B2, k1 + k2)
                    yield
                    cp('act', XX[:, :], k1 + k2, bk_, 0, 512)
                    gfree(bk_)
                    yield
                    bk_ = gbank()
                    mm2(bk_, 0, B1, B4, k1 + k4)
                    yield
                    op('dve', lambda e: e.tensor_tensor(out=W_(B4), in0=ps[bk_][:, 0:256], in1=B4, op=ALU.add), reads=[f'ps{bk_}'] + k4, writes=k4)
                    gfree(bk_)
                    yield
                for kl in range(1, 4):
                    op('dve', lambda e, kl=kl: e.tensor_tensor(out=v2(W_(B5)), in0=v2(ps[bA][:, cs_]), in1=mskb(2 * kl), op=ALU.mult), reads=[f'ps{bA}', 'MSK'], writes=k5)
                    bt_ = tn_from_pt()
                    yield
                    op('act', lambda e: e.activation(out=B3, in_=ps[bt_][:, :].bitcast(FP16)[:, 0:256], func=AF.Copy), reads=[f'ps{bt_}'], writes=k3)
                    gfree(bt_)
                    bz = gbank()
                    mm2(bz, 0, B5, B4, k5 + k4)
                    yield
                    cp('act', B1, k1, bz, 0, 256)
                    gfree(bz)
                    yield
                    bz = gbank()
                    mm2(bz, 0, B3, B1, k3 + k1)
                    yield
                    if kl < 3:
                        op('dve', lambda e: e.tensor_tensor(out=W_(B4), in0=ps[bz][:, 0:256], in1=B4, op=ALU.add), reads=[f'ps{bz}'] + k4, writes=k4)
                    else:
                        op('dve', lambda e: e.tensor_tensor(out=Pb_, in0=ps[bz][:, 0:256], in1=B4, op=ALU.add), reads=[f'ps{bz}'] + k4, writes=kP)
                    gfree(bz)
                    yield
                done.append(1)
                if len(done) == 2:
                    reserved.discard(bA); reserved.discard(bAT)
                us2 = [2 * hf, 2 * hf + 1]
                bW = gbank()
                for j, u in enumerate(us2):
                    op('pe', lambda e: e.matmul(ps[bW][:, j * 64:(j + 1) * 64], lhsT=U44[:, u, 2, :], rhs=TM4[:, u, 3, fsl], start=True, stop=True),
                       reads=kU + ['TM'], writes=[f'ps{bW}'], signal=(j == 1))
                yield
                op('act', lambda e: e.activation(out=Wt_, in_=ps[bW][:, 0:128], func=AF.Copy), reads=[f'ps{bW}'], writes=kW)
                gfree(bW)
                yield
                bU = gbank()
                for j, u in enumerate(us2):
                    op('pe', lambda e: e.matmul(ps[bU][:, j * 128:j * 128 + 64], lhsT=Pb_[:, j * 128:(j + 1) * 128], rhs=TM4[:, u, 0, fsl], start=True, stop=True),
                       reads=kP + ['TM'], writes=[f'ps{bU}'], signal=False)
                    op('pe', lambda e: e.matmul(ps[bU][:, j * 128 + 64:(j + 1) * 128], lhsT=Pb_[:, j * 128:(j + 1) * 128], rhs=Wt_[:, j * 64:(j + 1) * 64], start=True, stop=True),
                       reads=kP + kW, writes=[f'ps{bU}'], signal=(j == 1))
                yield
                op('act', lambda e: e.activation(out=AU_, in_=ps[bU][:, 0:256], func=AF.Copy), reads=[f'ps{bU}'], writes=kAU)
                gfree(bU)
                yield
                bRY = gbank()
                for j, u in enumerate(us2):
                    op('pe', lambda e: e.matmul(ps[bRY][hsl, j * 128:(j + 1) * 128], lhsT=AU_[:, j * 128:j * 128 + 64], rhs=U44[:, u, 1, :], start=True, stop=True),
                       reads=kAU + kU, writes=[f'ps{bRY}'], signal=False)
                for j, u in enumerate(us2):
                    op('pe', lambda e: e.matmul(ps[bRY][hsl, 256 + j * 128:256 + (j + 1) * 128], lhsT=AU_[:, j * 128 + 64:(j + 1) * 128], rhs=U44[:, u, 1, :], start=True, stop=False),
                       reads=kAU + kU, writes=[f'ps{bRY}'], signal=False)
                    op('pe', lambda e: e.matmul(ps[bRY][hsl, 256 + j * 128:256 + (j + 1) * 128], lhsT=TM4[:, u, 3, fsl], rhs=U44[:, u, 3, :], start=False, stop=True),
                       reads=['TM'] + kU, writes=[f'ps{bRY}'], signal=(j == 1))
                yield
                tsl = slice(fc * 512 + hf * 256, fc * 512 + (hf + 1) * 256)
                op('dve', lambda e: e.tensor_tensor(out=RH[hsl, tsl].rearrange("p (c t) -> p c t", c=2),
                                                    in0=ps[bRY][hsl, 0:256].rearrange("p (c t) -> p c t", c=2),
                                                    in1=ARf[hsl, fc * 1024 + hf * 512:fc * 1024 + (hf + 1) * 512].rearrange("p (c k t) -> p c k t", c=2, k=2)[:, :, 1, :], op=ALU.add),
                   reads=[f'ps{bRY}'] + kAR, writes=[f'RH{fc}_{hs}_{hf}'])
                op('dve', lambda e: e.tensor_copy(out=F1[hsl, fc, hf * 256:(hf + 1) * 256], in_=ps[bRY][hsl, 256:512]), reads=[f'ps{bRY}'], writes=[f'F1_{fc}'])
                gfree(bRY)
                yield
                bGH = gbank()
                for j, u in enumerate(us2):
                    op('pe', lambda e: e.matmul(ps[bGH][hsl, j * 64:(j + 1) * 64], lhsT=AU_[:, j * 128:j * 128 + 64], rhs=TM4[:, u, 1, fsl], start=True, stop=True),
                       reads=kAU + ['TM'], writes=[f'ps{bGH}'], signal=False)
                for j, u in enumerate(us2):
                    op('pe', lambda e: e.matmul(ps[bGH][hsl, 128 + j * 64:128 + (j + 1) * 64], lhsT=TM4[:, u, 1, fsl], rhs=AU_[:, j * 128 + 64:(j + 1) * 128], start=True, stop=False),
                       reads=kAU + ['TM'], writes=[f'ps{bGH}'], signal=False)
                    op('pe', lambda e: e.matmul(ps[bGH][hsl, 128 + j * 64:128 + (j + 1) * 64], lhsT=TM4[:, u, 2, fsl], rhs=TM4[:, u, 3, fsl], start=False, stop=True),
                       reads=['TM'], writes=[f'ps{bGH}'], signal=(j == 1))
                yield
                gsl = slice(fc * 256 + hf * 128, fc * 256 + (hf + 1) * 128)
                op('dve', lambda e: e.tensor_tensor(out=GT[hsl, gsl], in0=ps[bGH][hsl, 0:128], in1=ID2[hsl, 0:128], op=ALU.add),
                   reads=[f'ps{bGH}', 'ID2'], writes=[f'GT{fc}_{hs}_{hf}'])
                op('dve', lambda e: e.tensor_tensor(out=HH[hsl, fc * 256 + hf * 128:fc * 256 + (hf + 1) * 128].rearrange("p (u i) -> p u i", u=2),
                                                    in0=ps[bGH][hsl, 128:256].rearrange("p (u i) -> p u i", u=2),
                                                    in1=GC[hsl, fc * 4 + 2 * hf:fc * 4 + 2 * hf + 2].rearrange("p (u o) -> p u o", o=1).to_broadcast([64, 2, 64]), op=ALU.mult),
                   reads=[f'ps{bGH}', 'GC'], writes=[f'HH{fc}_{hs}_{hf}'])
                gfree(bGH)
                yield

            fine = [f'{n}{fc}_{hs}_{hf}' for n in ('RH', 'GT', 'HH') for fc in range(8) for hs in range(2) for hf in range(2)]
            op('pool', lambda e: e.memset(SCR[:, 1:2], 0.0), reads=[], writes=f3k + fine)
            for fc in range(8):
                srcs = [lambda c, fc=fc: ARf[:, fc * 1024 + c * 256:fc * 1024 + c * 256 + 128],
                        lambda c, fc=fc: BH[:, 16 + fc, c * 128:(c + 1) * 128],
                        lambda c, fc=fc: BH[:, 24 + fc, c * 128:(c + 1) * 128],
                        lambda c, fc=fc: XN[:, fc, c * 128:(c + 1) * 128]]
                skeys = [[f'BH{2*fc}', f'BH{2*fc+1}'], [f'BH{16+fc}'], [f'BH{24+fc}'], [f'XN{fc}']]
                for half in range(2):
                    bk = nbank()
                    psb = ps[bk][:, :].bitcast(BF16)
                    for cl in range(2):
                        c = half * 2 + cl
                        for kind in range(4):
                            o = (cl * 4 + kind) * 128
                            op('pe', lambda e, c=c, kind=kind, o=o, psb=psb, srcs=srcs: e.transpose(out=psb[:, o:o + 128], in_=srcs[kind](c), identity=identb[:]),
                               reads=skeys[kind] + ['identb'], writes=[f'ps{bk}'], signal=(cl == 1 and kind == 3))
                    op('act', lambda e, half=half, psb=psb: e.activation(out=TM[:, half * 1024:(half + 1) * 1024], in_=psb, func=AF.Copy),
                       reads=[f'ps{bk}'], writes=['TM'])
                gens = []
                for hs in range(2):
                    bA_, bAT_ = head_prologue(fc, hs, sets[hs])
                    done = []
                    for hf in range(2):
                        gens.append(half_steps(fc, hs, sets[hs], hf, bA_, bAT_, done))
                if _os0.environ.get('SEQG', '0') == '1':
                    for g in gens:
                        for _ in g:
                            pass
                    gens = []
                _hs = int(_os0.environ.get('HSTOP', '999'))
                _rounds = 0
                while gens:
                    if _rounds >= _hs:
                        reserved.clear()
                        break
                    _rounds += 1
                    for g in list(gens):
                        try:
                            next(g)
                        except StopIteration:
                            gens.remove(g)
            op('pool', lambda e: e.memset(SCR[:, 2:3], 0.0), reads=[], writes=f3k + fine + [f'XNP{c}' for c in range(8)] + sets[1]['kS'] + hk1)
            if BSTOP[0] <= 2:
                return
            for c in range(4):
                bY = [nbank(), nbank()]
                bZ = nbank()
                for fc in range(8):
                    for hs in range(2):
                        hsl = slice(64 * hs, 64 * hs + 64)
                        op('pe', lambda e, fc=fc, hsl=hsl, c=c: e.matmul(ps[bY[fc // 4]][hsl, (fc % 4) * 128:(fc % 4 + 1) * 128], lhsT=STt[hsl, fc, :],
                                                                        rhs=RH[hsl, fc * 512 + c * 128:fc * 512 + (c + 1) * 128], start=True, stop=True),
                           reads=['STt'] + f3k, writes=[f'ps{bY[fc // 4]}'], signal=(fc % 4 == 3 and hs == 1))
                for fc in range(8):
                    for hs in range(2):
                        hsl = slice(64 * hs, 64 * hs + 64)
                        op('pe', lambda e, fc=fc, hsl=hsl, c=c: e.matmul(ps[bZ][hsl, fc * 64:(fc + 1) * 64], lhsT=GT[hsl, fc * 256 + c * 64:fc * 256 + (c + 1) * 64],
                                                                        rhs=STt[hsl, fc, :], start=True, stop=True),
                           reads=['STt'] + f3k, writes=[f'ps{bZ}'], signal=(fc == 7 and hs == 1))
                for half in range(2):
                    op('dve', lambda e, half=half, c=c: e.tensor_tensor(out=F1[:, half * 4:half * 4 + 4, c * 128:(c + 1) * 128],
                                                                        in0=ps[bY[half]][:, :].rearrange("p (f t) -> p f t", f=4),
                                                                        in1=F1[:, half * 4:half * 4 + 4, c * 128:(c + 1) * 128], op=ALU.add),
                       reads=[f'ps{bY[half]}'] + [f'F1_{f}' for f in range(half * 4, half * 4 + 4)], writes=[f'F1_{f}' for f in range(half * 4, half * 4 + 4)])
                for fc in range(8):
                    op('dve', lambda e, fc=fc, c=c: e.scalar_tensor_tensor(out=STt[:, fc, :], in0=ps[bZ][:, fc * 64:(fc + 1) * 64], scalar=GC[:, fc * 4 + c:fc * 4 + c + 1],
                                                                           in1=HH[:, fc * 256 + c * 64:fc * 256 + (c + 1) * 64], op0=ALU.mult, op1=ALU.add),
                       reads=[f'ps{bZ}', 'GC'] + f3k, writes=['STt'])
            if BSTOP[0] <= 3:
                return
            def gn_chain(m):
                ta, tb_, tq = (PT[0], PT[1], PTb) if m % 2 == 0 else (PT[2], PT[3], LWA)
                ka, kb, kq = (['PT0'], ['PT1'], ['PTb']) if m % 2 == 0 else (['PT2'], ['PT3'], ['LWA0', 'LWA1'])
                op('act', lambda e: e.activation(out=tq, in_=F1[:, m, :], func=AF.Copy), reads=[f'F1_{m}'], writes=kq)
                yield
                b1 = nbank()
                op('pe', lambda e: e.matmul(ps[b1][:, :], lhsT=BO64[:], rhs=tq, start=True, stop=True), reads=['BO64'] + kq, writes=[f'ps{b1}'])
                yield
                op('dve', lambda e: e.tensor_tensor(out=ta[:], in0=F1[:, m, :], in1=ps[b1][:, :], op=ALU.subtract), reads=[f'F1_{m}', f'ps{b1}'], writes=ka)
                yield
                op('act', lambda e: e.activation(out=tq, in_=ta[:], func=AF.Square), reads=ka + kq, writes=kq)
                yield
                b2 = nbank()
                op('pe', lambda e: e.matmul(ps[b2][:, :], lhsT=BO64[:], rhs=tq, start=True, stop=True), reads=['BO64'] + kq, writes=[f'ps{b2}'])
                yield
                op('act', lambda e: e.activation(out=tb_[:], in_=ps[b2][:, :], func=AF.Ln, bias=64e-5), reads=[f'ps{b2}'], writes=kb)
                yield
                op('act', lambda e: e.activation(out=tb_[:], in_=tb_[:], func=AF.Exp, scale=-0.5), reads=kb, writes=kb)
                yield
                op('dve', lambda e: e.scalar_tensor_tensor(out=ta[:], in0=ta[:], scalar=vcol('b_gn_g', 0, m), in1=tb_[:], op0=ALU.mult, op1=ALU.mult),
                   reads=ka + kb + ['VT'], writes=ka)
                yield
                op('dve', lambda e: e.scalar_tensor_tensor(out=ta[:], in0=ta[:], scalar=vcol('b_gn_b', 0, m), in1=F2[:, m, 4:T + 4], op0=ALU.add, op1=ALU.add),
                   reads=ka + ['VT', f'F2_{m}'], writes=ka)
                yield
                op('dve', lambda e: e.tensor_tensor(out=BH[:, m, :], in0=ta[:], in1=SQ[:, m, :], op=ALU.mult), reads=ka + [f'SQ{m}'], writes=[f'BH{m}'])
                yield

            for m0 in range(0, 8, 2):
                gens = [gn_chain(m0), gn_chain(m0 + 1)]
                while gens:
                    for g in list(gens):
                        try:
                            next(g)
                        except StopIteration:
                            gens.remove(g)
            proj('b_w_o', 2, lambda kc: BH[:, kc, :], lambda kc: f'BH{kc}', 8, evac_branch(None))
            post_norm(7)

        for b in range(NB):
            if nstage >= 2:
                mem_prep(b, [0, 1] if nstage >= 5 else [0])
            for i in range(NT):
                load_tile(b, i)
                if nstage >= 1:
                    stage_A(b, i)
                if nstage >= 2:
                    stage_C(0)
                if nstage >= 3:
                    stage_M(0)
                if nstage >= 4:
                    stage_B(b, i)
                if nstage >= 5:
                    stage_C(1)
                if nstage >= 6:
                    stage_M(1)
                store_tile(b, i)
        S_.finish('sp', ['y'])
        for k in ('xin0', 'ptio0', 'ptio1'):
            if k in S_.dsem:
                S_.ops['sp'].append(lambda e, semh=S_.dsem[k], v=S_.dcnt[k]: e.wait_ge(semh, v))
        S_.emit()
        build.nops = S_.nops
    return nc


def make_masks():
    t = np.arange(128)[:, None]
    s_ = np.arange(128)[None, :]
    low = t > s_
    ms = []
    m0 = low & (t // 16 == s_ // 16)
    ms += [m0, m0.T]
    for blk in (16, 32, 64):
        mk = (t // (2 * blk) == s_ // (2 * blk)) & ((t // blk) % 2 == 1) & ((s_ // blk) % 2 == 0)
        ms += [mk, mk.T]
    return np.ascontiguousarray(np.stack(ms, axis=1).astype(np.float32).reshape(128, 8 * 128))


def _prep_inputs(inp):
    vecs = pack_vecs(inp)
    wts = pack_weights(inp)
    return vecs, wts


def kernel(**inputs):
    NB, S = 4, 2048
    x = np.asarray(inputs['x'], np.float32)
    mem = np.asarray(inputs['mem'], np.float32)
    vecs, wts = _prep_inputs(inputs)
    nc = build(NB, S)
    in_maps = []
    for c in range(8):
        in_maps.append({"x": np.ascontiguousarray(x[c * NB:(c + 1) * NB].reshape(NB * S, D)),
                        "mem": np.ascontiguousarray(mem[c * NB:(c + 1) * NB].reshape(NB * MEM, D)),
                        "vecs": vecs, "wts": wts, "masks": make_masks()})
    res = run_bass_kernel_spmd(nc, in_maps, core_ids=list(range(8)))
    out = np.concatenate([r["y"].reshape(NB, S, D) for r in res.results], axis=0)
    return out.astype(np.float32)
```

```python
import numpy as np
from contextlib import ExitStack
import concourse.bass as bass
import concourse.mybir as mybir
from concourse.bass_utils import run_bass_kernel_spmd

F32 = mybir.dt.float32
BF16 = mybir.dt.bfloat16
FP16 = mybir.dt.float16
AF = mybir.ActivationFunctionType
ALU = mybir.AluOpType
AX = mybir.AxisListType
import os as _os0
TDT = mybir.dt.float32r if _os0.environ.get('TDT', 'r') == 'r' else mybir.dt.float32

D = 1024
T = 512
MEM = 256
PW = 4096
NRING = 3

VEC_ORDER = [('ln_gains', 12), ('mem_norm', 1), ('a_conv_w', 4), ('a_conv_b', 1), ('a_b_in', 2), ('a_gate_b', 2),
             ('a_lambda', 1), ('a_b_out', 1), ('b_mu', 6), ('b_w0', 1), ('b_a0', 1), ('b_k_k', 1), ('b_k_a', 1),
             ('b_r_k', 1), ('b_gn_g', 1), ('b_gn_b', 1)]
VOFF = {}
_o = 0
for _n, _c in VEC_ORDER:
    VOFF[_n] = _o
    _o += _c
NVEC = _o


def pack_vecs(inp):
    rows = [np.asarray(inp[n], np.float32).reshape(-1) for n, _ in VEC_ORDER]
    v = np.concatenate(rows)
    assert v.size == NVEC * D
    return np.ascontiguousarray(v.reshape(NVEC * 8, 128))


def _mat_pieces(W, MW):
    K, N = W.shape
    KC = K // 128
    NPc = N // MW
    a = W.reshape(KC, 128, NPc, MW).transpose(2, 1, 0, 3).reshape(NPc, 128, KC * MW)
    if KC * MW < PW:
        a = np.concatenate([a, np.zeros((NPc, 128, PW - KC * MW), np.float32)], axis=2)
    return a


PIECES = {}
PGAIN = []


def _layout():
    PIECES.clear()
    PGAIN.clear()

    def add(name, cnt, gain, KC, MW):
        PIECES[name] = (len(PGAIN), cnt)
        for _ in range(cnt):
            PGAIN.append((None if gain is None else [(0, MW, ('VT', gain))], KC, MW))

    g = VOFF['ln_gains']
    add('w_in', 4, g + 0, 8, 512)
    add('gates', 1, None, 16, 256)
    add('a_w_out', 2, None, 8, 512)
    for l in range(2):
        add(f'wq{l}', 2, g + 6 * l + 2, 8, 512)
        add(f'wkv{l}', 4, VOFF['mem_norm'], 8, 512)
        add(f'wo{l}', 2, None, 8, 512)
        add(f'up{l}', 8, g + 6 * l + 4, 8, 512)
        add(f'down{l}', 8, None, 32, 128)
    for var in range(2):
        PIECES['rkv' + 'AB'[var]] = (len(PGAIN), 6)
        for mix in (0, 0, 2, 2, 3, 3):
            PGAIN.append(([(0, 512, ('DV', 2 * mix + var))], 8, 512))
    for var in range(2):
        PIECES['loraA' + 'ab'[var]] = (len(PGAIN), 1)
        PGAIN.append(([(0, 64, ('DV', 2 * 1 + var)), (64, 128, ('DV', 2 * 4 + var)), (128, 256, ('DV', 2 * 5 + var))], 8, 256))
    add('loraB', 1, None, 3, 1024)
    add('b_w_o', 2, None, 8, 512)


_layout()
NPIECE = len(PGAIN)


def pack_weights(inp):
    f = lambda k: np.asarray(inp[k], np.float32)
    out = np.zeros((NPIECE, 128, PW), np.float32)

    def put(name, arr):
        i0, cnt = PIECES[name]
        assert arr.shape[0] == cnt, (name, arr.shape)
        out[i0:i0 + cnt] = arr

    put('w_in', _mat_pieces(f('a_w_in')[0], 512))
    gw = f('a_gate_w')[0].reshape(8, 2, 128, 256)
    put('gates', gw.transpose(2, 0, 1, 3).reshape(1, 128, 8 * 2 * 256))
    put('a_w_out', _mat_pieces(f('a_w_out')[0], 512))
    for l in range(2):
        put(f'wq{l}', _mat_pieces(f('c_w_q')[l], 512))
        put(f'wkv{l}', _mat_pieces(f('c_w_kv')[l], 512))
        put(f'wo{l}', _mat_pieces(f('c_w_o')[l], 512))
        put(f'up{l}', _mat_pieces(f('m_w_up')[l], 512))
        put(f'down{l}', _mat_pieces(f('m_w_down')[l], 128))
    rkv = f('b_w_rkv')[0]
    rk6 = np.concatenate([_mat_pieces(rkv[i], 512) for i in range(3)], axis=0)
    put('rkvA', rk6)
    put('rkvB', rk6)
    la = np.concatenate([f('b_w1')[0], f('b_a1')[0], f('b_g1')[0]], axis=1)
    put('loraAa', _mat_pieces(la, 256))
    put('loraAb', _mat_pieces(la, 256))
    lb = np.zeros((128, 3, 1024), np.float32)
    lb[:64, 0] = f('b_w2')[0]
    lb[64:, 1] = f('b_a2')[0]
    lb[:, 2] = f('b_g2')[0]
    put('loraB', np.concatenate([lb.reshape(1, 128, 3072), np.zeros((1, 128, PW - 3072), np.float32)], axis=2))
    put('b_w_o', _mat_pieces(f('b_w_o')[0], 512))
    return out


class _Rec:
    def __init__(self):
        self.name = None

    def __getattr__(self, name):
        def f(*args, **kwargs):
            self.name, self.args, self.kwargs = name, args, kwargs
            return self
        return f


class Sched:
    ENG = ('pe', 'act', 'dve', 'pool', 'sp')

    def __init__(self, nc, es):
        self.nc = nc
        self.es = es
        self.ops = {e: [] for e in self.ENG}
        self.sem = {e: es.enter_context(nc.semaphore('s_' + e)) for e in self.ENG}
        self.cnt = {e: 0 for e in self.ENG}
        self.known = {e: {} for e in self.ENG}
        self.last_w = {}
        self.reads = {}
        self.dsem = {}
        self.dcnt = {}
        self.nops = 0

    def _deps(self, eng, reads, writes):
        acc = {}

        def need(dep):
            s, v = dep
            if acc.get(s, 0) < v:
                acc[s] = v
        for b in reads:
            w = self.last_w.get(b)
            if w:
                need(w)
        for b in writes:
            w = self.last_w.get(b)
            if w:
                need(w)
            for r in self.reads.get(b, {}).items():
                need(r)
        for s, v in acc.items():
            if self.known[eng].get(s, 0) >= v:
                continue
            if s == eng and eng in ('pe', 'sp'):
                continue
            if s in self.cnt:
                assert v <= self.cnt[s], f"wait on unsignaled {s} {v} > {self.cnt[s]}"
                semh = self.sem[s]
            else:
                semh = self.dsem[s]
            self.known[eng][s] = v
            self.ops[eng].append(lambda e, semh=semh, v=v: e.wait_ge(semh, v))

    def _record(self, reads, writes, tag):
        for b in reads:
            d = self.reads.setdefault(b, {})
            if d.get(tag[0], 0) < tag[1]:
                d[tag[0]] = tag[1]
        for b in writes:
            self.last_w[b] = tag
            self.reads[b] = {}

    def op(self, eng, fn, reads=(), writes=(), signal=True):
        self.nops += 1
        self._deps(eng, reads, writes)
        val = self.cnt[eng] + 1
        rec = _Rec()
        fn(rec)
        assert rec.name is not None
        if signal:
            self.cnt[eng] += 1
            semh = self.sem[eng]
            self.ops[eng].append(lambda e, r=rec, semh=semh: getattr(e, r.name)(*r.args, **r.kwargs).then_inc(semh, 1))
        else:
            self.ops[eng].append(lambda e, r=rec: getattr(e, r.name)(*r.args, **r.kwargs))
        self._record(reads, writes, (eng, val))

    def dma(self, eng, out, in_, reads=(), writes=(), key=None):
        self.nops += 1
        self._deps(eng, reads, writes)
        if key not in self.dsem:
            self.dsem[key] = self.es.enter_context(self.nc.semaphore('d_' + key))
            self.dcnt[key] = 0
        self.dcnt[key] += 16
        semh = self.dsem[key]
        self.ops[eng].append(lambda e, out=out, in_=in_, semh=semh: e.dma_start(out=out, in_=in_).then_inc(semh, 16))
        self._record(reads, writes, (key, self.dcnt[key]))

    def barrier(self):
        snap = dict(self.cnt)
        dsnap = dict(self.dcnt)
        for eng in self.ENG:
            for s, v in snap.items():
                if s == eng or v == 0 or self.known[eng].get(s, 0) >= v:
                    continue
                self.known[eng][s] = v
                self.ops[eng].append(lambda e, semh=self.sem[s], v=v: e.wait_ge(semh, v))
            for s, v in dsnap.items():
                if self.known[eng].get(s, 0) >= v:
                    continue
                self.known[eng][s] = v
                self.ops[eng].append(lambda e, semh=self.dsem[s], v=v: e.wait_ge(semh, v))

    def finish(self, eng, keys):
        acc = {}
        for b in keys:
            w = self.last_w.get(b)
            if w and acc.get(w[0], 0) < w[1]:
                acc[w[0]] = w[1]
        for s, v in acc.items():
            semh = self.sem[s] if s in self.cnt else self.dsem[s]
            self.ops[eng].append(lambda e, semh=semh, v=v: e.wait_ge(semh, v))

    def emit(self):
        with self.nc.Block() as block:
            @block.tensor
            def _(e):
                for f in self.ops['pe']:
                    f(e)

            @block.scalar
            def _(e):
                for f in self.ops['act']:
                    f(e)

            @block.vector
            def _(e):
                for f in self.ops['dve']:
                    f(e)

            @block.gpsimd
            def _(e):
                for f in self.ops['pool']:
                    f(e)

            @block.sync
            def _(e):
                for f in self.ops['sp']:
                    f(e)


STAGES = ['load', 'A', 'C0', 'M0', 'B', 'C1', 'M1']
BSTOP = [9]


def build(NB, S, stop='M1', use_gelu=True):
    NT = S // T
    nstage = STAGES.index(stop)
    nc = bass.Bass("TRN2", target_bir_lowering=False)
    x_d = nc.dram_tensor("x", [NB * S, D], F32, kind="ExternalInput").ap()
    mem_d = nc.dram_tensor("mem", [NB * MEM, D], F32, kind="ExternalInput").ap()
    vec_d = nc.dram_tensor("vecs", [NVEC * 8, 128], F32, kind="ExternalInput").ap()
    wts_d = nc.dram_tensor("wts", [NPIECE, 128, PW], F32, kind="ExternalInput").ap()
    msk_d = nc.dram_tensor("masks", [128, 8 * 128], F32, kind="ExternalInput").ap()
    y_d = nc.dram_tensor("y", [NB * S, D], F32, kind="ExternalOutput").ap()
    wsc = nc.dram_tensor("wsc", [NPIECE, 128, PW], BF16, kind="Internal").ap()

    with ExitStack() as es:
        S_ = Sched(nc, es)
        op = S_.op

        def sb(name, shape, dt):
            return es.enter_context(nc.sbuf_tensor(name, shape, dt))

        VT = sb("VT", [128, NVEC * 8], F32)
        ident = sb("ident", [128, 128], F32)
        identb = sb("identb", [128, 128], BF16)
        onesb = sb("onesb", [128, 128], BF16)
        CV2 = sb("CV2", [128, 16], F32)
        DV = sb("DV", [128, 13 * 8], F32)
        ps = [es.enter_context(nc.psum_tensor(f"ps{i}", [128, 512], F32)) for i in range(8)]
        bank_ctr = [0]

        reserved = set()

        def nbank():
            for _ in range(17):
                b = bank_ctr[0] % 8
                bank_ctr[0] += 1
                if b not in reserved:
                    return b
            raise RuntimeError('no free PSUM bank')

        def vcol(name, idx=0, c=0):
            j = (VOFF[name] + idx) * 8 + c
            return VT[:, j:j + 1]

        op('pool', lambda e: e.memset(ident[:], 0.0), writes=['ident'])
        op('pool', lambda e: e.affine_select(out=ident[:], in_=ident[:], pattern=[[-1, 128]], base=0,
                                             channel_multiplier=1, compare_op=ALU.not_equal, fill=1.0),
           reads=['ident'], writes=['ident'])
        op('pool', lambda e: e.tensor_copy(out=identb[:], in_=ident[:]), reads=['ident'], writes=['identb'])
        op('pool', lambda e: e.memset(onesb[:], 1.0), writes=['onesb'])

        with ExitStack() as es0:
            def sb0(name, shape, dt):
                return es0.enter_context(nc.sbuf_tensor(name, shape, dt))
            vst = [sb0(f"vst{i}", [128, 128], F32) for i in range(3)]
            nrows = NVEC * 8
            for i in range(3):
                r0 = i * 128
                r1 = min(nrows, r0 + 128)
                n = r1 - r0
                S_.dma('sp', vst[i][0:n, :], vec_d[r0:r1, :], writes=[f'vst{i}'], key=f'vst{i}')
                b = nbank()
                op('pe', lambda e, i=i, n=n, b=b: e.transpose(out=ps[b][:, 0:n], in_=vst[i][0:n, :], identity=ident[0:n, 0:n]),
                   reads=[f'vst{i}', 'ident'], writes=[f'ps{b}'])
                op('act', lambda e, r0=r0, n=n, b=b: e.activation(out=VT[:, r0:r0 + n], in_=ps[b][:, 0:n], func=AF.Copy),
                   reads=[f'ps{b}'], writes=['VT'])
            lam = VT[:, VOFF['a_lambda'] * 8:VOFF['a_lambda'] * 8 + 8]
            op('act', lambda e: e.activation(out=CV2[:, 0:8], in_=lam, func=AF.Exp, scale=-1.0), reads=['VT'], writes=['CV2'])
            op('act', lambda e: e.activation(out=CV2[:, 0:8], in_=CV2[:, 0:8], func=AF.Ln, bias=1.0), reads=['CV2'], writes=['CV2'])
            op('act', lambda e: e.activation(out=CV2[:, 0:8], in_=CV2[:, 0:8], func=AF.Copy, scale=-8.0), reads=['CV2'], writes=['CV2'])

            g6 = VT[:, (VOFF['ln_gains'] + 6) * 8:(VOFF['ln_gains'] + 6) * 8 + 8]
            for mi in range(6):
                mu_i = VT[:, (VOFF['b_mu'] + mi) * 8:(VOFF['b_mu'] + mi) * 8 + 8]
                op('dve', lambda e, mi=mi, mu_i=mu_i: e.tensor_tensor(out=DV[:, (2 * mi + 1) * 8:(2 * mi + 2) * 8], in0=mu_i, in1=g6, op=ALU.mult),
                   reads=['VT'], writes=['DV'])
                op('dve', lambda e, mi=mi: e.tensor_tensor(out=DV[:, (2 * mi) * 8:(2 * mi + 1) * 8], in0=g6, in1=DV[:, (2 * mi + 1) * 8:(2 * mi + 2) * 8], op=ALU.subtract),
                   reads=['VT', 'DV'], writes=['DV'])
            ka = VT[:, VOFF['b_k_a'] * 8:VOFF['b_k_a'] * 8 + 8]
            op('dve', lambda e: e.tensor_scalar(out=DV[:, 96:104], in0=ka, scalar1=-1.0, scalar2=1.0, op0=ALU.mult, op1=ALU.add),
               reads=['VT'], writes=['DV'])

            NST = 3
            stf = [sb0(f"stf{i}", [128, PW], F32) for i in range(NST)]
            stb = [sb0(f"stb{i}", [128, PW], BF16) for i in range(NST)]
            for pi in range(NPIECE):
                k = pi % NST
                gain, KC, MW = PGAIN[pi]
                S_.dma('sp', stf[k][:], wts_d[pi], writes=[f'stf{k}'], key=f'stf{k}')
                eng = ('dve', 'pool')[pi % 2] if gain is not None else ('act', 'dve', 'pool')[pi % 3]
                if gain is None:
                    if eng == 'act':
                        op('act', lambda e, k=k: e.activation(out=stb[k][:], in_=stf[k][:], func=AF.Copy),
                           reads=[f'stf{k}'], writes=[f'stb{k}'])
                    else:
                        op(eng, lambda e, k=k: e.tensor_copy(out=stb[k][:], in_=stf[k][:]),
                           reads=[f'stf{k}'], writes=[f'stb{k}'])
                else:
                    for kc in range(KC):
                        for (c0, c1, (tab, gi)) in gain:
                            gc = (VT if tab == 'VT' else DV)[:, gi * 8 + kc:gi * 8 + kc + 1]
                            op(eng, lambda e, k=k, kc=kc, MW=MW, gc=gc, c0=c0, c1=c1: e.tensor_scalar(
                                out=stb[k][:, kc * MW + c0:kc * MW + c1], in0=stf[k][:, kc * MW + c0:kc * MW + c1],
                                scalar1=gc, scalar2=1.0, op0=ALU.mult, op1=ALU.mult),
                               reads=[f'stf{k}', 'VT', 'DV'], writes=[f'stb{k}'])
                S_.dma('act', wsc[pi], stb[k][:], reads=[f'stb{k}'], writes=['wsc'], key=f'wsc{k}')
        S_.barrier()

        import os as _os2
        _ex = int(_os2.environ.get('EXTRA_SBUF', '0'))
        if _ex:
            DUMMY = sb('DUMMY', [128, _ex * 256], F32)
            op('pool', lambda e: e.memset(DUMMY[:, _ex * 256 - 512:], 1.0), writes=['DUMMY'])
        X = sb("X", [128, 8, T], F32)
        XN = sb("XN", [128, 8, T], BF16)
        SQ = sb("SQ", [128, 8, T], BF16)
        RS = sb("RS", [128, T], F32)
        F1 = sb("F1", [128, 8, T], F32)
        F2 = sb("F2", [128, 8, T + 4], F32)
        F3 = sb("F3", [128, 8, T], F32)
        BH = sb("BH", [128, 32, T], BF16)
        PT = [sb(f"PT{i}", [128, T], F32) for i in range(4)]
        TB = [sb(f"TB{i}", [128, T], F32) for i in range(5)]
        ring = [sb(f"ring{i}", [128, PW], BF16) for i in range(NRING)]
        xin = [sb("xin0", [128, D], F32)] * 2
        KT = [sb(f"KT{l}", [128, 8, MEM], BF16) for l in range(2)]
        VV = [sb(f"VV{l}", [128, 2, D], BF16) for l in range(2)]
        HST = sb("HST", [128, 8], F32)
        SMX = sb("SMX", [128, 72], F32)


        XNP = sb("XNP", [128, 8, T + 8], BF16)
        RW = sb("RW", [128, 6400], BF16)
        STt = sb("STt", [128, 8, 64], BF16)
        GC = sb("GC", [128, 32], F32)
        XL = sb("XL", [128, 8], BF16)
        SCR = sb("SCR", [128, 8], F32)
        IDH = sb("IDH", [128, 128], FP16)
        MSK = sb("MSK", [128, 8, 128], BF16)
        MK2 = sb("MK2", [128, 512], BF16)
        ID2 = sb("ID2", [128, 256], BF16)
        BOb = sb("BOb", [128, 128], BF16)
        BO64 = sb("BO64", [128, 128], BF16)
        S_.dma('sp', xin[0][:], msk_d, writes=['xin0'], key='xin0')
        op('dve', lambda e: e.tensor_copy(out=MSK[:].rearrange("p a t -> p (a t)"), in_=xin[0][:]), reads=['xin0'], writes=['MSK'])
        op('dve', lambda e: e.tensor_copy(out=IDH[:], in_=ident[:]), reads=['ident'], writes=['IDH'])
        op('pool', lambda e: e.memset(MK2[:], 1.0), writes=['MK2'])
        for kind in range(4):
            op('pool', lambda e, kind=kind: e.affine_select(out=MK2[:, kind * 128:(kind + 1) * 128], in_=MK2[:, kind * 128:(kind + 1) * 128],
                                                            pattern=[[1, 128]], base=0, channel_multiplier=-1,
                                                            compare_op=(ALU.is_gt if kind % 2 == 0 else ALU.is_ge), fill=0.0),
               reads=['MK2'], writes=['MK2'])
        op('pool', lambda e: e.memset(ID2[:], 0.0), writes=['ID2'])
        for hs in range(2):
            op('pool', lambda e, hs=hs: e.affine_select(out=ID2[64 * hs:64 * hs + 64, :], in_=ID2[64 * hs:64 * hs + 64, :],
                                                        pattern=[[0, 4], [-1, 64]], base=0, channel_multiplier=1,
                                                        compare_op=ALU.not_equal, fill=1.0), reads=['ID2'], writes=['ID2'])
        op('pool', lambda e: e.memset(BOb[:], 0.0), writes=['BOb'])
        op('pool', lambda e: e.memset(BO64[:], 0.0), writes=['BO64'])
        for hs in range(2):
            op('pool', lambda e, hs=hs: e.memset(BOb[64 * hs:64 * hs + 64, 64 * hs:64 * hs + 64], 1.0), reads=['BOb'], writes=['BOb'])
            op('pool', lambda e, hs=hs: e.memset(BO64[64 * hs:64 * hs + 64, 64 * hs:64 * hs + 64], 1.0 / 64.0), reads=['BO64'], writes=['BO64'])

        def bhb(i):
            return BH[:, 8 * i:8 * (i + 1), :]

        BHF = BH[:].rearrange("p a t -> p (a t)").bitcast(F32)

        def bhf(i, c):
            o = (i * 8 + c) * T
            return BHF[:, o:o + T]

        def gk(i, c):
            k = i * 8 + c
            return [f'BH{2 * k}', f'BH{2 * k + 1}']

        seq = []
        for b in range(NB):
            if nstage >= 2:
                seq += [('wkv0', i) for i in range(4)]
            if nstage >= 5:
                seq += [('wkv1', i) for i in range(4)]
            for i in range(NT):
                if nstage >= 1:
                    seq += [('w_in', j) for j in (2, 3, 0, 1)] + [('gates', 0)] + [('a_w_out', j) for j in range(2)]
                if nstage >= 2:
                    seq += [('wq0', j) for j in range(2)] + [('wo0', j) for j in range(2)]
                if nstage >= 3:
                    seq += [('up0', j) for j in range(8)] + [('down0', j) for j in range(8)]
                if nstage >= 4:
                    seq += [('loraAa', 0), ('loraAb', 0), ('loraB', 0)]
                    for pj in (2, 3, 0, 1, 4, 5):
                        seq += [('rkvA', pj), ('rkvB', pj)]
                    seq += [('b_w_o', j) for j in range(2)]
                if nstage >= 5:
                    seq += [('wq1', j) for j in range(2)] + [('wo1', j) for j in range(2)]
                if nstage >= 6:
                    seq += [('up1', j) for j in range(8)] + [('down1', j) for j in range(8)]
        wstate = {'issued': 0, 'used': 0}

        def w_issue():
            k = wstate['issued']
            if k >= len(seq):
                return
            name, j = seq[k]
            pi = PIECES[name][0] + j
            slot = k % NRING
            S_.dma('sp', ring[slot][:], wsc[pi], writes=[f'ring{slot}'], key=f'ring{slot}')
            wstate['issued'] += 1

        def w_next(name, j):
            k = wstate['used']
            assert seq[k] == (name, j), (seq[k], name, j)
            prev_live = k >= 1 and seq[k - 1][0] in ('rkvA', 'loraAa') and seq[k][0] in ('rkvB', 'loraAb')
            retired = k - 2 if prev_live else k - 1
            while wstate['issued'] < min(len(seq), retired + NRING + 1):
                w_issue()
            wstate['used'] += 1
            slot = k % NRING
            return ring[slot], f'ring{slot}'

        def ones_norm(src_keys):
            b = nbank()
            for c in range(8):
                op('pe', lambda e, c=c, b=b: e.matmul(ps[b][:, :], lhsT=onesb[:], rhs=SQ[:, c, :], start=(c == 0), stop=(c == 7)),
                   reads=['onesb'] + [f'SQ{c}'], writes=[f'ps{b}'], signal=(c == 7))
            op('act', lambda e, b=b: e.activation(out=PT[3][:], in_=ps[b][:, :], func=AF.Ln, scale=1.0 / D, bias=1e-6),
               reads=[f'ps{b}'], writes=['PT3'])
            op('act', lambda e: e.activation(out=RS[:], in_=PT[3][:], func=AF.Exp, scale=-0.5), reads=['PT3'], writes=['RS'])

        def norm_in():
            op('act', lambda e: e.activation(out=SQ[:], in_=X[:], func=AF.Square),
               reads=[f'X{c}' for c in range(8)], writes=[f'SQ{c}' for c in range(8)])
            ones_norm(None)
            for c in range(8):
                eng = 'pool' if c % 3 == 2 else 'dve'
                op(eng, lambda e, c=c: e.tensor_tensor(out=XN[:, c, :], in0=X[:, c, :], in1=RS[:], op=ALU.mult),
                   reads=[f'X{c}', 'RS'], writes=[f'XN{c}'])

        def post_norm(gidx):
            ones_norm(None)
            for c in range(8):
                op('pool', lambda e, c=c: e.tensor_tensor(out=F1[:, c, :], in0=F1[:, c, :], in1=RS[:], op=ALU.mult),
                   reads=[f'F1_{c}', 'RS'], writes=[f'F1_{c}'])
                gc = vcol('ln_gains', gidx, c)
                op('dve', lambda e, c=c, gc=gc: e.scalar_tensor_tensor(out=X[:, c, :], in0=F1[:, c, :], scalar=gc, in1=X[:, c, :],
                                                                        op0=ALU.mult, op1=ALU.add),
                   reads=[f'F1_{c}', f'X{c}', 'VT'], writes=[f'X{c}'])

        def proj(wname, npieces, src, srckeys, KC, evac, mper=4, n=T, order=None):
            for pj in (order if order is not None else range(npieces)):
                rg, rkey = w_next(wname, pj)
                MW = mper * 128
                for ml in range(mper):
                    m = pj * mper + ml
                    b = nbank()
                    for kc in range(KC):
                        op('pe', lambda e, rg=rg, kc=kc, ml=ml, b=b, MW=MW: e.matmul(
                            ps[b][:, 0:n], lhsT=rg[:, kc * MW + ml * 128:kc * MW + (ml + 1) * 128], rhs=src(kc),
                            start=(kc == 0), stop=(kc == KC - 1)),
                           reads=[rkey, srckeys(kc)], writes=[f'ps{b}'], signal=(kc == KC - 1))
                    evac(m, ps[b][:, 0:n], f'ps{b}')

        def evac_branch(bias_name):
            def ev(m, p, pk):
                if bias_name is None:
                    op('act', lambda e, m=m, p=p: e.activation(out=F1[:, m, :], in_=p, func=AF.Copy),
                       reads=[pk], writes=[f'F1_{m}'])
                    op('act', lambda e, m=m, p=p: e.activation(out=SQ[:, m, :], in_=p, func=AF.Square),
                       reads=[pk], writes=[f'SQ{m}'])
                else:
                    bc = vcol(bias_name, 0, m)
                    op('act', lambda e, m=m, p=p, bc=bc: e.activation(out=F1[:, m, :], in_=p, func=AF.Identity, bias=bc),
                       reads=[pk, 'VT'], writes=[f'F1_{m}'])
                    op('act', lambda e, m=m, p=p, bc=bc: e.activation(out=SQ[:, m, :], in_=p, func=AF.Square, bias=bc),
                       reads=[pk, 'VT'], writes=[f'SQ{m}'])
            return ev

        xkeys = [f'X{c}' for c in range(8)]

        def load_tile(b, i):
            for tb in range(4):
                r0 = b * S + i * T + tb * 128
                if tb % 2 == 0:
                    srcs = [xin[0][:, 0:512], xin[0][:, 512:1024]]
                    bkeys = ['xin0', 'xin0']
                    S_.dma('sp', xin[0][:], x_d[r0:r0 + 128, :], writes=['xin0'], key='xin0')
                else:
                    srcs = [PT[0][:], PT[1][:]]
                    bkeys = ['PT0', 'PT1']
                    for h_ in range(2):
                        S_.dma('sp', PT[h_][:], x_d[r0:r0 + 128, h_ * 512:(h_ + 1) * 512], writes=[bkeys[h_]], key=f'ptio{h_}')
                for half in range(2):
                    bk = nbank()
                    for cl in range(4):
                        c = half * 4 + cl
                        op('pe', lambda e: e.transpose(out=ps[bk][:, cl * 128:(cl + 1) * 128], in_=srcs[half][:, cl * 128:(cl + 1) * 128], identity=ident[:]),
                           reads=[bkeys[half], 'ident'], writes=[f'ps{bk}'], signal=(cl == 3))
                    op('act', lambda e: e.activation(
                        out=X[:, half * 4:half * 4 + 4, tb * 128:(tb + 1) * 128],
                        in_=ps[bk][:, :].rearrange("p (c t) -> p c t", c=4), func=AF.Copy),
                       reads=[f'ps{bk}'], writes=[f'X{c}' for c in range(half * 4, half * 4 + 4)])

        def store_tile(b, i):
            for tb in range(4):
                r0 = b * S + i * T + tb * 128
                if tb % 2 == 0:
                    dsts = [xin[0][:, 0:512], xin[0][:, 512:1024]]
                    bkeys = ['xin0', 'xin0']
                else:
                    dsts = [PT[0][:], PT[1][:]]
                    bkeys = ['PT0', 'PT1']
                for half in range(2):
                    bk = nbank()
                    for cl in range(4):
                        c = half * 4 + cl
                        op('pe', lambda e: e.transpose(out=ps[bk][:, cl * 128:(cl + 1) * 128], in_=X[:, c, tb * 128:(tb + 1) * 128], identity=ident[:]),
                           reads=[f'X{c}', 'ident'], writes=[f'ps{bk}'], signal=(cl == 3))
                    op('act', lambda e: e.activation(out=dsts[half], in_=ps[bk][:, :], func=AF.Copy),
                       reads=[f'ps{bk}'], writes=[bkeys[half]])
                if tb % 2 == 0:
                    S_.dma('sp', y_d[r0:r0 + 128, :], xin[0][:], reads=['xin0'], writes=['y'], key='xin0')
                else:
                    for h_ in range(2):
                        S_.dma('sp', y_d[r0:r0 + 128, h_ * 512:(h_ + 1) * 512], PT[h_][:], reads=[bkeys[h_]], writes=['y'], key=f'ptio{h_}')

        def stage_A(b, i):
            norm_in()
            vb = VOFF['a_b_in']

            def ev_in(m, p, pk):
                if m < 8:
                    bc = VT[:, vb * 8 + m:vb * 8 + m + 1]
                    if use_gelu:
                        op('act', lambda e, m=m, p=p, bc=bc: e.activation(out=F1[:, m, :], in_=p, func=AF.Gelu_apprx_tanh, bias=bc),
                           reads=[pk, 'VT'], writes=[f'F1_{m}'])
                    else:
                        op('act', lambda e, m=m, p=p, bc=bc: e.activation(out=F1[:, m, :], in_=p, func=AF.Identity, bias=bc),
                           reads=[pk, 'VT'], writes=[f'F1_{m}'])
                        op('pool', lambda e, m=m: e.tensor_tensor(out=PT[0][:], in0=F1[:, m, :], in1=F1[:, m, :], op=ALU.mult),
                           reads=[f'F1_{m}'], writes=['PT0'])
                        op('dve', lambda e: e.tensor_scalar(out=PT[0][:], in0=PT[0][:], scalar1=0.044715, scalar2=1.0, op0=ALU.mult, op1=ALU.add),
                           reads=['PT0'], writes=['PT0'])
                        op('pool', lambda e, m=m: e.tensor_tensor(out=PT[0][:], in0=PT[0][:], in1=F1[:, m, :], op=ALU.mult),
                           reads=[f'F1_{m}', 'PT0'], writes=['PT0'])
                        op('act', lambda e: e.activation(out=PT[0][:], in_=PT[0][:], func=AF.Sigmoid, scale=1.5957691216057308),
                           reads=['PT0'], writes=['PT0'])
                        op('dve', lambda e, m=m: e.tensor_tensor(out=F1[:, m, :], in0=F1[:, m, :], in1=PT[0][:], op=ALU.mult),
                           reads=[f'F1_{m}', 'PT0'], writes=[f'F1_{m}'])
                else:
                    c = m - 8
                    bc = VT[:, vb * 8 + m:vb * 8 + m + 1]
                    op('act', lambda e, c=c, p=p, bc=bc: e.activation(out=F2[:, c, 4:T + 4], in_=p, func=AF.Identity, bias=bc),
                       reads=[pk, 'VT'], writes=[f'F2_{c}'])
                    cw = [vcol('a_conv_w', k, c) for k in range(4)]
                    cb = vcol('a_conv_b', 0, c)
                    op('dve', lambda e, c=c, cw=cw, cb=cb: e.tensor_scalar(out=F3[:, c, :], in0=F2[:, c, 1:T + 1], scalar1=cw[0], scalar2=cb,
                                                                        op0=ALU.mult, op1=ALU.add),
                       reads=[f'F2_{c}', 'VT'], writes=[f'F3_{c}'])
                    for k in range(1, 4):
                        op('dve', lambda e, c=c, k=k, cw=cw: e.scalar_tensor_tensor(out=F3[:, c, :], in0=F2[:, c, 1 + k:T + 1 + k], scalar=cw[k],
                                                                                in1=F3[:, c, :], op0=ALU.mult, op1=ALU.add),
                           reads=[f'F2_{c}', f'F3_{c}', 'VT'], writes=[f'F3_{c}'])
                    op('pool', lambda e, c=c: e.tensor_copy(out=F2[:, c, 1:4], in_=F2[:, c, T + 1:T + 4]),
                       reads=[f'F2_{c}'], writes=[f'F2_{c}'])
                    op('dve', lambda e, c=c: e.tensor_copy(out=SQ[:, c, :], in_=F3[:, c, :]),
                       reads=[f'F3_{c}'], writes=[f'SQ{c}'])

            if i == 0:
                for c in range(8):
                    op('pool', lambda e, c=c: e.memset(F2[:, c, 0:4], 0.0), writes=[f'F2_{c}'])
                op('pool', lambda e: e.memset(HST[:], 0.0), writes=['HST'])
            proj('w_in', 4, lambda kc: XN[:, kc, :], lambda kc: f'XN{kc}', 8, ev_in, order=(2, 3, 0, 1))
            rg, rkey = w_next('gates', 0)
            gb = VOFF['a_gate_b']
            for gi in range(2):
                for c in range(8):
                    h, j = c // 2, c % 2
                    bk = nbank()
                    for kc in range(2):
                        o = ((gi * 4 + h) * 2 + kc) * 256 + j * 128
                        op('pe', lambda e, o=o, h=h, kc=kc, bk=bk: e.matmul(ps[bk][:, :], lhsT=rg[:, o:o + 128], rhs=SQ[:, 2 * h + kc, :],
                                                                          start=(kc == 0), stop=(kc == 1)),
                           reads=[rkey, f'SQ{2*h+kc}'], writes=[f'ps{bk}'], signal=(kc == 1))
                    bc = VT[:, (gb + gi) * 8 + c:(gb + gi) * 8 + c + 1]
                    op('act', lambda e, gi=gi, c=c, bk=bk, bc=bc: e.activation(out=bhf(gi, c), in_=ps[bk][:, :], func=AF.Sigmoid, bias=bc),
                       reads=[f'ps{bk}', 'VT'], writes=gk(gi, c))
            for c in range(8):
                cc = CV2[:, c:c + 1]
                op('act', lambda e, c=c, cc=cc: e.activation(out=bhf(0, c), in_=bhf(0, c), func=AF.Exp, scale=cc),
                   reads=gk(0, c) + ['CV2'], writes=gk(0, c))
                op('dve', lambda e, c=c: e.tensor_tensor(out=F2[:, c, 4:T + 4], in0=bhf(0, c), in1=bhf(0, c), op=ALU.mult),
                   reads=gk(0, c) + [f'F2_{c}'], writes=[f'F2_{c}'])
            for c in range(8):
                op('act', lambda e, c=c: e.activation(out=F2[:, c, 4:T + 4], in_=F2[:, c, 4:T + 4], func=AF.Sqrt, scale=-1.0, bias=1.0),
                   reads=[f'F2_{c}'], writes=[f'F2_{c}'])
            for c in range(8):
                op('dve', lambda e, c=c: e.tensor_tensor(out=bhf(1, c), in0=bhf(1, c), in1=F2[:, c, 4:T + 4], op=ALU.mult),
                   reads=gk(1, c) + [f'F2_{c}'], writes=gk(1, c))
                op('pool', lambda e, c=c: e.tensor_tensor(out=bhf(1, c), in0=bhf(1, c), in1=F3[:, c, :], op=ALU.mult),
                   reads=gk(1, c) + [f'F3_{c}'], writes=gk(1, c))
                op('dve', lambda e, c=c: e.tensor_tensor_scan(out=F3[:, c, :], data0=bhf(0, c), data1=bhf(1, c), initial=HST[:, c:c + 1],
                                                             op0=ALU.mult, op1=ALU.add),
                   reads=gk(0, c) + gk(1, c) + ['HST', f'F3_{c}'], writes=[f'F3_{c}'])
                op('pool', lambda e, c=c: e.tensor_copy(out=HST[:, c:c + 1], in_=F3[:, c, T - 1:T]),
                   reads=[f'F3_{c}'], writes=['HST'])
                op('pool', lambda e, c=c: e.tensor_tensor(out=XN[:, c, :], in0=F3[:, c, :], in1=F1[:, c, :], op=ALU.mult),
                   reads=[f'F3_{c}', f'F1_{c}'], writes=[f'XN{c}'])
            proj('a_w_out', 2, lambda kc: XN[:, kc, :], lambda kc: f'XN{kc}', 8, evac_branch('a_b_out'))
            post_norm(1)

        def mem_prep(b, layers):
            MT = F1[:].rearrange("p c t -> p (c t)")[:, 0:8 * MEM].rearrange("p (c t) -> p c t", c=8)
            MN = XN[:].rearrange("p c t -> p (c t)")[:, 0:8 * MEM].rearrange("p (c t) -> p c t", c=8)
            MSQ = SQ[:].rearrange("p c t -> p (c t)")[:, 0:8 * MEM].rearrange("p (c t) -> p c t", c=8)
            f1k = [f'F1_{c}' for c in range(8)]
            xnk = [f'XN{c}' for c in range(8)]
            sqk = [f'SQ{c}' for c in range(8)]
            for tb in range(2):
                r0 = b * MEM + tb * 128
                xb = xin[0]
                S_.dma('sp', xb[:], mem_d[r0:r0 + 128, :], writes=['xin0'], key='xin0')
                for half in range(2):
                    bk = nbank()
                    for cl in range(4):
                        c = half * 4 + cl
                        op('pe', lambda e, xb=xb, c=c, cl=cl, bk=bk: e.transpose(out=ps[bk][:, cl * 128:(cl + 1) * 128],
                                                                                in_=xb[:, c * 128:(c + 1) * 128], identity=ident[:]),
                           reads=['xin0', 'ident'], writes=[f'ps{bk}'], signal=(cl == 3))
                    op('act', lambda e, half=half, tb=tb, bk=bk: e.activation(
                        out=MT[:, half * 4:half * 4 + 4, tb * 128:(tb + 1) * 128],
                        in_=ps[bk][:, :].rearrange("p (c t) -> p c t", c=4), func=AF.Copy),
                       reads=[f'ps{bk}'], writes=f1k)
            op('act', lambda e: e.activation(out=MSQ, in_=MT, func=AF.Square), reads=f1k, writes=sqk)
            bk = nbank()
            for c in range(8):
                op('pe', lambda e, c=c, bk=bk: e.matmul(ps[bk][:, 0:MEM], lhsT=onesb[:], rhs=MSQ[:, c, :], start=(c == 0), stop=(c == 7)),
                   reads=['onesb'] + sqk, writes=[f'ps{bk}'], signal=(c == 7))
            op('act', lambda e, bk=bk: e.activation(out=PT[3][:, 0:MEM], in_=ps[bk][:, 0:MEM], func=AF.Ln, scale=1.0 / D, bias=1e-6),
               reads=[f'ps{bk}'], writes=['PT3'])
            op('act', lambda e: e.activation(out=RS[:, 0:MEM], in_=PT[3][:, 0:MEM], func=AF.Exp, scale=-0.5), reads=['PT3'], writes=['RS'])
            for c in range(8):
                op('dve', lambda e, c=c: e.tensor_tensor(out=MN[:, c, :], in0=MT[:, c, :], in1=RS[:, 0:MEM], op=ALU.mult),
                   reads=f1k + ['RS'], writes=xnk)
            for l in layers:
                for pj in range(2):
                    rg, rkey = w_next(f'wkv{l}', pj)
                    for ml in range(4):
                        m = pj * 4 + ml
                        bk = nbank()
                        for kc in range(8):
                            op('pe', lambda e, rg=rg, kc=kc, ml=ml, bk=bk: e.matmul(
                                ps[bk][:, 0:MEM], lhsT=rg[:, kc * 512 + ml * 128:kc * 512 + (ml + 1) * 128], rhs=MN[:, kc, :],
                                start=(kc == 0), stop=(kc == 7)),
                               reads=[rkey] + xnk, writes=[f'ps{bk}'], signal=(kc == 7))
                        op('act', lambda e, l=l, m=m, bk=bk: e.activation(out=KT[l][:, m, :], in_=ps[bk][:, 0:MEM], func=AF.Copy),
                           reads=[f'ps{bk}'], writes=[f'KT{l}'])
                for pj in range(2):
                    rg, rkey = w_next(f'wkv{l}', 2 + pj)
                    for mc in range(2):
                        bk = nbank()
                        for kc in range(8):
                            op('pe', lambda e, rg=rg, kc=kc, mc=mc, bk=bk: e.matmul(
                                ps[bk][:, :], lhsT=MN[:, kc, mc * 128:(mc + 1) * 128], rhs=rg[:, kc * 512:(kc + 1) * 512],
                                start=(kc == 0), stop=(kc == 7)),
                               reads=[rkey] + xnk, writes=[f'ps{bk}'], signal=(kc == 7))
                        op('act', lambda e, l=l, mc=mc, pj=pj, bk=bk: e.activation(out=VV[l][:, mc, pj * 512:(pj + 1) * 512], in_=ps[bk][:, :], func=AF.Copy),
                           reads=[f'ps{bk}'], writes=[f'VV{l}'])

        def stage_C(l):
            for c in range(8):
                eng = 'pool' if c % 3 == 2 else 'dve'
                op(eng, lambda e, c=c: e.tensor_copy(out=XN[:, c, :], in_=X[:, c, :]), reads=[f'X{c}'], writes=[f'XN{c}'])
            op('act', lambda e: e.activation(out=SQ[:], in_=X[:], func=AF.Square),
               reads=[f'X{c}' for c in range(8)], writes=[f'SQ{c}' for c in range(8)])
            bR = nbank()
            for tb in range(4):
                for c in range(8):
                    op('pe', lambda e, tb=tb, c=c: e.matmul(ps[bR][:, tb:tb + 1], lhsT=SQ[:, c, tb * 128:(tb + 1) * 128], rhs=onesb[:, 0:1],
                                                         start=(c == 0), stop=(c == 7)),
                       reads=['onesb', f'SQ{c}'], writes=[f'ps{bR}'], signal=(tb == 3 and c == 7))
            op('act', lambda e: e.activation(out=SMX[:, 48:52], in_=ps[bR][:, 0:4], func=AF.Ln, scale=1.0 / D, bias=1e-6),
               reads=[f'ps{bR}'], writes=['RSt'])
            op('act', lambda e: e.activation(out=SMX[:, 48:52], in_=SMX[:, 48:52], func=AF.Exp, scale=-0.5), reads=['RSt'], writes=['RSt'])
            QT = bhb(0)
            PN = bhb(1)
            PTt = bhb(2)
            OT = bhb(3)

            def ev_q(m, p, pk):
                op('act', lambda e, m=m, p=p: e.activation(out=QT[:, m, :], in_=p, func=AF.Copy, scale=1.0 / 16.0),
                   reads=[pk], writes=[f'BH{m}'])
            proj(f'wq{l}', 2, lambda kc: XN[:, kc, :], lambda kc: f'XN{kc}', 8, ev_q)
            def sm_chain(tb):
                pn = PN[:, 2 * tb:2 * tb + 2, :].rearrange("p a t -> p (a t)")
                pex = F3[:, 2 * tb:2 * tb + 2, :].rearrange("p a t -> p (a t)")
                banks = [nbank(), nbank()]
                for h in range(4):
                    bk = banks[h // 2]
                    for dc in range(2):
                        op('pe', lambda e: e.matmul(
                            ps[bk][:, (h % 2) * 256:(h % 2 + 1) * 256], lhsT=QT[:, 2 * h + dc, tb * 128:(tb + 1) * 128],
                            rhs=KT[l][:, 2 * h + dc, :], start=(dc == 0), stop=(dc == 1)),
                           reads=[f'BH{2*h+dc}', f'KT{l}'], writes=[f'ps{bk}'], signal=(dc == 1))
                yield
                for hb in range(2):
                    bk = banks[hb]
                    op('dve', lambda e: e.tensor_reduce(
                        out=SMX[:, tb * 4 + 2 * hb:tb * 4 + 2 * hb + 2], in_=ps[bk][:, :].rearrange("p (h k) -> p h k", h=2),
                        axis=AX.X, op=ALU.max, negate=True),
                       reads=[f'ps{bk}'], writes=[f'SMXm{tb}'])
                op('dve', lambda e: e.tensor_scalar(out=SMX[:, 52 + tb * 4:56 + tb * 4], in0=SMX[:, tb * 4:tb * 4 + 4],
                                                   scalar1=SMX[:, 48 + tb:49 + tb], scalar2=None, op0=ALU.mult),
                   reads=[f'SMXm{tb}', 'RSt'], writes=[f'SMXn{tb}'])
                yield
                for h in range(4):
                    bk = banks[h // 2]
                    op('act', lambda e: e.activation(
                        out=pex[:, h * 256:(h + 1) * 256], in_=ps[bk][:, (h % 2) * 256:(h % 2 + 1) * 256], func=AF.Exp,
                        scale=SMX[:, 48 + tb:49 + tb], bias=SMX[:, 52 + tb * 4 + h:53 + tb * 4 + h],
                        accum_out=SMX[:, 16 + tb * 4 + h:16 + tb * 4 + h + 1]),
                       reads=[f'ps{bk}', f'SMXn{tb}', 'RSt'], writes=[f'F3_{2*tb}', f'F3_{2*tb+1}', f'SMXs{tb}'])
                yield
                op('dve', lambda e: e.reciprocal(out=SMX[:, 32 + tb * 4:32 + tb * 4 + 4], in_=SMX[:, 16 + tb * 4:16 + tb * 4 + 4]),
                   reads=[f'SMXs{tb}'], writes=[f'SMXr{tb}'])
                for h in range(4):
                    op('dve', lambda e: e.tensor_scalar(
                        out=pn[:, h * 256:(h + 1) * 256], in0=pex[:, h * 256:(h + 1) * 256],
                        scalar1=SMX[:, 32 + tb * 4 + h:32 + tb * 4 + h + 1], scalar2=None, op0=ALU.mult),
                       reads=[f'F3_{2*tb}', f'F3_{2*tb+1}', f'SMXr{tb}'], writes=[f'BH{8+2*tb}', f'BH{9+2*tb}'])
                yield
                bk = nbank()
                psb = ps[bk][:, :].bitcast(BF16)
                for hm in range(8):
                    op('pe', lambda e: e.transpose(out=psb[:, hm * 128:(hm + 1) * 128], in_=pn[:, hm * 128:(hm + 1) * 128], identity=identb[:]),
                       reads=[f'BH{8+2*tb}', f'BH{9+2*tb}', 'identb'], writes=[f'ps{bk}'], signal=(hm == 7))
                yield
                op('act', lambda e: e.activation(out=PTt[:, :, tb * 128:(tb + 1) * 128],
                                                 in_=psb.rearrange("p (a t) -> p a t", a=8), func=AF.Copy),
                   reads=[f'ps{bk}'], writes=[f'BH{16+a}' for a in range(8)])
                yield

            gens = [sm_chain(tb) for tb in range(4)]
            while gens:
                for g in list(gens):
                    try:
                        next(g)
                    except StopIteration:
                        gens.remove(g)
            for m in range(8):
                h = m // 2
                bk = nbank()
                for mc in range(2):
                    op('pe', lambda e, m=m, h=h, mc=mc, bk=bk: e.matmul(ps[bk][:, :], lhsT=VV[l][:, mc, m * 128:(m + 1) * 128],
                                                                      rhs=PTt[:, 2 * h + mc, :], start=(mc == 0), stop=(mc == 1)),
                       reads=[f'VV{l}', f'BH{16+2*h+mc}'], writes=[f'ps{bk}'], signal=(mc == 1))
                op('act', lambda e, m=m, bk=bk: e.activation(out=OT[:, m, :], in_=ps[bk][:, :], func=AF.Copy),
                   reads=[f'ps{bk}'], writes=[f'BH{24+m}'])
            proj(f'wo{l}', 2, lambda kc: OT[:, kc, :], lambda kc: f'BH{24+kc}', 8, evac_branch(None))
            post_norm(6 * l + 3)

        def stage_M(l):
            for c in range(8):
                eng = 'pool' if c % 3 == 2 else 'dve'
                op(eng, lambda e, c=c: e.tensor_copy(out=XN[:, c, :], in_=X[:, c, :]), reads=[f'X{c}'], writes=[f'XN{c}'])
            op('act', lambda e: e.activation(out=SQ[:], in_=X[:], func=AF.Square),
               reads=[f'X{c}' for c in range(8)], writes=[f'SQ{c}' for c in range(8)])
            b = nbank()
            for c in range(8):
                op('pe', lambda e, c=c, b=b: e.matmul(ps[b][:, :], lhsT=onesb[:], rhs=SQ[:, c, :], start=(c == 0), stop=(c == 7)),
                   reads=['onesb'] + [f'SQ{c}'], writes=[f'ps{b}'], signal=(c == 7))
            op('act', lambda e, b=b: e.activation(out=PT[3][:], in_=ps[b][:, :], func=AF.Ln, scale=1.0 / D, bias=1e-6),
               reads=[f'ps{b}'], writes=['PT3'])
            op('act', lambda e: e.activation(out=RS[:], in_=PT[3][:], func=AF.Exp, scale=-1.0), reads=['PT3'], writes=['RS'])
            cnt = [0]

            def ev_down(m, p, pk):
                op('dve', lambda e, m=m, p=p: e.tensor_tensor(out=F1[:, m, :], in0=p, in1=RS[:], op=ALU.mult),
                   reads=[pk, 'RS'], writes=[f'F1_{m}'])
                op('act', lambda e, m=m: e.activation(out=SQ[:, m, :], in_=F1[:, m, :], func=AF.Square),
                   reads=[f'F1_{m}'], writes=[f'SQ{m}'])

            def ev_up(m, p, pk):
                k = cnt[0] % 2
                cnt[0] += 1
                op('act', lambda e, p=p, k=k: e.activation(out=PT[k][:], in_=p, func=AF.Square), reads=[pk], writes=[f'PT{k}'])
                op('dve', lambda e, m=m, p=p, k=k: e.scalar_tensor_tensor(out=BH[:, m, :], in0=p, scalar=0.0, in1=PT[k][:],
                                                                          op0=ALU.is_gt, op1=ALU.mult),
                   reads=[pk, f'PT{k}'], writes=[f'BH{m}'])
            proj(f'up{l}', 8, lambda kc: XN[:, kc, :], lambda kc: f'XN{kc}', 8, ev_up)
            proj(f'down{l}', 8, lambda kc: BH[:, kc, :], lambda kc: f'BH{kc}', 32, ev_down, mper=1)
            post_norm(6 * l + 5)

        _rw = [0]

        def rw_alloc(n):
            o = _rw[0]
            _rw[0] += n
            return RW[:, o:o + n]
        TM = rw_alloc(2048)
        U4 = rw_alloc(2048)
        Pb = rw_alloc(512)
        AU = rw_alloc(512)
        Wt = rw_alloc(256)
        LWA = rw_alloc(512)
        PTb = rw_alloc(512)
        LG = PTb
        F3B = F3[:].rearrange("p c t -> p (c t)").bitcast(BF16)
        RH = F3B[:, 0:4096]
        GT = F3B[:, 4096:6144]
        HH = F3B[:, 6144:8192]
        ARf = BH[:, 0:16, :].rearrange("p a t -> p (a t)")
        f3k = [f'F3_{c}' for c in range(8)]

        def ar_kind(fc, kind):
            return ARf[:, fc * 1024:(fc + 1) * 1024].rearrange("p (c k t) -> p c k t", c=4, k=2)[:, :, kind, :]

        def stage_B(b, i):
            for c in range(8):
                if i == 0:
                    op('pool', lambda e, c=c: e.memset(XNP[:, c, 0:8], 0.0), writes=[f'XNP{c}'])
                else:
                    op('pool', lambda e, c=c: e.tensor_copy(out=XNP[:, c, 7:8], in_=XL[:, c:c + 1]), reads=['XL', f'XNP{c}'], writes=[f'XNP{c}'])
            if i == 0:
                op('pool', lambda e: e.memset(STt[:], 0.0), writes=['STt'])
            op('act', lambda e: e.activation(out=SQ[:], in_=X[:], func=AF.Square), reads=xkeys, writes=[f'SQ{c}' for c in range(8)])
            ones_norm(None)
            for c in range(8):
                eng = 'pool' if c % 3 == 2 else 'dve'
                op(eng, lambda e, c=c: e.tensor_tensor(out=XNP[:, c, 8:T + 8], in0=X[:, c, :], in1=RS[:], op=ALU.mult),
                   reads=[f'X{c}', 'RS'], writes=[f'XNP{c}'])

            def proj2(nameA, nameB, pj, evac):
                rgA, kA = w_next(nameA, pj)
                rgB, kB = w_next(nameB, pj)
                return rgA, kA, rgB, kB

            def mm16(rgA, kA, rgB, kB, col0, ncol, bk, MW):
                for v_, (rg, rk, off) in enumerate(((rgA, kA, 8), (rgB, kB, 7))):
                    for kc in range(8):
                        op('pe', lambda e, rg=rg, kc=kc, off=off, v_=v_: e.matmul(
                            ps[bk][0:ncol, :], lhsT=rg[:, kc * MW + col0:kc * MW + col0 + ncol], rhs=XNP[:, kc, off:off + T],
                            start=(v_ == 0 and kc == 0), stop=(v_ == 1 and kc == 7)),
                           reads=[rk, f'XNP{kc}'], writes=[f'ps{bk}'], signal=(v_ == 1 and kc == 7))

            if BSTOP[0] <= 0.1:
                return
            rgA, kA = w_next('loraAa', 0)
            rgB, kB = w_next('loraAb', 0)
            bk = nbank()
            mm16(rgA, kA, rgB, kB, 0, 128, bk, 256)
            op('act', lambda e, bk=bk: e.activation(out=LWA[0:64, :], in_=ps[bk][0:64, :], func=AF.Tanh), reads=[f'ps{bk}'], writes=['LWA0'])
            op('act', lambda e, bk=bk: e.activation(out=LWA[64:128, :], in_=ps[bk][64:128, :], func=AF.Copy), reads=[f'ps{bk}'], writes=['LWA1'])
            bk = nbank()
            mm16(rgA, kA, rgB, kB, 128, 128, bk, 256)
            op('act', lambda e, bk=bk: e.activation(out=LG[:, :], in_=ps[bk][:, :], func=AF.Sigmoid), reads=[f'ps{bk}'], writes=['PTb'])
            if BSTOP[0] <= 0.3:
                return
            rgL, kL = w_next('loraB', 0)
            def lo_chain(m):
                tt, kt_ = (PT[0], ['PT0']) if m % 2 == 0 else (PT[2], ['PT2'])
                bw, ba, bg = nbank(), nbank(), nbank()
                op('pe', lambda e: e.matmul(ps[bw][:, :], lhsT=rgL[0:64, m * 128:(m + 1) * 128], rhs=LWA[0:64, :], start=True, stop=True),
                   reads=[kL, 'LWA0'], writes=[f'ps{bw}'])
                op('pe', lambda e: e.matmul(ps[ba][:, :], lhsT=rgL[64:128, 1024 + m * 128:1024 + (m + 1) * 128], rhs=LWA[64:128, :], start=True, stop=True),
                   reads=[kL, 'LWA1'], writes=[f'ps{ba}'])
                op('pe', lambda e: e.matmul(ps[bg][:, :], lhsT=rgL[:, 2048 + m * 128:2048 + (m + 1) * 128], rhs=LG[:, :], start=True, stop=True),
                   reads=[kL, 'PTb'], writes=[f'ps{bg}'])
                yield
                op('act', lambda e: e.activation(out=tt[:], in_=ps[bw][:, :], func=AF.Sigmoid, bias=vcol('b_w0', 0, m)),
                   reads=[f'ps{bw}', 'VT'], writes=kt_)
                yield
                for c in range(4):
                    op('dve', lambda e, c=c: e.tensor_tensor_scan(out=F1[:, m, c * 128:(c + 1) * 128], data0=onesb[:], data1=tt[:, c * 128:(c + 1) * 128],
                                                                 initial=0.0, op0=ALU.mult, op1=ALU.add),
                       reads=kt_ + ['onesb'], writes=[f'F1_{m}'])
                op('act', lambda e: e.activation(out=F2[:, m, 4:T + 4], in_=ps[ba][:, :], func=AF.Sigmoid, bias=vcol('b_a0', 0, m)),
                   reads=[f'ps{ba}', 'VT'], writes=[f'F2_{m}'])
                yield
                op('pool', lambda e: e.tensor_tensor(out=F3[:, m, :], in0=F1[:, m, :], in1=tt[:], op=ALU.subtract),
                   reads=[f'F1_{m}'] + kt_, writes=[f'F3_{m}'])
                op('act', lambda e: e.activation(out=SQ[:, m, :], in_=ps[bg][:, :], func=AF.Copy), reads=[f'ps{bg}'], writes=[f'SQ{m}'])
                yield

            for m0 in range(0, 8, 2):
                gens = [lo_chain(m0), lo_chain(m0 + 1)]
                while gens:
                    for g in list(gens):
                        try:
                            next(g)
                        except StopIteration:
                            gens.remove(g)
            if BSTOP[0] <= 0.5:
                return
            op('act', lambda e: e.activation(out=GC[:].rearrange("p (a c) -> p a c", a=8),
                                             in_=F1[:].rearrange("p a (c t) -> p a c t", c=4)[:, :, :, 127], func=AF.Exp, scale=-0.6065306597126334),
               reads=[f'F1_{c}' for c in range(8)], writes=['GC'])
            omk = lambda m: DV[:, 96 + m:97 + m]
            if BSTOP[0] <= 0.6:
                return
            TMf = RW[:, 0:2048].bitcast(F32)
            tsets = [(PT[0][:], PT[1][:], PT[2][:], PT[3][:], PTb, ['PT0'], ['PT1'], ['PT2'], ['PT3'], ['PTb']),
                     (xin[0][:, 0:512], xin[0][:, 512:1024], TMf[:, 0:512], TMf[:, 512:1024], LWA, ['xa'], ['xb'], ['tma'], ['tmb'], ['LWA0', 'LWA1'])]
            op('pool', lambda e: e.memset(SCR[:, 3:4], 0.0), reads=[], writes=['xin0', 'TM', 'xa', 'xb', 'tma', 'tmb'])
            def rr(gens):
                while gens:
                    for g in list(gens):
                        try:
                            next(g)
                        except StopIteration:
                            gens.remove(g)

            def k_chain(m, bk):
                pk = f'ps{bk}'
                t0, t1, t2, t3, tq, k0, k1, k2, k3, kq = tsets[m % 2]
                kkc = vcol('b_k_k', 0, m)
                op('act', lambda e: e.activation(out=t0, in_=ps[bk][:, :], func=AF.Copy, scale=kkc), reads=[pk, 'VT'], writes=k0)
                op('act', lambda e: e.activation(out=tq, in_=ps[bk][:, :], func=AF.Square, scale=kkc), reads=[pk, 'VT'], writes=kq)
                yield
                b2 = nbank()
                op('pe', lambda e: e.matmul(ps[b2][:, :], lhsT=BOb[:], rhs=tq, start=True, stop=True), reads=['BOb'] + kq, writes=[f'ps{b2}'])
                op('act', lambda e: e.activation(out=t2, in_=F3[:, m, :], func=AF.Exp, scale=-0.6065306597126334), reads=[f'F3_{m}'], writes=k2)
                yield
                op('act', lambda e: e.activation(out=t1, in_=ps[b2][:, :], func=AF.Ln, bias=1e-24), reads=[f'ps{b2}'], writes=k1)
                yield
                op('act', lambda e: e.activation(out=t1, in_=t1, func=AF.Exp, scale=-0.5), reads=k1, writes=k1)
                op('act', lambda e: e.activation(out=t3, in_=F1[:, m, :], func=AF.Exp, scale=0.6065306597126334), reads=[f'F1_{m}'], writes=k3)
                yield
                op('dve', lambda e: e.tensor_tensor(out=t0, in0=t0, in1=t1, op=ALU.mult), reads=k0 + k1, writes=k0)
                yield
                op('dve', lambda e: e.scalar_tensor_tensor(out=ar_kind(m, 0), in0=t0.rearrange("p (c t) -> p c t", c=4), scalar=-1.0,
                                                           in1=t2.rearrange("p (c t) -> p c t", c=4), op0=ALU.mult, op1=ALU.mult),
                   reads=k0 + k2, writes=[f'BH{2*m}', f'BH{2*m+1}'])
                op('dve', lambda e: e.tensor_scalar(out=t1, in0=F2[:, m, 4:T + 4], scalar1=vcol('b_k_a', 0, m), scalar2=omk(m), op0=ALU.mult, op1=ALU.add),
                   reads=[f'F2_{m}', 'VT', 'DV'] + k1, writes=k1)
                yield
                op('pool', lambda e: e.tensor_tensor(out=t0, in0=t0, in1=F2[:, m, 4:T + 4], op=ALU.mult), reads=k0 + [f'F2_{m}'], writes=k0)
                yield
                op('dve', lambda e: e.tensor_tensor(out=BH[:, 16 + m, :], in0=t0, in1=t3, op=ALU.mult), reads=k0 + k3, writes=[f'BH{16+m}'])
                op('dve', lambda e: e.tensor_tensor(out=F2[:, m, 4:T + 4], in0=ps[bk][:, :], in1=t1, op=ALU.mult),
                   reads=[pk, f'F2_{m}'] + k1, writes=[f'F2_{m}'])
                yield
                op('pool', lambda e: e.tensor_tensor(out=BH[:, 24 + m, :], in0=F2[:, m, 4:T + 4], in1=t3, op=ALU.mult),
                   reads=[f'F2_{m}'] + k3, writes=[f'BH{24+m}'])
                yield

            def r_chain(m, bk):
                pk = f'ps{bk}'
                t0, t1, t2, t3, tq, k0, k1, k2, k3, kq = tsets[m % 2]
                op('act', lambda e: e.activation(out=t2, in_=F1[:, m, :], func=AF.Exp, scale=-0.6065306597126334), reads=[f'F1_{m}'], writes=k2)
                yield
                op('dve', lambda e: e.tensor_tensor(out=ar_kind(m, 1), in0=ps[bk][:, :].rearrange("p (c t) -> p c t", c=4),
                                                    in1=t2.rearrange("p (c t) -> p c t", c=4), op=ALU.mult),
                   reads=[pk] + k2, writes=[f'BH{2*m}', f'BH{2*m+1}'])
                op('dve', lambda e: e.scalar_tensor_tensor(out=tq, in0=ps[bk][:, :], scalar=vcol('b_r_k', 0, m), in1=F2[:, m, 4:T + 4],
                                                           op0=ALU.mult, op1=ALU.mult),
                   reads=[pk, 'VT', f'F2_{m}'] + kq, writes=kq)
                yield
                b2 = nbank()
                op('pe', lambda e: e.matmul(ps[b2][:, :], lhsT=BOb[:], rhs=tq, start=True, stop=True), reads=['BOb'] + kq, writes=[f'ps{b2}'])
                yield
                op('act', lambda e: e.activation(out=F2[:, m, 4:T + 4], in_=ps[b2][:, :], func=AF.Copy), reads=[f'ps{b2}'], writes=[f'F2_{m}'])
                yield

            for pj in (2, 3):
                rgA, kA = w_next('rkvA', pj)
                rgB, kB = w_next('rkvB', pj)
                for pr in range(2):
                    gens = []
                    for ml in (2 * pr, 2 * pr + 1):
                        m = (pj - 2) * 4 + ml
                        bk = nbank()
                        mm16(rgA, kA, rgB, kB, ml * 128, 128, bk, 512)
                        gens.append(k_chain(m, bk))
                    rr(gens)
            if BSTOP[0] <= 0.7:
                return
            for pj in (0, 1):
                rgA, kA = w_next('rkvA', pj)
                rgB, kB = w_next('rkvB', pj)
                for pr in range(2):
                    gens = []
                    for ml in (2 * pr, 2 * pr + 1):
                        m = pj * 4 + ml
                        bk = nbank()
                        mm16(rgA, kA, rgB, kB, ml * 128, 128, bk, 512)
                        gens.append(r_chain(m, bk))
                    rr(gens)
            op('pool', lambda e: e.memset(SCR[:, 4:5], 0.0), reads=[], writes=['xin0', 'TM', 'xa', 'xb', 'tma', 'tmb'])
            if BSTOP[0] <= 0.8:
                return
            for pj in (4, 5):
                rgA, kA = w_next('rkvA', pj)
                rgB, kB = w_next('rkvB', pj)
                for ml in range(4):
                    m = (pj - 4) * 4 + ml
                    bk = nbank()
                    mm16(rgA, kA, rgB, kB, ml * 128, 128, bk, 512)
                    pk = f'ps{bk}'
                    import os as _os
                    _pm = int(_os.environ.get('P5MODE', '0'))
                    if _pm in (0, 1):
                        op('dve', lambda e, m=m, bk=bk: e.tensor_copy(out=XN[:, m, :], in_=ps[bk][:, :]), reads=[pk], writes=[f'XN{m}'])
                    if _pm in (0, 2):
                        op('dve', lambda e, m=m, bk=bk: e.tensor_tensor(out=F2[:, m, 4:T + 4], in0=ps[bk][:, :], in1=F2[:, m, 4:T + 4], op=ALU.mult),
                           reads=[pk, f'F2_{m}'], writes=[f'F2_{m}'])
            if BSTOP[0] <= 1:
                return
            TM4 = TM.rearrange("p (c k f) -> p c k f", c=4, k=4)
            op('pool', lambda e: e.tensor_copy(out=XL[:].rearrange("p (c o) -> p c o", o=1), in_=XNP[:, :, T + 7:T + 8]), reads=[f'XNP{c}' for c in range(8)], writes=['XL'])
            XNPf = XNP[:].rearrange("p c t -> p (c t)")
            sets = []
            for si in range(2):
                TBh = [TB[j][:].bitcast(FP16) for j in range(5)]
                if si == 0:
                    d = dict(Q=[TBh[0], TBh[1]], kQ=['TB0', 'TB1'], B5=TBh[4][:, 0:512], kB5='TB4s0',
                             U4=U4, Pb=Pb, AU=AU, Wt=Wt, kS=['U4', 'Pb', 'AU', 'Wt'])
                else:
                    d = dict(Q=[TBh[2], TBh[3]], kQ=['TB2', 'TB3'], B5=TBh[4][:, 512:1024], kB5='TB4s1',
                             U4=XNPf[:, 0:2048], Pb=XNPf[:, 2048:2560], AU=XNPf[:, 2560:3072], Wt=XNPf[:, 3072:3328], kS=['U4b', 'Pbb', 'AUb', 'Wtb'])
                sets.append(d)
            hk1 = [k + f'h{hf}' for k in sets[1]['kS'][1:] for hf in range(2)]
            op('pool', lambda e: e.memset(SCR[:, 0:1], 0.0), reads=[], writes=[f'XNP{c}' for c in range(8)] + sets[1]['kS'] + hk1)

            def head_prologue(fc, hs, st):
                hsl = slice(64 * hs, 64 * hs + 64)
                U4_ = st['U4']
                kU = [st['kS'][0]]
                At = lambda c: ARf[hsl, fc * 1024 + c * 256:fc * 1024 + c * 256 + 128]
                ARc = lambda c: ARf[hsl, fc * 1024 + c * 256:fc * 1024 + (c + 1) * 256]
                Bt = lambda c: BH[hsl, 16 + fc, c * 128:(c + 1) * 128]
                Kt = lambda c: BH[hsl, 24 + fc, c * 128:(c + 1) * 128]
                kAR = [f'BH{2*fc}', f'BH{2*fc+1}']
                bA = nbank(); reserved.add(bA)
                for c in range(4):
                    op('pe', lambda e, c=c: e.matmul(ps[bA][:, c * 128:(c + 1) * 128], lhsT=At(c), rhs=Bt(c), start=True, stop=True),
                       reads=kAR + [f'BH{16+fc}'], writes=[f'ps{bA}'], signal=(c == 3))
                bAT = nbank(); reserved.add(bAT)
                for c in range(4):
                    op('pe', lambda e, c=c: e.matmul(ps[bAT][:, c * 128:(c + 1) * 128], lhsT=Bt(c), rhs=At(c), start=True, stop=True),
                       reads=kAR + [f'BH{16+fc}'], writes=[f'ps{bAT}'], signal=(c == 3))
                for c in range(4):
                    bB = nbank()
                    op('pe', lambda e, c=c, bB=bB: e.matmul(ps[bB][:, 128:256], lhsT=Bt(c), rhs=ARc(c)[:, 128:256], start=True, stop=True),
                       reads=kAR + [f'BH{16+fc}'], writes=[f'ps{bB}'], signal=False)
                    op('pe', lambda e, c=c, bB=bB: e.matmul(ps[bB][:, 256:512], lhsT=Kt(c), rhs=ARc(c), start=True, stop=True),
                       reads=kAR + [f'BH{24+fc}'], writes=[f'ps{bB}'])
                    op('dve', lambda e, c=c, bB=bB: e.tensor_tensor(out=U4_[:, c * 512 + 128:(c + 1) * 512], in0=ps[bB][:, 128:512], in1=MK2[:, 128:512], op=ALU.mult),
                       reads=[f'ps{bB}', 'MK2'], writes=kU)
                return bA, bAT

            def half_steps(fc, hs, st, hf, bA, bAT, done):
                hsl = slice(64 * hs, 64 * hs + 64)
                fsl = slice(64 * hs, 64 * hs + 64)
                cs_ = slice(hf * 256, (hf + 1) * 256)
                QQ = st['Q'][hf]
                XX, TP = QQ[:, 0:512], QQ[:, 512:1024]
                B1, B2 = XX[:, 0:256], XX[:, 256:512]
                B3, B4 = TP[:, 0:256], TP[:, 256:512]
                B5 = st['B5'][:, cs_]
                kx = st['kQ'][hf]
                k1, k2, k3, k4, k5 = [kx + 'a'], [kx + 'b'], [kx + 'c'], [kx + 'd'], [st['kB5'] + f'h{hf}']
                dt_ = FP16
                idm = None
                U44 = st['U4'].rearrange("p (u k t) -> p u k t", u=4, k=4)
                Pb_ = st['Pb'][:, hf * 256:(hf + 1) * 256]
                AU_ = st['AU'][:, hf * 256:(hf + 1) * 256]
                Wt_ = st['Wt'][:, hf * 128:(hf + 1) * 128]
                kU = [st['kS'][0]]
                kP, kAU, kW = [[k + f'h{hf}'] for k in st['kS'][1:]]
                kAR = [f'BH{2*fc}', f'BH{2*fc+1}']
                v2 = lambda t: t.rearrange("p (u t) -> p u t", u=2)
                mskb = lambda j: MSK[:, j:j + 1, :].to_broadcast([128, 2, 128])
                idb = ident[:].rearrange("p (o t) -> p o t", o=1).to_broadcast([128, 2, 128])
                W_ = lambda t: t
                held = []

                def gbank():
                    bk_ = nbank()
                    reserved.add(bk_)
                    held.append(bk_)
                    return bk_

                def gfree(bk_):
                    reserved.discard(bk_)
                    held.remove(bk_)

                def mm2(bk_, col0, lhs, rhs, rk, acc=None, acck=None, last=True):
                    for u in range(2):
                        us = slice(u * 128, (u + 1) * 128)
                        os_ = slice(col0 + u * 128, col0 + (u + 1) * 128)
                        if acc is not None:
                            op('pe', lambda e: e.matmul(ps[bk_][:, os_], lhsT=W_(idm), rhs=W_(acc[:, us]), start=True, stop=False),
                               reads=acck + ['ident'], writes=[f'ps{bk_}'], signal=False)
                        op('pe', lambda e: e.matmul(ps[bk_][:, os_], lhsT=W_(lhs[:, us]), rhs=W_(rhs[:, us]), start=(acc is None), stop=True),
                           reads=rk, writes=[f'ps{bk_}'], signal=(last and u == 1))

                def cp(eng, dst, kd, bk_, col0, n):
                    if eng == 'act':
                        op('act', lambda e: e.activation(out=W_(dst), in_=ps[bk_][:, col0:col0 + n], func=AF.Copy), reads=[f'ps{bk_}'], writes=kd)
                    else:
                        op('dve', lambda e: e.tensor_copy(out=W_(dst), in_=ps[bk_][:, col0:col0 + n]), reads=[f'ps{bk_}'], writes=kd)

                op('dve', lambda e: e.tensor_tensor(out=v2(W_(B1)), in0=v2(ps[bA][:, cs_]), in1=mskb(0), op=ALU.mult), reads=[f'ps{bA}', 'MSK'], writes=k1)
                op('dve', lambda e: e.tensor_tensor(out=v2(W_(B2)), in0=v2(ps[bAT][:, cs_]), in1=mskb(1), op=ALU.mult), reads=[f'ps{bAT}', 'MSK'], writes=k2)
                e3 = 'pool'
                op(e3, lambda e: e.tensor_tensor(out=W_(TP[:, :]).rearrange("p (u t) -> p u t", u=4), in0=XX[:, :].rearrange("p (u t) -> p u t", u=4),
                                                 in1=ident[:].rearrange("p (o t) -> p o t", o=1).to_broadcast([128, 4, 128]), op=ALU.add),
                   reads=k1 + k2 + ['ident'], writes=k3 + k4)
                yield
                def tn_from_pt():
                    bt_ = gbank()
                    for u in range(2):
                        us = slice(u * 128, (u + 1) * 128)
                        op('pe', lambda e: e.transpose(out=ps[bt_][:, :].bitcast(FP16)[:, us], in_=B4[:, us], identity=IDH[:]),
                           reads=k4 + ['IDH'], writes=[f'ps{bt_}'], signal=(u == 1))
                    return bt_

                for lev in range(3):
                    bk_ = gbank()
                    mm2(bk_, 0, B2, B1, k1 + k2, last=False)
                    mm2(bk_, 256, B1, B2, k1 + k2)
                    yield
                    cp('act', XX[:, :], k1 + k2, bk_, 0, 512)
                    gfree(bk_)
                    yield
                    bk_ = gbank()
                    mm2(bk_, 0, B1, B4, k1 + k4)
                    yield
                    op('dve', lambda e: e.tensor_tensor(out=W_(B4), in0=ps[bk_][:, 0:256], in1=B4, op=ALU.add), reads=[f'ps{bk_}'] + k4, writes=k4)
                    gfree(bk_)
                    yield
                for kl in range(1, 4):
                    op('dve', lambda e, kl=kl: e.tensor_tensor(out=v2(W_(B5)), in0=v2(ps[bA][:, cs_]), in1=mskb(2 * kl), op=ALU.mult), reads=[f'ps{bA}', 'MSK'], writes=k5)
                    bt_ = tn_from_pt()
                    yield
                    op('act', lambda e: e.activation(out=B3, in_=ps[bt_][:, :].bitcast(FP16)[:, 0:256], func=AF.Copy), reads=[f'ps{bt_}'], writes=k3)
                    gfree(bt_)
                    bz = gbank()
                    mm2(bz, 0, B5, B4, k5 + k4)
                    yield
                    cp('act', B1, k1, bz, 0, 256)
                    gfree(bz)
                    yield
                    bz = gbank()
                    mm2(bz, 0, B3, B1, k3 + k1)
                    yield
                    if kl < 3:
                        op('dve', lambda e: e.tensor_tensor(out=W_(B4), in0=ps[bz][:, 0:256], in1=B4, op=ALU.add), reads=[f'ps{bz}'] + k4, writes=k4)
                    else:
                        op('dve', lambda e: e.tensor_tensor(out=Pb_, in0=ps[bz][:, 0:256], in1=B4, op=ALU.add), reads=[f'ps{bz}'] + k4, writes=kP)
                    gfree(bz)
                    yield
                done.append(1)
                if len(done) == 2:
                    reserved.discard(bA); reserved.discard(bAT)
                us2 = [2 * hf, 2 * hf + 1]
                bW = gbank()
                for j, u in enumerate(us2):
                    op('pe', lambda e: e.matmul(ps[bW][:, j * 64:(j + 1) * 64], lhsT=U44[:, u, 2, :], rhs=TM4[:, u, 3, fsl], start=True, stop=True),
                       reads=kU + ['TM'], writes=[f'ps{bW}'], signal=(j == 1))
                yield
                op('act', lambda e: e.activation(out=Wt_, in_=ps[bW][:, 0:128], func=AF.Copy), reads=[f'ps{bW}'], writes=kW)
                gfree(bW)
                yield
                bU = gbank()
                for j, u in enumerate(us2):
                    op('pe', lambda e: e.matmul(ps[bU][:, j * 128:j * 128 + 64], lhsT=Pb_[:, j * 128:(j + 1) * 128], rhs=TM4[:, u, 0, fsl], start=True, stop=True),
                       reads=kP + ['TM'], writes=[f'ps{bU}'], signal=False)
                    op('pe', lambda e: e.matmul(ps[bU][:, j * 128 + 64:(j + 1) * 128], lhsT=Pb_[:, j * 128:(j + 1) * 128], rhs=Wt_[:, j * 64:(j + 1) * 64], start=True, stop=True),
                       reads=kP + kW, writes=[f'ps{bU}'], signal=(j == 1))
                yield
                op('act', lambda e: e.activation(out=AU_, in_=ps[bU][:, 0:256], func=AF.Copy), reads=[f'ps{bU}'], writes=kAU)
                gfree(bU)
                yield
                bRY = gbank()
                for j, u in enumerate(us2):
                    op('pe', lambda e: e.matmul(ps[bRY][hsl, j * 128:(j + 1) * 128], lhsT=AU_[:, j * 128:j * 128 + 64], rhs=U44[:, u, 1, :], start=True, stop=True),
                       reads=kAU + kU, writes=[f'ps{bRY}'], signal=False)
                for j, u in enumerate(us2):
                    op('pe', lambda e: e.matmul(ps[bRY][hsl, 256 + j * 128:256 + (j + 1) * 128], lhsT=AU_[:, j * 128 + 64:(j + 1) * 128], rhs=U44[:, u, 1, :], start=True, stop=False),
                       reads=kAU + kU, writes=[f'ps{bRY}'], signal=False)
                    op('pe', lambda e: e.matmul(ps[bRY][hsl, 256 + j * 128:256 + (j + 1) * 128], lhsT=TM4[:, u, 3, fsl], rhs=U44[:, u, 3, :], start=False, stop=True),
                       reads=['TM'] + kU, writes=[f'ps{bRY}'], signal=(j == 1))
                yield
                tsl = slice(fc * 512 + hf * 256, fc * 512 + (hf + 1) * 256)
                op('dve', lambda e: e.tensor_tensor(out=RH[hsl, tsl].rearrange("p (c t) -> p c t", c=2),
                                                    in0=ps[bRY][hsl, 0:256].rearrange("p (c t) -> p c t", c=2),
                                                    in1=ARf[hsl, fc * 1024 + hf * 512:fc * 1024 + (hf + 1) * 512].rearrange("p (c k t) -> p c k t", c=2, k=2)[:, :, 1, :], op=ALU.add),
                   reads=[f'ps{bRY}'] + kAR, writes=[f'RH{fc}_{hs}_{hf}'])
                op('dve', lambda e: e.tensor_copy(out=F1[hsl, fc, hf * 256:(hf + 1) * 256], in_=ps[bRY][hsl, 256:512]), reads=[f'ps{bRY}'], writes=[f'F1_{fc}'])
                gfree(bRY)
                yield
                bGH = gbank()
                for j, u in enumerate(us2):
                    op('pe', lambda e: e.matmul(ps[bGH][hsl, j * 64:(j + 1) * 64], lhsT=AU_[:, j * 128:j * 128 + 64], rhs=TM4[:, u, 1, fsl], start=True, stop=True),
                       reads=kAU + ['TM'], writes=[f'ps{bGH}'], signal=False)
                for j, u in enumerate(us2):
                    op('pe', lambda e: e.matmul(ps[bGH][hsl, 128 + j * 64:128 + (j + 1) * 64], lhsT=TM4[:, u, 1, fsl], rhs=AU_[:, j * 128 + 64:(j + 1) * 128], start=True, stop=False),
                       reads=kAU + ['TM'], writes=[f'ps{bGH}'], signal=False)
                    op('pe', lambda e: e.matmul(ps[bGH][hsl, 128 + j * 64:128 + (j + 1) * 64], lhsT=TM4[:, u, 2, fsl], rhs=TM4[:, u, 3, fsl], start=False, stop=True),
                       reads=['TM'], writes=[f'ps{bGH}'], signal=(j == 1))
                yield
                gsl = slice(fc * 256 + hf * 128, fc * 256 + (hf + 1) * 128)
                op('dve', lambda e: e.tensor_tensor(out=GT[hsl, gsl], in0=ps[bGH][hsl, 0:128], in1=ID2[hsl, 0:128], op=ALU.add),
                   reads=[f'ps{bGH}', 'ID2'], writes=[f'GT{fc}_{hs}_{hf}'])
                op('dve', lambda e: e.tensor_tensor(out=HH[hsl, fc * 256 + hf * 128:fc * 256 + (hf + 1) * 128].rearrange("p (u i) -> p u i", u=2),
                                                    in0=ps[bGH][hsl, 128:256].rearrange("p (u i) -> p u i", u=2),
                                                    in1=GC[hsl, fc * 4 + 2 * hf:fc * 4 + 2 * hf + 2].rearrange("p (u o) -> p u o", o=1).to_broadcast([64, 2, 64]), op=ALU.mult),
                   reads=[f'ps{bGH}', 'GC'], writes=[f'HH{fc}_{hs}_{hf}'])
                gfree(bGH)
                yield

            fine = [f'{n}{fc}_{hs}_{hf}' for n in ('RH', 'GT', 'HH') for fc in range(8) for hs in range(2) for hf in range(2)]
            op('pool', lambda e: e.memset(SCR[:, 1:2], 0.0), reads=[], writes=f3k + fine)
            for fc in range(8):
                srcs = [lambda c, fc=fc: ARf[:, fc * 1024 + c * 256:fc * 1024 + c * 256 + 128],
                        lambda c, fc=fc: BH[:, 16 + fc, c * 128:(c + 1) * 128],
                        lambda c, fc=fc: BH[:, 24 + fc, c * 128:(c + 1) * 128],
                        lambda c, fc=fc: XN[:, fc, c * 128:(c + 1) * 128]]
                skeys = [[f'BH{2*fc}', f'BH{2*fc+1}'], [f'BH{16+fc}'], [f'BH{24+fc}'], [f'XN{fc}']]
                for half in range(2):
                    bk = nbank()
                    psb = ps[bk][:, :].bitcast(BF16)
                    for cl in range(2):
                        c = half * 2 + cl
                        for kind in range(4):
                            o = (cl * 4 + kind) * 128
                            op('pe', lambda e, c=c, kind=kind, o=o, psb=psb, srcs=srcs: e.transpose(out=psb[:, o:o + 128], in_=srcs[kind](c), identity=identb[:]),
                               reads=skeys[kind] + ['identb'], writes=[f'ps{bk}'], signal=(cl == 1 and kind == 3))
                    op('act', lambda e, half=half, psb=psb: e.activation(out=TM[:, half * 1024:(half + 1) * 1024], in_=psb, func=AF.Copy),
                       reads=[f'ps{bk}'], writes=['TM'])
                gens = []
                for hs in range(2):
                    bA_, bAT_ = head_prologue(fc, hs, sets[hs])
                    done = []
                    for hf in range(2):
                        gens.append(half_steps(fc, hs, sets[hs], hf, bA_, bAT_, done))
                if _os0.environ.get('SEQG', '0') == '1':
                    for g in gens:
                        for _ in g:
                            pass
                    gens = []
                _hs = int(_os0.environ.get('HSTOP', '999'))
                _rounds = 0
                while gens:
                    if _rounds >= _hs:
                        reserved.clear()
                        break
                    _rounds += 1
                    for g in list(gens):
                        try:
                            next(g)
                        except StopIteration:
                            gens.remove(g)
            op('pool', lambda e: e.memset(SCR[:, 2:3], 0.0), reads=[], writes=f3k + fine + [f'XNP{c}' for c in range(8)] + sets[1]['kS'] + hk1)
            if BSTOP[0] <= 2:
                return
            for c in range(4):
                bY = [nbank(), nbank()]
                bZ = nbank()
                for fc in range(8):
                    for hs in range(2):
                        hsl = slice(64 * hs, 64 * hs + 64)
                        op('pe', lambda e, fc=fc, hsl=hsl, c=c: e.matmul(ps[bY[fc // 4]][hsl, (fc % 4) * 128:(fc % 4 + 1) * 128], lhsT=STt[hsl, fc, :],
                                                                        rhs=RH[hsl, fc * 512 + c * 128:fc * 512 + (c + 1) * 128], start=True, stop=True),
                           reads=['STt'] + f3k, writes=[f'ps{bY[fc // 4]}'], signal=(fc % 4 == 3 and hs == 1))
                for fc in range(8):
                    for hs in range(2):
                        hsl = slice(64 * hs, 64 * hs + 64)
                        op('pe', lambda e, fc=fc, hsl=hsl, c=c: e.matmul(ps[bZ][hsl, fc * 64:(fc + 1) * 64], lhsT=GT[hsl, fc * 256 + c * 64:fc * 256 + (c + 1) * 64],
                                                                        rhs=STt[hsl, fc, :], start=True, stop=True),
                           reads=['STt'] + f3k, writes=[f'ps{bZ}'], signal=(fc == 7 and hs == 1))
                for half in range(2):
                    op('dve', lambda e, half=half, c=c: e.tensor_tensor(out=F1[:, half * 4:half * 4 + 4, c * 128:(c + 1) * 128],
                                                                        in0=ps[bY[half]][:, :].rearrange("p (f t) -> p f t", f=4),
                                                                        in1=F1[:, half * 4:half * 4 + 4, c * 128:(c + 1) * 128], op=ALU.add),
                       reads=[f'ps{bY[half]}'] + [f'F1_{f}' for f in range(half * 4, half * 4 + 4)], writes=[f'F1_{f}' for f in range(half * 4, half * 4 + 4)])
                for fc in range(8):
                    op('dve', lambda e, fc=fc, c=c: e.scalar_tensor_tensor(out=STt[:, fc, :], in0=ps[bZ][:, fc * 64:(fc + 1) * 64], scalar=GC[:, fc * 4 + c:fc * 4 + c + 1],
                                                                           in1=HH[:, fc * 256 + c * 64:fc * 256 + (c + 1) * 64], op0=ALU.mult, op1=ALU.add),
                       reads=[f'ps{bZ}', 'GC'] + f3k, writes=['STt'])
            if BSTOP[0] <= 3:
                return
            def gn_chain(m):
                ta, tb_, tq = (PT[0], PT[1], PTb) if m % 2 == 0 else (PT[2], PT[3], LWA)
                ka, kb, kq = (['PT0'], ['PT1'], ['PTb']) if m % 2 == 0 else (['PT2'], ['PT3'], ['LWA0', 'LWA1'])
                op('act', lambda e: e.activation(out=tq, in_=F1[:, m, :], func=AF.Copy), reads=[f'F1_{m}'], writes=kq)
                yield
                b1 = nbank()
                op('pe', lambda e: e.matmul(ps[b1][:, :], lhsT=BO64[:], rhs=tq, start=True, stop=True), reads=['BO64'] + kq, writes=[f'ps{b1}'])
                yield
                op('dve', lambda e: e.tensor_tensor(out=ta[:], in0=F1[:, m, :], in1=ps[b1][:, :], op=ALU.subtract), reads=[f'F1_{m}', f'ps{b1}'], writes=ka)
                yield
                op('act', lambda e: e.activation(out=tq, in_=ta[:], func=AF.Square), reads=ka + kq, writes=kq)
                yield
                b2 = nbank()
                op('pe', lambda e: e.matmul(ps[b2][:, :], lhsT=BO64[:], rhs=tq, start=True, stop=True), reads=['BO64'] + kq, writes=[f'ps{b2}'])
                yield
                op('act', lambda e: e.activation(out=tb_[:], in_=ps[b2][:, :], func=AF.Ln, bias=64e-5), reads=[f'ps{b2}'], writes=kb)
                yield
                op('act', lambda e: e.activation(out=tb_[:], in_=tb_[:], func=AF.Exp, scale=-0.5), reads=kb, writes=kb)
                yield
                op('dve', lambda e: e.scalar_tensor_tensor(out=ta[:], in0=ta[:], scalar=vcol('b_gn_g', 0, m), in1=tb_[:], op0=ALU.mult, op1=ALU.mult),
                   reads=ka + kb + ['VT'], writes=ka)
                yield
                op('dve', lambda e: e.scalar_tensor_tensor(out=ta[:], in0=ta[:], scalar=vcol('b_gn_b', 0, m), in1=F2[:, m, 4:T + 4], op0=ALU.add, op1=ALU.add),
                   reads=ka + ['VT', f'F2_{m}'], writes=ka)
                yield
                op('dve', lambda e: e.tensor_tensor(out=BH[:, m, :], in0=ta[:], in1=SQ[:, m, :], op=ALU.mult), reads=ka + [f'SQ{m}'], writes=[f'BH{m}'])
                yield

            for m0 in range(0, 8, 2):
                gens = [gn_chain(m0), gn_chain(m0 + 1)]
                while gens:
                    for g in list(gens):
                        try:
                            next(g)
                        except StopIteration:
                            gens.remove(g)
            proj('b_w_o', 2, lambda kc: BH[:, kc, :], lambda kc: f'BH{kc}', 8, evac_branch(None))
            post_norm(7)

        for b in range(NB):
            if nstage >= 2:
                mem_prep(b, [0, 1] if nstage >= 5 else [0])
            for i in range(NT):
                load_tile(b, i)
                if nstage >= 1:
                    stage_A(b, i)
                if nstage >= 2:
                    stage_C(0)
                if nstage >= 3:
                    stage_M(0)
                if nstage >= 4:
                    stage_B(b, i)
                if nstage >= 5:
                    stage_C(1)
                if nstage >= 6:
                    stage_M(1)
                store_tile(b, i)
        S_.finish('sp', ['y'])
        for k in ('xin0', 'ptio0', 'ptio1'):
            if k in S_.dsem:
                S_.ops['sp'].append(lambda e, semh=S_.dsem[k], v=S_.dcnt[k]: e.wait_ge(semh, v))
        S_.emit()
        build.nops = S_.nops
    return nc


def make_masks():
    t = np.arange(128)[:, None]
    s_ = np.arange(128)[None, :]
    low = t > s_
    ms = []
    m0 = low & (t // 16 == s_ // 16)
    ms += [m0, m0.T]
    for blk in (16, 32, 64):
        mk = (t // (2 * blk) == s_ // (2 * blk)) & ((t // blk) % 2 == 1) & ((s_ // blk) % 2 == 0)
        ms += [mk, mk.T]
    return np.ascontiguousarray(np.stack(ms, axis=1).astype(np.float32).reshape(128, 8 * 128))


def _prep_inputs(inp):
    vecs = pack_vecs(inp)
    wts = pack_weights(inp)
    return vecs, wts


def kernel(**inputs):
    NB, S = 4, 2048
    x = np.asarray(inputs['x'], np.float32)
    mem = np.asarray(inputs['mem'], np.float32)
    vecs, wts = _prep_inputs(inputs)
    nc = build(NB, S)
    in_maps = []
    for c in range(8):
        in_maps.append({"x": np.ascontiguousarray(x[c * NB:(c + 1) * NB].reshape(NB * S, D)),
                        "mem": np.ascontiguousarray(mem[c * NB:(c + 1) * NB].reshape(NB * MEM, D)),
                        "vecs": vecs, "wts": wts, "masks": make_masks()})
    res = run_bass_kernel_spmd(nc, in_maps, core_ids=list(range(8)))
    out = np.concatenate([r["y"].reshape(NB, S, D) for r in res.results], axis=0)
    return out.astype(np.float32)
```
